# Optimizing a Trainium2 kernel written in Bass

```python
import math
import jax
import jax.numpy as jnp
from jax import lax
import numpy as np

D_MODEL = 1024
BATCH = 2
SEQ = 8192
DEPTH = 2
DEC_BATCH = 16
DEC_SEQ = 32
PAST_LEN = 2048

CHUNK = 64
N_META = 16
Q_BLOCK = 128
EPS = 1e-6
N_EVEN = (DEPTH + 1) // 2
N_ODD = DEPTH // 2
D_FF = 4 * D_MODEL
DH = 64

D_S5 = D_MODEL // 2
S5_GROUP = 16
G_S5 = D_S5 // S5_GROUP
N_S5 = 64
H_FOX = D_MODEL // (2 * DH)
D_FOX = H_FOX * DH
D_SSM = D_MODEL // 2
P_SSM = 64
H_SSM = D_SSM // P_SSM
G_SSM = 2
N_SSM = 128
CONV_W = 4
CONV_DIM = D_SSM + 2 * G_SSM * N_SSM
H_SB = D_MODEL // (2 * DH)
D_SB = H_SB * DH

IN_AB = D_S5 + 3 * D_FOX + H_FOX
IN_CD = D_SSM + CONV_DIM + H_SSM + 3 * D_SB
MIX_AB = D_S5 + D_FOX
MIX_CD = D_SSM + D_SB

kernel_name = 'hybrid_streaming_encoder_step'


def rmsnorm(x, g):
    x32 = x.astype(jnp.float32)
    y = x32 * lax.rsqrt(jnp.mean(x32 * x32, axis=-1, keepdims=True) + EPS)
    return (y * g.astype(jnp.float32)).astype(x.dtype)


def swiglu(x, w_gate, w_up, w_down):
    return (jax.nn.silu(x @ w_gate) * (x @ w_up)) @ w_down


def ffn_half(h, g_pre, g_post, w_gate, w_up, w_down):
    return h + 0.5 * rmsnorm(swiglu(rmsnorm(h, g_pre), w_gate, w_up, w_down), g_post)


def sweep_query_blocks(block_fn, q_arrays, q_pos):
    lq = q_pos.shape[0]
    blk = min(Q_BLOCK, lq)
    nblk = -(-lq // blk)
    pad = nblk * blk - lq

    def to_blocks(a):
        a = jnp.pad(a, [(0, 0), (0, pad)] + [(0, 0)] * (a.ndim - 2))
        return jnp.moveaxis(a.reshape((a.shape[0], nblk, blk) + a.shape[2:]), 1, 0)

    blocks = tuple(to_blocks(a) for a in q_arrays)
    pos = jnp.pad(q_pos, (0, pad), mode='edge').reshape(nblk, blk)
    out = jnp.moveaxis(lax.map(block_fn, (blocks, pos)), 0, 1)
    out = out.reshape((out.shape[0], nblk * blk) + out.shape[3:])
    return out[:, :lq]


def fox_attention(q, k, v, f_q, f_k, q_pos, k_pos):
    scale = DH ** -0.5
    fk = jnp.swapaxes(f_k, 1, 2)

    def block(args):
        (qb, fqb), pb = args
        s = jnp.einsum('bqhd,bkhd->bhqk', qb, k, preferred_element_type=jnp.float32) * scale
        s = s + jnp.swapaxes(fqb, 1, 2)[..., None] - fk[:, :, None, :]
        mask = k_pos[None, :] <= pb[:, None]
        p = jax.nn.softmax(jnp.where(mask, s, -jnp.inf), axis=-1)
        return jnp.einsum('bhqk,bkhd->bqhd', p.astype(v.dtype), v)

    return sweep_query_blocks(block, (q, f_q), q_pos)


def stick_breaking_attention(q, k, v, q_pos, k_pos):
    scale = DH ** -0.5

    def block(args):
        (qb,), pb = args
        z = jnp.einsum('bqhd,bkhd->bhqk', qb, k, preferred_element_type=jnp.float32) * scale
        mask = k_pos[None, :] < pb[:, None]
        log_keep = jnp.where(mask, jax.nn.log_sigmoid(-z), 0.0)
        later = lax.cumsum(log_keep, axis=3, reverse=True) - log_keep
        w = jnp.where(mask, jnp.exp(jax.nn.log_sigmoid(z) + later), 0.0)
        return jnp.einsum('bhqk,bkhd->bqhd', w.astype(v.dtype), v)

    return sweep_query_blocks(block, (q,), q_pos)


def s5_mixer(u, h0_re, h0_im, a_re, a_im, log_dt, b_re, b_im, c_re, c_im, d_skip, w_glu, b_glu):
    f32 = jnp.float32
    bsz, L, _ = u.shape
    ug = u.reshape(bsz, L, G_S5, S5_GROUP).astype(f32)
    a_re = a_re.astype(f32)
    a_im = a_im.astype(f32)
    dt = jnp.exp(log_dt.astype(f32))[:, None]
    mag = jnp.exp(dt * a_re)
    ab_re = mag * jnp.cos(dt * a_im)
    ab_im = mag * jnp.sin(dt * a_im)
    den = a_re * a_re + a_im * a_im
    nr = ab_re - 1.0
    coef_re = (nr * a_re + ab_im * a_im) / den
    coef_im = (ab_im * a_re - nr * a_im) / den
    b_re = b_re.astype(f32)
    b_im = b_im.astype(f32)
    bb_re = coef_re[..., None] * b_re - coef_im[..., None] * b_im
    bb_im = coef_re[..., None] * b_im + coef_im[..., None] * b_re
    bu_re = jnp.einsum('gnc,blgc->blgn', bb_re, ug)
    bu_im = jnp.einsum('gnc,blgc->blgn', bb_im, ug)
    h0r = h0_re.astype(f32)
    h0i = h0_im.astype(f32)
    bu_re = bu_re.at[:, 0].add(ab_re * h0r - ab_im * h0i)
    bu_im = bu_im.at[:, 0].add(ab_re * h0i + ab_im * h0r)
    ar = jnp.broadcast_to(ab_re, bu_re.shape)
    ai = jnp.broadcast_to(ab_im, bu_im.shape)

    def combine(e1, e2):
        a1r, a1i, b1r, b1i = e1
        a2r, a2i, b2r, b2i = e2
        return (a1r * a2r - a1i * a2i, a1r * a2i + a1i * a2r,
                a2r * b1r - a2i * b1i + b2r, a2r * b1i + a2i * b1r + b2i)

    _, _, h_re, h_im = lax.associative_scan(combine, (ar, ai, bu_re, bu_im), axis=1)
    y = (jnp.einsum('gcn,blgn->blgc', c_re.astype(f32), h_re)
         - jnp.einsum('gcn,blgn->blgc', c_im.astype(f32), h_im)
         + d_skip.astype(f32) * ug).reshape(bsz, L, D_S5)
    g = jax.nn.gelu(y)
    out = g * jax.nn.sigmoid(g @ w_glu.astype(f32) + b_glu.astype(f32))
    return out.astype(u.dtype), h_re[:, -1], h_im[:, -1]


def ssd_scan(x, dt, a, bm, cm, h0):
    f32 = jnp.float32
    bsz, L = x.shape[:2]
    pad = (-L) % CHUNK

    def pad_left(t):
        return jnp.pad(t.astype(f32), [(0, 0), (pad, 0)] + [(0, 0)] * (t.ndim - 2))

    x, dt, bm, cm = pad_left(x), pad_left(dt), pad_left(bm), pad_left(cm)
    nc = (L + pad) // CHUNK
    e = H_SSM // G_SSM
    xc = x.reshape(bsz, nc, CHUNK, G_SSM, e, P_SSM)
    dtc = dt.reshape(bsz, nc, CHUNK, G_SSM, e)
    bc = bm.reshape(bsz, nc, CHUNK, G_SSM, N_SSM)
    cc = cm.reshape(bsz, nc, CHUNK, G_SSM, N_SSM)
    adt = dtc * a.reshape(G_SSM, e)
    xdt = xc * dtc[..., None]
    acs = jnp.cumsum(adt, axis=2)
    seg = acs[:, :, :, None] - acs[:, :, None, :]
    tri = jnp.tril(jnp.ones((CHUNK, CHUNK), bool))[:, :, None, None]
    lmat = jnp.where(tri, jnp.exp(jnp.where(tri, seg, 0.0)), 0.0)
    cb = jnp.einsum('bctgn,bcsgn->bctsg', cc, bc)
    y_diag = jnp.einsum('bctsg,bctsge,bcsgep->bctgep', cb, lmat, xdt)
    decay_states = jnp.exp(acs[:, :, -1:] - acs)
    chunk_states = jnp.einsum('bclgn,bclge,bclgep->bcgepn', bc, decay_states, xdt)
    chunk_decay = jnp.exp(acs[:, :, -1])

    def step(h, inp):
        s_c, d_c = inp
        return h * d_c[..., None, None] + s_c, h

    h0g = h0.astype(f32).reshape(bsz, G_SSM, e, P_SSM, N_SSM)
    h_final, h_prev = lax.scan(step, h0g, (jnp.moveaxis(chunk_states, 1, 0), jnp.moveaxis(chunk_decay, 1, 0)))
    h_prev = jnp.moveaxis(h_prev, 0, 1)
    y_off = jnp.einsum('bclgn,bcgepn,bclge->bclgep', cc, h_prev, jnp.exp(acs))
    y = (y_diag + y_off).reshape(bsz, nc * CHUNK, H_SSM, P_SSM)[:, pad:]
    return y, h_final.reshape(bsz, H_SSM, P_SSM, N_SSM)


def ssd_mixer(z, xbc, dt_raw, conv_buf, h0, conv_w, conv_b, dt_bias, a_log, d_skip, norm_g):
    f32 = jnp.float32
    bsz, L, _ = xbc.shape
    xp = jnp.concatenate([conv_buf.astype(xbc.dtype), xbc], axis=1)
    conv = sum((xp[:, w:w + L] * conv_w[w] for w in range(CONV_W)), conv_b)
    new_buf = xp[:, L:]
    act = jax.nn.silu(conv)
    xs = act[..., :D_SSM].reshape(bsz, L, H_SSM, P_SSM)
    bm = act[..., D_SSM:D_SSM + G_SSM * N_SSM].reshape(bsz, L, G_SSM, N_SSM)
    cm = act[..., D_SSM + G_SSM * N_SSM:].reshape(bsz, L, G_SSM, N_SSM)
    dt = jax.nn.softplus(dt_raw.astype(f32) + dt_bias.astype(f32))
    a = -jnp.exp(a_log.astype(f32))
    y, h_new = ssd_scan(xs, dt, a, bm, cm, h0)
    y = y + d_skip.astype(f32)[:, None] * xs.astype(f32)
    y = y.reshape(bsz, L, D_SSM) * jax.nn.silu(z.astype(f32))
    yg = y.reshape(bsz, L, G_SSM, D_SSM // G_SSM)
    yg = yg * lax.rsqrt(jnp.mean(yg * yg, axis=-1, keepdims=True) + EPS)
    y = yg.reshape(bsz, L, D_SSM) * norm_g.astype(f32)
    return y.astype(z.dtype), h_new, new_buf


def mixer_ab(xn, s5_re0, s5_im0, past_k, past_v, past_logf, w_in, b_f, a_re, a_im, log_dt,
             b_re, b_im, c_re, c_im, d_skip, w_glu, b_glu, w_out):
    bsz, L, _ = xn.shape
    n_past = past_k.shape[1]
    proj = xn @ w_in
    u = proj[..., :D_S5]
    q, k, v = jnp.split(proj[..., D_S5:D_S5 + 3 * D_FOX], 3, axis=-1)
    q = q.reshape(bsz, L, H_FOX, DH)
    k = k.reshape(bsz, L, H_FOX, DH)
    v = v.reshape(bsz, L, H_FOX, DH)
    logf = jax.nn.log_sigmoid((proj[..., D_S5 + 3 * D_FOX:] + b_f).astype(jnp.float32))
    s5_out, s5_re, s5_im = s5_mixer(u, s5_re0, s5_im0, a_re, a_im, log_dt, b_re, b_im, c_re, c_im,
                                    d_skip, w_glu, b_glu)
    k_all = jnp.concatenate([past_k.astype(k.dtype), k], axis=1)
    v_all = jnp.concatenate([past_v.astype(v.dtype), v], axis=1)
    f_cum = jnp.cumsum(jnp.concatenate([past_logf.astype(jnp.float32), logf], axis=1), axis=1)
    k_pos = jnp.arange(n_past + L)
    fox = fox_attention(q, k_all, v_all, f_cum[:, n_past:], f_cum, k_pos[n_past:], k_pos)
    mixed = jnp.concatenate([s5_out, fox.reshape(bsz, L, D_FOX).astype(s5_out.dtype)], axis=-1)
    return mixed @ w_out, s5_re, s5_im, k, v, logf


def mixer_cd(xn, ssd_h0, conv_buf, past_k, past_v, w_in, conv_w, conv_b, dt_bias, a_log, d_skip,
             norm_g, w_out):
    bsz, L, _ = xn.shape
    n_past = past_k.shape[1]
    proj = xn @ w_in
    z = proj[..., :D_SSM]
    xbc = proj[..., D_SSM:D_SSM + CONV_DIM]
    dt_raw = proj[..., D_SSM + CONV_DIM:D_SSM + CONV_DIM + H_SSM]
    q, k, v = jnp.split(proj[..., D_SSM + CONV_DIM + H_SSM:], 3, axis=-1)
    q = q.reshape(bsz, L, H_SB, DH)
    k = k.reshape(bsz, L, H_SB, DH)
    v = v.reshape(bsz, L, H_SB, DH)
    ssd_out, ssd_h, new_buf = ssd_mixer(z, xbc, dt_raw, conv_buf, ssd_h0, conv_w, conv_b, dt_bias,
                                        a_log, d_skip, norm_g)
    k_all = jnp.concatenate([past_k.astype(k.dtype), k], axis=1)
    v_all = jnp.concatenate([past_v.astype(v.dtype), v], axis=1)
    k_pos = jnp.arange(n_past + L)
    sb = stick_breaking_attention(q, k_all, v_all, k_pos[n_past:], k_pos)
    mixed = jnp.concatenate([ssd_out, sb.reshape(bsz, L, D_SB).astype(ssd_out.dtype)], axis=-1)
    return mixed @ w_out, ssd_h, new_buf, k, v


def trunk(h, s5_re, s5_im, fox_k, fox_v, fox_logf, ssd_h, conv_buf, sb_k, sb_v,
          norm_ffn1_pre, norm_ffn1_post, norm_mix_pre, norm_mix_post, norm_ffn2_pre, norm_ffn2_post,
          ffn1_w_gate, ffn1_w_up, ffn1_w_down, ffn2_w_gate, ffn2_w_up, ffn2_w_down,
          ab_w_in, fox_b_f, s5_a_re, s5_a_im, s5_log_dt, s5_b_re, s5_b_im, s5_c_re, s5_c_im, s5_d,
          s5_w_glu, s5_b_glu, ab_w_out,
          cd_w_in, ssd_conv_w, ssd_conv_b, ssd_dt_bias, ssd_a_log, ssd_d, ssd_norm, cd_w_out):
    ab_states, cd_states = [], []
    for i in range(DEPTH):
        j = i // 2
        h = ffn_half(h, norm_ffn1_pre[i], norm_ffn1_post[i], ffn1_w_gate[i], ffn1_w_up[i], ffn1_w_down[i])
        xn = rmsnorm(h, norm_mix_pre[i])
        if i % 2 == 0:
            out, *st = mixer_ab(xn, s5_re[j], s5_im[j], fox_k[j], fox_v[j], fox_logf[j], ab_w_in[j],
                                fox_b_f[j], s5_a_re[j], s5_a_im[j], s5_log_dt[j], s5_b_re[j], s5_b_im[j],
                                s5_c_re[j], s5_c_im[j], s5_d[j], s5_w_glu[j], s5_b_glu[j], ab_w_out[j])
            ab_states.append(st)
        else:
            out, *st = mixer_cd(xn, ssd_h[j], conv_buf[j], sb_k[j], sb_v[j], cd_w_in[j], ssd_conv_w[j],
                                ssd_conv_b[j], ssd_dt_bias[j], ssd_a_log[j], ssd_d[j], ssd_norm[j],
                                cd_w_out[j])
            cd_states.append(st)
        h = h + rmsnorm(out, norm_mix_post[i])
        h = ffn_half(h, norm_ffn2_pre[i], norm_ffn2_post[i], ffn2_w_gate[i], ffn2_w_up[i], ffn2_w_down[i])
    ab = [jnp.stack(z) for z in zip(*ab_states)]
    cd = [jnp.stack(z) for z in zip(*cd_states)]
    return (h, *ab, *cd)


def setup_inputs(seed: int = 0) -> dict:
    key = jax.random.key(seed)
    ks = iter(jax.random.split(key, 64))
    f32 = jnp.float32

    def nrm(shape, scale=1.0):
        return scale * jax.random.normal(next(ks), shape, f32)

    def gain(shape):
        return 1.0 + nrm(shape, 0.05)

    def log_uniform(shape, lo, hi):
        return jax.random.uniform(next(ks), shape, f32, math.log(lo), math.log(hi))

    dt_init = jnp.exp(log_uniform((N_ODD, H_SSM), 1e-3, 1e-1))
    return {
        'x_prompt': nrm((BATCH, SEQ, D_MODEL)),
        'x_sample': nrm((DEC_BATCH, DEC_SEQ, D_MODEL)),
        'state_s5_re': nrm((N_EVEN, DEC_BATCH, G_S5, N_S5), 0.3),
        'state_s5_im': nrm((N_EVEN, DEC_BATCH, G_S5, N_S5), 0.3),
        'cache_fox_k': nrm((N_EVEN, DEC_BATCH, PAST_LEN, H_FOX, DH)),
        'cache_fox_v': nrm((N_EVEN, DEC_BATCH, PAST_LEN, H_FOX, DH)),
        'cache_fox_logf': jax.nn.log_sigmoid(2.0 + nrm((N_EVEN, DEC_BATCH, PAST_LEN, H_FOX))),
        'state_ssd': nrm((N_ODD, DEC_BATCH, H_SSM, P_SSM, N_SSM), 0.1),
        'state_conv': nrm((N_ODD, DEC_BATCH, CONV_W - 1, CONV_DIM)),
        'cache_sb_k': nrm((N_ODD, DEC_BATCH, PAST_LEN, H_SB, DH)),
        'cache_sb_v': nrm((N_ODD, DEC_BATCH, PAST_LEN, H_SB, DH)),
        'meta_tokens': nrm((N_META, D_MODEL)),
        'norm_ffn1_pre': gain((DEPTH, D_MODEL)),
        'norm_ffn1_post': gain((DEPTH, D_MODEL)),
        'norm_mix_pre': gain((DEPTH, D_MODEL)),
        'norm_mix_post': gain((DEPTH, D_MODEL)),
        'norm_ffn2_pre': gain((DEPTH, D_MODEL)),
        'norm_ffn2_post': gain((DEPTH, D_MODEL)),
        'ffn1_w_gate': nrm((DEPTH, D_MODEL, D_FF), D_MODEL ** -0.5),
        'ffn1_w_up': nrm((DEPTH, D_MODEL, D_FF), D_MODEL ** -0.5),
        'ffn1_w_down': nrm((DEPTH, D_FF, D_MODEL), D_FF ** -0.5),
        'ffn2_w_gate': nrm((DEPTH, D_MODEL, D_FF), D_MODEL ** -0.5),
        'ffn2_w_up': nrm((DEPTH, D_MODEL, D_FF), D_MODEL ** -0.5),
        'ffn2_w_down': nrm((DEPTH, D_FF, D_MODEL), D_FF ** -0.5),
        'ab_w_in': nrm((N_EVEN, D_MODEL, IN_AB), D_MODEL ** -0.5),
        'fox_b_f': 2.0 + nrm((N_EVEN, H_FOX), 0.1),
        's5_a_re': -0.5 + nrm((N_EVEN, G_S5, N_S5), 0.01),
        's5_a_im': jnp.pi * jnp.arange(N_S5, dtype=f32) + nrm((N_EVEN, G_S5, N_S5), 0.01),
        's5_log_dt': log_uniform((N_EVEN, G_S5), 1e-3, 1e-1),
        's5_b_re': nrm((N_EVEN, G_S5, N_S5, S5_GROUP), (2 * S5_GROUP) ** -0.5),
        's5_b_im': nrm((N_EVEN, G_S5, N_S5, S5_GROUP), (2 * S5_GROUP) ** -0.5),
        's5_c_re': nrm((N_EVEN, G_S5, S5_GROUP, N_S5), N_S5 ** -0.5),
        's5_c_im': nrm((N_EVEN, G_S5, S5_GROUP, N_S5), N_S5 ** -0.5),
        's5_d': nrm((N_EVEN, G_S5, S5_GROUP)),
        's5_w_glu': nrm((N_EVEN, D_S5, D_S5), D_S5 ** -0.5),
        's5_b_glu': nrm((N_EVEN, D_S5), 0.01),
        'ab_w_out': nrm((N_EVEN, MIX_AB, D_MODEL), MIX_AB ** -0.5),
        'cd_w_in': nrm((N_ODD, D_MODEL, IN_CD), D_MODEL ** -0.5),
        'ssd_conv_w': nrm((N_ODD, CONV_W, CONV_DIM), CONV_W ** -0.5),
        'ssd_conv_b': nrm((N_ODD, CONV_DIM), 0.01),
        'ssd_dt_bias': dt_init + jnp.log(-jnp.expm1(-dt_init)),
        'ssd_a_log': jnp.log(jax.random.uniform(next(ks), (N_ODD, H_SSM), f32, 1.0, 16.0)),
        'ssd_d': 1.0 + nrm((N_ODD, H_SSM), 0.1),
        'ssd_norm': gain((N_ODD, D_SSM)),
        'cd_w_out': nrm((N_ODD, MIX_CD, D_MODEL), MIX_CD ** -0.5),
    }


def reference(x_prompt, x_sample, state_s5_re, state_s5_im, cache_fox_k, cache_fox_v, cache_fox_logf,
              state_ssd, state_conv, cache_sb_k, cache_sb_v, meta_tokens,
              norm_ffn1_pre, norm_ffn1_post, norm_mix_pre, norm_mix_post, norm_ffn2_pre, norm_ffn2_post,
              ffn1_w_gate, ffn1_w_up, ffn1_w_down, ffn2_w_gate, ffn2_w_up, ffn2_w_down,
              ab_w_in, fox_b_f, s5_a_re, s5_a_im, s5_log_dt, s5_b_re, s5_b_im, s5_c_re, s5_c_im, s5_d,
              s5_w_glu, s5_b_glu, ab_w_out,
              cd_w_in, ssd_conv_w, ssd_conv_b, ssd_dt_bias, ssd_a_log, ssd_d, ssd_norm, cd_w_out):
    weights = (norm_ffn1_pre, norm_ffn1_post, norm_mix_pre, norm_mix_post, norm_ffn2_pre, norm_ffn2_post,
               ffn1_w_gate, ffn1_w_up, ffn1_w_down, ffn2_w_gate, ffn2_w_up, ffn2_w_down,
               ab_w_in, fox_b_f, s5_a_re, s5_a_im, s5_log_dt, s5_b_re, s5_b_im, s5_c_re, s5_c_im, s5_d,
               s5_w_glu, s5_b_glu, ab_w_out,
               cd_w_in, ssd_conv_w, ssd_conv_b, ssd_dt_bias, ssd_a_log, ssd_d, ssd_norm, cd_w_out)
    bp = x_prompt.shape[0]
    dtp = x_prompt.dtype
    meta = jnp.broadcast_to(meta_tokens.astype(dtp)[None], (bp, N_META, D_MODEL))
    h_p = jnp.concatenate([meta, x_prompt], axis=1)
    (h_p, s5_re_p, s5_im_p, fox_k_p, fox_v_p, fox_logf_p, ssd_p, conv_p, sb_k_p, sb_v_p) = trunk(
        h_p,
        jnp.zeros((N_EVEN, bp, G_S5, N_S5), jnp.float32),
        jnp.zeros((N_EVEN, bp, G_S5, N_S5), jnp.float32),
        jnp.zeros((N_EVEN, bp, 0, H_FOX, DH), dtp),
        jnp.zeros((N_EVEN, bp, 0, H_FOX, DH), dtp),
        jnp.zeros((N_EVEN, bp, 0, H_FOX), jnp.float32),
        jnp.zeros((N_ODD, bp, H_SSM, P_SSM, N_SSM), jnp.float32),
        jnp.zeros((N_ODD, bp, CONV_W - 1, CONV_DIM), dtp),
        jnp.zeros((N_ODD, bp, 0, H_SB, DH), dtp),
        jnp.zeros((N_ODD, bp, 0, H_SB, DH), dtp),
        *weights)
    y_prompt = h_p[:, N_META:]
    (y_sample, s5_re_s, s5_im_s, fox_k_s, fox_v_s, fox_logf_s, ssd_s, conv_s, sb_k_s, sb_v_s) = trunk(
        x_sample, state_s5_re, state_s5_im, cache_fox_k, cache_fox_v, cache_fox_logf,
        state_ssd, state_conv, cache_sb_k, cache_sb_v, *weights)
    return (y_prompt, y_sample,
            s5_re_p, s5_im_p, fox_k_p, fox_v_p, fox_logf_p, ssd_p, conv_p, sb_k_p, sb_v_p,
            s5_re_s, s5_im_s, fox_k_s, fox_v_s, fox_logf_s, ssd_s, conv_s, sb_k_s, sb_v_s)
```

```python
import numpy as np
import concourse.bass as bass
import concourse.mybir as mybir
from concourse.bass_utils import run_bass_kernel_spmd

F32 = mybir.dt.float32
BF16 = mybir.dt.bfloat16
AF = mybir.ActivationFunctionType
ALU = mybir.AluOpType

NCORES = 8
D = 1024
DFF = 4096
TP = 2048
NT = 2128
NTT = 17
EPS = 1e-6
COLT = [(0, 512), (512, 512), (1024, 512), (1536, 512), (2048, 80)]


def tt_rows(t):
    return 128 if t < 16 else 80


class Tl:
    __slots__ = ("ap", "w", "r", "name", "excl")

    def __init__(self, ap, name="", excl=False):
        self.ap = ap
        self.w = None
        self.r = {}
        self.name = name
        self.excl = excl


class Eng:
    def __init__(self, nc, h, name, compute=True):
        self.h = h
        self.name = name
        self.sem = nc.alloc_semaphore("prog_" + name) if compute else None
        self.cnt = 0
        self.known = {}


class Ctx:
    def __init__(self, nc):
        self.nc = nc
        self.pe = Eng(nc, nc.tensor, "pe")
        self.act = Eng(nc, nc.scalar, "act")
        self.dve = Eng(nc, nc.vector, "dve")
        self.pool = Eng(nc, nc.gpsimd, "pool")
        self.sp = Eng(nc, nc.sync, "sp", compute=False)
        self.engs = [self.pe, self.act, self.dve, self.pool, self.sp]
        self.dsems = [[nc.alloc_semaphore(f"dma{i}"), 0] for i in range(56)]
        self.dnext = 0
        self.ccsem = nc.alloc_semaphore("ccsem")
        self.cccnt = 0
        self.uid = 0
        self.out_events = []

    def init_arena(self, n32):
        self.arena = self.nc.alloc_sbuf_tensor("arena", [128, n32], F32).ap()
        self.arena_n = n32
        self.arena_off = 0
        self.banks = [Tl(self.nc.alloc_psum_tensor(f"bank{i}", [128, 512], F32).ap(), f"bank{i}", excl=True) for i in range(8)]

    def sb(self, shape, dt=F32, name=None, hreg=False):
        shape = list(shape)
        esz = 4 if dt == F32 else 2
        n = 1
        for d in shape[1:]:
            n *= d
        n32 = (n * esz + 31) // 32 * 8
        if hreg:
            off = self.h_off
            assert off + n32 <= self.h_end, (name, off, n32, self.h_end)
            self.h_off = off + n32
        else:
            off = self.arena_off
            assert off + n32 <= self.arena_n, (name, off, n32, self.arena_n)
            self.arena_off = off + n32
        ap = self.arena[0:shape[0], off:off + (n * esz) // 4]
        if dt != F32:
            ap = ap.bitcast(dt)
        if len(shape) == 3:
            ap = ap.rearrange("p (a b) -> p a b", a=shape[1])
        elif len(shape) == 4:
            ap = ap.rearrange("p (a b c) -> p a b c", a=shape[1], b=shape[2])
        elif len(shape) == 5:
            ap = ap.rearrange("p (a b c d) -> p a b c d", a=shape[1], b=shape[2], c=shape[3])
        return ap

    def ps(self, shape, dt=F32, name=None):
        self.uid += 1
        return self.nc.alloc_psum_tensor(f"{name or 'ps'}_{self.uid}", list(shape), dt).ap()

    def tile(self, shape, dt=F32, name=None, hreg=False):
        return Tl(self.sb(shape, dt, name, hreg), name or "")

    def ptile(self, shape, dt=F32, name=None):
        return Tl(self.ps(shape, dt, name), name or "")

    def _need(self, E, waits, ev, same_ok):
        if ev is None:
            return
        sem, val, src = ev
        if same_ok and src is E:
            return
        k = sem.num
        if E.known.get(k, 0) >= val:
            return
        if waits.get(k, (None, 0))[1] < val:
            waits[k] = (sem, val)

    def _deps(self, E, reads, writes):
        waits = {}
        for t in reads:
            self._need(E, waits, t.w, False)
            if t.excl:
                for ev in t.r.values():
                    self._need(E, waits, ev, True)
        for t in writes:
            self._need(E, waits, t.w, True)
            for ev in t.r.values():
                self._need(E, waits, ev, True)
        for k, (sem, val) in waits.items():
            E.h.wait_ge(sem, val)
            E.known[k] = val

    def _commit(self, ev, reads, writes):
        k = ev[0].num
        for t in reads:
            t.r[k] = ev
        for t in writes:
            t.w = ev
            t.r = {}

    def op(self, E, fn, reads=(), writes=()):
        self._deps(E, reads, writes)
        ins = fn(E.h)
        E.cnt += 1
        ins.then_inc(E.sem, 1)
        self._commit((E.sem, E.cnt, E), reads, writes)

    def dma(self, Q, out_ap, in_ap, reads=(), writes=(), is_output=False):
        slot = self.dsems[self.dnext]
        self.dnext = (self.dnext + 1) % len(self.dsems)
        sem, tot = slot
        if tot > 0 and Q.known.get(sem.num, 0) < tot:
            Q.h.wait_ge(sem, tot)
            Q.known[sem.num] = tot
        self._deps(Q, reads, writes)
        Q.h.dma_start(out=out_ap, in_=in_ap).then_inc(sem, 16)
        slot[1] = tot + 16
        ev = (sem, tot + 16, None)
        self._commit(ev, reads, writes)
        if is_output:
            self.out_events.append(ev)
        return ev

    def allgather(self, in_t, out_t, groups):
        import os
        if os.environ.get("NOCC") == "1":
            return
        Q = self.pool
        self._deps(Q, [in_t], [out_t])
        Q.h.collective_compute("AllGather", ALU.bypass, replica_groups=groups,
                               ins=[in_t.ap.opt()], outs=[out_t.ap.opt()]).then_inc(self.ccsem)
        self.cccnt += 1
        self._commit((self.ccsem, self.cccnt, None), [in_t], [out_t])

    def barrier(self):
        for E in self.engs:
            for O in self.engs:
                if O is E or O.sem is None or O.cnt == 0:
                    continue
                if E.known.get(O.sem.num, 0) < O.cnt:
                    E.h.wait_ge(O.sem, O.cnt)
                    E.known[O.sem.num] = O.cnt
            for sem, tot in self.dsems:
                if tot > 0 and E.known.get(sem.num, 0) < tot:
                    E.h.wait_ge(sem, tot)
                    E.known[sem.num] = tot
            if self.cccnt and E.known.get(self.ccsem.num, 0) < self.cccnt:
                E.h.wait_ge(self.ccsem, self.cccnt)
                E.known[self.ccsem.num] = self.cccnt

    def finish(self):
        Q = self.sp
        for sem, val, _ in self.out_events:
            if Q.known.get(sem.num, 0) < val:
                Q.h.wait_ge(sem, val)
                Q.known[sem.num] = val


GROUPS = [(list(range(0, 8)), [(0, 512), (512, 512)], 0, 1024),
          (list(range(8, 17)), [(1024, 512), (1536, 512), (2048, 80)], 1024, 1104)]


class Builder:
    def __init__(self, stop=None):
        self.stop = stop
        self.nc = bass.Bass("TRN2", target_bir_lowering=False)
        self.cx = Ctx(self.nc)
        self.dram_in = {}
        self.dram_out = {}

    def din(self, name, shape, dt=F32):
        ap = self.nc.dram_tensor(name, list(shape), dt, kind="ExternalInput").ap()
        self.dram_in[name] = (tuple(shape), dt)
        return ap

    def dout(self, name, shape, dt=F32):
        ap = self.nc.dram_tensor(name, list(shape), dt, kind="ExternalOutput").ap()
        self.dram_out[name] = tuple(shape)
        return ap

    def build(self):
        cx = self.cx
        nc = self.nc
        self.xin = self.din("xin", [NT, D])
        self.cst = self.din("cst", [128, 1408])
        self.norms = self.din("norms", [12, D])
        self.wg = [[self.din(f"w_gate{f}_{l}", [D, DFF]) for l in range(2)] for f in range(2)]
        self.wu = [[self.din(f"w_up{f}_{l}", [D, DFF]) for l in range(2)] for f in range(2)]
        self.wd = [[self.din(f"w_down{f}_{l}", [DFF, D]) for l in range(2)] for f in range(2)]
        self.y = self.dout("y", [NT, D])
        self.c_foxk = self.din("c_foxk", [2, 2048, 512])
        self.c_foxv = self.din("c_foxv", [2, 2048, 512])
        self.c_foxf = self.din("c_foxf", [2, 2048, 8])
        self.declare_ab()
        self.declare_cd()

        cx.init_arena(53184)
        cx.h_base = cx.arena_off
        hbig = cx.sb([128, NTT, D], F32, "h")
        cx.h_end = cx.arena_off
        cx.h_off = cx.h_base
        self.h = [Tl(hbig[:, t, :], f"h{t}") for t in range(NTT)]
        self.cstt = cx.tile([128, 1408], F32, "cst")
        self.identb = cx.tile([128, 128], BF16, "identb")
        self.gpre = cx.tile([128, D], F32, "gpre")
        self.gpost = cx.tile([128, D], F32, "gpost")
        self.neghalf = cx.tile([128, 1], F32, "neghalf")
        self.nrm_ss = [cx.tile([128, 1], F32, "ss") for _ in range(2)]
        self.nrm_rstd = [cx.tile([128, 1], F32, "rstd") for _ in range(2)]
        self.mark = cx.arena_off
        cx.dma(cx.sp, self.cstt.ap, self.cst, writes=[self.cstt])
        self.ident = Tl(self.cstt.ap[:, 0:128], "ident")
        self.ident.w = self.cstt.w
        cx.op(cx.dve, lambda e: e.tensor_copy(out=self.identb.ap, in_=self.cstt.ap[:, 0:128]), [self.cstt], [self.identb])
        cx.op(cx.dve, lambda e: e.memset(self.neghalf.ap, -0.5), [], [self.neghalf])
        for t in range(NTT):
            r = tt_rows(t)
            cx.dma(cx.sp, self.h[t].ap[0:r, :], self.xin[t * 128:t * 128 + r, :], writes=[self.h[t]])

        self.alloc_ffn()
        self.steps = []
        n_ffn = {"ffn1_l0": 1, "ab0": 1, "ab1a": 1, "ab1b": 1, "ab1c": 1, "ab1": 1, "s5": 1, "fox": 1, "mix_l0": 1, "ffn2_l0": 2, "ffn1_l1": 3, "mix_l1": 3, None: 4}[self.stop]
        for fi in range(n_ffn):
            for g in range(2):
                for fb in range(8):
                    self.steps.append((fi % 2, fi // 2, g, fb))
        self.step_i = 0
        self.load_gu(0)
        self.load_d(0)
        si = 0
        while si < len(self.steps):
            which, layer, g, fb = self.steps[si]
            if which == 1:
                cx.barrier()
                self.alloc_ffn()
                self.load_gu(si)
                self.load_d(si)
            self.ffn(which, layer)
            si += 16
            if which == 0 and layer == 1:
                if self.stop == "ffn1_l1":
                    break
                self.mixer_cd()
                if self.stop == "mix_l1":
                    break
            if which == 0 and layer == 0:
                if self.stop == "ffn1_l0":
                    break
                self.mixer_ab()
                if self.stop in ("mix_l0", "ab0", "ab1a", "ab1b", "ab1c", "ab1", "s5", "fox"):
                    break
        self.write_y()
        cx.finish()
        return nc

    def load_gain(self, tl, idx):
        self.cx.dma(self.cx.sp, tl.ap, self.norms[idx:idx + 1, :].partition_broadcast(128), writes=[tl])

    def rms_stats(self, src, r, par):
        cx = self.cx
        ss, rstd = self.nrm_ss[par], self.nrm_rstd[par]
        cx.op(cx.dve, lambda e: e.scalar_tensor_tensor(out=self.junk.ap[0:r, :], in0=src.ap[0:r, :], scalar=1.0,
                                                       in1=src.ap[0:r, :], op0=ALU.mult, op1=ALU.mult,
                                                       accum_out=ss.ap[0:r, :]), [src], [self.junk, ss])
        cx.op(cx.dve, lambda e: e.tensor_scalar(out=ss.ap[0:r, :], in0=ss.ap[0:r, :], scalar1=1.0 / D,
                                                scalar2=EPS, op0=ALU.mult, op1=ALU.add), [ss], [ss])
        cx.op(cx.pool, lambda e: e.tensor_tensor(out=rstd.ap[0:r, :], in0=ss.ap[0:r, :],
                                                 in1=self.neghalf.ap[0:r, :], op=ALU.pow),
              [ss, self.neghalf], [rstd])
        return rstd

    def prenorm_T(self, tiles, gain, xnT, col_base):
        cx = self.cx
        for t in tiles:
            r = tt_rows(t)
            ht = self.h[t]
            xb = self.nrm_xb[t % 2]
            rstd = self.rms_stats(ht, r, t % 2)
            cx.op(cx.dve, lambda e: e.scalar_tensor_tensor(out=xb.ap[0:r, :], in0=ht.ap[0:r, :],
                                                           scalar=rstd.ap[0:r, :], in1=gain.ap[0:r, :],
                                                           op0=ALU.mult, op1=ALU.mult),
                  [ht, rstd, gain], [xb])
            bank = cx.banks[t % 2]
            pt = bank.ap.bitcast(BF16).rearrange("p (k c) -> p k c", k=8)
            for k in range(8):
                cx.op(cx.pe, lambda e: e.transpose(out=pt[:, k, 0:r], in_=xb.ap[0:r, k * 128:(k + 1) * 128],
                                                   identity=self.identb.ap[0:r, 0:r]),
                      [xb, self.identb], [bank])
            c0 = t * 128 - col_base
            cx.op(cx.act, lambda e: e.copy(out=xnT.ap[:, :, c0:c0 + r], in_=pt[:, :, 0:r]), [bank], [xnT])

    def alloc_ffn(self):
        cx = self.cx
        cx.arena_off = self.mark
        self.junk = cx.tile([128, D], F32, "junk")
        self.tmpn = cx.tile([128, D], F32, "tmpn")
        self.nrm_xb = [cx.tile([128, D], BF16, "xb") for _ in range(2)]
        self.wgb = [cx.tile([128, 8, 512], BF16, "wgb") for _ in range(2)]
        self.wub = [cx.tile([128, 8, 512], BF16, "wub") for _ in range(2)]
        self.wdb = [cx.tile([128, 4, D], BF16, "wdb") for _ in range(1)]
        self.xnT = cx.tile([128, 8, 1104], BF16, "xnT")
        self.hid = cx.tile([128, 4, 1104], BF16, "hid")
        accbig = cx.sb([128, 9, D], F32, "acc")
        self.acc = [Tl(accbig[:, t, :], f"acc{t}") for t in range(9)]
        self.sil = [cx.tile([128, 512], F32, "sil") for _ in range(2)]

    def load_gu(self, si):
        if si >= len(self.steps):
            return
        cx = self.cx
        which, layer, g, fb = self.steps[si]
        b = si % 2
        wgv = self.wg[which][layer].rearrange("(k p) f -> p k f", p=128)
        wuv = self.wu[which][layer].rearrange("(k p) f -> p k f", p=128)
        cx.dma(cx.pool, self.wgb[b].ap, wgv[:, :, fb * 512:(fb + 1) * 512], writes=[self.wgb[b]])
        cx.dma(cx.pool, self.wub[b].ap, wuv[:, :, fb * 512:(fb + 1) * 512], writes=[self.wub[b]])

    def load_d(self, si):
        if si >= len(self.steps):
            return
        cx = self.cx
        which, layer, g, fb = self.steps[si]
        wdv = self.wd[which][layer].rearrange("(c p) d -> p c d", p=128)
        cx.dma(cx.pool, self.wdb[0].ap, wdv[:, fb * 4:(fb + 1) * 4, :], writes=[self.wdb[0]])

    def ffn(self, which, layer):
        cx = self.cx
        g_pre = (0 if which == 0 else 4) * 2 + layer
        g_post = (1 if which == 0 else 5) * 2 + layer
        self.load_gain(self.gpre, g_pre)
        self.load_gain(self.gpost, g_post)
        cnt = 0
        for g, (tiles, colts, cbase, ncols) in enumerate(GROUPS):
            self.prenorm_T(tiles, self.gpre, self.xnT, cbase)
            for fb in range(8):
                si = self.step_i
                b = si % 2
                nxt_ok = si + 1 < len(self.steps) and not (self.steps[si][0] == 0 and self.steps[si + 1][0] == 1)
                if nxt_ok:
                    self.load_gu(si + 1)
                wgb, wub, wdb, hid = self.wgb[b], self.wub[b], self.wdb[0], self.hid
                for (c0g, cw) in colts:
                    c0 = c0g - cbase
                    for c in range(4):
                        gp, up, sl = cx.banks[2 + cnt % 2], cx.banks[4 + cnt % 2], self.sil[cnt % 2]
                        cnt += 1
                        for k in range(8):
                            cx.op(cx.pe, lambda e: e.matmul(gp.ap[:, 0:cw], lhsT=wgb.ap[:, k, c * 128:(c + 1) * 128],
                                                            rhs=self.xnT.ap[:, k, c0:c0 + cw], start=(k == 0), stop=(k == 7)),
                                  [wgb, self.xnT], [gp])
                        for k in range(8):
                            cx.op(cx.pe, lambda e: e.matmul(up.ap[:, 0:cw], lhsT=wub.ap[:, k, c * 128:(c + 1) * 128],
                                                            rhs=self.xnT.ap[:, k, c0:c0 + cw], start=(k == 0), stop=(k == 7)),
                                  [wub, self.xnT], [up])
                        cx.op(cx.act, lambda e: e.activation(out=sl.ap[:, 0:cw], in_=gp.ap[:, 0:cw], func=AF.Silu), [gp], [sl])
                        cx.op(cx.dve, lambda e: e.tensor_tensor(out=hid.ap[:, c, c0:c0 + cw], in0=sl.ap[:, 0:cw],
                                                                in1=up.ap[:, 0:cw], op=ALU.mult), [sl, up], [hid])
                for ti, t in enumerate(tiles):
                    r = tt_rows(t)
                    a = self.acc[ti]
                    for hf in range(2):
                        op_ = cx.banks[6 + (ti * 2 + hf) % 2]
                        for c in range(4):
                            cx.op(cx.pe, lambda e: e.matmul(op_.ap[0:r, :], lhsT=hid.ap[:, c, ti * 128:ti * 128 + r],
                                                            rhs=wdb.ap[:, c, hf * 512:(hf + 1) * 512],
                                                            start=(c == 0), stop=(c == 3)),
                                  [hid, wdb], [op_])
                        if fb == 0:
                            cx.op(cx.dve, lambda e: e.tensor_copy(out=a.ap[0:r, hf * 512:(hf + 1) * 512], in_=op_.ap[0:r, :]),
                                  [op_], [a])
                        else:
                            cx.op(cx.dve, lambda e: e.tensor_tensor(out=a.ap[0:r, hf * 512:(hf + 1) * 512],
                                                                    in0=a.ap[0:r, hf * 512:(hf + 1) * 512],
                                                                    in1=op_.ap[0:r, :], op=ALU.add), [op_, a], [a])
                if nxt_ok:
                    self.load_d(si + 1)
                self.step_i += 1
            self.postnorm_add(tiles, self.acc, self.gpost, 0.5)

    def postnorm_add(self, tiles, src, gain, coef):
        cx = self.cx
        for ti, t in enumerate(tiles):
            r = tt_rows(t)
            a = src[ti]
            ht = self.h[t]
            rstd = self.rms_stats(a, r, t % 2)
            cx.op(cx.dve, lambda e: e.scalar_tensor_tensor(out=self.tmpn.ap[0:r, :], in0=a.ap[0:r, :],
                                                           scalar=rstd.ap[0:r, :], in1=gain.ap[0:r, :],
                                                           op0=ALU.mult, op1=ALU.mult),
                  [a, rstd, gain], [self.tmpn])
            cx.op(cx.dve, lambda e: e.scalar_tensor_tensor(out=ht.ap[0:r, :], in0=self.tmpn.ap[0:r, :], scalar=coef,
                                                           in1=ht.ap[0:r, :], op0=ALU.mult, op1=ALU.add),
                  [self.tmpn, ht], [ht])

    def declare_ab(self):
        self.w_ab_in = self.din("ab_w_in", [D, 2056])
        self.fox_bf = self.din("fox_b_f", [1, 8])
        self.s5p = self.din("s5p", [128, 48])
        self.s5b = self.din("s5b", [128, 16 * 2 * 128])
        self.s5c = self.din("s5c", [128, 16 * 2 * 128])
        self.s5d = self.din("s5d", [128, 4 * 128])
        self.s5init = self.din("s5init", [128, 64])
        self.w_glu = self.din("s5_w_glu", [512, 512])
        self.b_glu = self.din("s5_b_glu", [128, 4])
        self.w_ab_out = self.din("ab_w_out", [D, D])
        self.sel = self.din("sel", [128, 32])
        self.o_s5st = self.dout("o_s5st", [128, 96])
        self.o_foxk = self.dout("o_foxk", [NT, 512])
        self.o_foxv = self.dout("o_foxv", [NT, 512])
        self.o_foxf = self.dout("o_foxf", [NT, 8])
        self.hsave = self.nc.dram_tensor("hsave", [NT, D], F32).ap()
        self.s5F_send = Tl(self.nc.dram_tensor("s5F_send", [128, 32], F32).ap(), "s5F_send")
        self.s5F_g = Tl(self.nc.dram_tensor("s5F_g", [512, 32], F32).ap(), "s5F_g")
        self.kT_send = [Tl(self.nc.dram_tensor(f"kT_send{i}", [256, 2048], BF16).ap(), "kT_send") for i in range(2)]
        self.kT_g = [Tl(self.nc.dram_tensor(f"kT_g{i}", [1024, 2048], BF16).ap(), "kT_g") for i in range(2)]
        self.V_send = [Tl(self.nc.dram_tensor(f"V_send{i}", [512, 528], BF16).ap(), "V_send") for i in range(4)]
        self.V_g = [Tl(self.nc.dram_tensor(f"V_g{i}", [2048, 528], BF16).ap(), "V_g") for i in range(4)]
        self.fk_send = Tl(self.nc.dram_tensor("fk_send", [2048, 8], F32).ap(), "fk_send")
        self.fk_g = Tl(self.nc.dram_tensor("fk_g", [8192, 8], F32).ap(), "fk_g")
        self.tot_send = Tl(self.nc.dram_tensor("tot_send", [128, 8], F32).ap(), "tot_send")
        self.tot_g = Tl(self.nc.dram_tensor("tot_g", [512, 8], F32).ap(), "tot_g")

    def cconst(self, i):
        return self.cstt.ap[:, i * 128:(i + 1) * 128]

    def spill_h(self):
        cx = self.cx
        for t in range(NTT):
            r = tt_rows(t)
            cx.dma(cx.sp, self.hsave[t * 128:t * 128 + r, :], self.h[t].ap[0:r, :], reads=[self.h[t]])

    def mixer_ab(self):
        cx = self.cx
        GR = [[0, 1, 2, 3], [4, 5, 6, 7]]
        cx.barrier()
        cx.arena_off = self.mark
        cx.h_off = cx.h_base
        self.mixS = cx.tile([128, 4, NT], BF16, "mixS")
        self.mixF = cx.tile([128, NTT, 512], BF16, "mixF")
        self.Vb = cx.tile([128, NTT, 8, 66], BF16, "Vb")
        self.logf = cx.tile([128, NTT, 8], F32, "logf")
        self.cum = cx.tile([128, NTT, 8], F32, "cum")
        self.totP = cx.tile([128, 8], F32, "totP")
        self.selt = cx.tile([128, 32], F32, "selt")
        cx.dma(cx.sp, self.selt.ap, self.sel, writes=[self.selt])
        ab_mark = cx.arena_off
        self.ab_mark = ab_mark
        self.load_gain(self.gpre, 2 * 2 + 0)
        xnT = cx.tile([128, 8, NT], BF16, "xnTf")
        mark2 = cx.arena_off
        self.junk = cx.tile([128, D], F32, "junk")
        self.nrm_xb = [cx.tile([128, D], BF16, "xb") for _ in range(2)]
        self.prenorm_T(list(range(NTT)), self.gpre, xnT, 0)
        self.spill_h()
        cx.barrier()
        if self.stop == "ab0":
            return
        cx.arena_off = mark2
        win = cx.tile([128, 8, 2056], BF16, "win")
        cx.dma(cx.pool, win.ap, self.w_ab_in.rearrange("(k p) f -> p k f", p=128), writes=[win])
        self.uT = cx.tile([128, 4, NT], BF16, "uT", hreg=True)
        self.qT = cx.tile([128, 4, NT], BF16, "qT", hreg=True)
        self.kT = cx.tile([128, 4, NT], BF16, "kT", hreg=True)
        stg = [cx.tile([128, 512], F32, "stg", hreg=True) for _ in range(2)]
        bfb = cx.tile([128, 8], F32, "bfb", hreg=True)
        cx.dma(cx.sp, bfb.ap, self.fox_bf.partition_broadcast(128), writes=[bfb])
        dests = [self.uT] * 4 + [self.qT] * 4 + [self.kT] * 4
        cnt = 0
        for j in range(12):
            dst = dests[j]
            for (c0, cw) in COLT:
                bank = cx.banks[2 + cnt % 2]
                cnt += 1
                for k in range(8):
                    cx.op(cx.pe, lambda e: e.matmul(bank.ap[:, 0:cw], lhsT=win.ap[:, k, j * 128:(j + 1) * 128],
                                                    rhs=xnT.ap[:, k, c0:c0 + cw], start=(k == 0), stop=(k == 7)),
                          [win, xnT], [bank])
                sc = 0.125 if 4 <= j < 8 else 1.0
                cx.op(cx.act, lambda e: e.activation(out=dst.ap[:, j % 4, c0:c0 + cw], in_=bank.ap[:, 0:cw],
                                                     func=AF.Copy, scale=sc), [bank], [dst])
        if self.stop == "ab1a":
            return
        cx.op(cx.dve, lambda e: e.memset(self.Vb.ap[:, :, :, 64:66], 1.0), [], [self.Vb])
        for t in range(NTT):
            r = tt_rows(t)
            c0 = t * 128
            import os
            dbg = os.environ.get("DBG", "")
            for which, (w0, oap) in enumerate(((1024, self.o_foxk), (1536, self.o_foxv))):
                if dbg and str(which) not in dbg:
                    continue
                bank = cx.banks[4 + which]
                st = stg[which]
                for k in range(8):
                    cx.op(cx.pe, lambda e: e.matmul(bank.ap[0:r, :], lhsT=xnT.ap[:, k, c0:c0 + r],
                                                    rhs=win.ap[:, k, w0:w0 + 512], start=(k == 0), stop=(k == 7)),
                          [win, xnT], [bank])
                cx.op(cx.act, lambda e: e.copy(out=st.ap[0:r, :], in_=bank.ap[0:r, :]), [bank], [st])
                cx.dma(cx.sp, oap[c0:c0 + r, :], st.ap[0:r, :], reads=[st], is_output=True)
                if which == 1:
                    cx.op(cx.dve, lambda e: e.tensor_copy(out=self.Vb.ap[0:r, t, :, 0:64],
                                                          in_=bank.ap[0:r, :].rearrange("p (h d) -> p h d", h=8)),
                          [bank], [self.Vb])
            if dbg and "2" not in dbg:
                continue
            bank = cx.banks[6]
            for k in range(8):
                cx.op(cx.pe, lambda e: e.matmul(bank.ap[0:r, 0:8], lhsT=xnT.ap[:, k, c0:c0 + r],
                                                rhs=win.ap[:, k, 2048:2056], start=(k == 0), stop=(k == 7)),
                      [win, xnT], [bank])
            lf = self.logf
            cx.op(cx.dve, lambda e: e.tensor_tensor(out=lf.ap[0:r, t, :], in0=bank.ap[0:r, 0:8], in1=bfb.ap[0:r, :],
                                                    op=ALU.add), [bank, bfb], [lf])
        if self.stop == "ab1b":
            return
        lf = self.logf
        cx.op(cx.act, lambda e: e.activation(out=lf.ap, in_=lf.ap, func=AF.Exp, scale=-1.0), [lf], [lf])
        cx.op(cx.act, lambda e: e.activation(out=lf.ap, in_=lf.ap, func=AF.Ln, bias=1.0, scale=1.0), [lf], [lf])
        cx.op(cx.dve, lambda e: e.tensor_scalar(out=lf.ap, in0=lf.ap, scalar1=-1.0, scalar2=None, op0=ALU.mult), [lf], [lf])
        cx.dma(cx.sp, self.o_foxf[0:2048, :].rearrange("(t p) h -> p t h", p=128), lf.ap[:, 0:16, :], reads=[lf], is_output=True)
        cx.dma(cx.sp, self.o_foxf[2048:2128, :], lf.ap[0:80, 16, :], reads=[lf], is_output=True)
        cumt = self.cum
        for t in range(16):
            bank = cx.banks[6 + t % 2]
            for tp in range(t):
                cx.op(cx.pe, lambda e: e.matmul(bank.ap[:, 0:8], lhsT=self.cconst(2), rhs=lf.ap[:, tp, :],
                                                start=(tp == 0), stop=False), [self.cstt, lf], [bank])
            cx.op(cx.pe, lambda e: e.matmul(bank.ap[:, 0:8], lhsT=self.cconst(1), rhs=lf.ap[:, t, :],
                                            start=(t == 0), stop=True), [self.cstt, lf], [bank])
            cx.op(cx.dve, lambda e: e.tensor_copy(out=cumt.ap[:, t, :], in_=bank.ap[:, 0:8]), [bank], [cumt])
        bank = cx.banks[6]
        cx.op(cx.pe, lambda e: e.matmul(bank.ap[0:80, 0:8], lhsT=self.cstt.ap[0:80, 384:464], rhs=lf.ap[0:80, 16, :],
                                        start=True, stop=True), [self.cstt, lf], [bank])
        cx.op(cx.dve, lambda e: e.tensor_copy(out=cumt.ap[0:80, 16, :], in_=bank.ap[0:80, 0:8]), [bank], [cumt])
        bank = cx.banks[7]
        for t in range(16):
            cx.op(cx.pe, lambda e: e.matmul(bank.ap[:, 0:8], lhsT=self.cconst(2), rhs=lf.ap[:, t, :],
                                            start=(t == 0), stop=(t == 15)), [self.cstt, lf], [bank])
        cx.op(cx.dve, lambda e: e.tensor_copy(out=self.totP.ap, in_=bank.ap[:, 0:8]), [bank], [self.totP])
        if self.stop == "ab1c":
            return
        for c in range(4):
            cx.dma(cx.sp, self.kT_send[c // 2].ap[(c % 2) * 128:(c % 2 + 1) * 128, :], self.kT.ap[:, c, 0:2048], reads=[self.kT],
                   writes=[self.kT_send[c // 2]])
        for i in range(4):
            cx.dma(cx.sp, self.V_send[i].ap.rearrange("(t p) c -> p t c", p=128),
                   self.Vb.ap[:, 4 * i:4 * i + 4, :, :].rearrange("p t h d -> p t (h d)"), reads=[self.Vb], writes=[self.V_send[i]])
        cx.dma(cx.sp, self.fk_send.ap.rearrange("(t p) h -> p t h", p=128), cumt.ap[:, 0:16, :], reads=[cumt],
               writes=[self.fk_send])
        cx.dma(cx.sp, self.tot_send.ap, self.totP.ap, reads=[self.totP], writes=[self.tot_send])
        for i in range(2):
            cx.allgather(self.kT_send[i], self.kT_g[i], GR)
        for i in range(4):
            cx.allgather(self.V_send[i], self.V_g[i], GR)
        cx.allgather(self.fk_send, self.fk_g, GR)
        cx.allgather(self.tot_send, self.tot_g, GR)
        cx.barrier()
        if self.stop == "ab1":
            return
        cx.arena_off = ab_mark
        self.h_mark_ab = cx.h_off = self.h_mark_after_qk()
        self.s5_phase()
        cx.barrier()
        if self.stop == "s5":
            return
        cx.arena_off = ab_mark
        cx.h_off = self.h_mark_ab
        self.fox_phase()
        if self.stop == "fox":
            return
        self.mixer_out(self.w_ab_out, 3 * 2 + 0)

    def h_mark_after_qk(self):
        return self.cx.h_base + 3 * ((4 * NT * 2 + 31) // 32 * 8)
    def s5_phase(self):
        cx = self.cx
        V, P = cx.dve, cx.pool
        TC = 128

        def t16(name):
            return cx.tile([128, 16], F32, name)

        prm = cx.tile([128, 48], F32, "s5prm")
        cx.dma(cx.sp, prm.ap, self.s5p, writes=[prm])
        a_re, a_im, ldt = prm.ap[:, 0:16], prm.ap[:, 16:32], prm.ap[:, 32:48]
        dt, mag, th, c_, s_, t1, t2 = (t16(n) for n in ("dt", "mag", "th", "c_", "s_", "t1", "t2"))
        halfpi = cx.tile([128, 1], F32, "halfpi")
        cx.op(V, lambda e: e.memset(halfpi.ap, float(np.pi / 2)), [], [halfpi])
        cx.op(cx.act, lambda e: e.activation(out=dt.ap, in_=ldt, func=AF.Exp), [prm], [dt])
        cx.op(V, lambda e: e.tensor_tensor(out=t1.ap, in0=dt.ap, in1=a_re, op=ALU.mult), [dt, prm], [t1])
        cx.op(cx.act, lambda e: e.activation(out=mag.ap, in_=t1.ap, func=AF.Exp), [t1], [mag])
        cx.op(V, lambda e: e.tensor_tensor(out=th.ap, in0=dt.ap, in1=a_im, op=ALU.mult), [dt, prm], [th])
        cx.op(cx.act, lambda e: e.activation(out=s_.ap, in_=th.ap, func=AF.Sin, scale=1.0 / 16), [th], [s_])
        cx.op(cx.act, lambda e: e.activation(out=c_.ap, in_=th.ap, func=AF.Sin, scale=1.0 / 16, bias=halfpi.ap),
              [th, halfpi], [c_])

        def csquare(cr, ci, outr, outi):
            cx.op(V, lambda e: e.tensor_tensor(out=t1.ap, in0=cr.ap, in1=cr.ap, op=ALU.mult), [cr], [t1])
            cx.op(V, lambda e: e.tensor_tensor(out=t2.ap, in0=ci.ap, in1=ci.ap, op=ALU.mult), [ci], [t2])
            cx.op(V, lambda e: e.tensor_tensor(out=outi.ap, in0=cr.ap, in1=ci.ap, op=ALU.mult), [cr, ci], [outi])
            cx.op(V, lambda e: e.tensor_scalar(out=outi.ap, in0=outi.ap, scalar1=2.0, scalar2=None, op0=ALU.mult), [outi], [outi])
            cx.op(V, lambda e: e.tensor_tensor(out=outr.ap, in0=t1.ap, in1=t2.ap, op=ALU.subtract), [t1, t2], [outr])

        for _ in range(4):
            csquare(c_, s_, c_, s_)
        Ur = [t16(f"Ur{k}") for k in range(7)]
        Ui = [t16(f"Ui{k}") for k in range(7)]
        cx.op(V, lambda e: e.tensor_copy(out=Ur[0].ap, in_=c_.ap), [c_], [Ur[0]])
        cx.op(V, lambda e: e.tensor_copy(out=Ui[0].ap, in_=s_.ap), [s_], [Ui[0]])
        for k in range(1, 7):
            csquare(Ur[k - 1], Ui[k - 1], Ur[k], Ui[k])
        ab_re, ab_im, cf_re, cf_im, den, nr = (t16(n) for n in ("ab_re", "ab_im", "cf_re", "cf_im", "den", "nr"))
        cx.op(V, lambda e: e.tensor_tensor(out=ab_re.ap, in0=mag.ap, in1=c_.ap, op=ALU.mult), [mag, c_], [ab_re])
        cx.op(V, lambda e: e.tensor_tensor(out=ab_im.ap, in0=mag.ap, in1=s_.ap, op=ALU.mult), [mag, s_], [ab_im])
        cx.op(V, lambda e: e.tensor_tensor(out=t1.ap, in0=a_re, in1=a_re, op=ALU.mult), [prm], [t1])
        cx.op(V, lambda e: e.tensor_tensor(out=t2.ap, in0=a_im, in1=a_im, op=ALU.mult), [prm], [t2])
        cx.op(V, lambda e: e.tensor_tensor(out=den.ap, in0=t1.ap, in1=t2.ap, op=ALU.add), [t1, t2], [den])
        cx.op(V, lambda e: e.reciprocal(out=den.ap, in_=den.ap), [den], [den])
        cx.op(V, lambda e: e.tensor_scalar(out=nr.ap, in0=ab_re.ap, scalar1=-1.0, scalar2=None, op0=ALU.add), [ab_re], [nr])
        cx.op(V, lambda e: e.tensor_tensor(out=t1.ap, in0=nr.ap, in1=a_re, op=ALU.mult), [nr, prm], [t1])
        cx.op(V, lambda e: e.tensor_tensor(out=t2.ap, in0=ab_im.ap, in1=a_im, op=ALU.mult), [ab_im, prm], [t2])
        cx.op(V, lambda e: e.tensor_tensor(out=cf_re.ap, in0=t1.ap, in1=t2.ap, op=ALU.add), [t1, t2], [cf_re])
        cx.op(V, lambda e: e.tensor_tensor(out=cf_re.ap, in0=cf_re.ap, in1=den.ap, op=ALU.mult), [cf_re, den], [cf_re])
        cx.op(V, lambda e: e.tensor_tensor(out=t1.ap, in0=ab_im.ap, in1=a_re, op=ALU.mult), [ab_im, prm], [t1])
        cx.op(V, lambda e: e.tensor_tensor(out=t2.ap, in0=nr.ap, in1=a_im, op=ALU.mult), [nr, prm], [t2])
        cx.op(V, lambda e: e.tensor_tensor(out=cf_im.ap, in0=t1.ap, in1=t2.ap, op=ALU.subtract), [t1, t2], [cf_im])
        cx.op(V, lambda e: e.tensor_tensor(out=cf_im.ap, in0=cf_im.ap, in1=den.ap, op=ALU.mult), [cf_im, den], [cf_im])
        p_re, p_im = t16("p_re"), t16("p_im")
        csquare(ab_re, ab_im, p_re, p_im)
        for _ in range(10):
            csquare(p_re, p_im, p_re, p_im)
        E_re = cx.tile([128, 16, TC], F32, "E_re")
        E_im = cx.tile([128, 16, TC], F32, "E_im")
        R_re = cx.tile([128, 16, TC], F32, "R_re")
        R_im = cx.tile([128, 16, TC], F32, "R_im")
        rtab = cx.tile([128, 16, TC], F32, "rtab")
        tmpa = cx.tile([128, TC], F32, "tmpa")
        tmpb = cx.tile([128, TC], F32, "tmpb")
        for st in range(16):
            cx.op(V, lambda e: e.tensor_copy(out=E_re.ap[:, st, 0:1], in_=Ur[0].ap[:, st:st + 1]), [Ur[0]], [E_re])
            cx.op(V, lambda e: e.tensor_copy(out=E_im.ap[:, st, 0:1], in_=Ui[0].ap[:, st:st + 1]), [Ui[0]], [E_im])
            for k in range(7):
                ln = 1 << k
                ur, ui = Ur[k].ap[:, st:st + 1], Ui[k].ap[:, st:st + 1]
                cx.op(V, lambda e: e.tensor_scalar(out=tmpa.ap[:, 0:ln], in0=E_im.ap[:, st, 0:ln], scalar1=ui, scalar2=None,
                                                   op0=ALU.mult), [E_im, Ui[k]], [tmpa])
                cx.op(V, lambda e: e.tensor_scalar(out=tmpb.ap[:, 0:ln], in0=E_im.ap[:, st, 0:ln], scalar1=ur, scalar2=None,
                                                   op0=ALU.mult), [E_im, Ur[k]], [tmpb])
                cx.op(V, lambda e: e.scalar_tensor_tensor(out=E_im.ap[:, st, ln:2 * ln], in0=E_re.ap[:, st, 0:ln], scalar=ui,
                                                          in1=tmpb.ap[:, 0:ln], op0=ALU.mult, op1=ALU.add),
                      [E_re, Ui[k], tmpb], [E_im])
                cx.op(V, lambda e: e.scalar_tensor_tensor(out=E_re.ap[:, st, ln:2 * ln], in0=E_re.ap[:, st, 0:ln], scalar=ur,
                                                          in1=tmpa.ap[:, 0:ln], op0=ALU.mult, op1=ALU.subtract),
                      [E_re, Ur[k], tmpa], [E_re])
            cr, ci = cf_re.ap[:, st:st + 1], cf_im.ap[:, st:st + 1]
            cx.op(V, lambda e: e.tensor_scalar(out=tmpa.ap, in0=E_im.ap[:, st, :], scalar1=ci, scalar2=None, op0=ALU.mult),
                  [E_im, cf_im], [tmpa])
            cx.op(V, lambda e: e.scalar_tensor_tensor(out=R_re.ap[:, st, :], in0=E_re.ap[:, st, :], scalar=cr, in1=tmpa.ap,
                                                      op0=ALU.mult, op1=ALU.add), [E_re, cf_re, tmpa], [R_re])
            cx.op(V, lambda e: e.tensor_scalar(out=tmpb.ap, in0=E_im.ap[:, st, :], scalar1=cr, scalar2=None, op0=ALU.mult),
                  [E_im, cf_re], [tmpb])
            cx.op(V, lambda e: e.scalar_tensor_tensor(out=R_im.ap[:, st, :], in0=E_re.ap[:, st, :], scalar=ci, in1=tmpb.ap,
                                                      op0=ALU.mult, op1=ALU.subtract), [E_re, cf_im, tmpb], [R_im])
            cx.op(V, lambda e: e.tensor_scalar(out=rtab.ap[:, st, :], in0=self.cconst(2), scalar1=mag.ap[:, st:st + 1],
                                               scalar2=None, op0=ALU.mult), [self.cstt, mag], [rtab])
        bl = cx.tile([128, 16, 2, 128], BF16, "bl")
        cl = cx.tile([128, 16, 2, 128], BF16, "cl")
        dl = cx.tile([128, 4, 128], BF16, "dl")
        wgl = cx.tile([128, 4, 512], BF16, "wgl")
        bgl = cx.tile([128, 4], F32, "bgl")
        cx.dma(cx.pool, bl.ap, self.s5b.rearrange("p (s r c) -> p s r c", s=16, r=2), writes=[bl])
        cx.dma(cx.pool, cl.ap, self.s5c.rearrange("p (s r c) -> p s r c", s=16, r=2), writes=[cl])
        cx.dma(cx.pool, dl.ap, self.s5d.rearrange("p (s c) -> p s c", s=4), writes=[dl])
        cx.dma(cx.pool, wgl.ap, self.w_glu.rearrange("(k p) f -> p k f", p=128), writes=[wgl])
        cx.dma(cx.sp, bgl.ap, self.b_glu, writes=[bgl])
        cx.op(V, lambda e: e.tensor_scalar(out=cl.ap[:, :, 1, :], in0=cl.ap[:, :, 1, :], scalar1=-1.0, scalar2=None,
                                           op0=ALU.mult), [cl], [cl])
        wks = [[cx.tile([128, TC], F32, f"wk{i}", hreg=True) for i in range(8)] for _ in range(2)]
        hre = [cx.tile([128, TC], BF16, f"hre{i}") for i in range(4)]
        him = [cx.tile([128, TC], BF16, f"him{i}") for i in range(4)]
        gf = [cx.tile([128, TC], F32, f"gf{i}") for i in range(4)]
        gb = [cx.tile([128, TC], BF16, f"gb{i}") for i in range(4)]
        sg = cx.tile([128, TC], F32, "sg")
        car_re, car_im = t16("car_re"), t16("car_im")
        c1 = [cx.tile([128, 1], F32, f"c1_{i}") for i in range(2)]
        uT, mixS = self.uT, self.mixS
        cnt = [0]

        def run(c0, n, want_y):
            pos = 0
            while pos < n:
                ln = min(TC, n - pos)
                cc0 = c0 + pos
                for cc in range(4):
                    for sl in range(4):
                        st = cc * 4 + sl
                        pr, pi = cx.banks[(cnt[0] % 2) * 2], cx.banks[(cnt[0] % 2) * 2 + 1]
                        cnt[0] += 1
                        wk = wks[cnt[0] % 2]
                        cx.op(cx.pe, lambda e: e.matmul(pr.ap[:, 0:ln], lhsT=bl.ap[:, st, 0, :], rhs=uT.ap[:, cc, cc0:cc0 + ln],
                                                        start=True, stop=True), [bl, uT], [pr])
                        cx.op(cx.pe, lambda e: e.matmul(pi.ap[:, 0:ln], lhsT=bl.ap[:, st, 1, :], rhs=uT.ap[:, cc, cc0:cc0 + ln],
                                                        start=True, stop=True), [bl, uT], [pi])
                        Rr, Ri = R_re.ap[:, st, 0:ln], R_im.ap[:, st, 0:ln]
                        Er, Ei = E_re.ap[:, st, 0:ln], E_im.ap[:, st, 0:ln]
                        w = [x.ap[:, 0:ln] for x in wk]
                        cx.op(V, lambda e: e.tensor_tensor(out=w[0], in0=pr.ap[:, 0:ln], in1=Rr, op=ALU.mult), [pr, R_re], [wk[0]])
                        cx.op(V, lambda e: e.tensor_tensor(out=w[1], in0=pi.ap[:, 0:ln], in1=Ri, op=ALU.mult), [pi, R_im], [wk[1]])
                        cx.op(V, lambda e: e.tensor_tensor(out=w[2], in0=pi.ap[:, 0:ln], in1=Rr, op=ALU.mult), [pi, R_re], [wk[2]])
                        cx.op(V, lambda e: e.tensor_tensor(out=w[3], in0=pr.ap[:, 0:ln], in1=Ri, op=ALU.mult), [pr, R_im], [wk[3]])
                        cx.op(P, lambda e: e.tensor_tensor(out=w[0], in0=w[0], in1=w[1], op=ALU.subtract), [wk[0], wk[1]], [wk[0]])
                        cx.op(P, lambda e: e.tensor_tensor(out=w[2], in0=w[2], in1=w[3], op=ALU.add), [wk[2], wk[3]], [wk[2]])
                        cx.op(V, lambda e: e.tensor_tensor_scan(out=w[4], data0=rtab.ap[:, st, 0:ln], data1=w[0],
                                                                initial=car_re.ap[:, st:st + 1], op0=ALU.mult, op1=ALU.add),
                              [rtab, wk[0], car_re], [wk[4]])
                        cx.op(V, lambda e: e.tensor_tensor_scan(out=w[5], data0=rtab.ap[:, st, 0:ln], data1=w[2],
                                                                initial=car_im.ap[:, st:st + 1], op0=ALU.mult, op1=ALU.add),
                              [rtab, wk[2], car_im], [wk[5]])
                        L = ln - 1
                        gr, gi = wk[4].ap[:, L:L + 1], wk[5].ap[:, L:L + 1]
                        er, ei = E_re.ap[:, st, L:L + 1], E_im.ap[:, st, L:L + 1]
                        cx.op(V, lambda e: e.tensor_tensor(out=c1[0].ap, in0=gi, in1=ei, op=ALU.mult), [wk[5], E_im], [c1[0]])
                        cx.op(V, lambda e: e.tensor_tensor(out=c1[1].ap, in0=gi, in1=er, op=ALU.mult), [wk[5], E_re], [c1[1]])
                        cx.op(V, lambda e: e.scalar_tensor_tensor(out=car_re.ap[:, st:st + 1], in0=gr, scalar=er, in1=c1[0].ap,
                                                                  op0=ALU.mult, op1=ALU.subtract), [wk[4], E_re, c1[0]], [car_re])
                        cx.op(V, lambda e: e.scalar_tensor_tensor(out=car_im.ap[:, st:st + 1], in0=gr, scalar=ei, in1=c1[1].ap,
                                                                  op0=ALU.mult, op1=ALU.add), [wk[4], E_im, c1[1]], [car_im])
                        if want_y:
                            cx.op(P, lambda e: e.tensor_tensor(out=w[6], in0=w[5], in1=Ei, op=ALU.mult), [wk[5], E_im], [wk[6]])
                            cx.op(P, lambda e: e.tensor_tensor(out=w[7], in0=w[5], in1=Er, op=ALU.mult), [wk[5], E_re], [wk[7]])
                            cx.op(V, lambda e: e.tensor_tensor(out=w[0], in0=w[4], in1=Er, op=ALU.mult), [wk[4], E_re], [wk[0]])
                            cx.op(V, lambda e: e.tensor_tensor(out=w[1], in0=w[4], in1=Ei, op=ALU.mult), [wk[4], E_im], [wk[1]])
                            cx.op(P, lambda e: e.tensor_tensor(out=hre[sl].ap[:, 0:ln], in0=w[0], in1=w[6], op=ALU.subtract),
                                  [wk[0], wk[6]], [hre[sl]])
                            cx.op(P, lambda e: e.tensor_tensor(out=him[sl].ap[:, 0:ln], in0=w[1], in1=w[7], op=ALU.add),
                                  [wk[1], wk[7]], [him[sl]])
                    if want_y:
                        yb = cx.banks[4 + cc % 2]
                        for sl in range(4):
                            st = cc * 4 + sl
                            cx.op(cx.pe, lambda e: e.matmul(yb.ap[:, 0:ln], lhsT=cl.ap[:, st, 0, :], rhs=hre[sl].ap[:, 0:ln],
                                                            start=(sl == 0), stop=False), [cl, hre[sl]], [yb])
                            cx.op(cx.pe, lambda e: e.matmul(yb.ap[:, 0:ln], lhsT=cl.ap[:, st, 1, :], rhs=him[sl].ap[:, 0:ln],
                                                            start=False, stop=False), [cl, him[sl]], [yb])
                        cx.op(cx.pe, lambda e: e.matmul(yb.ap[:, 0:ln], lhsT=dl.ap[:, cc, :], rhs=uT.ap[:, cc, cc0:cc0 + ln],
                                                        start=False, stop=True), [dl, uT], [yb])
                        cx.op(cx.act, lambda e: e.activation(out=gf[cc].ap[:, 0:ln], in_=yb.ap[:, 0:ln], func=AF.Gelu_apprx_tanh),
                              [yb], [gf[cc]])
                        cx.op(P, lambda e: e.tensor_copy(out=gb[cc].ap[:, 0:ln], in_=gf[cc].ap[:, 0:ln]), [gf[cc]], [gb[cc]])
                if want_y:
                    for co in range(4):
                        zb = cx.banks[6 + co % 2]
                        for ci in range(4):
                            cx.op(cx.pe, lambda e: e.matmul(zb.ap[:, 0:ln], lhsT=wgl.ap[:, ci, co * 128:(co + 1) * 128],
                                                            rhs=gb[ci].ap[:, 0:ln], start=(ci == 0), stop=(ci == 3)),
                                  [wgl, gb[ci]], [zb])
                        cx.op(cx.act, lambda e: e.activation(out=sg.ap[:, 0:ln], in_=zb.ap[:, 0:ln], func=AF.Sigmoid,
                                                             bias=bgl.ap[:, co:co + 1], scale=1.0), [zb, bgl], [sg])
                        cx.op(V, lambda e: e.tensor_tensor(out=mixS.ap[:, co, cc0:cc0 + ln], in0=gf[co].ap[:, 0:ln],
                                                           in1=sg.ap[:, 0:ln], op=ALU.mult), [gf[co], sg], [mixS])
                pos += ln

        def set_carry_zero():
            cx.op(V, lambda e: e.memset(car_re.ap, 0.0), [], [car_re])
            cx.op(V, lambda e: e.memset(car_im.ap, 0.0), [], [car_im])

        st_out = cx.tile([128, 96], F32, "s5st_out")
        ini = cx.tile([128, 64], F32, "s5ini")
        cx.dma(cx.sp, ini.ap, self.s5init, writes=[ini])
        for si in range(2):
            cx.op(V, lambda e: e.tensor_copy(out=car_re.ap, in_=ini.ap[:, si * 32:si * 32 + 16]), [ini], [car_re])
            cx.op(V, lambda e: e.tensor_copy(out=car_im.ap, in_=ini.ap[:, si * 32 + 16:si * 32 + 32]), [ini], [car_im])
            run(2048 + 32 * si, 32, True)
            cx.op(V, lambda e: e.tensor_copy(out=st_out.ap[:, 32 + si * 32:48 + si * 32], in_=car_re.ap), [car_re], [st_out])
            cx.op(V, lambda e: e.tensor_copy(out=st_out.ap[:, 48 + si * 32:64 + si * 32], in_=car_im.ap), [car_im], [st_out])
        set_carry_zero()
        run(2112, 16, True)
        m_re, m_im = t16("m_re"), t16("m_im")
        cx.op(V, lambda e: e.tensor_copy(out=m_re.ap, in_=car_re.ap), [car_re], [m_re])
        cx.op(V, lambda e: e.tensor_copy(out=m_im.ap, in_=car_im.ap), [car_im], [m_im])
        set_carry_zero()
        run(0, 2048, False)
        fsend = cx.tile([128, 32], F32, "fsend")
        cx.op(V, lambda e: e.tensor_copy(out=fsend.ap[:, 0:16], in_=car_re.ap), [car_re], [fsend])
        cx.op(V, lambda e: e.tensor_copy(out=fsend.ap[:, 16:32], in_=car_im.ap), [car_im], [fsend])
        cx.dma(cx.sp, self.s5F_send.ap, fsend.ap, reads=[fsend], writes=[self.s5F_send])
        cx.allgather(self.s5F_send, self.s5F_g, [[0, 1, 2, 3], [4, 5, 6, 7]])
        fg = cx.tile([128, 4, 32], F32, "fg")
        cx.dma(cx.sp, fg.ap, self.s5F_g.ap.rearrange("(j p) c -> p j c", p=128), reads=[self.s5F_g], writes=[fg])
        s_re, s_im, n_re, n_im = t16("s_re"), t16("s_im"), t16("n_re"), t16("n_im")
        cx.op(V, lambda e: e.tensor_copy(out=s_re.ap, in_=m_re.ap), [m_re], [s_re])
        cx.op(V, lambda e: e.tensor_copy(out=s_im.ap, in_=m_im.ap), [m_im], [s_im])
        selt = self.selt
        cx.op(V, lambda e: e.tensor_scalar(out=car_re.ap, in0=s_re.ap, scalar1=selt.ap[:, 0:1], scalar2=None, op0=ALU.mult),
              [s_re, selt], [car_re])
        cx.op(V, lambda e: e.tensor_scalar(out=car_im.ap, in0=s_im.ap, scalar1=selt.ap[:, 0:1], scalar2=None, op0=ALU.mult),
              [s_im, selt], [car_im])
        for j in range(3):
            cx.op(V, lambda e: e.tensor_tensor(out=t1.ap, in0=p_re.ap, in1=s_re.ap, op=ALU.mult), [p_re, s_re], [t1])
            cx.op(V, lambda e: e.tensor_tensor(out=t2.ap, in0=p_im.ap, in1=s_im.ap, op=ALU.mult), [p_im, s_im], [t2])
            cx.op(V, lambda e: e.tensor_tensor(out=n_re.ap, in0=t1.ap, in1=t2.ap, op=ALU.subtract), [t1, t2], [n_re])
            cx.op(V, lambda e: e.tensor_tensor(out=t1.ap, in0=p_re.ap, in1=s_im.ap, op=ALU.mult), [p_re, s_im], [t1])
            cx.op(V, lambda e: e.tensor_tensor(out=t2.ap, in0=p_im.ap, in1=s_re.ap, op=ALU.mult), [p_im, s_re], [t2])
            cx.op(V, lambda e: e.tensor_tensor(out=n_im.ap, in0=t1.ap, in1=t2.ap, op=ALU.add), [t1, t2], [n_im])
            cx.op(V, lambda e: e.tensor_tensor(out=s_re.ap, in0=n_re.ap, in1=fg.ap[:, j, 0:16], op=ALU.add), [n_re, fg], [s_re])
            cx.op(V, lambda e: e.tensor_tensor(out=s_im.ap, in0=n_im.ap, in1=fg.ap[:, j, 16:32], op=ALU.add), [n_im, fg], [s_im])
            cx.op(V, lambda e: e.scalar_tensor_tensor(out=car_re.ap, in0=s_re.ap, scalar=selt.ap[:, j + 1:j + 2], in1=car_re.ap,
                                                      op0=ALU.mult, op1=ALU.add), [s_re, selt, car_re], [car_re])
            cx.op(V, lambda e: e.scalar_tensor_tensor(out=car_im.ap, in0=s_im.ap, scalar=selt.ap[:, j + 1:j + 2], in1=car_im.ap,
                                                      op0=ALU.mult, op1=ALU.add), [s_im, selt, car_im], [car_im])
        run(0, 2048, True)
        cx.op(V, lambda e: e.tensor_copy(out=st_out.ap[:, 0:16], in_=car_re.ap), [car_re], [st_out])
        cx.op(V, lambda e: e.tensor_copy(out=st_out.ap[:, 16:32], in_=car_im.ap), [car_im], [st_out])
        cx.dma(cx.sp, self.o_s5st, st_out.ap, reads=[st_out], is_output=True)
    def cumsum_tiles(self, lf2, ntile, within_out, tot_out):
        cx = self.cx
        V = cx.dve
        n = ntile * 8
        ba, bb = cx.banks[6], cx.banks[7]
        cx.op(cx.pe, lambda e: e.matmul(ba.ap[:, 0:n], lhsT=self.cconst(1), rhs=lf2, start=True, stop=True), [self.cstt, self.lf_src], [ba])
        cx.op(cx.pe, lambda e: e.matmul(bb.ap[:, 0:n], lhsT=self.cconst(2), rhs=lf2, start=True, stop=True), [self.cstt, self.lf_src], [bb])
        tots = self.cs_tots
        incl = self.cs_incl
        cx.op(V, lambda e: e.tensor_copy(out=tots.ap[:, 0:n], in_=bb.ap[:, 0:n]), [bb], [tots])
        t3 = tots.ap[:, 0:n].rearrange("p (t h) -> p t h", h=8)
        i3 = incl.ap[:, 0:n].rearrange("p (t h) -> p t h", h=8)
        for h in range(8):
            cx.op(V, lambda e: e.tensor_tensor_scan(out=i3[:, :, h], data0=self.cconst(2)[:, 0:ntile], data1=t3[:, :, h],
                                                    initial=0.0, op0=ALU.mult, op1=ALU.add), [tots, self.cstt], [incl])
        cx.op(V, lambda e: e.tensor_copy(out=tot_out.ap, in_=i3[:, ntile - 1, :]), [incl], [tot_out])
        cx.op(V, lambda e: e.tensor_tensor(out=incl.ap[:, 0:n], in0=incl.ap[:, 0:n], in1=tots.ap[:, 0:n], op=ALU.subtract),
              [incl, tots], [incl])
        cx.op(V, lambda e: e.tensor_tensor(out=within_out, in0=incl.ap[:, 0:n], in1=ba.ap[:, 0:n], op=ALU.add),
              [incl, ba], [self.cs_dst])

    def fox_phase(self):
        cx = self.cx
        V, P, A, PE = cx.dve, cx.pool, cx.act, cx.pe
        selt, cum, lf = self.selt, self.cum, self.logf
        totg = cx.tile([128, 4, 8], F32, "totg")
        cx.dma(cx.sp, totg.ap, self.tot_g.ap.rearrange("(j p) h -> p j h", p=128), reads=[self.tot_g], writes=[totg])
        totmeta = cx.tile([128, 8], F32, "totmeta")
        bk = cx.banks[7]
        cx.op(PE, lambda e: e.matmul(bk.ap[:, 0:8], lhsT=self.cstt.ap[0:80, 512:640], rhs=lf.ap[0:80, 16, :], start=True, stop=True),
              [self.cstt, lf], [bk])
        cx.op(V, lambda e: e.tensor_copy(out=totmeta.ap, in_=bk.ap[:, 0:8]), [bk], [totmeta])
        dl = cx.tile([128, 4, 8], F32, "delta")
        for j in range(3):
            cx.op(V, lambda e: e.tensor_scalar(out=dl.ap[:, j, :], in0=totg.ap[:, 0, :], scalar1=selt.ap[:, 8 + j * 4:9 + j * 4],
                                               scalar2=selt.ap[:, 4 + j:5 + j], op0=ALU.mult, op1=ALU.add), [totg, selt], [dl])
            for jp in range(1, 4):
                cx.op(V, lambda e: e.scalar_tensor_tensor(out=dl.ap[:, j, :], in0=totg.ap[:, jp, :],
                                                          scalar=selt.ap[:, 8 + j * 4 + jp:9 + j * 4 + jp], in1=dl.ap[:, j, :],
                                                          op0=ALU.mult, op1=ALU.add), [totg, selt, dl], [dl])
        cx.op(V, lambda e: e.tensor_copy(out=dl.ap[:, 3, :], in_=totmeta.ap), [totmeta], [dl])
        for jp in range(4):
            cx.op(V, lambda e: e.scalar_tensor_tensor(out=dl.ap[:, 3, :], in0=totg.ap[:, jp, :], scalar=selt.ap[:, 20 + jp:21 + jp],
                                                      in1=dl.ap[:, 3, :], op0=ALU.mult, op1=ALU.add), [totg, selt, dl], [dl])
        cref = cx.tile([128, 16, 8], F32, "cref")
        cx.op(PE, lambda e: e.matmul(bk.ap[:, 0:128], lhsT=self.cstt.ap[:, 896:1024], rhs=cum.ap[:, 0:16, :].rearrange("p t h -> p (t h)"),
                                     start=True, stop=True), [self.cstt, cum], [bk])
        cx.op(V, lambda e: e.tensor_copy(out=cref.ap.rearrange("p t h -> p (t h)"), in_=bk.ap[:, 0:128]), [bk], [cref])
        cref16 = cx.tile([128, 3, 8], F32, "cref16")
        for si in range(3):
            cx.op(PE, lambda e: e.matmul(bk.ap[:, 0:8], lhsT=self.cstt.ap[0:80, 1024 + si * 128:1152 + si * 128], rhs=cum.ap[0:80, 16, :],
                                         start=True, stop=True), [self.cstt, cum], [bk])
            cx.op(V, lambda e: e.tensor_copy(out=cref16.ap[:, si, :], in_=bk.ap[:, 0:8]), [bk], [cref16])
        fkg = cx.tile([128, 3, 16, 8], F32, "fkg")
        cx.dma(cx.sp, fkg.ap, self.fk_g.ap[0:6144, :].rearrange("(j t p) h -> p j t h", p=128, t=16), reads=[self.fk_g], writes=[fkg])
        for j in range(3):
            for t in range(16):
                cx.op(V, lambda e: e.tensor_tensor(out=fkg.ap[:, j, t, :], in0=dl.ap[:, j, :], in1=fkg.ap[:, j, t, :], op=ALU.subtract),
                      [dl, fkg], [fkg])
        lfc = cx.tile([128, 2, 16, 8], F32, "lfc")
        cx.dma(cx.sp, lfc.ap, self.c_foxf.rearrange("s (t p) h -> p s t h", p=128), writes=[lfc])
        cumc = cx.tile([128, 2, 16, 8], F32, "cumc")
        totc = cx.tile([128, 2, 8], F32, "totc")
        self.cs_tots = cx.tile([128, 128], F32, "cs_tots", hreg=True)
        self.cs_incl = cx.tile([128, 128], F32, "cs_incl", hreg=True)
        for s in range(2):
            self.lf_src = lfc
            self.cs_dst = cumc
            tt_ = Tl(totc.ap[:, s, :], "totc_s")
            self.cumsum_tiles(lfc.ap[:, s, :, :].rearrange("p t h -> p (t h)"), 16,
                              cumc.ap[:, s, :, :].rearrange("p t h -> p (t h)"), tt_)
            totc.w = tt_.w
            for t in range(16):
                cx.op(V, lambda e: e.tensor_tensor(out=cumc.ap[:, s, t, :], in0=totc.ap[:, s, :], in1=cumc.ap[:, s, t, :], op=ALU.subtract),
                      [totc, cumc], [cumc])
        kg = [cx.tile([128, 3, 2048], BF16, f"kg{i}") for i in range(2)]
        vg = [cx.tile([128, 3, 16, 132], BF16, f"vg{i}", hreg=(i == 1)) for i in range(2)]
        kc = cx.tile([128, 2, 2048], BF16, "kc")
        vc = cx.tile([128, 2, 16, 2, 66], BF16, "vc")
        kstg = cx.tile([128, 4, 128], F32, "kstg")
        pt_seg = [[cx.tile([128, 80], BF16, f"ptseg{s_}{i}") for i in range(3)] for s_ in range(2)]
        for s_ in range(2):
            for i in range(3):
                cx.op(V, lambda e: e.memset(pt_seg[s_][i].ap, 0.0), [], [pt_seg[s_][i]])
        bias_o = cx.tile([128, 16, 16], F32, "bias_o")
        bias_g = cx.tile([128, 3, 16, 16], F32, "bias_g")
        bias_m = cx.tile([128, 16], F32, "bias_m")
        bias_s = cx.tile([128, 2, 17], F32, "bias_s")
        bias_mm = cx.tile([128, 1], F32, "bias_mm")
        pts = [cx.tile([128, 512], BF16, f"pt{i}", hreg=True) for i in range(3)]
        rec = cx.tile([128, 1], F32, "rec")
        qms = [cx.tile([128, NT], BF16, f"qm{i}") for i in range(2)]
        maskb = cx.tile([128, 128], BF16, "maskb")
        cx.op(V, lambda e: e.tensor_copy(out=maskb.ap, in_=self.cconst(1)), [self.cstt], [maskb])
        mask80 = cx.tile([128, 128], BF16, "mask80")
        cx.op(V, lambda e: e.tensor_copy(out=mask80.ap, in_=self.cconst(3)), [self.cstt], [mask80])
        cx.op(P, lambda e: e.memset(vc.ap[:, :, :, :, 64:66], 1.0), [], [vc])
        qT, kT, Vb, mixF = self.qT, self.kT, self.Vb, self.mixF
        sb_i = [0]
        ob_i = [0]
        pt_i = [0]

        def load_pair(c):
            b = c % 2
            cx.dma(cx.sp, kg[b].ap, self.kT_g[c // 2].ap[0:768, :].rearrange("(j r) k -> r j k", r=256)[(c % 2) * 128:(c % 2 + 1) * 128, :, :],
                   reads=[self.kT_g[c // 2]], writes=[kg[b]])
            for i in range(4):
                for j in range(3):
                    cx.dma(cx.sp, vg[b].ap[:, j, 4 * i:4 * i + 4, :],
                           self.V_g[i].ap[j * 512:(j + 1) * 512, :].rearrange("(t p) x -> p t x", p=128)[:, :, c * 132:(c + 1) * 132],
                           reads=[self.V_g[i]], writes=[vg[b]])

        load_pair(0)
        for c in range(4):
            if c + 1 < 4:
                load_pair(c + 1)
            b = c % 2
            for s in range(2):
                for t in range(16):
                    if t % 4 == 0:
                        cx.dma(cx.sp, kstg.ap, self.c_foxk[s].rearrange("(t p) x -> p t x", p=128)[:, t:t + 4, c * 128:(c + 1) * 128],
                               writes=[kstg])
                    tb = cx.banks[5 + t % 2]
                    cx.op(PE, lambda e: e.transpose(out=tb.ap[:, 0:128], in_=kstg.ap[:, t % 4, :], identity=self.cconst(0)),
                          [kstg, self.cstt], [tb])
                    cx.op(A, lambda e: e.copy(out=kc.ap[:, s, t * 128:(t + 1) * 128], in_=tb.ap[:, 0:128]), [tb], [kc])
                for hh in range(2):
                    h = 2 * c + hh
                    cx.dma(cx.pool, vc.ap[:, s, :, hh, 0:64],
                           self.c_foxv[s].rearrange("(t p) x -> p t x", p=128)[:, :, h * 64:(h + 1) * 64], writes=[vc])
            for hh in range(2):
                h = 2 * c + hh
                pb = hh * 64
                qm = qms[hh]
                cx.op(P, lambda e: e.tensor_copy(out=qm.ap, in_=qT.ap[:, c, :]), [qT], [qm])
                cx.op(P, lambda e: e.memset(qm.ap[(1 - hh) * 64:(1 - hh) * 64 + 64, :], 0.0), [], [qm])
                for kt in range(16):
                    cx.op(V, lambda e: e.tensor_scalar(out=bias_o.ap[:, kt, :], in0=cref.ap[:, :, h], scalar1=cum.ap[:, kt, h:h + 1],
                                                       scalar2=None, op0=ALU.subtract), [cref, cum], [bias_o])
                    for j in range(3):
                        cx.op(V, lambda e: e.tensor_scalar(out=bias_g.ap[:, j, kt, :], in0=cref.ap[:, :, h],
                                                           scalar1=fkg.ap[:, j, kt, h:h + 1], scalar2=None, op0=ALU.add),
                              [cref, fkg], [bias_g])
                cx.op(V, lambda e: e.tensor_tensor(out=bias_mm.ap, in0=dl.ap[:, 3, h:h + 1], in1=cum.ap[:, 16, h:h + 1], op=ALU.subtract),
                      [dl, cum], [bias_mm])
                cx.op(V, lambda e: e.tensor_scalar(out=bias_m.ap, in0=cref.ap[:, :, h], scalar1=bias_mm.ap[:, 0:1], scalar2=None,
                                                   op0=ALU.add), [cref, bias_mm], [bias_m])
                for s in range(2):
                    cx.op(V, lambda e: e.tensor_scalar(out=bias_s.ap[:, s, 0:16], in0=cumc.ap[:, s, :, h], scalar1=cref16.ap[:, s, h:h + 1],
                                                       scalar2=None, op0=ALU.add), [cumc, cref16], [bias_s])
                    cx.op(V, lambda e: e.tensor_tensor(out=bias_s.ap[:, s, 16:17], in0=cref16.ap[:, s, h:h + 1], in1=cum.ap[:, 16, h:h + 1],
                                                       op=ALU.subtract), [cref16, cum], [bias_s])
                cx.op(V, lambda e: e.tensor_tensor(out=bias_mm.ap, in0=cref16.ap[:, 2, h:h + 1], in1=cum.ap[:, 16, h:h + 1],
                                                   op=ALU.subtract), [cref16, cum], [bias_mm])

                def block(qc0, nq, sources, outs):
                    ob = cx.banks[3 + ob_i[0] % 2]
                    ob_i[0] += 1
                    first = [True]
                    npv = sum(len(sr[6]) for sr in sources)
                    ipv = [0]
                    pts_used = {}

                    def A_(n):
                        (kap, nk, vap, r0, r1, pieces, pvs, ptt) = sources[n]
                        sbk = cx.banks[sb_i[0] % 3]
                        sb_i[0] += 1
                        if ptt is None:
                            pt = pts[pt_i[0] % 3]
                            pt_i[0] += 1
                        else:
                            pt = ptt
                        pts_used[n] = pt
                        cx.op(PE, lambda e: e.matmul(sbk.ap[0:nk, 0:nq], lhsT=kap, rhs=qm.ap[:, qc0:qc0 + nq],
                                                     start=True, stop=True), [*self.cur_k, qm], [sbk])
                        for (c0, w, bap, m) in pieces:
                            cx.op(A, lambda e: e.activation(out=pt.ap[0:nk, c0:c0 + w], in_=sbk.ap[0:nk, c0:c0 + w], func=AF.Exp,
                                                            bias=bap, scale=1.0), [sbk, *self.cur_bias], [pt])
                            if m is not None:
                                cx.op(P, lambda e: e.tensor_tensor(out=pt.ap[0:nk, c0:c0 + w], in0=pt.ap[0:nk, c0:c0 + w], in1=m,
                                                                   op=ALU.mult), [pt, maskb, mask80], [pt])

                    def C_(n):
                        (kap, nk, vap, r0, r1, pieces, pvs, ptt) = sources[n]
                        pt = pts_used[n]
                        for (c0, w, oc) in pvs:
                            ipv[0] += 1
                            cx.op(PE, lambda e: e.matmul(ob.ap[0:w, oc:oc + 65], lhsT=pt.ap[r0:r1, c0:c0 + w], rhs=vap,
                                                         start=first[0], stop=(ipv[0] == npv), skip_group_check=True),
                                  [pt, *self.cur_v], [ob])
                            first[0] = False

                    NS = len(sources)
                    for n in range(NS + 1):
                        if n < NS:
                            A_(n)
                        if n >= 1:
                            C_(n - 1)
                    for (oc, w, ot) in outs:
                        cx.op(V, lambda e: e.reciprocal(out=rec.ap[0:w, 0:1], in_=ob.ap[0:w, oc + 64:oc + 65]), [ob], [rec])
                        cx.op(V, lambda e: e.tensor_scalar(out=mixF.ap[0:w, ot, h * 64:(h + 1) * 64], in0=ob.ap[0:w, oc:oc + 64],
                                                           scalar1=rec.ap[0:w, 0:1], scalar2=None, op0=ALU.mult), [ob, rec], [mixF])

                for B in range(4):
                    srcs = []
                    pcs = [(sub * 128, 128, bias_m.ap[0:80, 4 * B + sub:4 * B + sub + 1], None) for sub in range(4)]
                    pvs = [(sub * 128, 128, sub * 65) for sub in range(4)]
                    srcs.append((kT.ap[0:128, c, 2048:2128], 80, Vb.ap[64:80, 16, h, 0:65], 64, 80, pcs, pvs, None))
                    for j in range(3):
                        for kt in range(16):
                            pcs = [(sub * 128, 128, bias_g.ap[:, j, kt, 4 * B + sub:4 * B + sub + 1], None) for sub in range(4)]
                            srcs.append((kg[b].ap[0:128, j, kt * 128:(kt + 1) * 128], 128, vg[b].ap[:, j, kt, hh * 66:hh * 66 + 65],
                                         0, 128, pcs, pvs, None))
                    for kt in range(4 * B + 4):
                        subs = [sub for sub in range(4) if 4 * B + sub >= kt]
                        pcs = [(sub * 128, 128, bias_o.ap[:, kt, 4 * B + sub:4 * B + sub + 1],
                                (maskb.ap if 4 * B + sub == kt else None)) for sub in subs]
                        srcs.append((kT.ap[0:128, c, kt * 128:(kt + 1) * 128], 128, Vb.ap[:, kt, h, 0:65], 0, 128, pcs,
                                     [(sub * 128, 128, sub * 65) for sub in subs], None))
                    self.fox_block_srcs(block, B * 512, 512, srcs, [(sub * 65, 128, 4 * B + sub) for sub in range(4)],
                                        [kT, kg[b]], [Vb, vg[b]], [bias_m, bias_g, bias_o])
                srcs = []
                for s in range(2):
                    for kt in range(16):
                        srcs.append((kc.ap[0:128, s, kt * 128:(kt + 1) * 128], 128, vc.ap[:, s, kt, hh, 0:65], 0, 128,
                                     [(32 * s, 32, bias_s.ap[:, s, kt:kt + 1], None)], [(0, 80, 0)], pt_seg[s][kt % 3]))
                pcs = [(0, 32, bias_s.ap[0:80, 0, 16:17], mask80.ap[0:80, 0:32]), (32, 32, bias_s.ap[0:80, 1, 16:17], mask80.ap[0:80, 32:64]),
                       (64, 16, bias_mm.ap[0:80, 0:1], mask80.ap[0:80, 64:80])]
                srcs.append((kT.ap[0:128, c, 2048:2128], 80, Vb.ap[0:80, 16, h, 0:65], 0, 80, pcs, [(0, 80, 0)], None))
                self.fox_block_srcs(block, 2048, 80, srcs, [(0, 80, 16)], [kT, kc], [Vb, vc], [bias_s, bias_mm])

    def fox_block_srcs(self, block, qc0, nq, srcs, out_tiles, ktls, vtls, btls):
        self.cur_k = list(ktls)
        self.cur_v = list(vtls)
        self.cur_bias = list(btls)
        block(qc0, nq, srcs, out_tiles)
    def mixer_out(self, w_out_dram, gain_idx):
        cx = self.cx
        cx.barrier()
        cx.arena_off = self.ab_mark
        cx.h_off = cx.h_base
        for t in range(NTT):
            r = tt_rows(t)
            cx.dma(cx.sp, self.h[t].ap[0:r, :], self.hsave[t * 128:t * 128 + r, :], writes=[self.h[t]])
        wout = cx.tile([128, 8, D], BF16, "wout")
        cx.dma(cx.pool, wout.ap, w_out_dram.rearrange("(k p) n -> p k n", p=128), writes=[wout])
        self.load_gain(self.gpost, gain_idx)
        self.junk = cx.tile([128, D], F32, "junk")
        self.tmpn = cx.tile([128, D], F32, "tmpn")
        mT = [cx.tile([128, 4, 128], BF16, f"mT{i}") for i in range(2)]
        ot = [Tl(cx.sb([128, D], F32, f"ot{i}"), f"ot{i}") for i in range(2)]
        mixS, mixF = self.mixS, self.mixF
        for t in range(NTT):
            r = tt_rows(t)
            bank = cx.banks[t % 2]
            pt = bank.ap.bitcast(BF16).rearrange("p (k c) -> p k c", k=8)
            for k in range(4):
                cx.op(cx.pe, lambda e: e.transpose(out=pt[:, k, 0:r], in_=mixF.ap[0:r, t, k * 128:(k + 1) * 128],
                                                   identity=self.identb.ap[0:r, 0:r]), [mixF, self.identb], [bank])
            m = mT[t % 2]
            cx.op(cx.act, lambda e: e.copy(out=m.ap[:, :, 0:r], in_=pt[:, 0:4, 0:r]), [bank], [m])
            o = ot[t % 2]
            for hf in range(2):
                ob = cx.banks[2 + (2 * t + hf) % 4]
                for k in range(4):
                    cx.op(cx.pe, lambda e: e.matmul(ob.ap[0:r, :], lhsT=mixS.ap[:, k, t * 128:t * 128 + r],
                                                    rhs=wout.ap[:, k, hf * 512:(hf + 1) * 512], start=(k == 0), stop=False),
                          [mixS, wout], [ob])
                for k in range(4):
                    cx.op(cx.pe, lambda e: e.matmul(ob.ap[0:r, :], lhsT=m.ap[:, k, 0:r],
                                                    rhs=wout.ap[:, 4 + k, hf * 512:(hf + 1) * 512], start=False, stop=(k == 3)),
                          [m, wout], [ob])
                cx.op(cx.act, lambda e: e.copy(out=o.ap[0:r, hf * 512:(hf + 1) * 512], in_=ob.ap[0:r, :]), [ob], [o])
            self.postnorm_add([t], [o], self.gpost, 1.0)
        cx.barrier()

    def declare_cd(self):
        self.w_cd_in = self.din("cd_w_in", [D, 3080])
        self.cst2 = self.din("cst2", [128, 896])
        self.convw = self.din("convw", [128, 8 * 5])
        self.ssdp = self.din("ssdp", [1, 24])
        self.ssd_norm = self.din("ssd_norm", [1, 512])
        self.ssdinit = self.din("ssdinit", [2, 128, 512])
        self.convinit = self.din("convinit", [128, 2 * 8 * 3])
        self.w_cd_out = self.din("cd_w_out", [D, D])
        self.c_sbk = self.din("c_sbk", [2, 2048, 512])
        self.c_sbv = self.din("c_sbv", [2, 2048, 512])
        self.o_ssd = self.dout("o_ssd", [3, 128, 512])
        self.o_conv = self.dout("o_conv", [3, 3, 1024])
        self.o_sbk = self.dout("o_sbk", [NT, 512])
        self.o_sbv = self.dout("o_sbv", [NT, 512])
        mk = lambda n, sh, dt=F32: Tl(self.nc.dram_tensor(n, sh, dt).ap(), n)
        self.tail_send, self.tail_g = mk("tail_send", [128, 24]), mk("tail_g", [512, 24])
        self.ssF_send, self.ssF_g = mk("ssF_send", [128, 520]), mk("ssF_g", [512, 520])
        self.k2_send = [mk(f"k2_send{i}", [256, 2048], BF16) for i in range(2)]
        self.k2_g = [mk(f"k2_g{i}", [1024, 2048], BF16) for i in range(2)]
        self.v2_send = [mk(f"v2_send{i}", [512, 512], BF16) for i in range(4)]
        self.v2_g = [mk(f"v2_g{i}", [2048, 512], BF16) for i in range(4)]

    def c2(self, i):
        return self.cst2t.ap[:, i * 128:(i + 1) * 128]

    def mixer_cd(self):
        cx = self.cx
        V, P, A, PE = cx.dve, cx.pool, cx.act, cx.pe
        GR = [[0, 1, 2, 3], [4, 5, 6, 7]]
        SEGS = [(0, 2048, 0), (2048, 32, 1), (2080, 32, 2), (2112, 16, 3)]
        cx.barrier()
        cx.arena_off = self.mark
        cx.h_off = cx.h_base
        self.mixC = cx.tile([128, NTT, 512], BF16, "mixC")
        self.dtt = cx.tile([128, NTT, 8], F32, "dtt")
        self.selt = cx.tile([128, 32], F32, "selt")
        cx.dma(cx.sp, self.selt.ap, self.sel, writes=[self.selt])
        self.cst2t = cx.tile([128, 896], F32, "cst2t")
        cx.dma(cx.sp, self.cst2t.ap, self.cst2, writes=[self.cst2t])
        sp24 = cx.tile([128, 24], F32, "sp24")
        cx.dma(cx.sp, sp24.ap, self.ssdp.partition_broadcast(128), writes=[sp24])
        cwt = cx.tile([128, 8, 5], F32, "cwt")
        cx.dma(cx.sp, cwt.ap, self.convw.rearrange("p (c w) -> p c w", w=5), writes=[cwt])
        self.sb_mark = cx.arena_off
        self.xdt = cx.tile([128, NTT, 512], BF16, "xdt")
        self.zs = cx.tile([128, NTT, 512], BF16, "zs")
        self.cd_mark = cx.arena_off
        self.load_gain(self.gpre, 2 * 2 + 1)
        xnT = cx.tile([128, 8, NT], BF16, "xnTf")
        mark2 = cx.arena_off
        self.junk = cx.tile([128, D], F32, "junk")
        self.nrm_xb = [cx.tile([128, D], BF16, "xb") for _ in range(2)]
        self.prenorm_T(list(range(NTT)), self.gpre, xnT, 0)
        self.spill_h()
        cx.barrier()
        cx.arena_off = mark2
        self.qT = cx.tile([128, 4, NT], BF16, "qT", hreg=True)
        self.kT = cx.tile([128, 4, NT], BF16, "kT", hreg=True)
        self.Vb = cx.tile([128, NTT, 8, 64], BF16, "Vb2", hreg=True)
        self.h_mark_cd = cx.h_off
        self.BCT = cx.tile([128, 4, NT], BF16, "BCT", hreg=True)
        wsl = [cx.tile([128, 8, 512], BF16, f"wsl{i}") for i in range(2)]
        wv = self.w_cd_in.rearrange("(k p) f -> p k f", p=128)
        wi = [0]

        def load_w(c0, n):
            t_ = wsl[wi[0] % 2]
            wi[0] += 1
            cx.dma(cx.pool, t_.ap[:, :, 0:n], wv[:, :, c0:c0 + n], writes=[t_])
            return t_

        stg = [cx.tile([128, 512], F32, f"stg{i}") for i in range(2)]
        a_b = cx.tile([128, 8], F32, "a_b")
        cx.op(A, lambda e: e.activation(out=a_b.ap, in_=sp24.ap[:, 8:16], func=AF.Exp), [sp24], [a_b])
        cx.op(V, lambda e: e.tensor_scalar(out=a_b.ap, in0=a_b.ap, scalar1=-1.0, scalar2=None, op0=ALU.mult), [a_b], [a_b])
        self.a_b = a_b
        self.sp24 = sp24

        wk_ = None
        for part, (c0w, n) in enumerate(((0, 512), (2056, 512), (2568, 512), (1536, 8))):
            w_ = load_w(c0w, n)
            for t in range(NTT):
                r = tt_rows(t)
                c0 = t * 128
                bank = cx.banks[4 + t % 2]
                for k in range(8):
                    cx.op(PE, lambda e: e.matmul(bank.ap[0:r, 0:n], lhsT=xnT.ap[:, k, c0:c0 + r], rhs=w_.ap[:, k, 0:n],
                                                 start=(k == 0), stop=(k == 7)), [w_, xnT], [bank])
                if part == 0:
                    cx.op(A, lambda e: e.activation(out=self.zs.ap[0:r, t, :], in_=bank.ap[0:r, :], func=AF.Silu), [bank], [self.zs])
                elif part == 3:
                    cx.op(V, lambda e: e.tensor_tensor(out=self.dtt.ap[0:r, t, :], in0=bank.ap[0:r, 0:8], in1=sp24.ap[0:r, 0:8],
                                                       op=ALU.add), [bank, sp24], [self.dtt])
                else:
                    st = stg[t % 2]
                    cx.op(A, lambda e: e.copy(out=st.ap[0:r, :], in_=bank.ap[0:r, :]), [bank], [st])
                    oap = self.o_sbk if part == 1 else self.o_sbv
                    cx.dma(cx.sp, oap[c0:c0 + r, :], st.ap[0:r, :], reads=[st], is_output=True)
                    if part == 2:
                        cx.op(V, lambda e: e.tensor_copy(out=self.Vb.ap[0:r, t, :, :],
                                                         in_=st.ap[0:r, :].rearrange("p (h d) -> p h d", h=8)), [st], [self.Vb])
        dtt = self.dtt
        cx.op(A, lambda e: e.activation(out=dtt.ap, in_=dtt.ap, func=AF.Exp), [dtt], [dtt])
        cx.op(A, lambda e: e.activation(out=dtt.ap, in_=dtt.ap, func=AF.Ln, bias=1.0, scale=1.0), [dtt], [dtt])
        cnt = 0
        for j in range(8):
            dst = self.qT if j < 4 else self.kT
            c0w = (1544 if j < 4 else 2056) + (j % 4) * 128
            if j % 4 == 0:
                w_ = load_w(c0w, 512)
            for (c0, cw) in COLT:
                bank = cx.banks[2 + cnt % 2]
                cnt += 1
                for k in range(8):
                    cx.op(PE, lambda e: e.matmul(bank.ap[:, 0:cw], lhsT=w_.ap[:, k, (j % 4) * 128:(j % 4 + 1) * 128],
                                                 rhs=xnT.ap[:, k, c0:c0 + cw], start=(k == 0), stop=(k == 7)), [w_, xnT], [bank])
                sc = 0.125 if j < 4 else 1.0
                cx.op(A, lambda e: e.activation(out=dst.ap[:, j % 4, c0:c0 + cw], in_=bank.ap[:, 0:cw], func=AF.Copy, scale=sc),
                      [bank], [dst])
        for c in range(4):
            cx.dma(cx.sp, self.k2_send[c // 2].ap[(c % 2) * 128:(c % 2 + 1) * 128, :], self.kT.ap[:, c, 0:2048], reads=[self.kT],
                   writes=[self.k2_send[c // 2]])
        for i in range(4):
            cx.dma(cx.sp, self.v2_send[i].ap.rearrange("(t p) c -> p t c", p=128),
                   self.Vb.ap[:, 4 * i:4 * i + 4, :, :].rearrange("p t h d -> p t (h d)"), reads=[self.Vb], writes=[self.v2_send[i]])
        for i in range(2):
            cx.allgather(self.k2_send[i], self.k2_g[i], GR)
        for i in range(4):
            cx.allgather(self.v2_send[i], self.v2_g[i], GR)
        XW = [cx.tile([128, 515], F32, f"XW{i}") for i in range(2)]
        cv = [cx.tile([128, 512], F32, f"cv{i}") for i in range(2)]
        hist = cx.tile([128, 8, 3], F32, "hist")
        tails = cx.tile([128, 4, 8, 3], F32, "tails")
        tg = cx.tile([128, 4, 24], F32, "tg")
        ci_t = cx.tile([128, 2, 8, 3], F32, "ci_t")
        cx.dma(cx.sp, ci_t.ap, self.convinit.rearrange("p (s c w) -> p s c w", s=2, c=8), writes=[ci_t])
        xst = [cx.tile([128, 512], F32, f"xst{i}") for i in range(2)]
        x16 = cx.tile([128, 80], F32, "x16")
        wx = [load_w(512, 512), load_w(1024, 512)]
        xi = [0]

        def xbc_tile(ch, c0, cw, dst_ap):
            bank = cx.banks[2 + xi[0] % 2]
            xi[0] += 1
            w_ = wx[ch // 4]
            for k in range(8):
                cx.op(PE, lambda e: e.matmul(bank.ap[:, 0:cw], lhsT=w_.ap[:, k, (ch % 4) * 128:(ch % 4 + 1) * 128],
                                             rhs=xnT.ap[:, k, c0:c0 + cw], start=(k == 0), stop=(k == 7)), [w_, xnT], [bank])
            return bank

        for ch in range(8):
            bank = xbc_tile(ch, 2045, 83 - 0, None)
            cx.op(V, lambda e: e.tensor_copy(out=tails.ap[:, 0, ch, :], in_=bank.ap[:, 0:3]), [bank], [tails])
            cx.op(V, lambda e: e.tensor_copy(out=tails.ap[:, 1, ch, :], in_=bank.ap[:, 32:35]), [bank], [tails])
            cx.op(V, lambda e: e.tensor_copy(out=tails.ap[:, 2, ch, :], in_=bank.ap[:, 64:67]), [bank], [tails])
            cx.op(V, lambda e: e.tensor_copy(out=tails.ap[:, 3, ch, :], in_=bank.ap[:, 80:83]), [bank], [tails])
        cx.dma(cx.sp, self.tail_send.ap, tails.ap[:, 0, :, :].rearrange("p c w -> p (c w)"), reads=[tails], writes=[self.tail_send])
        cx.allgather(self.tail_send, self.tail_g, GR)
        cx.dma(cx.sp, tg.ap, self.tail_g.ap.rearrange("(j p) x -> p j x", p=128), reads=[self.tail_g], writes=[tg])
        selt = self.selt
        h2 = hist.ap.rearrange("p c w -> p (c w)")
        cx.op(V, lambda e: e.tensor_scalar(out=h2, in0=tails.ap[:, 3, :, :].rearrange("p c w -> p (c w)"), scalar1=selt.ap[:, 0:1],
                                           scalar2=None, op0=ALU.mult), [tails, selt], [hist])
        for j in range(3):
            cx.op(V, lambda e: e.scalar_tensor_tensor(out=h2, in0=tg.ap[:, j, :], scalar=selt.ap[:, 27 + j:28 + j], in1=h2,
                                                      op0=ALU.mult, op1=ALU.add), [tg, selt, hist], [hist])
        cst_o = [cx.tile([128, 128], F32, f"cst_o{i}") for i in range(2)]
        for sid, oi in ((0, 0), (1, 1), (2, 2)):
            for ch in range(8):
                bank = cx.banks[6 + ch % 2]
                co = cst_o[ch % 2]
                cx.op(PE, lambda e: e.transpose(out=bank.ap[0:3, 0:128], in_=tails.ap[:, sid, ch, :], identity=self.cconst(0)),
                      [tails, self.cstt], [bank])
                cx.op(V, lambda e: e.tensor_copy(out=co.ap[0:3, :], in_=bank.ap[0:3, 0:128]), [bank], [co])
                cx.dma(cx.sp, self.o_conv[oi, :, ch * 128:(ch + 1) * 128], co.ap[0:3, :], reads=[co], is_output=True)
        pieces = [(0, 512, 0, None), (512, 512, 0, None), (1024, 512, 0, None), (1536, 512, 0, None),
                  (2048, 32, 1, 0), (2080, 32, 2, 1), (2112, 16, 3, None)]
        xw_i = 0
        for ch in range(8):
            for (c0, cw, sid, si) in pieces:
                xw = XW[xw_i % 2]
                cvt = cv[xw_i % 2]
                xw_i += 1
                bank = xbc_tile(ch, c0, cw, None)
                if c0 == 0:
                    cx.op(V, lambda e: e.tensor_copy(out=xw.ap[:, 0:3], in_=hist.ap[:, ch, :]), [hist], [xw])
                elif sid == 0:
                    prev = XW[(xw_i - 2) % 2]
                    cx.op(V, lambda e: e.tensor_copy(out=xw.ap[:, 0:3], in_=prev.ap[:, 512:515]), [prev], [xw])
                elif si is not None:
                    cx.op(V, lambda e: e.tensor_copy(out=xw.ap[:, 0:3], in_=ci_t.ap[:, si, ch, :]), [ci_t], [xw])
                else:
                    cx.op(V, lambda e: e.memset(xw.ap[:, 0:3], 0.0), [], [xw])
                cx.op(A, lambda e: e.copy(out=xw.ap[:, 3:3 + cw], in_=bank.ap[:, 0:cw]), [bank], [xw])
                cx.op(V, lambda e: e.tensor_scalar(out=cvt.ap[:, 0:cw], in0=xw.ap[:, 3:3 + cw], scalar1=cwt.ap[:, ch, 3:4],
                                                   scalar2=cwt.ap[:, ch, 4:5], op0=ALU.mult, op1=ALU.add), [xw, cwt], [cvt])
                for w in range(3):
                    cx.op(V, lambda e: e.scalar_tensor_tensor(out=cvt.ap[:, 0:cw], in0=xw.ap[:, w:w + cw], scalar=cwt.ap[:, ch, w:w + 1],
                                                              in1=cvt.ap[:, 0:cw], op0=ALU.mult, op1=ALU.add), [xw, cwt, cvt], [cvt])
                if ch >= 4:
                    cx.op(A, lambda e: e.activation(out=self.BCT.ap[:, ch - 4, c0:c0 + cw], in_=cvt.ap[:, 0:cw], func=AF.Silu),
                          [cvt], [self.BCT])
                else:
                    if sid == 0:
                        xs_ = xst[xw_i % 2]
                        off = 0
                    else:
                        xs_ = x16
                        off = c0 - 2048
                    cx.op(A, lambda e: e.activation(out=xs_.ap[:, off:off + cw], in_=cvt.ap[:, 0:cw], func=AF.Silu), [cvt], [xs_])
                    if sid in (1, 2):
                        continue
                    tot_w = cw if sid == 0 else 80
                    base = c0 if sid == 0 else 2048
                    nsub = (tot_w + 127) // 128
                    for sb_ in range(nsub):
                        w_ = min(128, tot_w - sb_ * 128)
                        t = (base + sb_ * 128) // 128
                        tb = cx.banks[6 + sb_ % 2]
                        cx.op(PE, lambda e: e.transpose(out=tb.ap[0:w_, 0:128], in_=xs_.ap[:, sb_ * 128:sb_ * 128 + w_],
                                                        identity=self.cconst(0)), [xs_, self.cstt], [tb])
                        for hh in range(2):
                            h = 2 * ch + hh
                            cx.op(A, lambda e: e.activation(out=self.xdt.ap[0:w_, t, h * 64:(h + 1) * 64],
                                                            in_=tb.ap[0:w_, hh * 64:(hh + 1) * 64], func=AF.Copy,
                                                            scale=self.dtt.ap[0:w_, t, h:h + 1]), [tb, self.dtt], [self.xdt])
        cx.barrier()
        cx.arena_off = self.cd_mark
        self.ssd_phase()
        cx.barrier()
        cx.arena_off = self.sb_mark
        cx.h_off = self.h_mark_cd
        self.sb_phase()
        self.mixer_out2(self.w_cd_out, 3 * 2 + 1)

    def ssd_phase(self):
        cx = self.cx
        V, P, A, PE = cx.dve, cx.pool, cx.act, cx.pe
        xdt, zs, BCT, dtt, a_b, selt = self.xdt, self.zs, self.BCT, self.dtt, self.a_b, self.selt
        HT = cx.tile([128, 512], F32, "HT")
        HTb = cx.tile([128, 512], BF16, "HTb")
        adt = cx.tile([128, 8], F32, "adt")
        acs = cx.tile([128, 8], F32, "acs")
        tot = cx.tile([128, 8], F32, "tot")
        eacs = cx.tile([128, 8], F32, "eacs")
        dec = cx.tile([128, 8], F32, "dec")
        etot = cx.tile([128, 8], F32, "etot")
        ddt = cx.tile([128, 8], F32, "ddt")
        logD = cx.tile([128, 8], F32, "logD")
        Am = [cx.tile([128, 128], F32, f"Am{i}") for i in range(2)]
        Ex = [cx.tile([128, 128], F32, f"Ex{i}") for i in range(2)]
        CBm = [cx.tile([128, 128], F32, f"CBm{i}") for i in range(2)]
        Wt = [cx.tile([128, 128], BF16, f"Wt{i}") for i in range(8)]
        Btok = [cx.tile([128, 128], BF16, f"Btok{i}") for i in range(2)]
        xdd = cx.tile([128, 512], BF16, "xdd")
        ysb = cx.tile([128, 512], F32, "ysb")
        yg = cx.tile([128, 512], F32, "yg")
        ss2 = cx.tile([128, 2], F32, "ss2")
        rs2 = cx.tile([128, 2], F32, "rs2")
        db = cx.tile([128, 8], F32, "db")
        cx.op(V, lambda e: e.tensor_copy(out=db.ap, in_=self.sp24.ap[:, 16:24]), [self.sp24], [db])
        ng = cx.tile([128, 512], F32, "ng")
        cx.dma(cx.sp, ng.ap, self.ssd_norm.partition_broadcast(128), writes=[ng])
        nh2 = cx.tile([128, 2], F32, "nh2")
        cx.op(V, lambda e: e.memset(nh2.ap, -0.5), [], [nh2])
        identf = self.cconst(0)
        TRI, ONES = self.cconst(1), self.cconst(2)
        SL, BSL80, BT80, BO80 = self.c2(0), self.c2(1), self.cconst(3), self.c2(2)
        ci = [0]

        def chunk(t, rows, want_y, seg16):
            r = rows
            tri = BT80[0:r, 0:r] if seg16 else TRI[0:r, 0:r]
            ones = BO80[0:r, 0:r] if seg16 else ONES[0:r, 0:r]
            sl = BSL80[0:r, 0:r] if seg16 else SL[0:r, 0:r]
            cstl = [self.cstt, self.cst2t]
            cx.op(V, lambda e: e.tensor_tensor(out=adt.ap[0:r, :], in0=dtt.ap[0:r, t, :], in1=a_b.ap[0:r, :], op=ALU.mult),
                  [dtt, a_b], [adt])
            b0, b1 = cx.banks[0], cx.banks[1]
            cx.op(PE, lambda e: e.matmul(b0.ap[0:r, 0:8], lhsT=tri, rhs=adt.ap[0:r, :], start=True, stop=True), [adt, *cstl], [b0])
            cx.op(PE, lambda e: e.matmul(b1.ap[0:r, 0:8], lhsT=ones, rhs=adt.ap[0:r, :], start=True, stop=True), [adt, *cstl], [b1])
            cx.op(V, lambda e: e.tensor_copy(out=acs.ap[0:r, :], in_=b0.ap[0:r, 0:8]), [b0], [acs])
            cx.op(V, lambda e: e.tensor_copy(out=tot.ap[0:r, :], in_=b1.ap[0:r, 0:8]), [b1], [tot])
            cx.op(A, lambda e: e.activation(out=eacs.ap[0:r, :], in_=acs.ap[0:r, :], func=AF.Exp), [acs], [eacs])
            cx.op(A, lambda e: e.activation(out=etot.ap[0:r, :], in_=tot.ap[0:r, :], func=AF.Exp), [tot], [etot])
            cx.op(V, lambda e: e.tensor_tensor(out=dec.ap[0:r, :], in0=tot.ap[0:r, :], in1=acs.ap[0:r, :], op=ALU.subtract), [tot, acs], [dec])
            cx.op(A, lambda e: e.activation(out=dec.ap[0:r, :], in_=dec.ap[0:r, :], func=AF.Exp), [dec], [dec])
            c0 = t * 128
            if want_y:
                cx.op(V, lambda e: e.reciprocal(out=ddt.ap[0:r, :], in_=dtt.ap[0:r, t, :]), [dtt], [ddt])
                cx.op(V, lambda e: e.tensor_tensor(out=ddt.ap[0:r, :], in0=ddt.ap[0:r, :], in1=db.ap[0:r, :], op=ALU.mult), [ddt, db], [ddt])
                for g in range(2):
                    cb = cx.banks[2 + g]
                    cx.op(PE, lambda e: e.matmul(cb.ap[0:r, 0:r], lhsT=BCT.ap[:, g, c0:c0 + r], rhs=BCT.ap[:, 2 + g, c0:c0 + r],
                                                 start=True, stop=True), [BCT], [cb])
                    cx.op(V, lambda e: e.tensor_tensor(out=CBm[g].ap[0:r, 0:r], in0=cb.ap[0:r, 0:r], in1=tri, op=ALU.mult),
                          [cb, *cstl], [CBm[g]])
                yb = cx.banks[6]
                for h in range(8):
                    am, ex = Am[h % 2], Ex[h % 2]
                    sb_ = cx.banks[4 + h % 2]
                    cx.op(P, lambda e: e.tensor_scalar(out=am.ap[0:r, 0:r], in0=sl, scalar1=adt.ap[0:r, h:h + 1], scalar2=None,
                                                       op0=ALU.mult), [adt, *cstl], [am])
                    cx.op(PE, lambda e: e.matmul(sb_.ap[0:r, 0:r], lhsT=am.ap[0:r, 0:r], rhs=tri, start=True, stop=True),
                          [am, *cstl], [sb_])
                    cx.op(A, lambda e: e.activation(out=ex.ap[0:r, 0:r], in_=sb_.ap[0:r, 0:r], func=AF.Exp), [sb_], [ex])
                    cx.op(V, lambda e: e.tensor_tensor(out=ex.ap[0:r, 0:r], in0=ex.ap[0:r, 0:r], in1=CBm[h // 4].ap[0:r, 0:r], op=ALU.mult),
                          [ex, CBm[h // 4]], [ex])
                    cx.op(V, lambda e: e.scalar_tensor_tensor(out=Wt[h].ap[0:r, 0:r], in0=identf[0:r, 0:r], scalar=ddt.ap[0:r, h:h + 1],
                                                              in1=ex.ap[0:r, 0:r], op0=ALU.mult, op1=ALU.add), [ddt, ex, self.cstt], [Wt[h]])
                    cx.op(PE, lambda e: e.matmul(yb.ap[0:r, h * 64:(h + 1) * 64], lhsT=Wt[h].ap[0:r, 0:r], rhs=xdt.ap[0:r, t, h * 64:(h + 1) * 64],
                                                 start=(h == 0), stop=True, skip_group_check=True), [Wt[h], xdt], [yb])
                ob = cx.banks[7]
                if seg16:
                    CT16_, Hsb_ = self._yoff16
                    for g in range(2):
                        for si_ in range(2):
                            cx.op(PE, lambda e: e.matmul(ob.ap[0:r, g * 256:(g + 1) * 256], lhsT=CT16_.ap[:, si_, g, :],
                                                         rhs=Hsb_[si_].ap[:, g * 256:(g + 1) * 256], start=(g == 0 and si_ == 0), stop=(si_ == 1),
                                                         skip_group_check=True), [CT16_, Hsb_[si_]], [ob])
                else:
                    for g in range(2):
                        cx.op(PE, lambda e: e.matmul(ob.ap[0:r, g * 256:(g + 1) * 256], lhsT=BCT.ap[:, 2 + g, c0:c0 + r],
                                                     rhs=HTb.ap[:, g * 256:(g + 1) * 256], start=(g == 0), stop=True, skip_group_check=True),
                              [BCT, HTb], [ob])
                cx.op(A, lambda e: e.copy(out=ysb.ap[0:r, :], in_=yb.ap[0:r, :]), [yb], [ysb])
                for h in range(8):
                    cx.op(V, lambda e: e.scalar_tensor_tensor(out=ysb.ap[0:r, h * 64:(h + 1) * 64], in0=ob.ap[0:r, h * 64:(h + 1) * 64],
                                                              scalar=eacs.ap[0:r, h:h + 1], in1=ysb.ap[0:r, h * 64:(h + 1) * 64],
                                                              op0=ALU.mult, op1=ALU.add), [ob, eacs, ysb], [ysb])
                cx.op(V, lambda e: e.tensor_tensor(out=yg.ap[0:r, :], in0=ysb.ap[0:r, :], in1=zs.ap[0:r, t, :], op=ALU.mult), [ysb, zs], [yg])
                for g in range(2):
                    cx.op(V, lambda e: e.scalar_tensor_tensor(out=ysb.ap[0:r, g * 256:(g + 1) * 256], in0=yg.ap[0:r, g * 256:(g + 1) * 256],
                                                              scalar=1.0, in1=yg.ap[0:r, g * 256:(g + 1) * 256], op0=ALU.mult, op1=ALU.mult,
                                                              accum_out=ss2.ap[0:r, g:g + 1]), [yg], [ysb, ss2])
                cx.op(V, lambda e: e.tensor_scalar(out=ss2.ap[0:r, :], in0=ss2.ap[0:r, :], scalar1=1.0 / 256, scalar2=EPS,
                                                   op0=ALU.mult, op1=ALU.add), [ss2], [ss2])
                cx.op(P, lambda e: e.tensor_tensor(out=rs2.ap[0:r, :], in0=ss2.ap[0:r, :], in1=nh2.ap[0:r, :], op=ALU.pow), [ss2, nh2], [rs2])
                for g in range(2):
                    cx.op(V, lambda e: e.scalar_tensor_tensor(out=self.mixC.ap[0:r, t, g * 256:(g + 1) * 256],
                                                              in0=yg.ap[0:r, g * 256:(g + 1) * 256], scalar=rs2.ap[0:r, g:g + 1],
                                                              in1=ng.ap[0:r, g * 256:(g + 1) * 256], op0=ALU.mult, op1=ALU.mult),
                          [yg, rs2, ng], [self.mixC])

        def prep_B(t, r):
            c0 = t * 128
            for g in range(2):
                tb = cx.banks[0 + g]
                tbv = tb.ap.bitcast(BF16)
                cx.op(PE, lambda e: e.transpose(out=tbv[0:r, 0:128], in_=BCT.ap[:, g, c0:c0 + r], identity=self.identb.ap),
                      [BCT, self.identb], [tb])
                cx.op(V, lambda e: e.tensor_copy(out=Btok[g].ap[0:r, :], in_=tbv[0:r, 0:128]), [tb], [Btok[g]])

        def upd(t, r0, r1, Ht, Htb, et):
            for h in range(8):
                cx.op(A, lambda e: e.activation(out=xdd.ap[r0:r1, h * 64:(h + 1) * 64], in_=xdt.ap[r0:r1, t, h * 64:(h + 1) * 64],
                                                func=AF.Copy, scale=dec.ap[r0:r1, h:h + 1]), [xdt, dec], [xdd])
            hb = cx.banks[3]
            for g in range(2):
                cx.op(PE, lambda e: e.matmul(hb.ap[:, g * 256:(g + 1) * 256], lhsT=Btok[g].ap[r0:r1, :], rhs=xdd.ap[r0:r1, g * 256:(g + 1) * 256],
                                             start=True, stop=True, skip_group_check=True), [Btok[g], xdd], [hb])
            for h in range(8):
                cx.op(V, lambda e: e.scalar_tensor_tensor(out=Ht.ap[:, h * 64:(h + 1) * 64], in0=Ht.ap[:, h * 64:(h + 1) * 64],
                                                          scalar=et.ap[:, h:h + 1], in1=hb.ap[:, h * 64:(h + 1) * 64],
                                                          op0=ALU.mult, op1=ALU.add), [Ht, et, hb], [Ht])
            cx.op(P, lambda e: e.tensor_copy(out=Htb.ap, in_=Ht.ap), [Ht], [Htb])

        o_st = self.o_ssd
        Hs = [cx.tile([128, 512], F32, f"Hs{i}") for i in range(3)]
        Hsb = [cx.tile([128, 512], BF16, f"Hsb{i}") for i in range(3)]
        for s_ in range(2):
            cx.dma(cx.sp, Hs[s_].ap, self.ssdinit[s_], writes=[Hs[s_]])
            cx.op(P, lambda e: e.tensor_copy(out=Hsb[s_].ap, in_=Hs[s_].ap), [Hs[s_]], [Hsb[s_]])
        cx.op(V, lambda e: e.memset(Hs[2].ap, 0.0), [], [Hs[2]])
        cx.op(V, lambda e: e.memset(Hsb[2].ap, 0.0), [], [Hsb[2]])
        CT16 = cx.tile([128, 3, 2, 80], BF16, "CT16")
        cx.op(V, lambda e: e.memset(CT16.ap, 0.0), [], [CT16])
        for si_, (a_, b_) in enumerate(((0, 32), (32, 64), (64, 80))):
            for g in range(2):
                cx.op(V, lambda e: e.tensor_copy(out=CT16.ap[:, si_, g, a_:b_], in_=BCT.ap[:, 2 + g, 2048 + a_:2048 + b_]), [BCT], [CT16])
        self._yoff16 = (CT16, Hsb)
        chunk(16, 80, True, True)
        prep_B(16, 80)
        et16 = cx.tile([128, 8], F32, "et16")
        for si_, (a_, b_, selc) in enumerate(((0, 32, self.cstt.ap[0:80, 640:768]), (32, 64, self.cstt.ap[0:80, 768:896]),
                                               (64, 80, self.cstt.ap[0:80, 512:640]))):
            bk = cx.banks[2]
            cx.op(PE, lambda e: e.matmul(bk.ap[:, 0:8], lhsT=selc, rhs=adt.ap[0:80, :], start=True, stop=True), [self.cstt, adt], [bk])
            cx.op(A, lambda e: e.activation(out=et16.ap, in_=bk.ap[:, 0:8], func=AF.Exp), [bk], [et16])
            upd(16, a_, b_, Hs[si_], Hsb[si_], et16)
            if si_ < 2:
                cx.dma(cx.sp, o_st[1 + si_], Hs[si_].ap, reads=[Hs[si_]], is_output=True)
        self._yoff16 = None
        cx.op(V, lambda e: e.memset(HT.ap, 0.0), [], [HT])
        cx.op(V, lambda e: e.memset(logD.ap, 0.0), [], [logD])
        for t in range(16):
            chunk(t, 128, False, False)
            prep_B(t, 128)
            upd(t, 0, 128, HT, HTb, etot)
            cx.op(V, lambda e: e.tensor_tensor(out=logD.ap, in0=logD.ap, in1=tot.ap, op=ALU.add), [logD, tot], [logD])
        fs = cx.tile([128, 520], F32, "fs")
        cx.op(V, lambda e: e.tensor_copy(out=fs.ap[:, 0:512], in_=HT.ap), [HT], [fs])
        cx.op(V, lambda e: e.tensor_copy(out=fs.ap[:, 512:520], in_=logD.ap), [logD], [fs])
        cx.dma(cx.sp, self.ssF_send.ap, fs.ap, reads=[fs], writes=[self.ssF_send])
        cx.allgather(self.ssF_send, self.ssF_g, [[0, 1, 2, 3], [4, 5, 6, 7]])
        fg = cx.tile([128, 3, 520], F32, "fg")
        cx.dma(cx.sp, fg.ap, self.ssF_g.ap[0:384, :].rearrange("(j p) c -> p j c", p=128), reads=[self.ssF_g], writes=[fg])
        Sc = Hs[2]
        ed = cx.tile([128, 8], F32, "ed")
        cx.op(V, lambda e: e.tensor_scalar(out=HT.ap, in0=Sc.ap, scalar1=selt.ap[:, 0:1], scalar2=None, op0=ALU.mult), [Sc, selt], [HT])
        for j in range(3):
            cx.op(A, lambda e: e.activation(out=ed.ap, in_=fg.ap[:, j, 512:520], func=AF.Exp), [fg], [ed])
            for h in range(8):
                cx.op(V, lambda e: e.scalar_tensor_tensor(out=Sc.ap[:, h * 64:(h + 1) * 64], in0=Sc.ap[:, h * 64:(h + 1) * 64],
                                                          scalar=ed.ap[:, h:h + 1], in1=fg.ap[:, j, h * 64:(h + 1) * 64],
                                                          op0=ALU.mult, op1=ALU.add), [Sc, ed, fg], [Sc])
            cx.op(V, lambda e: e.scalar_tensor_tensor(out=HT.ap, in0=Sc.ap, scalar=selt.ap[:, j + 1:j + 2], in1=HT.ap,
                                                      op0=ALU.mult, op1=ALU.add), [Sc, selt, HT], [HT])
        cx.op(P, lambda e: e.tensor_copy(out=HTb.ap, in_=HT.ap), [HT], [HTb])
        self._yoffP = HTb
        for t in range(16):
            chunk(t, 128, True, False)
            prep_B(t, 128)
            upd(t, 0, 128, HT, HTb, etot)
        cx.dma(cx.sp, o_st[0], HT.ap, reads=[HT], is_output=True)
    def sb_phase(self):
        cx = self.cx
        V, P, A, PE = cx.dve, cx.pool, cx.act, cx.pe
        selt = self.selt
        self.mixF = cx.tile([128, NTT, 512], BF16, "mixD")
        self.cd_mark2 = cx.arena_off
        qT, kT, Vb, mixF = self.qT, self.kT, self.Vb, self.mixF
        NTRI = cx.tile([128, 128], BF16, "ntri")
        NONE_ = cx.tile([128, 128], BF16, "none")
        cx.op(V, lambda e: e.tensor_scalar(out=NONE_.ap, in0=self.cconst(2), scalar1=-1.0, scalar2=None, op0=ALU.mult), [self.cstt], [NONE_])
        cx.op(V, lambda e: e.tensor_scalar(out=NTRI.ap, in0=self.c2(5), scalar1=-1.0, scalar2=None, op0=ALU.mult), [self.cst2t], [NTRI])
        ntri_j = [cx.tile([128, 128], BF16, f"ntri{j}") for j in range(3)]
        none_j = [cx.tile([128, 128], BF16, f"none{j}") for j in range(3)]
        for j in range(3):
            cx.op(V, lambda e: e.tensor_scalar(out=ntri_j[j].ap, in0=self.c2(5), scalar1=selt.ap[:, 24 + j:25 + j], scalar2=-1.0,
                                               op0=ALU.mult, op1=ALU.mult), [self.cst2t, selt], [ntri_j[j]])
            cx.op(V, lambda e: e.tensor_scalar(out=none_j[j].ap, in0=self.cconst(2), scalar1=selt.ap[:, 24 + j:25 + j], scalar2=-1.0,
                                               op0=ALU.mult, op1=ALU.mult), [self.cstt, selt], [none_j[j]])
        maskS = cx.tile([128, 128], BF16, "maskS")
        cx.op(V, lambda e: e.tensor_copy(out=maskS.ap, in_=self.c2(3)), [self.cst2t], [maskS])
        bsu80 = cx.tile([128, 128], BF16, "bsu80")
        cx.op(V, lambda e: e.tensor_copy(out=bsu80.ap, in_=self.c2(4)), [self.cst2t], [bsu80])
        zero_b = cx.tile([128, 1], F32, "zero_b")
        cx.op(V, lambda e: e.memset(zero_b.ap, 0.0), [], [zero_b])
        kg = [cx.tile([128, 3, 2048], BF16, f"kg{i}") for i in range(2)]
        vg = [cx.tile([128, 3, 16, 128], BF16, f"vg{i}", hreg=(i == 1)) for i in range(2)]
        kc = cx.tile([128, 2, 2048], BF16, "kc")
        vc = cx.tile([128, 2, 16, 2, 64], BF16, "vc")
        kstg = cx.tile([128, 4, 128], F32, "kstg")
        ef = [cx.tile([128, 512], F32, f"ef{i}") for i in range(5)]
        spb = [cx.tile([128, 512], BF16, f"spb{i}") for i in range(3)]
        uf = [cx.tile([128, 512], F32, f"uf{i}", hreg=(i == 2)) for i in range(3)]
        wb = [cx.tile([128, 512], BF16, f"wb{i}", hreg=True) for i in range(3)]
        Csb = cx.tile([128, 512], F32, "Csb")
        qms = [cx.tile([128, NT], BF16, f"qm{i}") for i in range(2)]
        w_seg = [[cx.tile([128, 80], BF16, f"wseg{s_}{i}") for i in range(3)] for s_ in range(3)]
        for s_ in range(3):
            for i in range(3):
                cx.op(V, lambda e: e.memset(w_seg[s_][i].ap, 0.0), [], [w_seg[s_][i]])
        w_own = [cx.tile([128, 80], BF16, f"wown{s_}") for s_ in range(3)]
        for s_ in range(3):
            cx.op(V, lambda e: e.memset(w_own[s_].ap, 0.0), [], [w_own[s_]])
        it = [0]
        ob_i = [0]

        def load_pair(c):
            b = c % 2
            cx.dma(cx.sp, kg[b].ap, self.k2_g[c // 2].ap[0:768, :].rearrange("(j r) k -> r j k", r=256)[(c % 2) * 128:(c % 2 + 1) * 128, :, :],
                   reads=[self.k2_g[c // 2]], writes=[kg[b]])
            for i in range(4):
                for j in range(3):
                    cx.dma(cx.sp, vg[b].ap[:, j, 4 * i:4 * i + 4, :],
                           self.v2_g[i].ap[j * 512:(j + 1) * 512, :].rearrange("(t p) x -> p t x", p=128)[:, :, c * 128:(c + 1) * 128],
                           reads=[self.v2_g[i]], writes=[vg[b]])

        load_pair(0)
        for c in range(4):
            if c + 1 < 4:
                load_pair(c + 1)
            b = c % 2
            for s in range(2):
                for t in range(16):
                    if t % 4 == 0:
                        cx.dma(cx.sp, kstg.ap, self.c_sbk[s].rearrange("(t p) x -> p t x", p=128)[:, t:t + 4, c * 128:(c + 1) * 128], writes=[kstg])
                    tb = cx.banks[6 + t % 2]
                    cx.op(PE, lambda e: e.transpose(out=tb.ap[:, 0:128], in_=kstg.ap[:, t % 4, :], identity=self.cconst(0)), [kstg, self.cstt], [tb])
                    cx.op(A, lambda e: e.copy(out=kc.ap[:, s, t * 128:(t + 1) * 128], in_=tb.ap[:, 0:128]), [tb], [kc])
                for hh in range(2):
                    h = 2 * c + hh
                    cx.dma(cx.pool, vc.ap[:, s, :, hh, :], self.c_sbv[s].rearrange("(t p) x -> p t x", p=128)[:, :, h * 64:(h + 1) * 64], writes=[vc])
            for hh in range(2):
                h = 2 * c + hh
                pb = hh * 64
                qm = qms[hh]
                cx.op(P, lambda e: e.tensor_copy(out=qm.ap, in_=qT.ap[:, c, :]), [qT], [qm])
                cx.op(P, lambda e: e.memset(qm.ap[(1 - hh) * 64:(1 - hh) * 64 + 64, :], 0.0), [], [qm])

                def run_steps(steps, ob, outs):
                    N = len(steps)
                    first_pv = [True]
                    npv = sum(len(st_["pvs"]) for st_ in steps)
                    ipv = [0]

                    def A_(n):
                        st_ = steps[n]
                        i = n % 2
                        nk, n_ = st_["nk"], st_["nq"] - st_["lo"]
                        zb, e_, sp_ = cx.banks[0 + i], ef[n % 5], spb[n % 3]
                        cx.op(PE, lambda e: e.matmul(zb.ap[0:nk, 0:n_], lhsT=st_["kap"], rhs=qm.ap[:, st_["qc0"] + st_["lo"]:st_["qc0"] + st_["nq"]],
                                                     start=True, stop=True), [st_["ktl"], qm], [zb])
                        cx.op(A, lambda e: e.activation(out=e_.ap[0:nk, 0:n_], in_=zb.ap[0:nk, 0:n_], func=AF.Exp), [zb], [e_])
                        cx.op(A, lambda e: e.activation(out=sp_.ap[0:nk, 0:n_], in_=e_.ap[0:nk, 0:n_], func=AF.Ln, bias=1.0, scale=1.0), [e_], [sp_])
                        m = st_["mask"]
                        if m is not None:
                            mw = m.shape[1]
                            cx.op(P, lambda e: e.tensor_tensor(out=sp_.ap[0:nk, 0:mw], in0=sp_.ap[0:nk, 0:mw], in1=m, op=ALU.mult),
                                  [sp_, maskS, bsu80], [sp_])

                    def B_(n):
                        st_ = steps[n]
                        i = n % 2
                        nk, n_ = st_["nk"], st_["nq"] - st_["lo"]
                        cl = st_["csb_lo"]
                        lb, cb, e_, sp_, u_ = cx.banks[2 + i], cx.banks[4 + i], ef[n % 5], spb[n % 3], uf[n % 3]
                        w_ = wb[n % 3] if st_["wtile"] is None else st_["wtile"]
                        wlo = st_["wlo"]
                        if st_["clear"] is not None:
                            cx.op(V, lambda e: e.memset(Csb.ap[:, 0:st_["clear"]], 0.0), [], [Csb])
                        cx.op(PE, lambda e: e.matmul(lb.ap[0:nk, 0:n_], lhsT=st_["ntri"].ap[0:nk, 0:nk], rhs=sp_.ap[0:nk, 0:n_], start=True, stop=True),
                              [st_["ntri"], sp_], [lb])
                        cx.op(PE, lambda e: e.matmul(cb.ap[:, 0:n_], lhsT=st_["none"].ap[0:nk, :], rhs=sp_.ap[0:nk, 0:n_], start=True, stop=True),
                              [st_["none"], sp_], [cb])
                        cx.op(V, lambda e: e.tensor_tensor(out=u_.ap[0:nk, 0:n_], in0=lb.ap[0:nk, 0:n_], in1=Csb.ap[0:nk, cl:cl + n_], op=ALU.add),
                              [lb, Csb], [u_])
                        cx.op(V, lambda e: e.tensor_tensor(out=Csb.ap[:, cl:cl + n_], in0=Csb.ap[:, cl:cl + n_], in1=cb.ap[:, 0:n_], op=ALU.add),
                              [cb, Csb], [Csb])
                        cx.op(A, lambda e: e.activation(out=u_.ap[0:nk, 0:n_], in_=u_.ap[0:nk, 0:n_], func=AF.Exp, bias=st_["vbias"], scale=1.0),
                              [u_, selt, zero_b], [u_])
                        hv = (n_ // 2) if n_ >= 256 else 0
                        if hv:
                            cx.op(V, lambda e: e.tensor_tensor(out=w_.ap[0:nk, wlo:wlo + hv], in0=e_.ap[0:nk, 0:hv], in1=u_.ap[0:nk, 0:hv], op=ALU.mult),
                                  [e_, u_], [w_])
                        cx.op(P, lambda e: e.tensor_tensor(out=w_.ap[0:nk, wlo + hv:wlo + n_], in0=e_.ap[0:nk, hv:n_], in1=u_.ap[0:nk, hv:n_], op=ALU.mult),
                              [e_, u_], [w_])
                        m = st_["mask"]
                        if m is not None:
                            mw = m.shape[1]
                            cx.op(P, lambda e: e.tensor_tensor(out=w_.ap[0:nk, wlo:wlo + mw], in0=w_.ap[0:nk, wlo:wlo + mw], in1=m, op=ALU.mult),
                                  [w_, maskS, bsu80], [w_])
                        st_["w"] = w_

                    def C_(n):
                        st_ = steps[n]
                        w_ = st_["w"]
                        for (r0, r1, c0_, c1_, vap, vtl, orows, oc0, oc1) in st_["pvs"]:
                            ipv[0] += 1
                            cx.op(PE, lambda e: e.matmul(ob.ap[0:orows, oc0:oc1], lhsT=w_.ap[r0:r1, c0_:c1_], rhs=vap, start=first_pv[0],
                                                         stop=(ipv[0] == npv), skip_group_check=True), [w_, vtl], [ob])
                            first_pv[0] = False

                    for n in range(N + 4):
                        if n < N:
                            A_(n)
                        if 2 <= n <= N + 1:
                            B_(n - 2)
                        if 4 <= n:
                            C_(n - 4)
                    for (orows, oc0, oc1, ot_) in outs:
                        cx.op(A, lambda e: e.copy(out=mixF.ap[0:orows, ot_, h * 64:(h + 1) * 64], in_=ob.ap[0:orows, oc0:oc1]), [ob], [mixF])

                for B in range(4):
                    ob = cx.banks[6 + ob_i[0] % 2]
                    ob_i[0] += 1
                    steps = []

                    def mk(kap, nk, ktl, lo, ntri_t, none_t, mask, vbias, vap, vtl, r0, r1):
                        pvs = [(r0, r1, sub * 128 - lo, (sub + 1) * 128 - lo, vap, vtl, 128, sub * 64, (sub + 1) * 64) for sub in range(lo // 128, 4)]
                        return dict(kap=kap, nk=nk, ktl=ktl, qc0=B * 512, nq=512, lo=lo, ntri=ntri_t, none=none_t, mask=mask, vbias=vbias,
                                    wtile=None, wlo=0, csb_lo=lo, clear=None, pvs=pvs)

                    for kt in range(4 * B + 3, -1, -1):
                        lo = max(0, kt - 4 * B) * 128
                        diag = kt >= 4 * B
                        steps.append(mk(kT.ap[0:128, c, kt * 128:(kt + 1) * 128], 128, kT, lo, NTRI, NONE_, (maskS.ap if diag else None),
                                        zero_b.ap[:, 0:1], Vb.ap[:, kt, h, :], Vb, 0, 128))
                    for j in range(2, -1, -1):
                        for kt in range(15, -1, -1):
                            steps.append(mk(kg[b].ap[0:128, j, kt * 128:(kt + 1) * 128], 128, kg[b], 0, ntri_j[j], none_j[j], None,
                                            selt.ap[:, 4 + j:5 + j], vg[b].ap[:, j, kt, hh * 64:(hh + 1) * 64], vg[b], 0, 128))
                    steps.append(mk(kT.ap[0:128, c, 2048:2128], 80, kT, 0, NTRI, NONE_, None, zero_b.ap[0:80, 0:1],
                                    Vb.ap[64:80, 16, h, :], Vb, 64, 80))
                    steps[0]["clear"] = 512
                    run_steps(steps, ob, [(128, sub * 64, (sub + 1) * 64, 4 * B + sub) for sub in range(4)])
                ob = cx.banks[6 + ob_i[0] % 2]
                ob_i[0] += 1
                steps = []
                for s in range(3):
                    a_, n_ = (0, 32) if s == 0 else ((32, 32) if s == 1 else (64, 16))
                    steps.append(dict(kap=kT.ap[0:128, c, 2048:2128], nk=80, ktl=kT, qc0=2048 + a_, nq=n_, lo=0, ntri=NTRI, none=NONE_,
                                      mask=bsu80.ap[0:80, a_:a_ + n_], vbias=zero_b.ap[0:80, 0:1], wtile=w_own[s], wlo=a_, csb_lo=0, clear=n_,
                                      pvs=[(0, 80, 0, 80, Vb.ap[0:80, 16, h, :], Vb, 80, 0, 64)]))
                    if s < 2:
                        for kt in range(15, -1, -1):
                            steps.append(dict(kap=kc.ap[0:128, s, kt * 128:(kt + 1) * 128], nk=128, ktl=kc, qc0=2048 + a_, nq=n_, lo=0, ntri=NTRI,
                                              none=NONE_, mask=None, vbias=zero_b.ap[:, 0:1], wtile=w_seg[s][kt % 3], wlo=a_, csb_lo=0, clear=None,
                                              pvs=[(0, 128, 0, 80, vc.ap[:, s, kt, hh, :], vc, 80, 0, 64)]))
                run_steps(steps, ob, [(80, 0, 64, 16)])

    def mixer_out2(self, w_out_dram, gain_idx):
        cx = self.cx
        cx.barrier()
        cx.arena_off = self.cd_mark2
        cx.h_off = cx.h_base
        for t in range(NTT):
            r = tt_rows(t)
            cx.dma(cx.sp, self.h[t].ap[0:r, :], self.hsave[t * 128:t * 128 + r, :], writes=[self.h[t]])
        wout = cx.tile([128, 8, D], BF16, "wout")
        cx.dma(cx.pool, wout.ap, w_out_dram.rearrange("(k p) n -> p k n", p=128), writes=[wout])
        self.load_gain(self.gpost, gain_idx)
        self.junk = cx.tile([128, D], F32, "junk")
        self.tmpn = cx.tile([128, D], F32, "tmpn")
        mT = [cx.tile([128, 8, 128], BF16, f"mT{i}") for i in range(2)]
        ot = [Tl(cx.sb([128, D], F32, f"ot{i}"), f"ot{i}") for i in range(2)]
        for t in range(NTT):
            r = tt_rows(t)
            bank = cx.banks[t % 2]
            pt = bank.ap.bitcast(BF16).rearrange("p (k c) -> p k c", k=8)
            for k in range(8):
                src = self.mixC if k < 4 else self.mixF
                cx.op(cx.pe, lambda e: e.transpose(out=pt[:, k, 0:r], in_=src.ap[0:r, t, (k % 4) * 128:(k % 4 + 1) * 128],
                                                   identity=self.identb.ap[0:r, 0:r]), [src, self.identb], [bank])
            m = mT[t % 2]
            cx.op(cx.act, lambda e: e.copy(out=m.ap[:, :, 0:r], in_=pt[:, :, 0:r]), [bank], [m])
            o = ot[t % 2]
            for hf in range(2):
                ob = cx.banks[2 + (2 * t + hf) % 4]
                for k in range(8):
                    cx.op(cx.pe, lambda e: e.matmul(ob.ap[0:r, :], lhsT=m.ap[:, k, 0:r], rhs=wout.ap[:, k, hf * 512:(hf + 1) * 512],
                                                    start=(k == 0), stop=(k == 7)), [m, wout], [ob])
                cx.op(cx.act, lambda e: e.copy(out=o.ap[0:r, hf * 512:(hf + 1) * 512], in_=ob.ap[0:r, :]), [ob], [o])
            self.postnorm_add([t], [o], self.gpost, 1.0)
        cx.barrier()

    def write_y(self):
        cx = self.cx
        for t in range(NTT):
            r = tt_rows(t)
            cx.dma(cx.sp, self.y[t * 128:t * 128 + r, :], self.h[t].ap[0:r, :], reads=[self.h[t]], is_output=True)


def make_consts():
    c = np.zeros((128, 1408), np.float32)
    c[:, 0:128] = np.eye(128, dtype=np.float32)
    tri = np.triu(np.ones((128, 128), np.float32))
    c[:, 128:256] = tri
    c[:, 256:384] = 1.0
    bt = np.zeros((128, 128), np.float32)
    for (a, b) in ((0, 32), (32, 64), (64, 80)):
        bt[a:b, a:b] = tri[a:b, a:b]
    c[:, 384:512] = bt
    c[64:80, 512:640] = 1.0
    c[0:32, 640:768] = 1.0
    c[32:64, 768:896] = 1.0
    c[127, 896:1024] = 1.0
    c[31, 1024:1152] = 1.0
    c[63, 1152:1280] = 1.0
    c[79, 1280:1408] = 1.0
    return c


def state_layout(a):
    return np.ascontiguousarray(a.reshape(16, 2, 64).transpose(1, 2, 0).reshape(128, 16))


def state_unlayout(a):
    return np.ascontiguousarray(a.reshape(2, 64, 16).transpose(2, 0, 1).reshape(32, 64))


def make_consts2():
    c = np.zeros((128, 896), np.float32)
    tri = np.triu(np.ones((128, 128), np.float32))
    sl = np.tril(np.ones((128, 128), np.float32), -1)
    su = np.triu(np.ones((128, 128), np.float32), 1)
    blocks = ((0, 32), (32, 64), (64, 80))
    c[:, 0:128] = sl
    for (a, b) in blocks:
        c[a:b, 128 + a:128 + b] = sl[a:b, a:b]
        c[a:b, 256 + a:256 + b] = 1.0
        c[a:b, 512 + a:512 + b] = su[a:b, a:b]
    c[:, 384:512] = su
    c[:, 640:768] = tri.T
    return c


def cd_inputs(inputs, c):
    m = {}
    m["cd_w_in"] = inputs["cd_w_in"][0]
    m["cst2"] = make_consts2()
    cw = inputs["ssd_conv_w"][0]
    cb = inputs["ssd_conv_b"][0]
    cc = np.concatenate([cw, cb[None]], 0).reshape(5, 8, 128).transpose(2, 1, 0)
    m["convw"] = cc.reshape(128, 40)
    m["ssdp"] = np.concatenate([inputs["ssd_dt_bias"][0], inputs["ssd_a_log"][0], inputs["ssd_d"][0]])[None]
    m["ssd_norm"] = inputs["ssd_norm"]
    st = inputs["state_ssd"][0, 2 * c:2 * c + 2]
    m["ssdinit"] = st.transpose(0, 3, 1, 2).reshape(2, 128, 512)
    cv = inputs["state_conv"][0, 2 * c:2 * c + 2]
    m["convinit"] = cv.reshape(2, 3, 8, 128).transpose(3, 0, 2, 1).reshape(128, 48)
    m["cd_w_out"] = inputs["cd_w_out"][0]
    m["c_sbk"] = inputs["cache_sb_k"][0, 2 * c:2 * c + 2].reshape(2, 2048, 512)
    m["c_sbv"] = inputs["cache_sb_v"][0, 2 * c:2 * c + 2].reshape(2, 2048, 512)
    return m


def ab_inputs(inputs, c):
    q = c % 4
    m = {}
    m["ab_w_in"] = inputs["ab_w_in"][0]
    m["fox_b_f"] = inputs["fox_b_f"].reshape(1, 8)
    a_re, a_im = inputs["s5_a_re"][0], inputs["s5_a_im"][0]
    ldt = np.repeat(inputs["s5_log_dt"][0][:, None], 64, axis=1)
    m["s5p"] = np.concatenate([state_layout(a_re), state_layout(a_im), state_layout(ldt)], axis=1)
    b = np.stack([inputs["s5_b_re"][0], inputs["s5_b_im"][0]], 0)
    cc = np.stack([inputs["s5_c_re"][0], inputs["s5_c_im"][0]], 0)
    bl = np.zeros((128, 16, 2, 128), np.float32)
    cl = np.zeros((128, 16, 2, 128), np.float32)
    for st in range(16):
        for gl in range(2):
            g = 2 * st + gl
            g8 = g % 8
            for ri in range(2):
                bl[g8 * 16:(g8 + 1) * 16, st, ri, gl * 64:(gl + 1) * 64] = b[ri, g].T
                cl[gl * 64:(gl + 1) * 64, st, ri, g8 * 16:(g8 + 1) * 16] = cc[ri, g].T
    m["s5b"] = bl.reshape(128, -1)
    m["s5c"] = cl.reshape(128, -1)
    dd = inputs["s5_d"][0].reshape(4, 128)
    dl = np.zeros((128, 4, 128), np.float32)
    for k in range(4):
        dl[np.arange(128), k, np.arange(128)] = dd[k]
    m["s5d"] = dl.reshape(128, -1)
    m["s5init"] = np.concatenate([state_layout(inputs["state_s5_re"][0, 2 * c]), state_layout(inputs["state_s5_im"][0, 2 * c]),
                                  state_layout(inputs["state_s5_re"][0, 2 * c + 1]), state_layout(inputs["state_s5_im"][0, 2 * c + 1])], axis=1)
    m["s5_w_glu"] = inputs["s5_w_glu"][0]
    m["s5_b_glu"] = np.ascontiguousarray(inputs["s5_b_glu"][0].reshape(4, 128).T)
    m["ab_w_out"] = inputs["ab_w_out"][0]
    sel = np.zeros((128, 32), np.float32)
    sel[:, q] = 1.0
    for j in range(3):
        sel[:, 4 + j] = 0.0 if j < q else -30000.0
        sel[:, 24 + j] = 1.0 if j < q else 0.0
        sel[:, 27 + j] = 1.0 if j == q - 1 else 0.0
        for jp in range(4):
            sel[:, 8 + j * 4 + jp] = 1.0 if (j <= jp < q) else 0.0
    for jp in range(4):
        sel[:, 20 + jp] = 1.0 if jp < q else 0.0
    m["sel"] = sel
    m["c_foxk"] = inputs["cache_fox_k"][0, 2 * c:2 * c + 2].reshape(2, 2048, 512)
    m["c_foxv"] = inputs["cache_fox_v"][0, 2 * c:2 * c + 2].reshape(2, 2048, 512)
    m["c_foxf"] = inputs["cache_fox_logf"][0, 2 * c:2 * c + 2]
    return m


def core_inputs(inputs, c):
    seq, q = c // 4, c % 4
    xin = np.concatenate([inputs["x_prompt"][seq, q * TP:(q + 1) * TP], inputs["x_sample"][2 * c],
                          inputs["x_sample"][2 * c + 1], inputs["meta_tokens"]], axis=0)
    m = {"xin": np.ascontiguousarray(xin, np.float32), "cst": make_consts()}
    m["norms"] = np.ascontiguousarray(np.concatenate(
        [inputs[k] for k in ("norm_ffn1_pre", "norm_ffn1_post", "norm_mix_pre", "norm_mix_post",
                             "norm_ffn2_pre", "norm_ffn2_post")], axis=0), np.float32)
    for f, nm in enumerate(("ffn1", "ffn2")):
        for l in range(2):
            m[f"w_gate{f}_{l}"] = inputs[f"{nm}_w_gate"][l]
            m[f"w_up{f}_{l}"] = inputs[f"{nm}_w_up"][l]
            m[f"w_down{f}_{l}"] = inputs[f"{nm}_w_down"][l]
    m.update(ab_inputs(inputs, c))
    m.update(cd_inputs(inputs, c))
    return m


def run(inputs, stop=None):
    inputs = {k: np.asarray(v) for k, v in inputs.items()}
    b = Builder(stop)
    nc = b.build()
    in_maps = []
    for c in range(NCORES):
        m = core_inputs(inputs, c)
        in_maps.append({k: np.ascontiguousarray(m[k], np.float32) for k in b.dram_in})
    res = run_bass_kernel_spmd(nc, in_maps, core_ids=list(range(NCORES)))
    return res.results


def kernel(**inputs):
    res = run(inputs)
    f32 = np.float32
    y_p = np.zeros((2, 8192, D), f32)
    y_s = np.zeros((16, 32, D), f32)
    L0 = 8208
    s5_re_p = np.zeros((1, 2, 32, 64), f32); s5_im_p = np.zeros((1, 2, 32, 64), f32)
    fox_k_p = np.zeros((1, 2, L0, 8, 64), f32); fox_v_p = np.zeros((1, 2, L0, 8, 64), f32)
    fox_f_p = np.zeros((1, 2, L0, 8), f32)
    ssd_p = np.zeros((1, 2, 8, 64, 128), f32); conv_p = np.zeros((1, 2, 3, 1024), f32)
    sb_k_p = np.zeros((1, 2, L0, 8, 64), f32); sb_v_p = np.zeros((1, 2, L0, 8, 64), f32)
    s5_re_s = np.zeros((1, 16, 32, 64), f32); s5_im_s = np.zeros((1, 16, 32, 64), f32)
    fox_k_s = np.zeros((1, 16, 32, 8, 64), f32); fox_v_s = np.zeros((1, 16, 32, 8, 64), f32)
    fox_f_s = np.zeros((1, 16, 32, 8), f32)
    ssd_s = np.zeros((1, 16, 8, 64, 128), f32); conv_s = np.zeros((1, 16, 3, 1024), f32)
    sb_k_s = np.zeros((1, 16, 32, 8, 64), f32); sb_v_s = np.zeros((1, 16, 32, 8, 64), f32)

    def hst(a):
        return a.reshape(128, 8, 64).transpose(1, 2, 0)

    for c in range(NCORES):
        r = res[c]
        seq, q = c // 4, c % 4
        y = r["y"]
        y_p[seq, q * TP:(q + 1) * TP] = y[0:TP]
        lo, hi = 16 + q * TP, 16 + (q + 1) * TP
        for dstp, dsts, key in ((fox_k_p, fox_k_s, "o_foxk"), (fox_v_p, fox_v_s, "o_foxv"), (sb_k_p, sb_k_s, "o_sbk"), (sb_v_p, sb_v_s, "o_sbv")):
            a = r[key]
            dstp[0, seq, lo:hi] = a[0:TP].reshape(TP, 8, 64)
            if q == 0:
                dstp[0, seq, 0:16] = a[2112:2128].reshape(16, 8, 64)
            dsts[0, 2 * c] = a[2048:2080].reshape(32, 8, 64)
            dsts[0, 2 * c + 1] = a[2080:2112].reshape(32, 8, 64)
        a = r["o_foxf"]
        fox_f_p[0, seq, lo:hi] = a[0:TP]
        if q == 0:
            fox_f_p[0, seq, 0:16] = a[2112:2128]
        fox_f_s[0, 2 * c] = a[2048:2080]
        fox_f_s[0, 2 * c + 1] = a[2080:2112]
        st = r["o_s5st"]
        for i in range(2):
            y_s[2 * c + i] = y[2048 + 32 * i:2080 + 32 * i]
            s5_re_s[0, 2 * c + i] = state_unlayout(st[:, 32 + 32 * i:48 + 32 * i])
            s5_im_s[0, 2 * c + i] = state_unlayout(st[:, 48 + 32 * i:64 + 32 * i])
            ssd_s[0, 2 * c + i] = hst(r["o_ssd"][1 + i])
            conv_s[0, 2 * c + i] = r["o_conv"][1 + i]
        if q == 3:
            s5_re_p[0, seq] = state_unlayout(st[:, 0:16])
            s5_im_p[0, seq] = state_unlayout(st[:, 16:32])
            ssd_p[0, seq] = hst(r["o_ssd"][0])
            conv_p[0, seq] = r["o_conv"][0]
    return (y_p, y_s, s5_re_p, s5_im_p, fox_k_p, fox_v_p, fox_f_p, ssd_p, conv_p, sb_k_p, sb_v_p,
            s5_re_s, s5_im_s, fox_k_s, fox_v_s, fox_f_s, ssd_s, conv_s, sb_k_s, sb_v_s)
```

```python
import numpy as np
import concourse.bass as bass
import concourse.mybir as mybir
from concourse.bass_utils import run_bass_kernel_spmd

F32 = mybir.dt.float32
BF16 = mybir.dt.bfloat16
AF = mybir.ActivationFunctionType
ALU = mybir.AluOpType

NCORES = 8
D = 1024
DFF = 4096
TP = 2048
NT = 2128
NTT = 17
EPS = 1e-6
COLT = [(0, 512), (512, 512), (1024, 512), (1536, 512), (2048, 80)]


def tt_rows(t):
    return 128 if t < 16 else 80


class Tl:
    __slots__ = ("ap", "w", "r", "name", "excl")

    def __init__(self, ap, name="", excl=False):
        self.ap = ap
        self.w = None
        self.r = {}
        self.name = name
        self.excl = excl


class Eng:
    def __init__(self, nc, h, name, compute=True):
        self.h = h
        self.name = name
        self.sem = nc.alloc_semaphore("prog_" + name) if compute else None
        self.cnt = 0
        self.known = {}


class Ctx:
    def __init__(self, nc):
        self.nc = nc
        self.pe = Eng(nc, nc.tensor, "pe")
        self.act = Eng(nc, nc.scalar, "act")
        self.dve = Eng(nc, nc.vector, "dve")
        self.pool = Eng(nc, nc.gpsimd, "pool")
        self.sp = Eng(nc, nc.sync, "sp", compute=False)
        self.engs = [self.pe, self.act, self.dve, self.pool, self.sp]
        self.dsems = [[nc.alloc_semaphore(f"dma{i}"), 0] for i in range(56)]
        self.dnext = 0
        self.ccsem = nc.alloc_semaphore("ccsem")
        self.cccnt = 0
        self.uid = 0
        self.out_events = []

    def init_arena(self, n32):
        self.arena = self.nc.alloc_sbuf_tensor("arena", [128, n32], F32).ap()
        self.arena_n = n32
        self.arena_off = 0
        self.banks = [Tl(self.nc.alloc_psum_tensor(f"bank{i}", [128, 512], F32).ap(), f"bank{i}", excl=True) for i in range(8)]

    def sb(self, shape, dt=F32, name=None, hreg=False):
        shape = list(shape)
        esz = 4 if dt == F32 else 2
        n = 1
        for d in shape[1:]:
            n *= d
        n32 = (n * esz + 31) // 32 * 8
        if hreg:
            off = self.h_off
            assert off + n32 <= self.h_end, (name, off, n32, self.h_end)
            self.h_off = off + n32
        else:
            off = self.arena_off
            assert off + n32 <= self.arena_n, (name, off, n32, self.arena_n)
            self.arena_off = off + n32
        ap = self.arena[0:shape[0], off:off + (n * esz) // 4]
        if dt != F32:
            ap = ap.bitcast(dt)
        if len(shape) == 3:
            ap = ap.rearrange("p (a b) -> p a b", a=shape[1])
        elif len(shape) == 4:
            ap = ap.rearrange("p (a b c) -> p a b c", a=shape[1], b=shape[2])
        elif len(shape) == 5:
            ap = ap.rearrange("p (a b c d) -> p a b c d", a=shape[1], b=shape[2], c=shape[3])
        return ap

    def ps(self, shape, dt=F32, name=None):
        self.uid += 1
        return self.nc.alloc_psum_tensor(f"{name or 'ps'}_{self.uid}", list(shape), dt).ap()

    def tile(self, shape, dt=F32, name=None, hreg=False):
        return Tl(self.sb(shape, dt, name, hreg), name or "")

    def ptile(self, shape, dt=F32, name=None):
        return Tl(self.ps(shape, dt, name), name or "")

    def _need(self, E, waits, ev, same_ok):
        if ev is None:
            return
        sem, val, src = ev
        if same_ok and src is E:
            return
        k = sem.num
        if E.known.get(k, 0) >= val:
            return
        if waits.get(k, (None, 0))[1] < val:
            waits[k] = (sem, val)

    def _deps(self, E, reads, writes):
        waits = {}
        for t in reads:
            self._need(E, waits, t.w, False)
            if t.excl:
                for ev in t.r.values():
                    self._need(E, waits, ev, True)
        for t in writes:
            self._need(E, waits, t.w, True)
            for ev in t.r.values():
                self._need(E, waits, ev, True)
        for k, (sem, val) in waits.items():
            E.h.wait_ge(sem, val)
            E.known[k] = val

    def _commit(self, ev, reads, writes):
        k = ev[0].num
        for t in reads:
            t.r[k] = ev
        for t in writes:
            t.w = ev
            t.r = {}

    def op(self, E, fn, reads=(), writes=()):
        self._deps(E, reads, writes)
        ins = fn(E.h)
        E.cnt += 1
        ins.then_inc(E.sem, 1)
        self._commit((E.sem, E.cnt, E), reads, writes)

    def dma(self, Q, out_ap, in_ap, reads=(), writes=(), is_output=False):
        slot = self.dsems[self.dnext]
        self.dnext = (self.dnext + 1) % len(self.dsems)
        sem, tot = slot
        if tot > 0 and Q.known.get(sem.num, 0) < tot:
            Q.h.wait_ge(sem, tot)
            Q.known[sem.num] = tot
        self._deps(Q, reads, writes)
        Q.h.dma_start(out=out_ap, in_=in_ap).then_inc(sem, 16)
        slot[1] = tot + 16
        ev = (sem, tot + 16, None)
        self._commit(ev, reads, writes)
        if is_output:
            self.out_events.append(ev)
        return ev

    def allgather(self, in_t, out_t, groups):
        import os
        if os.environ.get("NOCC") == "1":
            return
        Q = self.pool
        self._deps(Q, [in_t], [out_t])
        Q.h.collective_compute("AllGather", ALU.bypass, replica_groups=groups,
                               ins=[in_t.ap.opt()], outs=[out_t.ap.opt()]).then_inc(self.ccsem)
        self.cccnt += 1
        self._commit((self.ccsem, self.cccnt, None), [in_t], [out_t])

    def barrier(self):
        for E in self.engs:
            for O in self.engs:
                if O is E or O.sem is None or O.cnt == 0:
                    continue
                if E.known.get(O.sem.num, 0) < O.cnt:
                    E.h.wait_ge(O.sem, O.cnt)
                    E.known[O.sem.num] = O.cnt
            for sem, tot in self.dsems:
                if tot > 0 and E.known.get(sem.num, 0) < tot:
                    E.h.wait_ge(sem, tot)
                    E.known[sem.num] = tot
            if self.cccnt and E.known.get(self.ccsem.num, 0) < self.cccnt:
                E.h.wait_ge(self.ccsem, self.cccnt)
                E.known[self.ccsem.num] = self.cccnt

    def finish(self):
        Q = self.sp
        for sem, val, _ in self.out_events:
            if Q.known.get(sem.num, 0) < val:
                Q.h.wait_ge(sem, val)
                Q.known[sem.num] = val


GROUPS = [(list(range(0, 8)), [(0, 512), (512, 512)], 0, 1024),
          (list(range(8, 17)), [(1024, 512), (1536, 512), (2048, 80)], 1024, 1104)]


class Builder:
    def __init__(self, stop=None):
        self.stop = stop
        self.nc = bass.Bass("TRN2", target_bir_lowering=False)
        self.cx = Ctx(self.nc)
        self.dram_in = {}
        self.dram_out = {}

    def din(self, name, shape, dt=F32):
        ap = self.nc.dram_tensor(name, list(shape), dt, kind="ExternalInput").ap()
        self.dram_in[name] = (tuple(shape), dt)
        return ap

    def dout(self, name, shape, dt=F32):
        ap = self.nc.dram_tensor(name, list(shape), dt, kind="ExternalOutput").ap()
        self.dram_out[name] = tuple(shape)
        return ap

    def build(self):
        cx = self.cx
        nc = self.nc
        self.xin = self.din("xin", [NT, D])
        self.cst = self.din("cst", [128, 1408])
        self.norms = self.din("norms", [12, D])
        self.wg = [[self.din(f"w_gate{f}_{l}", [D, DFF]) for l in range(2)] for f in range(2)]
        self.wu = [[self.din(f"w_up{f}_{l}", [D, DFF]) for l in range(2)] for f in range(2)]
        self.wd = [[self.din(f"w_down{f}_{l}", [DFF, D]) for l in range(2)] for f in range(2)]
        self.y = self.dout("y", [NT, D])
        self.c_foxk = self.din("c_foxk", [2, 2048, 512])
        self.c_foxv = self.din("c_foxv", [2, 2048, 512])
        self.c_foxf = self.din("c_foxf", [2, 2048, 8])
        self.declare_ab()
        self.declare_cd()

        cx.init_arena(53184)
        cx.h_base = cx.arena_off
        hbig = cx.sb([128, NTT, D], F32, "h")
        cx.h_end = cx.arena_off
        cx.h_off = cx.h_base
        self.h = [Tl(hbig[:, t, :], f"h{t}") for t in range(NTT)]
        self.cstt = cx.tile([128, 1408], F32, "cst")
        self.identb = cx.tile([128, 128], BF16, "identb")
        self.gpre = cx.tile([128, D], F32, "gpre")
        self.gpost = cx.tile([128, D], F32, "gpost")
        self.neghalf = cx.tile([128, 1], F32, "neghalf")
        self.nrm_ss = [cx.tile([128, 1], F32, "ss") for _ in range(2)]
        self.nrm_rstd = [cx.tile([128, 1], F32, "rstd") for _ in range(2)]
        self.mark = cx.arena_off
        cx.dma(cx.sp, self.cstt.ap, self.cst, writes=[self.cstt])
        self.ident = Tl(self.cstt.ap[:, 0:128], "ident")
        self.ident.w = self.cstt.w
        cx.op(cx.dve, lambda e: e.tensor_copy(out=self.identb.ap, in_=self.cstt.ap[:, 0:128]), [self.cstt], [self.identb])
        cx.op(cx.dve, lambda e: e.memset(self.neghalf.ap, -0.5), [], [self.neghalf])
        for t in range(NTT):
            r = tt_rows(t)
            cx.dma(cx.sp, self.h[t].ap[0:r, :], self.xin[t * 128:t * 128 + r, :], writes=[self.h[t]])

        self.alloc_ffn()
        self.steps = []
        n_ffn = {"ffn1_l0": 1, "ab0": 1, "ab1a": 1, "ab1b": 1, "ab1c": 1, "ab1": 1, "s5": 1, "fox": 1, "mix_l0": 1, "ffn2_l0": 2, "ffn1_l1": 3, "mix_l1": 3, None: 4}[self.stop]
        for fi in range(n_ffn):
            for g in range(2):
                for fb in range(8):
                    self.steps.append((fi % 2, fi // 2, g, fb))
        self.step_i = 0
        self.load_gu(0)
        self.load_d(0)
        si = 0
        while si < len(self.steps):
            which, layer, g, fb = self.steps[si]
            if which == 1:
                cx.barrier()
                self.alloc_ffn()
                self.load_gu(si)
                self.load_d(si)
            self.ffn(which, layer)
            si += 16
            if which == 0 and layer == 1:
                if self.stop == "ffn1_l1":
                    break
                self.mixer_cd()
                if self.stop == "mix_l1":
                    break
            if which == 0 and layer == 0:
                if self.stop == "ffn1_l0":
                    break
                self.mixer_ab()
                if self.stop in ("mix_l0", "ab0", "ab1a", "ab1b", "ab1c", "ab1", "s5", "fox"):
                    break
        self.write_y()
        cx.finish()
        return nc

    def load_gain(self, tl, idx):
        self.cx.dma(self.cx.sp, tl.ap, self.norms[idx:idx + 1, :].partition_broadcast(128), writes=[tl])

    def rms_stats(self, src, r, par):
        cx = self.cx
        ss, rstd = self.nrm_ss[par], self.nrm_rstd[par]
        cx.op(cx.dve, lambda e: e.scalar_tensor_tensor(out=self.junk.ap[0:r, :], in0=src.ap[0:r, :], scalar=1.0,
                                                       in1=src.ap[0:r, :], op0=ALU.mult, op1=ALU.mult,
                                                       accum_out=ss.ap[0:r, :]), [src], [self.junk, ss])
        cx.op(cx.dve, lambda e: e.tensor_scalar(out=ss.ap[0:r, :], in0=ss.ap[0:r, :], scalar1=1.0 / D,
                                                scalar2=EPS, op0=ALU.mult, op1=ALU.add), [ss], [ss])
        cx.op(cx.pool, lambda e: e.tensor_tensor(out=rstd.ap[0:r, :], in0=ss.ap[0:r, :],
                                                 in1=self.neghalf.ap[0:r, :], op=ALU.pow),
              [ss, self.neghalf], [rstd])
        return rstd

    def prenorm_T(self, tiles, gain, xnT, col_base):
        cx = self.cx
        for t in tiles:
            r = tt_rows(t)
            ht = self.h[t]
            xb = self.nrm_xb[t % 2]
            rstd = self.rms_stats(ht, r, t % 2)
            cx.op(cx.dve, lambda e: e.scalar_tensor_tensor(out=xb.ap[0:r, :], in0=ht.ap[0:r, :],
                                                           scalar=rstd.ap[0:r, :], in1=gain.ap[0:r, :],
                                                           op0=ALU.mult, op1=ALU.mult),
                  [ht, rstd, gain], [xb])
            bank = cx.banks[t % 2]
            pt = bank.ap.bitcast(BF16).rearrange("p (k c) -> p k c", k=8)
            for k in range(8):
                cx.op(cx.pe, lambda e: e.transpose(out=pt[:, k, 0:r], in_=xb.ap[0:r, k * 128:(k + 1) * 128],
                                                   identity=self.identb.ap[0:r, 0:r]),
                      [xb, self.identb], [bank])
            c0 = t * 128 - col_base
            cx.op(cx.act, lambda e: e.copy(out=xnT.ap[:, :, c0:c0 + r], in_=pt[:, :, 0:r]), [bank], [xnT])

    def alloc_ffn(self):
        cx = self.cx
        cx.arena_off = self.mark
        self.junk = cx.tile([128, D], F32, "junk")
        self.tmpn = cx.tile([128, D], F32, "tmpn")
        self.nrm_xb = [cx.tile([128, D], BF16, "xb") for _ in range(2)]
        self.wgb = [cx.tile([128, 8, 512], BF16, "wgb") for _ in range(2)]
        self.wub = [cx.tile([128, 8, 512], BF16, "wub") for _ in range(2)]
        self.wdb = [cx.tile([128, 4, D], BF16, "wdb") for _ in range(1)]
        self.xnT = cx.tile([128, 8, 1104], BF16, "xnT")
        self.hid = cx.tile([128, 4, 1104], BF16, "hid")
        accbig = cx.sb([128, 9, D], F32, "acc")
        self.acc = [Tl(accbig[:, t, :], f"acc{t}") for t in range(9)]
        self.sil = [cx.tile([128, 512], F32, "sil") for _ in range(2)]

    def load_gu(self, si):
        if si >= len(self.steps):
            return
        cx = self.cx
        which, layer, g, fb = self.steps[si]
        b = si % 2
        wgv = self.wg[which][layer].rearrange("(k p) f -> p k f", p=128)
        wuv = self.wu[which][layer].rearrange("(k p) f -> p k f", p=128)
        cx.dma(cx.pool, self.wgb[b].ap, wgv[:, :, fb * 512:(fb + 1) * 512], writes=[self.wgb[b]])
        cx.dma(cx.pool, self.wub[b].ap, wuv[:, :, fb * 512:(fb + 1) * 512], writes=[self.wub[b]])

    def load_d(self, si):
        if si >= len(self.steps):
            return
        cx = self.cx
        which, layer, g, fb = self.steps[si]
        wdv = self.wd[which][layer].rearrange("(c p) d -> p c d", p=128)
        cx.dma(cx.pool, self.wdb[0].ap, wdv[:, fb * 4:(fb + 1) * 4, :], writes=[self.wdb[0]])

    def ffn(self, which, layer):
        cx = self.cx
        g_pre = (0 if which == 0 else 4) * 2 + layer
        g_post = (1 if which == 0 else 5) * 2 + layer
        self.load_gain(self.gpre, g_pre)
        self.load_gain(self.gpost, g_post)
        cnt = 0
        for g, (tiles, colts, cbase, ncols) in enumerate(GROUPS):
            self.prenorm_T(tiles, self.gpre, self.xnT, cbase)
            for fb in range(8):
                si = self.step_i
                b = si % 2
                nxt_ok = si + 1 < len(self.steps) and not (self.steps[si][0] == 0 and self.steps[si + 1][0] == 1)
                if nxt_ok:
                    self.load_gu(si + 1)
                wgb, wub, wdb, hid = self.wgb[b], self.wub[b], self.wdb[0], self.hid
                for (c0g, cw) in colts:
                    c0 = c0g - cbase
                    for c in range(4):
                        gp, up, sl = cx.banks[2 + cnt % 2], cx.banks[4 + cnt % 2], self.sil[cnt % 2]
                        cnt += 1
                        for k in range(8):
                            cx.op(cx.pe, lambda e: e.matmul(gp.ap[:, 0:cw], lhsT=wgb.ap[:, k, c * 128:(c + 1) * 128],
                                                            rhs=self.xnT.ap[:, k, c0:c0 + cw], start=(k == 0), stop=(k == 7)),
                                  [wgb, self.xnT], [gp])
                        for k in range(8):
                            cx.op(cx.pe, lambda e: e.matmul(up.ap[:, 0:cw], lhsT=wub.ap[:, k, c * 128:(c + 1) * 128],
                                                            rhs=self.xnT.ap[:, k, c0:c0 + cw], start=(k == 0), stop=(k == 7)),
                                  [wub, self.xnT], [up])
                        cx.op(cx.act, lambda e: e.activation(out=sl.ap[:, 0:cw], in_=gp.ap[:, 0:cw], func=AF.Silu), [gp], [sl])
                        cx.op(cx.dve, lambda e: e.tensor_tensor(out=hid.ap[:, c, c0:c0 + cw], in0=sl.ap[:, 0:cw],
                                                                in1=up.ap[:, 0:cw], op=ALU.mult), [sl, up], [hid])
                for ti, t in enumerate(tiles):
                    r = tt_rows(t)
                    a = self.acc[ti]
                    for hf in range(2):
                        op_ = cx.banks[6 + (ti * 2 + hf) % 2]
                        for c in range(4):
                            cx.op(cx.pe, lambda e: e.matmul(op_.ap[0:r, :], lhsT=hid.ap[:, c, ti * 128:ti * 128 + r],
                                                            rhs=wdb.ap[:, c, hf * 512:(hf + 1) * 512],
                                                            start=(c == 0), stop=(c == 3)),
                                  [hid, wdb], [op_])
                        if fb == 0:
                            cx.op(cx.dve, lambda e: e.tensor_copy(out=a.ap[0:r, hf * 512:(hf + 1) * 512], in_=op_.ap[0:r, :]),
                                  [op_], [a])
                        else:
                            cx.op(cx.dve, lambda e: e.tensor_tensor(out=a.ap[0:r, hf * 512:(hf + 1) * 512],
                                                                    in0=a.ap[0:r, hf * 512:(hf + 1) * 512],
                                                                    in1=op_.ap[0:r, :], op=ALU.add), [op_, a], [a])
                if nxt_ok:
                    self.load_d(si + 1)
                self.step_i += 1
            self.postnorm_add(tiles, self.acc, self.gpost, 0.5)

    def postnorm_add(self, tiles, src, gain, coef):
        cx = self.cx
        for ti, t in enumerate(tiles):
            r = tt_rows(t)
            a = src[ti]
            ht = self.h[t]
            rstd = self.rms_stats(a, r, t % 2)
            cx.op(cx.dve, lambda e: e.scalar_tensor_tensor(out=self.tmpn.ap[0:r, :], in0=a.ap[0:r, :],
                                                           scalar=rstd.ap[0:r, :], in1=gain.ap[0:r, :],
                                                           op0=ALU.mult, op1=ALU.mult),
                  [a, rstd, gain], [self.tmpn])
            cx.op(cx.dve, lambda e: e.scalar_tensor_tensor(out=ht.ap[0:r, :], in0=self.tmpn.ap[0:r, :], scalar=coef,
                                                           in1=ht.ap[0:r, :], op0=ALU.mult, op1=ALU.add),
                  [self.tmpn, ht], [ht])

    def declare_ab(self):
        self.w_ab_in = self.din("ab_w_in", [D, 2056])
        self.fox_bf = self.din("fox_b_f", [1, 8])
        self.s5p = self.din("s5p", [128, 48])
        self.s5b = self.din("s5b", [128, 16 * 2 * 128])
        self.s5c = self.din("s5c", [128, 16 * 2 * 128])
        self.s5d = self.din("s5d", [128, 4 * 128])
        self.s5init = self.din("s5init", [128, 64])
        self.w_glu = self.din("s5_w_glu", [512, 512])
        self.b_glu = self.din("s5_b_glu", [128, 4])
        self.w_ab_out = self.din("ab_w_out", [D, D])
        self.sel = self.din("sel", [128, 32])
        self.o_s5st = self.dout("o_s5st", [128, 96])
        self.o_foxk = self.dout("o_foxk", [NT, 512])
        self.o_foxv = self.dout("o_foxv", [NT, 512])
        self.o_foxf = self.dout("o_foxf", [NT, 8])
        self.hsave = self.nc.dram_tensor("hsave", [NT, D], F32).ap()
        self.s5F_send = Tl(self.nc.dram_tensor("s5F_send", [128, 32], F32).ap(), "s5F_send")
        self.s5F_g = Tl(self.nc.dram_tensor("s5F_g", [512, 32], F32).ap(), "s5F_g")
        self.kT_send = [Tl(self.nc.dram_tensor(f"kT_send{i}", [256, 2048], BF16).ap(), "kT_send") for i in range(2)]
        self.kT_g = [Tl(self.nc.dram_tensor(f"kT_g{i}", [1024, 2048], BF16).ap(), "kT_g") for i in range(2)]
        self.V_send = [Tl(self.nc.dram_tensor(f"V_send{i}", [512, 528], BF16).ap(), "V_send") for i in range(4)]
        self.V_g = [Tl(self.nc.dram_tensor(f"V_g{i}", [2048, 528], BF16).ap(), "V_g") for i in range(4)]
        self.fk_send = Tl(self.nc.dram_tensor("fk_send", [2048, 8], F32).ap(), "fk_send")
        self.fk_g = Tl(self.nc.dram_tensor("fk_g", [8192, 8], F32).ap(), "fk_g")
        self.tot_send = Tl(self.nc.dram_tensor("tot_send", [128, 8], F32).ap(), "tot_send")
        self.tot_g = Tl(self.nc.dram_tensor("tot_g", [512, 8], F32).ap(), "tot_g")

    def cconst(self, i):
        return self.cstt.ap[:, i * 128:(i + 1) * 128]

    def spill_h(self):
        cx = self.cx
        for t in range(NTT):
            r = tt_rows(t)
            cx.dma(cx.sp, self.hsave[t * 128:t * 128 + r, :], self.h[t].ap[0:r, :], reads=[self.h[t]])

    def mixer_ab(self):
        cx = self.cx
        GR = [[0, 1, 2, 3], [4, 5, 6, 7]]
        cx.barrier()
        cx.arena_off = self.mark
        cx.h_off = cx.h_base
        self.mixS = cx.tile([128, 4, NT], BF16, "mixS")
        self.mixF = cx.tile([128, NTT, 512], BF16, "mixF")
        self.Vb = cx.tile([128, NTT, 8, 66], BF16, "Vb")
        self.logf = cx.tile([128, NTT, 8], F32, "logf")
        self.cum = cx.tile([128, NTT, 8], F32, "cum")
        self.totP = cx.tile([128, 8], F32, "totP")
        self.selt = cx.tile([128, 32], F32, "selt")
        cx.dma(cx.sp, self.selt.ap, self.sel, writes=[self.selt])
        ab_mark = cx.arena_off
        self.ab_mark = ab_mark
        self.load_gain(self.gpre, 2 * 2 + 0)
        xnT = cx.tile([128, 8, NT], BF16, "xnTf")
        mark2 = cx.arena_off
        self.junk = cx.tile([128, D], F32, "junk")
        self.nrm_xb = [cx.tile([128, D], BF16, "xb") for _ in range(2)]
        self.prenorm_T(list(range(NTT)), self.gpre, xnT, 0)
        self.spill_h()
        cx.barrier()
        if self.stop == "ab0":
            return
        cx.arena_off = mark2
        win = cx.tile([128, 8, 2056], BF16, "win")
        cx.dma(cx.pool, win.ap, self.w_ab_in.rearrange("(k p) f -> p k f", p=128), writes=[win])
        self.uT = cx.tile([128, 4, NT], BF16, "uT", hreg=True)
        self.qT = cx.tile([128, 4, NT], BF16, "qT", hreg=True)
        self.kT = cx.tile([128, 4, NT], BF16, "kT", hreg=True)
        stg = [cx.tile([128, 512], F32, "stg", hreg=True) for _ in range(2)]
        bfb = cx.tile([128, 8], F32, "bfb", hreg=True)
        cx.dma(cx.sp, bfb.ap, self.fox_bf.partition_broadcast(128), writes=[bfb])
        dests = [self.uT] * 4 + [self.qT] * 4 + [self.kT] * 4
        cnt = 0
        for j in range(12):
            dst = dests[j]
            for (c0, cw) in COLT:
                bank = cx.banks[2 + cnt % 2]
                cnt += 1
                for k in range(8):
                    cx.op(cx.pe, lambda e: e.matmul(bank.ap[:, 0:cw], lhsT=win.ap[:, k, j * 128:(j + 1) * 128],
                                                    rhs=xnT.ap[:, k, c0:c0 + cw], start=(k == 0), stop=(k == 7)),
                          [win, xnT], [bank])
                sc = 0.125 if 4 <= j < 8 else 1.0
                cx.op(cx.act, lambda e: e.activation(out=dst.ap[:, j % 4, c0:c0 + cw], in_=bank.ap[:, 0:cw],
                                                     func=AF.Copy, scale=sc), [bank], [dst])
        if self.stop == "ab1a":
            return
        cx.op(cx.dve, lambda e: e.memset(self.Vb.ap[:, :, :, 64:66], 1.0), [], [self.Vb])
        for t in range(NTT):
            r = tt_rows(t)
            c0 = t * 128
            import os
            dbg = os.environ.get("DBG", "")
            for which, (w0, oap) in enumerate(((1024, self.o_foxk), (1536, self.o_foxv))):
                if dbg and str(which) not in dbg:
                    continue
                bank = cx.banks[4 + which]
                st = stg[which]
                for k in range(8):
                    cx.op(cx.pe, lambda e: e.matmul(bank.ap[0:r, :], lhsT=xnT.ap[:, k, c0:c0 + r],
                                                    rhs=win.ap[:, k, w0:w0 + 512], start=(k == 0), stop=(k == 7)),
                          [win, xnT], [bank])
                cx.op(cx.act, lambda e: e.copy(out=st.ap[0:r, :], in_=bank.ap[0:r, :]), [bank], [st])
                cx.dma(cx.sp, oap[c0:c0 + r, :], st.ap[0:r, :], reads=[st], is_output=True)
                if which == 1:
                    cx.op(cx.dve, lambda e: e.tensor_copy(out=self.Vb.ap[0:r, t, :, 0:64],
                                                          in_=bank.ap[0:r, :].rearrange("p (h d) -> p h d", h=8)),
                          [bank], [self.Vb])
            if dbg and "2" not in dbg:
                continue
            bank = cx.banks[6]
            for k in range(8):
                cx.op(cx.pe, lambda e: e.matmul(bank.ap[0:r, 0:8], lhsT=xnT.ap[:, k, c0:c0 + r],
                                                rhs=win.ap[:, k, 2048:2056], start=(k == 0), stop=(k == 7)),
                      [win, xnT], [bank])
            lf = self.logf
            cx.op(cx.dve, lambda e: e.tensor_tensor(out=lf.ap[0:r, t, :], in0=bank.ap[0:r, 0:8], in1=bfb.ap[0:r, :],
                                                    op=ALU.add), [bank, bfb], [lf])
        if self.stop == "ab1b":
            return
        lf = self.logf
        cx.op(cx.act, lambda e: e.activation(out=lf.ap, in_=lf.ap, func=AF.Exp, scale=-1.0), [lf], [lf])
        cx.op(cx.act, lambda e: e.activation(out=lf.ap, in_=lf.ap, func=AF.Ln, bias=1.0, scale=1.0), [lf], [lf])
        cx.op(cx.dve, lambda e: e.tensor_scalar(out=lf.ap, in0=lf.ap, scalar1=-1.0, scalar2=None, op0=ALU.mult), [lf], [lf])
        cx.dma(cx.sp, self.o_foxf[0:2048, :].rearrange("(t p) h -> p t h", p=128), lf.ap[:, 0:16, :], reads=[lf], is_output=True)
        cx.dma(cx.sp, self.o_foxf[2048:2128, :], lf.ap[0:80, 16, :], reads=[lf], is_output=True)
        cumt = self.cum
        for t in range(16):
            bank = cx.banks[6 + t % 2]
            for tp in range(t):
                cx.op(cx.pe, lambda e: e.matmul(bank.ap[:, 0:8], lhsT=self.cconst(2), rhs=lf.ap[:, tp, :],
                                                start=(tp == 0), stop=False), [self.cstt, lf], [bank])
            cx.op(cx.pe, lambda e: e.matmul(bank.ap[:, 0:8], lhsT=self.cconst(1), rhs=lf.ap[:, t, :],
                                            start=(t == 0), stop=True), [self.cstt, lf], [bank])
            cx.op(cx.dve, lambda e: e.tensor_copy(out=cumt.ap[:, t, :], in_=bank.ap[:, 0:8]), [bank], [cumt])
        bank = cx.banks[6]
        cx.op(cx.pe, lambda e: e.matmul(bank.ap[0:80, 0:8], lhsT=self.cstt.ap[0:80, 384:464], rhs=lf.ap[0:80, 16, :],
                                        start=True, stop=True), [self.cstt, lf], [bank])
        cx.op(cx.dve, lambda e: e.tensor_copy(out=cumt.ap[0:80, 16, :], in_=bank.ap[0:80, 0:8]), [bank], [cumt])
        bank = cx.banks[7]
        for t in range(16):
            cx.op(cx.pe, lambda e: e.matmul(bank.ap[:, 0:8], lhsT=self.cconst(2), rhs=lf.ap[:, t, :],
                                            start=(t == 0), stop=(t == 15)), [self.cstt, lf], [bank])
        cx.op(cx.dve, lambda e: e.tensor_copy(out=self.totP.ap, in_=bank.ap[:, 0:8]), [bank], [self.totP])
        if self.stop == "ab1c":
            return
        for c in range(4):
            cx.dma(cx.sp, self.kT_send[c // 2].ap[(c % 2) * 128:(c % 2 + 1) * 128, :], self.kT.ap[:, c, 0:2048], reads=[self.kT],
                   writes=[self.kT_send[c // 2]])
        for i in range(4):
            cx.dma(cx.sp, self.V_send[i].ap.rearrange("(t p) c -> p t c", p=128),
                   self.Vb.ap[:, 4 * i:4 * i + 4, :, :].rearrange("p t h d -> p t (h d)"), reads=[self.Vb], writes=[self.V_send[i]])
        cx.dma(cx.sp, self.fk_send.ap.rearrange("(t p) h -> p t h", p=128), cumt.ap[:, 0:16, :], reads=[cumt],
               writes=[self.fk_send])
        cx.dma(cx.sp, self.tot_send.ap, self.totP.ap, reads=[self.totP], writes=[self.tot_send])
        for i in range(2):
            cx.allgather(self.kT_send[i], self.kT_g[i], GR)
        for i in range(4):
            cx.allgather(self.V_send[i], self.V_g[i], GR)
        cx.allgather(self.fk_send, self.fk_g, GR)
        cx.allgather(self.tot_send, self.tot_g, GR)
        cx.barrier()
        if self.stop == "ab1":
            return
        cx.arena_off = ab_mark
        self.h_mark_ab = cx.h_off = self.h_mark_after_qk()
        self.s5_phase()
        cx.barrier()
        if self.stop == "s5":
            return
        cx.arena_off = ab_mark
        cx.h_off = self.h_mark_ab
        self.fox_phase()
        if self.stop == "fox":
            return
        self.mixer_out(self.w_ab_out, 3 * 2 + 0)

    def h_mark_after_qk(self):
        return self.cx.h_base + 3 * ((4 * NT * 2 + 31) // 32 * 8)
    def s5_phase(self):
        cx = self.cx
        V, P = cx.dve, cx.pool
        TC = 128

        def t16(name):
            return cx.tile([128, 16], F32, name)

        prm = cx.tile([128, 48], F32, "s5prm")
        cx.dma(cx.sp, prm.ap, self.s5p, writes=[prm])
        a_re, a_im, ldt = prm.ap[:, 0:16], prm.ap[:, 16:32], prm.ap[:, 32:48]
        dt, mag, th, c_, s_, t1, t2 = (t16(n) for n in ("dt", "mag", "th", "c_", "s_", "t1", "t2"))
        halfpi = cx.tile([128, 1], F32, "halfpi")
        cx.op(V, lambda e: e.memset(halfpi.ap, float(np.pi / 2)), [], [halfpi])
        cx.op(cx.act, lambda e: e.activation(out=dt.ap, in_=ldt, func=AF.Exp), [prm], [dt])
        cx.op(V, lambda e: e.tensor_tensor(out=t1.ap, in0=dt.ap, in1=a_re, op=ALU.mult), [dt, prm], [t1])
        cx.op(cx.act, lambda e: e.activation(out=mag.ap, in_=t1.ap, func=AF.Exp), [t1], [mag])
        cx.op(V, lambda e: e.tensor_tensor(out=th.ap, in0=dt.ap, in1=a_im, op=ALU.mult), [dt, prm], [th])
        cx.op(cx.act, lambda e: e.activation(out=s_.ap, in_=th.ap, func=AF.Sin, scale=1.0 / 16), [th], [s_])
        cx.op(cx.act, lambda e: e.activation(out=c_.ap, in_=th.ap, func=AF.Sin, scale=1.0 / 16, bias=halfpi.ap),
              [th, halfpi], [c_])

        def csquare(cr, ci, outr, outi):
            cx.op(V, lambda e: e.tensor_tensor(out=t1.ap, in0=cr.ap, in1=cr.ap, op=ALU.mult), [cr], [t1])
            cx.op(V, lambda e: e.tensor_tensor(out=t2.ap, in0=ci.ap, in1=ci.ap, op=ALU.mult), [ci], [t2])
            cx.op(V, lambda e: e.tensor_tensor(out=outi.ap, in0=cr.ap, in1=ci.ap, op=ALU.mult), [cr, ci], [outi])
            cx.op(V, lambda e: e.tensor_scalar(out=outi.ap, in0=outi.ap, scalar1=2.0, scalar2=None, op0=ALU.mult), [outi], [outi])
            cx.op(V, lambda e: e.tensor_tensor(out=outr.ap, in0=t1.ap, in1=t2.ap, op=ALU.subtract), [t1, t2], [outr])

        for _ in range(4):
            csquare(c_, s_, c_, s_)
        Ur = [t16(f"Ur{k}") for k in range(7)]
        Ui = [t16(f"Ui{k}") for k in range(7)]
        cx.op(V, lambda e: e.tensor_copy(out=Ur[0].ap, in_=c_.ap), [c_], [Ur[0]])
        cx.op(V, lambda e: e.tensor_copy(out=Ui[0].ap, in_=s_.ap), [s_], [Ui[0]])
        for k in range(1, 7):
            csquare(Ur[k - 1], Ui[k - 1], Ur[k], Ui[k])
        ab_re, ab_im, cf_re, cf_im, den, nr = (t16(n) for n in ("ab_re", "ab_im", "cf_re", "cf_im", "den", "nr"))
        cx.op(V, lambda e: e.tensor_tensor(out=ab_re.ap, in0=mag.ap, in1=c_.ap, op=ALU.mult), [mag, c_], [ab_re])
        cx.op(V, lambda e: e.tensor_tensor(out=ab_im.ap, in0=mag.ap, in1=s_.ap, op=ALU.mult), [mag, s_], [ab_im])
        cx.op(V, lambda e: e.tensor_tensor(out=t1.ap, in0=a_re, in1=a_re, op=ALU.mult), [prm], [t1])
        cx.op(V, lambda e: e.tensor_tensor(out=t2.ap, in0=a_im, in1=a_im, op=ALU.mult), [prm], [t2])
        cx.op(V, lambda e: e.tensor_tensor(out=den.ap, in0=t1.ap, in1=t2.ap, op=ALU.add), [t1, t2], [den])
        cx.op(V, lambda e: e.reciprocal(out=den.ap, in_=den.ap), [den], [den])
        cx.op(V, lambda e: e.tensor_scalar(out=nr.ap, in0=ab_re.ap, scalar1=-1.0, scalar2=None, op0=ALU.add), [ab_re], [nr])
        cx.op(V, lambda e: e.tensor_tensor(out=t1.ap, in0=nr.ap, in1=a_re, op=ALU.mult), [nr, prm], [t1])
        cx.op(V, lambda e: e.tensor_tensor(out=t2.ap, in0=ab_im.ap, in1=a_im, op=ALU.mult), [ab_im, prm], [t2])
        cx.op(V, lambda e: e.tensor_tensor(out=cf_re.ap, in0=t1.ap, in1=t2.ap, op=ALU.add), [t1, t2], [cf_re])
        cx.op(V, lambda e: e.tensor_tensor(out=cf_re.ap, in0=cf_re.ap, in1=den.ap, op=ALU.mult), [cf_re, den], [cf_re])
        cx.op(V, lambda e: e.tensor_tensor(out=t1.ap, in0=ab_im.ap, in1=a_re, op=ALU.mult), [ab_im, prm], [t1])
        cx.op(V, lambda e: e.tensor_tensor(out=t2.ap, in0=nr.ap, in1=a_im, op=ALU.mult), [nr, prm], [t2])
        cx.op(V, lambda e: e.tensor_tensor(out=cf_im.ap, in0=t1.ap, in1=t2.ap, op=ALU.subtract), [t1, t2], [cf_im])
        cx.op(V, lambda e: e.tensor_tensor(out=cf_im.ap, in0=cf_im.ap, in1=den.ap, op=ALU.mult), [cf_im, den], [cf_im])
        p_re, p_im = t16("p_re"), t16("p_im")
        csquare(ab_re, ab_im, p_re, p_im)
        for _ in range(10):
            csquare(p_re, p_im, p_re, p_im)
        E_re = cx.tile([128, 16, TC], F32, "E_re")
        E_im = cx.tile([128, 16, TC], F32, "E_im")
        R_re = cx.tile([128, 16, TC], F32, "R_re")
        R_im = cx.tile([128, 16, TC], F32, "R_im")
        rtab = cx.tile([128, 16, TC], F32, "rtab")
        tmpa = cx.tile([128, TC], F32, "tmpa")
        tmpb = cx.tile([128, TC], F32, "tmpb")
        for st in range(16):
            cx.op(V, lambda e: e.tensor_copy(out=E_re.ap[:, st, 0:1], in_=Ur[0].ap[:, st:st + 1]), [Ur[0]], [E_re])
            cx.op(V, lambda e: e.tensor_copy(out=E_im.ap[:, st, 0:1], in_=Ui[0].ap[:, st:st + 1]), [Ui[0]], [E_im])
            for k in range(7):
                ln = 1 << k
                ur, ui = Ur[k].ap[:, st:st + 1], Ui[k].ap[:, st:st + 1]
                cx.op(V, lambda e: e.tensor_scalar(out=tmpa.ap[:, 0:ln], in0=E_im.ap[:, st, 0:ln], scalar1=ui, scalar2=None,
                                                   op0=ALU.mult), [E_im, Ui[k]], [tmpa])
                cx.op(V, lambda e: e.tensor_scalar(out=tmpb.ap[:, 0:ln], in0=E_im.ap[:, st, 0:ln], scalar1=ur, scalar2=None,
                                                   op0=ALU.mult), [E_im, Ur[k]], [tmpb])
                cx.op(V, lambda e: e.scalar_tensor_tensor(out=E_im.ap[:, st, ln:2 * ln], in0=E_re.ap[:, st, 0:ln], scalar=ui,
                                                          in1=tmpb.ap[:, 0:ln], op0=ALU.mult, op1=ALU.add),
                      [E_re, Ui[k], tmpb], [E_im])
                cx.op(V, lambda e: e.scalar_tensor_tensor(out=E_re.ap[:, st, ln:2 * ln], in0=E_re.ap[:, st, 0:ln], scalar=ur,
                                                          in1=tmpa.ap[:, 0:ln], op0=ALU.mult, op1=ALU.subtract),
                      [E_re, Ur[k], tmpa], [E_re])
            cr, ci = cf_re.ap[:, st:st + 1], cf_im.ap[:, st:st + 1]
            cx.op(V, lambda e: e.tensor_scalar(out=tmpa.ap, in0=E_im.ap[:, st, :], scalar1=ci, scalar2=None, op0=ALU.mult),
                  [E_im, cf_im], [tmpa])
            cx.op(V, lambda e: e.scalar_tensor_tensor(out=R_re.ap[:, st, :], in0=E_re.ap[:, st, :], scalar=cr, in1=tmpa.ap,
                                                      op0=ALU.mult, op1=ALU.add), [E_re, cf_re, tmpa], [R_re])
            cx.op(V, lambda e: e.tensor_scalar(out=tmpb.ap, in0=E_im.ap[:, st, :], scalar1=cr, scalar2=None, op0=ALU.mult),
                  [E_im, cf_re], [tmpb])
            cx.op(V, lambda e: e.scalar_tensor_tensor(out=R_im.ap[:, st, :], in0=E_re.ap[:, st, :], scalar=ci, in1=tmpb.ap,
                                                      op0=ALU.mult, op1=ALU.subtract), [E_re, cf_im, tmpb], [R_im])
            cx.op(V, lambda e: e.tensor_scalar(out=rtab.ap[:, st, :], in0=self.cconst(2), scalar1=mag.ap[:, st:st + 1],
                                               scalar2=None, op0=ALU.mult), [self.cstt, mag], [rtab])
        bl = cx.tile([128, 16, 2, 128], BF16, "bl")
        cl = cx.tile([128, 16, 2, 128], BF16, "cl")
        dl = cx.tile([128, 4, 128], BF16, "dl")
        wgl = cx.tile([128, 4, 512], BF16, "wgl")
        bgl = cx.tile([128, 4], F32, "bgl")
        cx.dma(cx.pool, bl.ap, self.s5b.rearrange("p (s r c) -> p s r c", s=16, r=2), writes=[bl])
        cx.dma(cx.pool, cl.ap, self.s5c.rearrange("p (s r c) -> p s r c", s=16, r=2), writes=[cl])
        cx.dma(cx.pool, dl.ap, self.s5d.rearrange("p (s c) -> p s c", s=4), writes=[dl])
        cx.dma(cx.pool, wgl.ap, self.w_glu.rearrange("(k p) f -> p k f", p=128), writes=[wgl])
        cx.dma(cx.sp, bgl.ap, self.b_glu, writes=[bgl])
        cx.op(V, lambda e: e.tensor_scalar(out=cl.ap[:, :, 1, :], in0=cl.ap[:, :, 1, :], scalar1=-1.0, scalar2=None,
                                           op0=ALU.mult), [cl], [cl])
        wks = [[cx.tile([128, TC], F32, f"wk{i}", hreg=True) for i in range(8)] for _ in range(2)]
        hre = [cx.tile([128, TC], BF16, f"hre{i}") for i in range(4)]
        him = [cx.tile([128, TC], BF16, f"him{i}") for i in range(4)]
        gf = [cx.tile([128, TC], F32, f"gf{i}") for i in range(4)]
        gb = [cx.tile([128, TC], BF16, f"gb{i}") for i in range(4)]
        sg = cx.tile([128, TC], F32, "sg")
        car_re, car_im = t16("car_re"), t16("car_im")
        c1 = [cx.tile([128, 1], F32, f"c1_{i}") for i in range(2)]
        uT, mixS = self.uT, self.mixS
        cnt = [0]

        def run(c0, n, want_y):
            pos = 0
            while pos < n:
                ln = min(TC, n - pos)
                cc0 = c0 + pos
                for cc in range(4):
                    for sl in range(4):
                        st = cc * 4 + sl
                        pr, pi = cx.banks[(cnt[0] % 2) * 2], cx.banks[(cnt[0] % 2) * 2 + 1]
                        cnt[0] += 1
                        wk = wks[cnt[0] % 2]
                        cx.op(cx.pe, lambda e: e.matmul(pr.ap[:, 0:ln], lhsT=bl.ap[:, st, 0, :], rhs=uT.ap[:, cc, cc0:cc0 + ln],
                                                        start=True, stop=True), [bl, uT], [pr])
                        cx.op(cx.pe, lambda e: e.matmul(pi.ap[:, 0:ln], lhsT=bl.ap[:, st, 1, :], rhs=uT.ap[:, cc, cc0:cc0 + ln],
                                                        start=True, stop=True), [bl, uT], [pi])
                        Rr, Ri = R_re.ap[:, st, 0:ln], R_im.ap[:, st, 0:ln]
                        Er, Ei = E_re.ap[:, st, 0:ln], E_im.ap[:, st, 0:ln]
                        w = [x.ap[:, 0:ln] for x in wk]
                        cx.op(V, lambda e: e.tensor_tensor(out=w[0], in0=pr.ap[:, 0:ln], in1=Rr, op=ALU.mult), [pr, R_re], [wk[0]])
                        cx.op(V, lambda e: e.tensor_tensor(out=w[1], in0=pi.ap[:, 0:ln], in1=Ri, op=ALU.mult), [pi, R_im], [wk[1]])
                        cx.op(V, lambda e: e.tensor_tensor(out=w[2], in0=pi.ap[:, 0:ln], in1=Rr, op=ALU.mult), [pi, R_re], [wk[2]])
                        cx.op(V, lambda e: e.tensor_tensor(out=w[3], in0=pr.ap[:, 0:ln], in1=Ri, op=ALU.mult), [pr, R_im], [wk[3]])
                        cx.op(P, lambda e: e.tensor_tensor(out=w[0], in0=w[0], in1=w[1], op=ALU.subtract), [wk[0], wk[1]], [wk[0]])
                        cx.op(P, lambda e: e.tensor_tensor(out=w[2], in0=w[2], in1=w[3], op=ALU.add), [wk[2], wk[3]], [wk[2]])
                        cx.op(V, lambda e: e.tensor_tensor_scan(out=w[4], data0=rtab.ap[:, st, 0:ln], data1=w[0],
                                                                initial=car_re.ap[:, st:st + 1], op0=ALU.mult, op1=ALU.add),
                              [rtab, wk[0], car_re], [wk[4]])
                        cx.op(V, lambda e: e.tensor_tensor_scan(out=w[5], data0=rtab.ap[:, st, 0:ln], data1=w[2],
                                                                initial=car_im.ap[:, st:st + 1], op0=ALU.mult, op1=ALU.add),
                              [rtab, wk[2], car_im], [wk[5]])
                        L = ln - 1
                        gr, gi = wk[4].ap[:, L:L + 1], wk[5].ap[:, L:L + 1]
                        er, ei = E_re.ap[:, st, L:L + 1], E_im.ap[:, st, L:L + 1]
                        cx.op(V, lambda e: e.tensor_tensor(out=c1[0].ap, in0=gi, in1=ei, op=ALU.mult), [wk[5], E_im], [c1[0]])
                        cx.op(V, lambda e: e.tensor_tensor(out=c1[1].ap, in0=gi, in1=er, op=ALU.mult), [wk[5], E_re], [c1[1]])
                        cx.op(V, lambda e: e.scalar_tensor_tensor(out=car_re.ap[:, st:st + 1], in0=gr, scalar=er, in1=c1[0].ap,
                                                                  op0=ALU.mult, op1=ALU.subtract), [wk[4], E_re, c1[0]], [car_re])
                        cx.op(V, lambda e: e.scalar_tensor_tensor(out=car_im.ap[:, st:st + 1], in0=gr, scalar=ei, in1=c1[1].ap,
                                                                  op0=ALU.mult, op1=ALU.add), [wk[4], E_im, c1[1]], [car_im])
                        if want_y:
                            cx.op(P, lambda e: e.tensor_tensor(out=w[6], in0=w[5], in1=Ei, op=ALU.mult), [wk[5], E_im], [wk[6]])
                            cx.op(P, lambda e: e.tensor_tensor(out=w[7], in0=w[5], in1=Er, op=ALU.mult), [wk[5], E_re], [wk[7]])
                            cx.op(V, lambda e: e.tensor_tensor(out=w[0], in0=w[4], in1=Er, op=ALU.mult), [wk[4], E_re], [wk[0]])
                            cx.op(V, lambda e: e.tensor_tensor(out=w[1], in0=w[4], in1=Ei, op=ALU.mult), [wk[4], E_im], [wk[1]])
                            cx.op(P, lambda e: e.tensor_tensor(out=hre[sl].ap[:, 0:ln], in0=w[0], in1=w[6], op=ALU.subtract),
                                  [wk[0], wk[6]], [hre[sl]])
                            cx.op(P, lambda e: e.tensor_tensor(out=him[sl].ap[:, 0:ln], in0=w[1], in1=w[7], op=ALU.add),
                                  [wk[1], wk[7]], [him[sl]])
                    if want_y:
                        yb = cx.banks[4 + cc % 2]
                        for sl in range(4):
                            st = cc * 4 + sl
                            cx.op(cx.pe, lambda e: e.matmul(yb.ap[:, 0:ln], lhsT=cl.ap[:, st, 0, :], rhs=hre[sl].ap[:, 0:ln],
                                                            start=(sl == 0), stop=False), [cl, hre[sl]], [yb])
                            cx.op(cx.pe, lambda e: e.matmul(yb.ap[:, 0:ln], lhsT=cl.ap[:, st, 1, :], rhs=him[sl].ap[:, 0:ln],
                                                            start=False, stop=False), [cl, him[sl]], [yb])
                        cx.op(cx.pe, lambda e: e.matmul(yb.ap[:, 0:ln], lhsT=dl.ap[:, cc, :], rhs=uT.ap[:, cc, cc0:cc0 + ln],
                                                        start=False, stop=True), [dl, uT], [yb])
                        cx.op(cx.act, lambda e: e.activation(out=gf[cc].ap[:, 0:ln], in_=yb.ap[:, 0:ln], func=AF.Gelu_apprx_tanh),
                              [yb], [gf[cc]])
                        cx.op(P, lambda e: e.tensor_copy(out=gb[cc].ap[:, 0:ln], in_=gf[cc].ap[:, 0:ln]), [gf[cc]], [gb[cc]])
                if want_y:
                    for co in range(4):
                        zb = cx.banks[6 + co % 2]
                        for ci in range(4):
                            cx.op(cx.pe, lambda e: e.matmul(zb.ap[:, 0:ln], lhsT=wgl.ap[:, ci, co * 128:(co + 1) * 128],
                                                            rhs=gb[ci].ap[:, 0:ln], start=(ci == 0), stop=(ci == 3)),
                                  [wgl, gb[ci]], [zb])
                        cx.op(cx.act, lambda e: e.activation(out=sg.ap[:, 0:ln], in_=zb.ap[:, 0:ln], func=AF.Sigmoid,
                                                             bias=bgl.ap[:, co:co + 1], scale=1.0), [zb, bgl], [sg])
                        cx.op(V, lambda e: e.tensor_tensor(out=mixS.ap[:, co, cc0:cc0 + ln], in0=gf[co].ap[:, 0:ln],
                                                           in1=sg.ap[:, 0:ln], op=ALU.mult), [gf[co], sg], [mixS])
                pos += ln

        def set_carry_zero():
            cx.op(V, lambda e: e.memset(car_re.ap, 0.0), [], [car_re])
            cx.op(V, lambda e: e.memset(car_im.ap, 0.0), [], [car_im])

        st_out = cx.tile([128, 96], F32, "s5st_out")
        ini = cx.tile([128, 64], F32, "s5ini")
        cx.dma(cx.sp, ini.ap, self.s5init, writes=[ini])
        for si in range(2):
            cx.op(V, lambda e: e.tensor_copy(out=car_re.ap, in_=ini.ap[:, si * 32:si * 32 + 16]), [ini], [car_re])
            cx.op(V, lambda e: e.tensor_copy(out=car_im.ap, in_=ini.ap[:, si * 32 + 16:si * 32 + 32]), [ini], [car_im])
            run(2048 + 32 * si, 32, True)
            cx.op(V, lambda e: e.tensor_copy(out=st_out.ap[:, 32 + si * 32:48 + si * 32], in_=car_re.ap), [car_re], [st_out])
            cx.op(V, lambda e: e.tensor_copy(out=st_out.ap[:, 48 + si * 32:64 + si * 32], in_=car_im.ap), [car_im], [st_out])
        set_carry_zero()
        run(2112, 16, True)
        m_re, m_im = t16("m_re"), t16("m_im")
        cx.op(V, lambda e: e.tensor_copy(out=m_re.ap, in_=car_re.ap), [car_re], [m_re])
        cx.op(V, lambda e: e.tensor_copy(out=m_im.ap, in_=car_im.ap), [car_im], [m_im])
        set_carry_zero()
        run(0, 2048, False)
        fsend = cx.tile([128, 32], F32, "fsend")
        cx.op(V, lambda e: e.tensor_copy(out=fsend.ap[:, 0:16], in_=car_re.ap), [car_re], [fsend])
        cx.op(V, lambda e: e.tensor_copy(out=fsend.ap[:, 16:32], in_=car_im.ap), [car_im], [fsend])
        cx.dma(cx.sp, self.s5F_send.ap, fsend.ap, reads=[fsend], writes=[self.s5F_send])
        cx.allgather(self.s5F_send, self.s5F_g, [[0, 1, 2, 3], [4, 5, 6, 7]])
        fg = cx.tile([128, 4, 32], F32, "fg")
        cx.dma(cx.sp, fg.ap, self.s5F_g.ap.rearrange("(j p) c -> p j c", p=128), reads=[self.s5F_g], writes=[fg])
        s_re, s_im, n_re, n_im = t16("s_re"), t16("s_im"), t16("n_re"), t16("n_im")
        cx.op(V, lambda e: e.tensor_copy(out=s_re.ap, in_=m_re.ap), [m_re], [s_re])
        cx.op(V, lambda e: e.tensor_copy(out=s_im.ap, in_=m_im.ap), [m_im], [s_im])
        selt = self.selt
        cx.op(V, lambda e: e.tensor_scalar(out=car_re.ap, in0=s_re.ap, scalar1=selt.ap[:, 0:1], scalar2=None, op0=ALU.mult),
              [s_re, selt], [car_re])
        cx.op(V, lambda e: e.tensor_scalar(out=car_im.ap, in0=s_im.ap, scalar1=selt.ap[:, 0:1], scalar2=None, op0=ALU.mult),
              [s_im, selt], [car_im])
        for j in range(3):
            cx.op(V, lambda e: e.tensor_tensor(out=t1.ap, in0=p_re.ap, in1=s_re.ap, op=ALU.mult), [p_re, s_re], [t1])
            cx.op(V, lambda e: e.tensor_tensor(out=t2.ap, in0=p_im.ap, in1=s_im.ap, op=ALU.mult), [p_im, s_im], [t2])
            cx.op(V, lambda e: e.tensor_tensor(out=n_re.ap, in0=t1.ap, in1=t2.ap, op=ALU.subtract), [t1, t2], [n_re])
            cx.op(V, lambda e: e.tensor_tensor(out=t1.ap, in0=p_re.ap, in1=s_im.ap, op=ALU.mult), [p_re, s_im], [t1])
            cx.op(V, lambda e: e.tensor_tensor(out=t2.ap, in0=p_im.ap, in1=s_re.ap, op=ALU.mult), [p_im, s_re], [t2])
            cx.op(V, lambda e: e.tensor_tensor(out=n_im.ap, in0=t1.ap, in1=t2.ap, op=ALU.add), [t1, t2], [n_im])
            cx.op(V, lambda e: e.tensor_tensor(out=s_re.ap, in0=n_re.ap, in1=fg.ap[:, j, 0:16], op=ALU.add), [n_re, fg], [s_re])
            cx.op(V, lambda e: e.tensor_tensor(out=s_im.ap, in0=n_im.ap, in1=fg.ap[:, j, 16:32], op=ALU.add), [n_im, fg], [s_im])
            cx.op(V, lambda e: e.scalar_tensor_tensor(out=car_re.ap, in0=s_re.ap, scalar=selt.ap[:, j + 1:j + 2], in1=car_re.ap,
                                                      op0=ALU.mult, op1=ALU.add), [s_re, selt, car_re], [car_re])
            cx.op(V, lambda e: e.scalar_tensor_tensor(out=car_im.ap, in0=s_im.ap, scalar=selt.ap[:, j + 1:j + 2], in1=car_im.ap,
                                                      op0=ALU.mult, op1=ALU.add), [s_im, selt, car_im], [car_im])
        run(0, 2048, True)
        cx.op(V, lambda e: e.tensor_copy(out=st_out.ap[:, 0:16], in_=car_re.ap), [car_re], [st_out])
        cx.op(V, lambda e: e.tensor_copy(out=st_out.ap[:, 16:32], in_=car_im.ap), [car_im], [st_out])
        cx.dma(cx.sp, self.o_s5st, st_out.ap, reads=[st_out], is_output=True)
    def cumsum_tiles(self, lf2, ntile, within_out, tot_out):
        cx = self.cx
        V = cx.dve
        n = ntile * 8
        ba, bb = cx.banks[6], cx.banks[7]
        cx.op(cx.pe, lambda e: e.matmul(ba.ap[:, 0:n], lhsT=self.cconst(1), rhs=lf2, start=True, stop=True), [self.cstt, self.lf_src], [ba])
        cx.op(cx.pe, lambda e: e.matmul(bb.ap[:, 0:n], lhsT=self.cconst(2), rhs=lf2, start=True, stop=True), [self.cstt, self.lf_src], [bb])
        tots = self.cs_tots
        incl = self.cs_incl
        cx.op(V, lambda e: e.tensor_copy(out=tots.ap[:, 0:n], in_=bb.ap[:, 0:n]), [bb], [tots])
        t3 = tots.ap[:, 0:n].rearrange("p (t h) -> p t h", h=8)
        i3 = incl.ap[:, 0:n].rearrange("p (t h) -> p t h", h=8)
        for h in range(8):
            cx.op(V, lambda e: e.tensor_tensor_scan(out=i3[:, :, h], data0=self.cconst(2)[:, 0:ntile], data1=t3[:, :, h],
                                                    initial=0.0, op0=ALU.mult, op1=ALU.add), [tots, self.cstt], [incl])
        cx.op(V, lambda e: e.tensor_copy(out=tot_out.ap, in_=i3[:, ntile - 1, :]), [incl], [tot_out])
        cx.op(V, lambda e: e.tensor_tensor(out=incl.ap[:, 0:n], in0=incl.ap[:, 0:n], in1=tots.ap[:, 0:n], op=ALU.subtract),
              [incl, tots], [incl])
        cx.op(V, lambda e: e.tensor_tensor(out=within_out, in0=incl.ap[:, 0:n], in1=ba.ap[:, 0:n], op=ALU.add),
              [incl, ba], [self.cs_dst])

    def fox_phase(self):
        cx = self.cx
        V, P, A, PE = cx.dve, cx.pool, cx.act, cx.pe
        selt, cum, lf = self.selt, self.cum, self.logf
        totg = cx.tile([128, 4, 8], F32, "totg")
        cx.dma(cx.sp, totg.ap, self.tot_g.ap.rearrange("(j p) h -> p j h", p=128), reads=[self.tot_g], writes=[totg])
        totmeta = cx.tile([128, 8], F32, "totmeta")
        bk = cx.banks[7]
        cx.op(PE, lambda e: e.matmul(bk.ap[:, 0:8], lhsT=self.cstt.ap[0:80, 512:640], rhs=lf.ap[0:80, 16, :], start=True, stop=True),
              [self.cstt, lf], [bk])
        cx.op(V, lambda e: e.tensor_copy(out=totmeta.ap, in_=bk.ap[:, 0:8]), [bk], [totmeta])
        dl = cx.tile([128, 4, 8], F32, "delta")
        for j in range(3):
            cx.op(V, lambda e: e.tensor_scalar(out=dl.ap[:, j, :], in0=totg.ap[:, 0, :], scalar1=selt.ap[:, 8 + j * 4:9 + j * 4],
                                               scalar2=selt.ap[:, 4 + j:5 + j], op0=ALU.mult, op1=ALU.add), [totg, selt], [dl])
            for jp in range(1, 4):
                cx.op(V, lambda e: e.scalar_tensor_tensor(out=dl.ap[:, j, :], in0=totg.ap[:, jp, :],
                                                          scalar=selt.ap[:, 8 + j * 4 + jp:9 + j * 4 + jp], in1=dl.ap[:, j, :],
                                                          op0=ALU.mult, op1=ALU.add), [totg, selt, dl], [dl])
        cx.op(V, lambda e: e.tensor_copy(out=dl.ap[:, 3, :], in_=totmeta.ap), [totmeta], [dl])
        for jp in range(4):
            cx.op(V, lambda e: e.scalar_tensor_tensor(out=dl.ap[:, 3, :], in0=totg.ap[:, jp, :], scalar=selt.ap[:, 20 + jp:21 + jp],
                                                      in1=dl.ap[:, 3, :], op0=ALU.mult, op1=ALU.add), [totg, selt, dl], [dl])
        cref = cx.tile([128, 16, 8], F32, "cref")
        cx.op(PE, lambda e: e.matmul(bk.ap[:, 0:128], lhsT=self.cstt.ap[:, 896:1024], rhs=cum.ap[:, 0:16, :].rearrange("p t h -> p (t h)"),
                                     start=True, stop=True), [self.cstt, cum], [bk])
        cx.op(V, lambda e: e.tensor_copy(out=cref.ap.rearrange("p t h -> p (t h)"), in_=bk.ap[:, 0:128]), [bk], [cref])
        cref16 = cx.tile([128, 3, 8], F32, "cref16")
        for si in range(3):
            cx.op(PE, lambda e: e.matmul(bk.ap[:, 0:8], lhsT=self.cstt.ap[0:80, 1024 + si * 128:1152 + si * 128], rhs=cum.ap[0:80, 16, :],
                                         start=True, stop=True), [self.cstt, cum], [bk])
            cx.op(V, lambda e: e.tensor_copy(out=cref16.ap[:, si, :], in_=bk.ap[:, 0:8]), [bk], [cref16])
        fkg = cx.tile([128, 3, 16, 8], F32, "fkg")
        cx.dma(cx.sp, fkg.ap, self.fk_g.ap[0:6144, :].rearrange("(j t p) h -> p j t h", p=128, t=16), reads=[self.fk_g], writes=[fkg])
        for j in range(3):
            for t in range(16):
                cx.op(V, lambda e: e.tensor_tensor(out=fkg.ap[:, j, t, :], in0=dl.ap[:, j, :], in1=fkg.ap[:, j, t, :], op=ALU.subtract),
                      [dl, fkg], [fkg])
        lfc = cx.tile([128, 2, 16, 8], F32, "lfc")
        cx.dma(cx.sp, lfc.ap, self.c_foxf.rearrange("s (t p) h -> p s t h", p=128), writes=[lfc])
        cumc = cx.tile([128, 2, 16, 8], F32, "cumc")
        totc = cx.tile([128, 2, 8], F32, "totc")
        self.cs_tots = cx.tile([128, 128], F32, "cs_tots", hreg=True)
        self.cs_incl = cx.tile([128, 128], F32, "cs_incl", hreg=True)
        for s in range(2):
            self.lf_src = lfc
            self.cs_dst = cumc
            tt_ = Tl(totc.ap[:, s, :], "totc_s")
            self.cumsum_tiles(lfc.ap[:, s, :, :].rearrange("p t h -> p (t h)"), 16,
                              cumc.ap[:, s, :, :].rearrange("p t h -> p (t h)"), tt_)
            totc.w = tt_.w
            for t in range(16):
                cx.op(V, lambda e: e.tensor_tensor(out=cumc.ap[:, s, t, :], in0=totc.ap[:, s, :], in1=cumc.ap[:, s, t, :], op=ALU.subtract),
                      [totc, cumc], [cumc])
        kg1 = cx.tile([128, 3, 2048], BF16, "kg")
        kg = [kg1, kg1]
        dfull = cx.tile([128, 512], F32, "dfull")
        dr1 = cx.tile([128, 512], F32, "dr1")
        dhb = [cx.tile([128, 512], BF16, f"dhb{i}") for i in range(3)]
        dq4 = cx.tile([128, 2048], BF16, "dq4")
        cx.op(V, lambda e: e.memset(dq4.ap, 0.0), [], [dq4])
        sel3 = cx.tile([128, 128], BF16, "sel3")
        cx.op(V, lambda e: e.memset(sel3.ap, 0.0), [], [sel3])
        for r_ in (0, 32, 64):
            cx.op(V, lambda e: e.memset(sel3.ap[r_:r_ + 1, :], 1.0), [], [sel3])
        vg = [cx.tile([128, 3, 16, 132], BF16, f"vg{i}", hreg=(i == 1)) for i in range(2)]
        kc = cx.tile([128, 2, 2048], BF16, "kc")
        vc = cx.tile([128, 2, 16, 2, 66], BF16, "vc")
        kstg = cx.tile([128, 4, 128], F32, "kstg")
        pt_seg = [[cx.tile([128, 80], BF16, f"ptseg{s_}{i}") for i in range(3)] for s_ in range(2)]
        for s_ in range(2):
            for i in range(3):
                cx.op(V, lambda e: e.memset(pt_seg[s_][i].ap, 0.0), [], [pt_seg[s_][i]])
        bias_o = cx.tile([128, 16, 16], F32, "bias_o")
        bias_g = cx.tile([128, 3, 16, 16], F32, "bias_g")
        bias_m = cx.tile([128, 16], F32, "bias_m")
        bias_s = cx.tile([128, 2, 17], F32, "bias_s")
        bias_mm = cx.tile([128, 1], F32, "bias_mm")
        pts = [cx.tile([128, 512], BF16, f"pt{i}", hreg=True) for i in range(3)]
        rec = cx.tile([128, 1], F32, "rec")
        qms = [cx.tile([128, NT], BF16, f"qm{i}") for i in range(2)]
        maskb = cx.tile([128, 128], BF16, "maskb")
        cx.op(V, lambda e: e.tensor_copy(out=maskb.ap, in_=self.cconst(1)), [self.cstt], [maskb])
        mask80 = cx.tile([128, 128], BF16, "mask80")
        cx.op(V, lambda e: e.tensor_copy(out=mask80.ap, in_=self.cconst(3)), [self.cstt], [mask80])
        cx.op(P, lambda e: e.memset(vc.ap[:, :, :, :, 64:66], 1.0), [], [vc])
        qT, kT, Vb, mixF = self.qT, self.kT, self.Vb, self.mixF
        sb_i = [0]
        ob_i = [0]
        pt_i = [0]

        def load_k(c):
            b = c % 2
            cx.dma(cx.sp, kg[b].ap, self.kT_g[c // 2].ap[0:768, :].rearrange("(j r) k -> r j k", r=256)[(c % 2) * 128:(c % 2 + 1) * 128, :, :],
                   reads=[self.kT_g[c // 2]], writes=[kg[b]])

        def load_v(c):
            b = c % 2
            for i in range(4):
                for j in range(3):
                    cx.dma(cx.sp, vg[b].ap[:, j, 4 * i:4 * i + 4, :],
                           self.V_g[i].ap[j * 512:(j + 1) * 512, :].rearrange("(t p) x -> p t x", p=128)[:, :, c * 132:(c + 1) * 132],
                           reads=[self.V_g[i]], writes=[vg[b]])

        load_v(0)
        for c in range(4):
            load_k(c)
            if c + 1 < 4:
                load_v(c + 1)
            b = c % 2
            for s in range(2):
                for t in range(16):
                    if t % 4 == 0:
                        cx.dma(cx.sp, kstg.ap, self.c_foxk[s].rearrange("(t p) x -> p t x", p=128)[:, t:t + 4, c * 128:(c + 1) * 128],
                               writes=[kstg])
                    tb = cx.banks[5 + t % 2]
                    cx.op(PE, lambda e: e.transpose(out=tb.ap[:, 0:128], in_=kstg.ap[:, t % 4, :], identity=self.cconst(0)),
                          [kstg, self.cstt], [tb])
                    cx.op(A, lambda e: e.copy(out=kc.ap[:, s, t * 128:(t + 1) * 128], in_=tb.ap[:, 0:128]), [tb], [kc])
                for hh in range(2):
                    h = 2 * c + hh
                    cx.dma(cx.pool, vc.ap[:, s, :, hh, 0:64],
                           self.c_foxv[s].rearrange("(t p) x -> p t x", p=128)[:, :, h * 64:(h + 1) * 64], writes=[vc])
            for hh in range(2):
                h = 2 * c + hh
                pb = hh * 64
                qm = qms[hh]
                cx.op(P, lambda e: e.tensor_copy(out=qm.ap, in_=qT.ap[:, c, :]), [qT], [qm])
                cx.op(P, lambda e: e.memset(qm.ap[(1 - hh) * 64:(1 - hh) * 64 + 64, :], 0.0), [], [qm])
                for kt in range(16):
                    cx.op(V, lambda e: e.tensor_scalar(out=bias_o.ap[:, kt, :], in0=cref.ap[:, :, h], scalar1=cum.ap[:, kt, h:h + 1],
                                                       scalar2=None, op0=ALU.subtract), [cref, cum], [bias_o])
                    for j in range(3):
                        cx.op(V, lambda e: e.tensor_scalar(out=bias_g.ap[:, j, kt, :], in0=cref.ap[:, :, h],
                                                           scalar1=fkg.ap[:, j, kt, h:h + 1], scalar2=None, op0=ALU.add),
                              [cref, fkg], [bias_g])
                cx.op(V, lambda e: e.tensor_tensor(out=bias_mm.ap, in0=dl.ap[:, 3, h:h + 1], in1=cum.ap[:, 16, h:h + 1], op=ALU.subtract),
                      [dl, cum], [bias_mm])
                cx.op(V, lambda e: e.tensor_scalar(out=bias_m.ap, in0=cref.ap[:, :, h], scalar1=bias_mm.ap[:, 0:1], scalar2=None,
                                                   op0=ALU.add), [cref, bias_mm], [bias_m])
                for s in range(2):
                    cx.op(V, lambda e: e.tensor_scalar(out=bias_s.ap[:, s, 0:16], in0=cumc.ap[:, s, :, h], scalar1=cref16.ap[:, s, h:h + 1],
                                                       scalar2=None, op0=ALU.add), [cumc, cref16], [bias_s])
                    cx.op(V, lambda e: e.tensor_tensor(out=bias_s.ap[:, s, 16:17], in0=cref16.ap[:, s, h:h + 1], in1=cum.ap[:, 16, h:h + 1],
                                                       op=ALU.subtract), [cref16, cum], [bias_s])
                cx.op(V, lambda e: e.tensor_tensor(out=bias_mm.ap, in0=cref16.ap[:, 2, h:h + 1], in1=cum.ap[:, 16, h:h + 1],
                                                   op=ALU.subtract), [cref16, cum], [bias_mm])

                for B_ in range(4):
                    for sub in range(4):
                        qs_ = 4 * B_ + sub
                        cx.op(V, lambda e: e.tensor_scalar(out=dfull.ap[0:65, sub * 128:(sub + 1) * 128], in0=self.cconst(2)[0:65, :],
                                                           scalar1=cref.ap[0:65, qs_, h:h + 1], scalar2=cref.ap[0:65, 4 * B_ + 3, h:h + 1],
                                                           op0=ALU.mult, op1=ALU.subtract), [cref, self.cstt], [dfull])
                    cx.op(V, lambda e: e.tensor_copy(out=dhb[0].ap[0:65, :], in_=dfull.ap[0:65, :]), [dfull], [dhb[0]])
                    cx.op(V, lambda e: e.tensor_tensor(out=dr1.ap[0:65, :], in0=dfull.ap[0:65, :], in1=dhb[0].ap[0:65, :], op=ALU.subtract),
                          [dfull, dhb[0]], [dr1])
                    cx.op(V, lambda e: e.tensor_copy(out=dhb[1].ap[0:65, :], in_=dr1.ap[0:65, :]), [dr1], [dhb[1]])
                    cx.op(V, lambda e: e.tensor_tensor(out=dfull.ap[0:65, :], in0=dr1.ap[0:65, :], in1=dhb[1].ap[0:65, :], op=ALU.subtract),
                          [dr1, dhb[1]], [dfull])
                    cx.op(V, lambda e: e.tensor_copy(out=dhb[2].ap[0:65, :], in_=dfull.ap[0:65, :]), [dfull], [dhb[2]])
                    for i_, r_ in enumerate((0, 32, 64)):
                        cx.op(V, lambda e: e.tensor_copy(out=dq4.ap[r_:r_ + 1, B_ * 512:(B_ + 1) * 512], in_=dhb[i_].ap[r_:r_ + 1, :]),
                              [dhb[i_]], [dq4])

                def block(qc0, nq, sources, outs):
                    ob = cx.banks[3 + ob_i[0] % 2]
                    ob_i[0] += 1
                    first = [True]
                    npv = sum(len(sr[6]) for sr in sources)
                    ipv = [0]
                    pts_used = {}

                    def A_(n):
                        (kap, nk, vap, r0, r1, pieces, pvs, ptt) = sources[n]
                        sbk = cx.banks[sb_i[0] % 3]
                        sb_i[0] += 1
                        if ptt is None:
                            pt = pts[pt_i[0] % 3]
                            pt_i[0] += 1
                        else:
                            pt = ptt
                        pts_used[n] = pt
                        use_d = self.cur_dq
                        cx.op(PE, lambda e: e.matmul(sbk.ap[0:nk, 0:nq], lhsT=kap, rhs=qm.ap[:, qc0:qc0 + nq],
                                                     start=True, stop=(not use_d)), [*self.cur_k, qm], [sbk])
                        if use_d:
                            cx.op(PE, lambda e: e.matmul(sbk.ap[0:nk, 0:nq], lhsT=sel3.ap[:, 0:nk], rhs=dq4.ap[:, qc0:qc0 + nq],
                                                         start=False, stop=True), [sel3, dq4], [sbk])
                        for (c0, w, bap, m) in pieces:
                            cx.op(A, lambda e: e.activation(out=pt.ap[0:nk, c0:c0 + w], in_=sbk.ap[0:nk, c0:c0 + w], func=AF.Exp,
                                                            bias=bap, scale=1.0), [sbk, *self.cur_bias], [pt])
                            if m is not None:
                                mw = m.shape[1]
                                cx.op(P, lambda e: e.tensor_tensor(out=pt.ap[0:nk, c0:c0 + mw], in0=pt.ap[0:nk, c0:c0 + mw], in1=m,
                                                                   op=ALU.mult), [pt, maskb, mask80], [pt])

                    def C_(n):
                        (kap, nk, vap, r0, r1, pieces, pvs, ptt) = sources[n]
                        pt = pts_used[n]
                        for (c0, w, oc) in pvs:
                            ipv[0] += 1
                            cx.op(PE, lambda e: e.matmul(ob.ap[0:w, oc:oc + 65], lhsT=pt.ap[r0:r1, c0:c0 + w], rhs=vap,
                                                         start=first[0], stop=(ipv[0] == npv), skip_group_check=True),
                                  [pt, *self.cur_v], [ob])
                            first[0] = False

                    NS = len(sources)
                    for n in range(NS + 1):
                        if n < NS:
                            A_(n)
                        if n >= 1:
                            C_(n - 1)
                    for (oc, w, ot) in outs:
                        cx.op(V, lambda e: e.reciprocal(out=rec.ap[0:w, 0:1], in_=ob.ap[0:w, oc + 64:oc + 65]), [ob], [rec])
                        cx.op(V, lambda e: e.tensor_scalar(out=mixF.ap[0:w, ot, h * 64:(h + 1) * 64], in0=ob.ap[0:w, oc:oc + 64],
                                                           scalar1=rec.ap[0:w, 0:1], scalar2=None, op0=ALU.mult), [ob, rec], [mixF])

                for B in range(4):
                    srcs = []
                    pcs = [(0, 512, bias_m.ap[0:80, 4 * B + 3:4 * B + 4], None)]
                    pvs = [(sub * 128, 128, sub * 65) for sub in range(4)]
                    srcs.append((kT.ap[0:128, c, 2048:2128], 80, Vb.ap[64:80, 16, h, 0:65], 64, 80, pcs, pvs, None))
                    for j in range(3):
                        for kt in range(16):
                            pcs = [(0, 512, bias_g.ap[:, j, kt, 4 * B + 3:4 * B + 4], None)]
                            srcs.append((kg[b].ap[0:128, j, kt * 128:(kt + 1) * 128], 128, vg[b].ap[:, j, kt, hh * 66:hh * 66 + 65],
                                         0, 128, pcs, pvs, None))
                    for kt in range(4 * B + 4):
                        subs = [sub for sub in range(4) if 4 * B + sub >= kt]
                        c_lo = subs[0] * 128
                        pcs = [(c_lo, 512 - c_lo, bias_o.ap[:, kt, 4 * B + 3:4 * B + 4], (maskb.ap if kt >= 4 * B else None))]
                        srcs.append((kT.ap[0:128, c, kt * 128:(kt + 1) * 128], 128, Vb.ap[:, kt, h, 0:65], 0, 128, pcs,
                                     [(sub * 128, 128, sub * 65) for sub in subs], None))
                    self.cur_dq = True
                    self.fox_block_srcs(block, B * 512, 512, srcs, [(sub * 65, 128, 4 * B + sub) for sub in range(4)],
                                        [kT, kg[b]], [Vb, vg[b]], [bias_m, bias_g, bias_o])
                    self.cur_dq = False
                srcs = []
                for s in range(2):
                    for kt in range(16):
                        srcs.append((kc.ap[0:128, s, kt * 128:(kt + 1) * 128], 128, vc.ap[:, s, kt, hh, 0:65], 0, 128,
                                     [(32 * s, 32, bias_s.ap[:, s, kt:kt + 1], None)], [(0, 80, 0)], pt_seg[s][kt % 3]))
                pcs = [(0, 32, bias_s.ap[0:80, 0, 16:17], mask80.ap[0:80, 0:32]), (32, 32, bias_s.ap[0:80, 1, 16:17], mask80.ap[0:80, 32:64]),
                       (64, 16, bias_mm.ap[0:80, 0:1], mask80.ap[0:80, 64:80])]
                srcs.append((kT.ap[0:128, c, 2048:2128], 80, Vb.ap[0:80, 16, h, 0:65], 0, 80, pcs, [(0, 80, 0)], None))
                self.cur_dq = False
                self.fox_block_srcs(block, 2048, 80, srcs, [(0, 80, 16)], [kT, kc], [Vb, vc], [bias_s, bias_mm])

    def fox_block_srcs(self, block, qc0, nq, srcs, out_tiles, ktls, vtls, btls):
        self.cur_k = list(ktls)
        self.cur_v = list(vtls)
        self.cur_bias = list(btls)
        block(qc0, nq, srcs, out_tiles)
    def mixer_out(self, w_out_dram, gain_idx):
        cx = self.cx
        cx.barrier()
        cx.arena_off = self.ab_mark
        cx.h_off = cx.h_base
        for t in range(NTT):
            r = tt_rows(t)
            cx.dma(cx.sp, self.h[t].ap[0:r, :], self.hsave[t * 128:t * 128 + r, :], writes=[self.h[t]])
        wout = cx.tile([128, 8, D], BF16, "wout")
        cx.dma(cx.pool, wout.ap, w_out_dram.rearrange("(k p) n -> p k n", p=128), writes=[wout])
        self.load_gain(self.gpost, gain_idx)
        self.junk = cx.tile([128, D], F32, "junk")
        self.tmpn = cx.tile([128, D], F32, "tmpn")
        mT = [cx.tile([128, 4, 128], BF16, f"mT{i}") for i in range(2)]
        ot = [Tl(cx.sb([128, D], F32, f"ot{i}"), f"ot{i}") for i in range(2)]
        mixS, mixF = self.mixS, self.mixF
        for t in range(NTT):
            r = tt_rows(t)
            bank = cx.banks[t % 2]
            pt = bank.ap.bitcast(BF16).rearrange("p (k c) -> p k c", k=8)
            for k in range(4):
                cx.op(cx.pe, lambda e: e.transpose(out=pt[:, k, 0:r], in_=mixF.ap[0:r, t, k * 128:(k + 1) * 128],
                                                   identity=self.identb.ap[0:r, 0:r]), [mixF, self.identb], [bank])
            m = mT[t % 2]
            cx.op(cx.act, lambda e: e.copy(out=m.ap[:, :, 0:r], in_=pt[:, 0:4, 0:r]), [bank], [m])
            o = ot[t % 2]
            for hf in range(2):
                ob = cx.banks[2 + (2 * t + hf) % 4]
                for k in range(4):
                    cx.op(cx.pe, lambda e: e.matmul(ob.ap[0:r, :], lhsT=mixS.ap[:, k, t * 128:t * 128 + r],
                                                    rhs=wout.ap[:, k, hf * 512:(hf + 1) * 512], start=(k == 0), stop=False),
                          [mixS, wout], [ob])
                for k in range(4):
                    cx.op(cx.pe, lambda e: e.matmul(ob.ap[0:r, :], lhsT=m.ap[:, k, 0:r],
                                                    rhs=wout.ap[:, 4 + k, hf * 512:(hf + 1) * 512], start=False, stop=(k == 3)),
                          [m, wout], [ob])
                cx.op(cx.act, lambda e: e.copy(out=o.ap[0:r, hf * 512:(hf + 1) * 512], in_=ob.ap[0:r, :]), [ob], [o])
            self.postnorm_add([t], [o], self.gpost, 1.0)
        cx.barrier()

    def declare_cd(self):
        self.w_cd_in = self.din("cd_w_in", [D, 3080])
        self.cst2 = self.din("cst2", [128, 896])
        self.convw = self.din("convw", [128, 8 * 5])
        self.ssdp = self.din("ssdp", [1, 24])
        self.ssd_norm = self.din("ssd_norm", [1, 512])
        self.ssdinit = self.din("ssdinit", [2, 128, 512])
        self.convinit = self.din("convinit", [128, 2 * 8 * 3])
        self.w_cd_out = self.din("cd_w_out", [D, D])
        self.c_sbk = self.din("c_sbk", [2, 2048, 512])
        self.c_sbv = self.din("c_sbv", [2, 2048, 512])
        self.o_ssd = self.dout("o_ssd", [3, 128, 512])
        self.o_conv = self.dout("o_conv", [3, 3, 1024])
        self.o_sbk = self.dout("o_sbk", [NT, 512])
        self.o_sbv = self.dout("o_sbv", [NT, 512])
        mk = lambda n, sh, dt=F32: Tl(self.nc.dram_tensor(n, sh, dt).ap(), n)
        self.tail_send, self.tail_g = mk("tail_send", [128, 24]), mk("tail_g", [512, 24])
        self.ssF_send, self.ssF_g = mk("ssF_send", [128, 520]), mk("ssF_g", [512, 520])
        self.k2_send = [mk(f"k2_send{i}", [256, 2048], BF16) for i in range(2)]
        self.k2_g = [mk(f"k2_g{i}", [1024, 2048], BF16) for i in range(2)]
        self.v2_send = [mk(f"v2_send{i}", [512, 512], BF16) for i in range(4)]
        self.v2_g = [mk(f"v2_g{i}", [2048, 512], BF16) for i in range(4)]

    def c2(self, i):
        return self.cst2t.ap[:, i * 128:(i + 1) * 128]

    def mixer_cd(self):
        cx = self.cx
        V, P, A, PE = cx.dve, cx.pool, cx.act, cx.pe
        GR = [[0, 1, 2, 3], [4, 5, 6, 7]]
        SEGS = [(0, 2048, 0), (2048, 32, 1), (2080, 32, 2), (2112, 16, 3)]
        cx.barrier()
        cx.arena_off = self.mark
        cx.h_off = cx.h_base
        self.mixC = cx.tile([128, NTT, 512], BF16, "mixC")
        self.dtt = cx.tile([128, NTT, 8], F32, "dtt")
        self.selt = cx.tile([128, 32], F32, "selt")
        cx.dma(cx.sp, self.selt.ap, self.sel, writes=[self.selt])
        self.cst2t = cx.tile([128, 896], F32, "cst2t")
        cx.dma(cx.sp, self.cst2t.ap, self.cst2, writes=[self.cst2t])
        sp24 = cx.tile([128, 24], F32, "sp24")
        cx.dma(cx.sp, sp24.ap, self.ssdp.partition_broadcast(128), writes=[sp24])
        cwt = cx.tile([128, 8, 5], F32, "cwt")
        cx.dma(cx.sp, cwt.ap, self.convw.rearrange("p (c w) -> p c w", w=5), writes=[cwt])
        self.sb_mark = cx.arena_off
        self.xdt = cx.tile([128, NTT, 512], BF16, "xdt")
        self.zs = cx.tile([128, NTT, 512], BF16, "zs")
        self.cd_mark = cx.arena_off
        self.load_gain(self.gpre, 2 * 2 + 1)
        xnT = cx.tile([128, 8, NT], BF16, "xnTf")
        mark2 = cx.arena_off
        self.junk = cx.tile([128, D], F32, "junk")
        self.nrm_xb = [cx.tile([128, D], BF16, "xb") for _ in range(2)]
        self.prenorm_T(list(range(NTT)), self.gpre, xnT, 0)
        self.spill_h()
        cx.barrier()
        cx.arena_off = mark2
        self.qT = cx.tile([128, 4, NT], BF16, "qT", hreg=True)
        self.kT = cx.tile([128, 4, NT], BF16, "kT", hreg=True)
        self.Vb = cx.tile([128, NTT, 8, 64], BF16, "Vb2", hreg=True)
        self.h_mark_cd = cx.h_off
        self.BCT = cx.tile([128, 4, NT], BF16, "BCT", hreg=True)
        wsl = [cx.tile([128, 8, 512], BF16, f"wsl{i}") for i in range(2)]
        wv = self.w_cd_in.rearrange("(k p) f -> p k f", p=128)
        wi = [0]

        def load_w(c0, n):
            t_ = wsl[wi[0] % 2]
            wi[0] += 1
            cx.dma(cx.pool, t_.ap[:, :, 0:n], wv[:, :, c0:c0 + n], writes=[t_])
            return t_

        stg = [cx.tile([128, 512], F32, f"stg{i}") for i in range(2)]
        a_b = cx.tile([128, 8], F32, "a_b")
        cx.op(A, lambda e: e.activation(out=a_b.ap, in_=sp24.ap[:, 8:16], func=AF.Exp), [sp24], [a_b])
        cx.op(V, lambda e: e.tensor_scalar(out=a_b.ap, in0=a_b.ap, scalar1=-1.0, scalar2=None, op0=ALU.mult), [a_b], [a_b])
        self.a_b = a_b
        self.sp24 = sp24

        wk_ = None
        for part, (c0w, n) in enumerate(((0, 512), (2056, 512), (2568, 512), (1536, 8))):
            w_ = load_w(c0w, n)
            for t in range(NTT):
                r = tt_rows(t)
                c0 = t * 128
                bank = cx.banks[4 + t % 2]
                for k in range(8):
                    cx.op(PE, lambda e: e.matmul(bank.ap[0:r, 0:n], lhsT=xnT.ap[:, k, c0:c0 + r], rhs=w_.ap[:, k, 0:n],
                                                 start=(k == 0), stop=(k == 7)), [w_, xnT], [bank])
                if part == 0:
                    cx.op(A, lambda e: e.activation(out=self.zs.ap[0:r, t, :], in_=bank.ap[0:r, :], func=AF.Silu), [bank], [self.zs])
                elif part == 3:
                    cx.op(V, lambda e: e.tensor_tensor(out=self.dtt.ap[0:r, t, :], in0=bank.ap[0:r, 0:8], in1=sp24.ap[0:r, 0:8],
                                                       op=ALU.add), [bank, sp24], [self.dtt])
                else:
                    st = stg[t % 2]
                    cx.op(A, lambda e: e.copy(out=st.ap[0:r, :], in_=bank.ap[0:r, :]), [bank], [st])
                    oap = self.o_sbk if part == 1 else self.o_sbv
                    cx.dma(cx.sp, oap[c0:c0 + r, :], st.ap[0:r, :], reads=[st], is_output=True)
                    if part == 2:
                        cx.op(V, lambda e: e.tensor_copy(out=self.Vb.ap[0:r, t, :, :],
                                                         in_=st.ap[0:r, :].rearrange("p (h d) -> p h d", h=8)), [st], [self.Vb])
        dtt = self.dtt
        cx.op(A, lambda e: e.activation(out=dtt.ap, in_=dtt.ap, func=AF.Exp), [dtt], [dtt])
        cx.op(A, lambda e: e.activation(out=dtt.ap, in_=dtt.ap, func=AF.Ln, bias=1.0, scale=1.0), [dtt], [dtt])
        cnt = 0
        for j in range(8):
            dst = self.qT if j < 4 else self.kT
            c0w = (1544 if j < 4 else 2056) + (j % 4) * 128
            if j % 4 == 0:
                w_ = load_w(c0w, 512)
            for (c0, cw) in COLT:
                bank = cx.banks[2 + cnt % 2]
                cnt += 1
                for k in range(8):
                    cx.op(PE, lambda e: e.matmul(bank.ap[:, 0:cw], lhsT=w_.ap[:, k, (j % 4) * 128:(j % 4 + 1) * 128],
                                                 rhs=xnT.ap[:, k, c0:c0 + cw], start=(k == 0), stop=(k == 7)), [w_, xnT], [bank])
                sc = 0.125 if j < 4 else 1.0
                cx.op(A, lambda e: e.activation(out=dst.ap[:, j % 4, c0:c0 + cw], in_=bank.ap[:, 0:cw], func=AF.Copy, scale=sc),
                      [bank], [dst])
        for c in range(4):
            cx.dma(cx.sp, self.k2_send[c // 2].ap[(c % 2) * 128:(c % 2 + 1) * 128, :], self.kT.ap[:, c, 0:2048], reads=[self.kT],
                   writes=[self.k2_send[c // 2]])
        for i in range(4):
            cx.dma(cx.sp, self.v2_send[i].ap.rearrange("(t p) c -> p t c", p=128),
                   self.Vb.ap[:, 4 * i:4 * i + 4, :, :].rearrange("p t h d -> p t (h d)"), reads=[self.Vb], writes=[self.v2_send[i]])
        for i in range(2):
            cx.allgather(self.k2_send[i], self.k2_g[i], GR)
        for i in range(4):
            cx.allgather(self.v2_send[i], self.v2_g[i], GR)
        XW = [cx.tile([128, 515], F32, f"XW{i}") for i in range(2)]
        cv = [cx.tile([128, 512], F32, f"cv{i}") for i in range(2)]
        hist = cx.tile([128, 8, 3], F32, "hist")
        tails = cx.tile([128, 4, 8, 3], F32, "tails")
        tg = cx.tile([128, 4, 24], F32, "tg")
        ci_t = cx.tile([128, 2, 8, 3], F32, "ci_t")
        cx.dma(cx.sp, ci_t.ap, self.convinit.rearrange("p (s c w) -> p s c w", s=2, c=8), writes=[ci_t])
        xst = [cx.tile([128, 512], F32, f"xst{i}") for i in range(2)]
        x16 = cx.tile([128, 80], F32, "x16")
        wx = [load_w(512, 512), load_w(1024, 512)]
        xi = [0]

        def xbc_tile(ch, c0, cw, dst_ap):
            bank = cx.banks[2 + xi[0] % 2]
            xi[0] += 1
            w_ = wx[ch // 4]
            for k in range(8):
                cx.op(PE, lambda e: e.matmul(bank.ap[:, 0:cw], lhsT=w_.ap[:, k, (ch % 4) * 128:(ch % 4 + 1) * 128],
                                             rhs=xnT.ap[:, k, c0:c0 + cw], start=(k == 0), stop=(k == 7)), [w_, xnT], [bank])
            return bank

        for ch in range(8):
            bank = xbc_tile(ch, 2045, 83 - 0, None)
            cx.op(V, lambda e: e.tensor_copy(out=tails.ap[:, 0, ch, :], in_=bank.ap[:, 0:3]), [bank], [tails])
            cx.op(V, lambda e: e.tensor_copy(out=tails.ap[:, 1, ch, :], in_=bank.ap[:, 32:35]), [bank], [tails])
            cx.op(V, lambda e: e.tensor_copy(out=tails.ap[:, 2, ch, :], in_=bank.ap[:, 64:67]), [bank], [tails])
            cx.op(V, lambda e: e.tensor_copy(out=tails.ap[:, 3, ch, :], in_=bank.ap[:, 80:83]), [bank], [tails])
        cx.dma(cx.sp, self.tail_send.ap, tails.ap[:, 0, :, :].rearrange("p c w -> p (c w)"), reads=[tails], writes=[self.tail_send])
        cx.allgather(self.tail_send, self.tail_g, GR)
        cx.dma(cx.sp, tg.ap, self.tail_g.ap.rearrange("(j p) x -> p j x", p=128), reads=[self.tail_g], writes=[tg])
        selt = self.selt
        h2 = hist.ap.rearrange("p c w -> p (c w)")
        cx.op(V, lambda e: e.tensor_scalar(out=h2, in0=tails.ap[:, 3, :, :].rearrange("p c w -> p (c w)"), scalar1=selt.ap[:, 0:1],
                                           scalar2=None, op0=ALU.mult), [tails, selt], [hist])
        for j in range(3):
            cx.op(V, lambda e: e.scalar_tensor_tensor(out=h2, in0=tg.ap[:, j, :], scalar=selt.ap[:, 27 + j:28 + j], in1=h2,
                                                      op0=ALU.mult, op1=ALU.add), [tg, selt, hist], [hist])
        cst_o = [cx.tile([128, 128], F32, f"cst_o{i}") for i in range(2)]
        for sid, oi in ((0, 0), (1, 1), (2, 2)):
            for ch in range(8):
                bank = cx.banks[6 + ch % 2]
                co = cst_o[ch % 2]
                cx.op(PE, lambda e: e.transpose(out=bank.ap[0:3, 0:128], in_=tails.ap[:, sid, ch, :], identity=self.cconst(0)),
                      [tails, self.cstt], [bank])
                cx.op(V, lambda e: e.tensor_copy(out=co.ap[0:3, :], in_=bank.ap[0:3, 0:128]), [bank], [co])
                cx.dma(cx.sp, self.o_conv[oi, :, ch * 128:(ch + 1) * 128], co.ap[0:3, :], reads=[co], is_output=True)
        pieces = [(0, 512, 0, None), (512, 512, 0, None), (1024, 512, 0, None), (1536, 512, 0, None),
                  (2048, 32, 1, 0), (2080, 32, 2, 1), (2112, 16, 3, None)]
        xw_i = 0
        for ch in range(8):
            for (c0, cw, sid, si) in pieces:
                xw = XW[xw_i % 2]
                cvt = cv[xw_i % 2]
                xw_i += 1
                bank = xbc_tile(ch, c0, cw, None)
                if c0 == 0:
                    cx.op(V, lambda e: e.tensor_copy(out=xw.ap[:, 0:3], in_=hist.ap[:, ch, :]), [hist], [xw])
                elif sid == 0:
                    prev = XW[(xw_i - 2) % 2]
                    cx.op(V, lambda e: e.tensor_copy(out=xw.ap[:, 0:3], in_=prev.ap[:, 512:515]), [prev], [xw])
                elif si is not None:
                    cx.op(V, lambda e: e.tensor_copy(out=xw.ap[:, 0:3], in_=ci_t.ap[:, si, ch, :]), [ci_t], [xw])
                else:
                    cx.op(V, lambda e: e.memset(xw.ap[:, 0:3], 0.0), [], [xw])
                cx.op(A, lambda e: e.copy(out=xw.ap[:, 3:3 + cw], in_=bank.ap[:, 0:cw]), [bank], [xw])
                cx.op(V, lambda e: e.tensor_scalar(out=cvt.ap[:, 0:cw], in0=xw.ap[:, 3:3 + cw], scalar1=cwt.ap[:, ch, 3:4],
                                                   scalar2=cwt.ap[:, ch, 4:5], op0=ALU.mult, op1=ALU.add), [xw, cwt], [cvt])
                for w in range(3):
                    cx.op(V, lambda e: e.scalar_tensor_tensor(out=cvt.ap[:, 0:cw], in0=xw.ap[:, w:w + cw], scalar=cwt.ap[:, ch, w:w + 1],
                                                              in1=cvt.ap[:, 0:cw], op0=ALU.mult, op1=ALU.add), [xw, cwt, cvt], [cvt])
                if ch >= 4:
                    cx.op(A, lambda e: e.activation(out=self.BCT.ap[:, ch - 4, c0:c0 + cw], in_=cvt.ap[:, 0:cw], func=AF.Silu),
                          [cvt], [self.BCT])
                else:
                    if sid == 0:
                        xs_ = xst[xw_i % 2]
                        off = 0
                    else:
                        xs_ = x16
                        off = c0 - 2048
                    cx.op(A, lambda e: e.activation(out=xs_.ap[:, off:off + cw], in_=cvt.ap[:, 0:cw], func=AF.Silu), [cvt], [xs_])
                    if sid in (1, 2):
                        continue
                    tot_w = cw if sid == 0 else 80
                    base = c0 if sid == 0 else 2048
                    nsub = (tot_w + 127) // 128
                    for sb_ in range(nsub):
                        w_ = min(128, tot_w - sb_ * 128)
                        t = (base + sb_ * 128) // 128
                        tb = cx.banks[6 + sb_ % 2]
                        cx.op(PE, lambda e: e.transpose(out=tb.ap[0:w_, 0:128], in_=xs_.ap[:, sb_ * 128:sb_ * 128 + w_],
                                                        identity=self.cconst(0)), [xs_, self.cstt], [tb])
                        for hh in range(2):
                            h = 2 * ch + hh
                            cx.op(A, lambda e: e.activation(out=self.xdt.ap[0:w_, t, h * 64:(h + 1) * 64],
                                                            in_=tb.ap[0:w_, hh * 64:(hh + 1) * 64], func=AF.Copy,
                                                            scale=self.dtt.ap[0:w_, t, h:h + 1]), [tb, self.dtt], [self.xdt])
        cx.barrier()
        cx.arena_off = self.cd_mark
        self.ssd_phase()
        cx.barrier()
        cx.arena_off = self.sb_mark
        cx.h_off = self.h_mark_cd
        self.sb_phase()
        self.mixer_out2(self.w_cd_out, 3 * 2 + 1)

    def ssd_phase(self):
        cx = self.cx
        V, P, A, PE = cx.dve, cx.pool, cx.act, cx.pe
        xdt, zs, BCT, dtt, a_b, selt = self.xdt, self.zs, self.BCT, self.dtt, self.a_b, self.selt
        HT = cx.tile([128, 512], F32, "HT")
        HTb = cx.tile([128, 512], BF16, "HTb")
        adt = cx.tile([128, 8], F32, "adt")
        acs = cx.tile([128, 8], F32, "acs")
        tot = cx.tile([128, 8], F32, "tot")
        eacs = cx.tile([128, 8], F32, "eacs")
        dec = cx.tile([128, 8], F32, "dec")
        etot = cx.tile([128, 8], F32, "etot")
        ddt = cx.tile([128, 8], F32, "ddt")
        logD = cx.tile([128, 8], F32, "logD")
        Am = [cx.tile([128, 128], F32, f"Am{i}") for i in range(2)]
        Ex = [cx.tile([128, 128], F32, f"Ex{i}") for i in range(2)]
        CBm = [cx.tile([128, 128], F32, f"CBm{i}") for i in range(2)]
        Wt = [cx.tile([128, 128], BF16, f"Wt{i}") for i in range(8)]
        Btok = [cx.tile([128, 128], BF16, f"Btok{i}") for i in range(2)]
        xdd = cx.tile([128, 512], BF16, "xdd")
        ysb = cx.tile([128, 512], F32, "ysb")
        yg = cx.tile([128, 512], F32, "yg")
        ss2 = cx.tile([128, 2], F32, "ss2")
        rs2 = cx.tile([128, 2], F32, "rs2")
        db = cx.tile([128, 8], F32, "db")
        cx.op(V, lambda e: e.tensor_copy(out=db.ap, in_=self.sp24.ap[:, 16:24]), [self.sp24], [db])
        ng = cx.tile([128, 512], F32, "ng")
        cx.dma(cx.sp, ng.ap, self.ssd_norm.partition_broadcast(128), writes=[ng])
        nh2 = cx.tile([128, 2], F32, "nh2")
        cx.op(V, lambda e: e.memset(nh2.ap, -0.5), [], [nh2])
        identf = self.cconst(0)
        TRI, ONES = self.cconst(1), self.cconst(2)
        SL, BSL80, BT80, BO80 = self.c2(0), self.c2(1), self.cconst(3), self.c2(2)
        ci = [0]

        def chunk(t, rows, want_y, seg16):
            r = rows
            tri = BT80[0:r, 0:r] if seg16 else TRI[0:r, 0:r]
            ones = BO80[0:r, 0:r] if seg16 else ONES[0:r, 0:r]
            sl = BSL80[0:r, 0:r] if seg16 else SL[0:r, 0:r]
            cstl = [self.cstt, self.cst2t]
            cx.op(V, lambda e: e.tensor_tensor(out=adt.ap[0:r, :], in0=dtt.ap[0:r, t, :], in1=a_b.ap[0:r, :], op=ALU.mult),
                  [dtt, a_b], [adt])
            b0, b1 = cx.banks[0], cx.banks[1]
            cx.op(PE, lambda e: e.matmul(b0.ap[0:r, 0:8], lhsT=tri, rhs=adt.ap[0:r, :], start=True, stop=True), [adt, *cstl], [b0])
            cx.op(PE, lambda e: e.matmul(b1.ap[0:r, 0:8], lhsT=ones, rhs=adt.ap[0:r, :], start=True, stop=True), [adt, *cstl], [b1])
            cx.op(V, lambda e: e.tensor_copy(out=acs.ap[0:r, :], in_=b0.ap[0:r, 0:8]), [b0], [acs])
            cx.op(V, lambda e: e.tensor_copy(out=tot.ap[0:r, :], in_=b1.ap[0:r, 0:8]), [b1], [tot])
            cx.op(A, lambda e: e.activation(out=eacs.ap[0:r, :], in_=acs.ap[0:r, :], func=AF.Exp), [acs], [eacs])
            cx.op(A, lambda e: e.activation(out=etot.ap[0:r, :], in_=tot.ap[0:r, :], func=AF.Exp), [tot], [etot])
            cx.op(V, lambda e: e.tensor_tensor(out=dec.ap[0:r, :], in0=tot.ap[0:r, :], in1=acs.ap[0:r, :], op=ALU.subtract), [tot, acs], [dec])
            cx.op(A, lambda e: e.activation(out=dec.ap[0:r, :], in_=dec.ap[0:r, :], func=AF.Exp), [dec], [dec])
            c0 = t * 128
            if want_y:
                cx.op(V, lambda e: e.reciprocal(out=ddt.ap[0:r, :], in_=dtt.ap[0:r, t, :]), [dtt], [ddt])
                cx.op(V, lambda e: e.tensor_tensor(out=ddt.ap[0:r, :], in0=ddt.ap[0:r, :], in1=db.ap[0:r, :], op=ALU.mult), [ddt, db], [ddt])
                for g in range(2):
                    cb = cx.banks[2 + g]
                    cx.op(PE, lambda e: e.matmul(cb.ap[0:r, 0:r], lhsT=BCT.ap[:, g, c0:c0 + r], rhs=BCT.ap[:, 2 + g, c0:c0 + r],
                                                 start=True, stop=True), [BCT], [cb])
                    cx.op(V, lambda e: e.tensor_tensor(out=CBm[g].ap[0:r, 0:r], in0=cb.ap[0:r, 0:r], in1=tri, op=ALU.mult),
                          [cb, *cstl], [CBm[g]])
                yb = cx.banks[6]
                for h in range(8):
                    am, ex = Am[h % 2], Ex[h % 2]
                    sb_ = cx.banks[4 + h % 2]
                    cx.op(P, lambda e: e.tensor_scalar(out=am.ap[0:r, 0:r], in0=sl, scalar1=adt.ap[0:r, h:h + 1], scalar2=None,
                                                       op0=ALU.mult), [adt, *cstl], [am])
                    cx.op(PE, lambda e: e.matmul(sb_.ap[0:r, 0:r], lhsT=am.ap[0:r, 0:r], rhs=tri, start=True, stop=True),
                          [am, *cstl], [sb_])
                    cx.op(A, lambda e: e.activation(out=ex.ap[0:r, 0:r], in_=sb_.ap[0:r, 0:r], func=AF.Exp), [sb_], [ex])
                    cx.op(V, lambda e: e.tensor_tensor(out=ex.ap[0:r, 0:r], in0=ex.ap[0:r, 0:r], in1=CBm[h // 4].ap[0:r, 0:r], op=ALU.mult),
                          [ex, CBm[h // 4]], [ex])
                    cx.op(V, lambda e: e.scalar_tensor_tensor(out=Wt[h].ap[0:r, 0:r], in0=identf[0:r, 0:r], scalar=ddt.ap[0:r, h:h + 1],
                                                              in1=ex.ap[0:r, 0:r], op0=ALU.mult, op1=ALU.add), [ddt, ex, self.cstt], [Wt[h]])
                    cx.op(PE, lambda e: e.matmul(yb.ap[0:r, h * 64:(h + 1) * 64], lhsT=Wt[h].ap[0:r, 0:r], rhs=xdt.ap[0:r, t, h * 64:(h + 1) * 64],
                                                 start=(h == 0), stop=True, skip_group_check=True), [Wt[h], xdt], [yb])
                ob = cx.banks[7]
                if seg16:
                    CT16_, Hsb_ = self._yoff16
                    for g in range(2):
                        for si_ in range(2):
                            cx.op(PE, lambda e: e.matmul(ob.ap[0:r, g * 256:(g + 1) * 256], lhsT=CT16_.ap[:, si_, g, :],
                                                         rhs=Hsb_[si_].ap[:, g * 256:(g + 1) * 256], start=(g == 0 and si_ == 0), stop=(si_ == 1),
                                                         skip_group_check=True), [CT16_, Hsb_[si_]], [ob])
                else:
                    for g in range(2):
                        cx.op(PE, lambda e: e.matmul(ob.ap[0:r, g * 256:(g + 1) * 256], lhsT=BCT.ap[:, 2 + g, c0:c0 + r],
                                                     rhs=HTb.ap[:, g * 256:(g + 1) * 256], start=(g == 0), stop=True, skip_group_check=True),
                              [BCT, HTb], [ob])
                cx.op(A, lambda e: e.copy(out=ysb.ap[0:r, :], in_=yb.ap[0:r, :]), [yb], [ysb])
                for h in range(8):
                    cx.op(V, lambda e: e.scalar_tensor_tensor(out=ysb.ap[0:r, h * 64:(h + 1) * 64], in0=ob.ap[0:r, h * 64:(h + 1) * 64],
                                                              scalar=eacs.ap[0:r, h:h + 1], in1=ysb.ap[0:r, h * 64:(h + 1) * 64],
                                                              op0=ALU.mult, op1=ALU.add), [ob, eacs, ysb], [ysb])
                cx.op(V, lambda e: e.tensor_tensor(out=yg.ap[0:r, :], in0=ysb.ap[0:r, :], in1=zs.ap[0:r, t, :], op=ALU.mult), [ysb, zs], [yg])
                for g in range(2):
                    cx.op(V, lambda e: e.scalar_tensor_tensor(out=ysb.ap[0:r, g * 256:(g + 1) * 256], in0=yg.ap[0:r, g * 256:(g + 1) * 256],
                                                              scalar=1.0, in1=yg.ap[0:r, g * 256:(g + 1) * 256], op0=ALU.mult, op1=ALU.mult,
                                                              accum_out=ss2.ap[0:r, g:g + 1]), [yg], [ysb, ss2])
                cx.op(V, lambda e: e.tensor_scalar(out=ss2.ap[0:r, :], in0=ss2.ap[0:r, :], scalar1=1.0 / 256, scalar2=EPS,
                                                   op0=ALU.mult, op1=ALU.add), [ss2], [ss2])
                cx.op(P, lambda e: e.tensor_tensor(out=rs2.ap[0:r, :], in0=ss2.ap[0:r, :], in1=nh2.ap[0:r, :], op=ALU.pow), [ss2, nh2], [rs2])
                for g in range(2):
                    cx.op(V, lambda e: e.scalar_tensor_tensor(out=self.mixC.ap[0:r, t, g * 256:(g + 1) * 256],
                                                              in0=yg.ap[0:r, g * 256:(g + 1) * 256], scalar=rs2.ap[0:r, g:g + 1],
                                                              in1=ng.ap[0:r, g * 256:(g + 1) * 256], op0=ALU.mult, op1=ALU.mult),
                          [yg, rs2, ng], [self.mixC])

        def prep_B(t, r):
            c0 = t * 128
            for g in range(2):
                tb = cx.banks[0 + g]
                tbv = tb.ap.bitcast(BF16)
                cx.op(PE, lambda e: e.transpose(out=tbv[0:r, 0:128], in_=BCT.ap[:, g, c0:c0 + r], identity=self.identb.ap),
                      [BCT, self.identb], [tb])
                cx.op(V, lambda e: e.tensor_copy(out=Btok[g].ap[0:r, :], in_=tbv[0:r, 0:128]), [tb], [Btok[g]])

        def upd(t, r0, r1, Ht, Htb, et):
            for h in range(8):
                cx.op(A, lambda e: e.activation(out=xdd.ap[r0:r1, h * 64:(h + 1) * 64], in_=xdt.ap[r0:r1, t, h * 64:(h + 1) * 64],
                                                func=AF.Copy, scale=dec.ap[r0:r1, h:h + 1]), [xdt, dec], [xdd])
            hb = cx.banks[3]
            for g in range(2):
                cx.op(PE, lambda e: e.matmul(hb.ap[:, g * 256:(g + 1) * 256], lhsT=Btok[g].ap[r0:r1, :], rhs=xdd.ap[r0:r1, g * 256:(g + 1) * 256],
                                             start=True, stop=True, skip_group_check=True), [Btok[g], xdd], [hb])
            for h in range(8):
                cx.op(V, lambda e: e.scalar_tensor_tensor(out=Ht.ap[:, h * 64:(h + 1) * 64], in0=Ht.ap[:, h * 64:(h + 1) * 64],
                                                          scalar=et.ap[:, h:h + 1], in1=hb.ap[:, h * 64:(h + 1) * 64],
                                                          op0=ALU.mult, op1=ALU.add), [Ht, et, hb], [Ht])
            cx.op(P, lambda e: e.tensor_copy(out=Htb.ap, in_=Ht.ap), [Ht], [Htb])

        o_st = self.o_ssd
        Hs = [cx.tile([128, 512], F32, f"Hs{i}") for i in range(3)]
        Hsb = [cx.tile([128, 512], BF16, f"Hsb{i}") for i in range(3)]
        for s_ in range(2):
            cx.dma(cx.sp, Hs[s_].ap, self.ssdinit[s_], writes=[Hs[s_]])
            cx.op(P, lambda e: e.tensor_copy(out=Hsb[s_].ap, in_=Hs[s_].ap), [Hs[s_]], [Hsb[s_]])
        cx.op(V, lambda e: e.memset(Hs[2].ap, 0.0), [], [Hs[2]])
        cx.op(V, lambda e: e.memset(Hsb[2].ap, 0.0), [], [Hsb[2]])
        CT16 = cx.tile([128, 3, 2, 80], BF16, "CT16")
        cx.op(V, lambda e: e.memset(CT16.ap, 0.0), [], [CT16])
        for si_, (a_, b_) in enumerate(((0, 32), (32, 64), (64, 80))):
            for g in range(2):
                cx.op(V, lambda e: e.tensor_copy(out=CT16.ap[:, si_, g, a_:b_], in_=BCT.ap[:, 2 + g, 2048 + a_:2048 + b_]), [BCT], [CT16])
        self._yoff16 = (CT16, Hsb)
        chunk(16, 80, True, True)
        prep_B(16, 80)
        et16 = cx.tile([128, 8], F32, "et16")
        for si_, (a_, b_, selc) in enumerate(((0, 32, self.cstt.ap[0:80, 640:768]), (32, 64, self.cstt.ap[0:80, 768:896]),
                                               (64, 80, self.cstt.ap[0:80, 512:640]))):
            bk = cx.banks[2]
            cx.op(PE, lambda e: e.matmul(bk.ap[:, 0:8], lhsT=selc, rhs=adt.ap[0:80, :], start=True, stop=True), [self.cstt, adt], [bk])
            cx.op(A, lambda e: e.activation(out=et16.ap, in_=bk.ap[:, 0:8], func=AF.Exp), [bk], [et16])
            upd(16, a_, b_, Hs[si_], Hsb[si_], et16)
            if si_ < 2:
                cx.dma(cx.sp, o_st[1 + si_], Hs[si_].ap, reads=[Hs[si_]], is_output=True)
        self._yoff16 = None
        cx.op(V, lambda e: e.memset(HT.ap, 0.0), [], [HT])
        cx.op(V, lambda e: e.memset(logD.ap, 0.0), [], [logD])
        for t in range(16):
            chunk(t, 128, False, False)
            prep_B(t, 128)
            upd(t, 0, 128, HT, HTb, etot)
            cx.op(V, lambda e: e.tensor_tensor(out=logD.ap, in0=logD.ap, in1=tot.ap, op=ALU.add), [logD, tot], [logD])
        fs = cx.tile([128, 520], F32, "fs")
        cx.op(V, lambda e: e.tensor_copy(out=fs.ap[:, 0:512], in_=HT.ap), [HT], [fs])
        cx.op(V, lambda e: e.tensor_copy(out=fs.ap[:, 512:520], in_=logD.ap), [logD], [fs])
        cx.dma(cx.sp, self.ssF_send.ap, fs.ap, reads=[fs], writes=[self.ssF_send])
        cx.allgather(self.ssF_send, self.ssF_g, [[0, 1, 2, 3], [4, 5, 6, 7]])
        fg = cx.tile([128, 3, 520], F32, "fg")
        cx.dma(cx.sp, fg.ap, self.ssF_g.ap[0:384, :].rearrange("(j p) c -> p j c", p=128), reads=[self.ssF_g], writes=[fg])
        Sc = Hs[2]
        ed = cx.tile([128, 8], F32, "ed")
        cx.op(V, lambda e: e.tensor_scalar(out=HT.ap, in0=Sc.ap, scalar1=selt.ap[:, 0:1], scalar2=None, op0=ALU.mult), [Sc, selt], [HT])
        for j in range(3):
            cx.op(A, lambda e: e.activation(out=ed.ap, in_=fg.ap[:, j, 512:520], func=AF.Exp), [fg], [ed])
            for h in range(8):
                cx.op(V, lambda e: e.scalar_tensor_tensor(out=Sc.ap[:, h * 64:(h + 1) * 64], in0=Sc.ap[:, h * 64:(h + 1) * 64],
                                                          scalar=ed.ap[:, h:h + 1], in1=fg.ap[:, j, h * 64:(h + 1) * 64],
                                                          op0=ALU.mult, op1=ALU.add), [Sc, ed, fg], [Sc])
            cx.op(V, lambda e: e.scalar_tensor_tensor(out=HT.ap, in0=Sc.ap, scalar=selt.ap[:, j + 1:j + 2], in1=HT.ap,
                                                      op0=ALU.mult, op1=ALU.add), [Sc, selt, HT], [HT])
        cx.op(P, lambda e: e.tensor_copy(out=HTb.ap, in_=HT.ap), [HT], [HTb])
        self._yoffP = HTb
        for t in range(16):
            chunk(t, 128, True, False)
            prep_B(t, 128)
            upd(t, 0, 128, HT, HTb, etot)
        cx.dma(cx.sp, o_st[0], HT.ap, reads=[HT], is_output=True)
    def sb_phase(self):
        cx = self.cx
        V, P, A, PE = cx.dve, cx.pool, cx.act, cx.pe
        selt = self.selt
        self.mixF = cx.tile([128, NTT, 512], BF16, "mixD")
        self.cd_mark2 = cx.arena_off
        qT, kT, Vb, mixF = self.qT, self.kT, self.Vb, self.mixF
        NTRI = cx.tile([128, 128], BF16, "ntri")
        NONE_ = cx.tile([128, 128], BF16, "none")
        cx.op(V, lambda e: e.tensor_scalar(out=NONE_.ap, in0=self.cconst(2), scalar1=-1.0, scalar2=None, op0=ALU.mult), [self.cstt], [NONE_])
        cx.op(V, lambda e: e.tensor_scalar(out=NTRI.ap, in0=self.c2(5), scalar1=-1.0, scalar2=None, op0=ALU.mult), [self.cst2t], [NTRI])
        ntri_j = [cx.tile([128, 128], BF16, f"ntri{j}") for j in range(3)]
        none_j = [cx.tile([128, 128], BF16, f"none{j}") for j in range(3)]
        for j in range(3):
            cx.op(V, lambda e: e.tensor_scalar(out=ntri_j[j].ap, in0=self.c2(5), scalar1=selt.ap[:, 24 + j:25 + j], scalar2=-1.0,
                                               op0=ALU.mult, op1=ALU.mult), [self.cst2t, selt], [ntri_j[j]])
            cx.op(V, lambda e: e.tensor_scalar(out=none_j[j].ap, in0=self.cconst(2), scalar1=selt.ap[:, 24 + j:25 + j], scalar2=-1.0,
                                               op0=ALU.mult, op1=ALU.mult), [self.cstt, selt], [none_j[j]])
        maskS = cx.tile([128, 128], BF16, "maskS")
        cx.op(V, lambda e: e.tensor_copy(out=maskS.ap, in_=self.c2(3)), [self.cst2t], [maskS])
        bsu80 = cx.tile([128, 128], BF16, "bsu80")
        cx.op(V, lambda e: e.tensor_copy(out=bsu80.ap, in_=self.c2(4)), [self.cst2t], [bsu80])
        zero_b = cx.tile([128, 1], F32, "zero_b")
        cx.op(V, lambda e: e.memset(zero_b.ap, 0.0), [], [zero_b])
        kg = [cx.tile([128, 3, 2048], BF16, f"kg{i}") for i in range(2)]
        vg = [cx.tile([128, 3, 16, 128], BF16, f"vg{i}", hreg=(i == 1)) for i in range(2)]
        kc = cx.tile([128, 2, 2048], BF16, "kc")
        vc = cx.tile([128, 2, 16, 2, 64], BF16, "vc")
        kstg = cx.tile([128, 4, 128], F32, "kstg")
        ef = [cx.tile([128, 512], F32, f"ef{i}") for i in range(5)]
        spb = [cx.tile([128, 512], BF16, f"spb{i}") for i in range(3)]
        uf = [cx.tile([128, 512], F32, f"uf{i}", hreg=(i == 2)) for i in range(3)]
        wb = [cx.tile([128, 512], BF16, f"wb{i}", hreg=True) for i in range(3)]
        Csb = cx.tile([128, 512], F32, "Csb")
        qms = [cx.tile([128, NT], BF16, f"qm{i}") for i in range(2)]
        w_seg = [[cx.tile([128, 80], BF16, f"wseg{s_}{i}") for i in range(3)] for s_ in range(3)]
        for s_ in range(3):
            for i in range(3):
                cx.op(V, lambda e: e.memset(w_seg[s_][i].ap, 0.0), [], [w_seg[s_][i]])
        w_own = [cx.tile([128, 80], BF16, f"wown{s_}") for s_ in range(3)]
        for s_ in range(3):
            cx.op(V, lambda e: e.memset(w_own[s_].ap, 0.0), [], [w_own[s_]])
        it = [0]
        ob_i = [0]

        def load_pair(c):
            b = c % 2
            cx.dma(cx.sp, kg[b].ap, self.k2_g[c // 2].ap[0:768, :].rearrange("(j r) k -> r j k", r=256)[(c % 2) * 128:(c % 2 + 1) * 128, :, :],
                   reads=[self.k2_g[c // 2]], writes=[kg[b]])
            for i in range(4):
                for j in range(3):
                    cx.dma(cx.sp, vg[b].ap[:, j, 4 * i:4 * i + 4, :],
                           self.v2_g[i].ap[j * 512:(j + 1) * 512, :].rearrange("(t p) x -> p t x", p=128)[:, :, c * 128:(c + 1) * 128],
                           reads=[self.v2_g[i]], writes=[vg[b]])

        load_pair(0)
        for c in range(4):
            if c + 1 < 4:
                load_pair(c + 1)
            b = c % 2
            for s in range(2):
                for t in range(16):
                    if t % 4 == 0:
                        cx.dma(cx.sp, kstg.ap, self.c_sbk[s].rearrange("(t p) x -> p t x", p=128)[:, t:t + 4, c * 128:(c + 1) * 128], writes=[kstg])
                    tb = cx.banks[6 + t % 2]
                    cx.op(PE, lambda e: e.transpose(out=tb.ap[:, 0:128], in_=kstg.ap[:, t % 4, :], identity=self.cconst(0)), [kstg, self.cstt], [tb])
                    cx.op(A, lambda e: e.copy(out=kc.ap[:, s, t * 128:(t + 1) * 128], in_=tb.ap[:, 0:128]), [tb], [kc])
                for hh in range(2):
                    h = 2 * c + hh
                    cx.dma(cx.pool, vc.ap[:, s, :, hh, :], self.c_sbv[s].rearrange("(t p) x -> p t x", p=128)[:, :, h * 64:(h + 1) * 64], writes=[vc])
            for hh in range(2):
                h = 2 * c + hh
                pb = hh * 64
                qm = qms[hh]
                cx.op(P, lambda e: e.tensor_copy(out=qm.ap, in_=qT.ap[:, c, :]), [qT], [qm])
                cx.op(P, lambda e: e.memset(qm.ap[(1 - hh) * 64:(1 - hh) * 64 + 64, :], 0.0), [], [qm])

                def run_steps(steps, ob, outs):
                    N = len(steps)
                    first_pv = [True]
                    npv = sum(len(st_["pvs"]) for st_ in steps)
                    ipv = [0]

                    def A_(n):
                        st_ = steps[n]
                        i = n % 2
                        nk, n_ = st_["nk"], st_["nq"] - st_["lo"]
                        zb, e_, sp_ = cx.banks[0 + i], ef[n % 5], spb[n % 3]
                        cx.op(PE, lambda e: e.matmul(zb.ap[0:nk, 0:n_], lhsT=st_["kap"], rhs=qm.ap[:, st_["qc0"] + st_["lo"]:st_["qc0"] + st_["nq"]],
                                                     start=True, stop=True), [st_["ktl"], qm], [zb])
                        cx.op(A, lambda e: e.activation(out=e_.ap[0:nk, 0:n_], in_=zb.ap[0:nk, 0:n_], func=AF.Exp), [zb], [e_])
                        cx.op(A, lambda e: e.activation(out=sp_.ap[0:nk, 0:n_], in_=e_.ap[0:nk, 0:n_], func=AF.Ln, bias=1.0, scale=1.0), [e_], [sp_])
                        m = st_["mask"]
                        if m is not None:
                            mw = m.shape[1]
                            cx.op(P, lambda e: e.tensor_tensor(out=sp_.ap[0:nk, 0:mw], in0=sp_.ap[0:nk, 0:mw], in1=m, op=ALU.mult),
                                  [sp_, maskS, bsu80], [sp_])

                    def B_(n):
                        st_ = steps[n]
                        i = n % 2
                        nk, n_ = st_["nk"], st_["nq"] - st_["lo"]
                        cl = st_["csb_lo"]
                        lb, cb, e_, sp_, u_ = cx.banks[2 + i], cx.banks[4 + i], ef[n % 5], spb[n % 3], uf[n % 3]
                        w_ = wb[n % 3] if st_["wtile"] is None else st_["wtile"]
                        wlo = st_["wlo"]
                        if st_["clear"] is not None:
                            cx.op(V, lambda e: e.memset(Csb.ap[:, 0:st_["clear"]], 0.0), [], [Csb])
                        cx.op(PE, lambda e: e.matmul(lb.ap[0:nk, 0:n_], lhsT=st_["ntri"].ap[0:nk, 0:nk], rhs=sp_.ap[0:nk, 0:n_], start=True, stop=True),
                              [st_["ntri"], sp_], [lb])
                        cx.op(PE, lambda e: e.matmul(cb.ap[:, 0:n_], lhsT=st_["none"].ap[0:nk, :], rhs=sp_.ap[0:nk, 0:n_], start=True, stop=True),
                              [st_["none"], sp_], [cb])
                        cx.op(V, lambda e: e.tensor_tensor(out=u_.ap[0:nk, 0:n_], in0=lb.ap[0:nk, 0:n_], in1=Csb.ap[0:nk, cl:cl + n_], op=ALU.add),
                              [lb, Csb], [u_])
                        cx.op(V, lambda e: e.tensor_tensor(out=Csb.ap[:, cl:cl + n_], in0=Csb.ap[:, cl:cl + n_], in1=cb.ap[:, 0:n_], op=ALU.add),
                              [cb, Csb], [Csb])
                        cx.op(A, lambda e: e.activation(out=u_.ap[0:nk, 0:n_], in_=u_.ap[0:nk, 0:n_], func=AF.Exp, bias=st_["vbias"], scale=1.0),
                              [u_, selt, zero_b], [u_])
                        hv = (n_ // 2) if n_ >= 256 else 0
                        if hv:
                            cx.op(V, lambda e: e.tensor_tensor(out=w_.ap[0:nk, wlo:wlo + hv], in0=e_.ap[0:nk, 0:hv], in1=u_.ap[0:nk, 0:hv], op=ALU.mult),
                                  [e_, u_], [w_])
                        cx.op(P, lambda e: e.tensor_tensor(out=w_.ap[0:nk, wlo + hv:wlo + n_], in0=e_.ap[0:nk, hv:n_], in1=u_.ap[0:nk, hv:n_], op=ALU.mult),
                              [e_, u_], [w_])
                        m = st_["mask"]
                        if m is not None:
                            mw = m.shape[1]
                            cx.op(P, lambda e: e.tensor_tensor(out=w_.ap[0:nk, wlo:wlo + mw], in0=w_.ap[0:nk, wlo:wlo + mw], in1=m, op=ALU.mult),
                                  [w_, maskS, bsu80], [w_])
                        st_["w"] = w_

                    def C_(n):
                        st_ = steps[n]
                        w_ = st_["w"]
                        for (r0, r1, c0_, c1_, vap, vtl, orows, oc0, oc1) in st_["pvs"]:
                            ipv[0] += 1
                            cx.op(PE, lambda e: e.matmul(ob.ap[0:orows, oc0:oc1], lhsT=w_.ap[r0:r1, c0_:c1_], rhs=vap, start=first_pv[0],
                                                         stop=(ipv[0] == npv), skip_group_check=True), [w_, vtl], [ob])
                            first_pv[0] = False

                    for n in range(N + 4):
                        if n < N:
                            A_(n)
                        if 2 <= n <= N + 1:
                            B_(n - 2)
                        if 4 <= n:
                            C_(n - 4)
                    for (orows, oc0, oc1, ot_) in outs:
                        cx.op(A, lambda e: e.copy(out=mixF.ap[0:orows, ot_, h * 64:(h + 1) * 64], in_=ob.ap[0:orows, oc0:oc1]), [ob], [mixF])

                for B in range(4):
                    ob = cx.banks[6 + ob_i[0] % 2]
                    ob_i[0] += 1
                    steps = []

                    def mk(kap, nk, ktl, lo, ntri_t, none_t, mask, vbias, vap, vtl, r0, r1):
                        pvs = [(r0, r1, sub * 128 - lo, (sub + 1) * 128 - lo, vap, vtl, 128, sub * 64, (sub + 1) * 64) for sub in range(lo // 128, 4)]
                        return dict(kap=kap, nk=nk, ktl=ktl, qc0=B * 512, nq=512, lo=lo, ntri=ntri_t, none=none_t, mask=mask, vbias=vbias,
                                    wtile=None, wlo=0, csb_lo=lo, clear=None, pvs=pvs)

                    for kt in range(4 * B + 3, -1, -1):
                        lo = max(0, kt - 4 * B) * 128
                        diag = kt >= 4 * B
                        steps.append(mk(kT.ap[0:128, c, kt * 128:(kt + 1) * 128], 128, kT, lo, NTRI, NONE_, (maskS.ap if diag else None),
                                        zero_b.ap[:, 0:1], Vb.ap[:, kt, h, :], Vb, 0, 128))
                    for j in range(2, -1, -1):
                        for kt in range(15, -1, -1):
                            steps.append(mk(kg[b].ap[0:128, j, kt * 128:(kt + 1) * 128], 128, kg[b], 0, ntri_j[j], none_j[j], None,
                                            selt.ap[:, 4 + j:5 + j], vg[b].ap[:, j, kt, hh * 64:(hh + 1) * 64], vg[b], 0, 128))
                    steps.append(mk(kT.ap[0:128, c, 2048:2128], 80, kT, 0, NTRI, NONE_, None, zero_b.ap[0:80, 0:1],
                                    Vb.ap[64:80, 16, h, :], Vb, 64, 80))
                    steps[0]["clear"] = 512
                    run_steps(steps, ob, [(128, sub * 64, (sub + 1) * 64, 4 * B + sub) for sub in range(4)])
                ob = cx.banks[6 + ob_i[0] % 2]
                ob_i[0] += 1
                steps = []
                for s in range(3):
                    a_, n_ = (0, 32) if s == 0 else ((32, 32) if s == 1 else (64, 16))
                    steps.append(dict(kap=kT.ap[0:128, c, 2048:2128], nk=80, ktl=kT, qc0=2048 + a_, nq=n_, lo=0, ntri=NTRI, none=NONE_,
                                      mask=bsu80.ap[0:80, a_:a_ + n_], vbias=zero_b.ap[0:80, 0:1], wtile=w_own[s], wlo=a_, csb_lo=0, clear=n_,
                                      pvs=[(0, 80, 0, 80, Vb.ap[0:80, 16, h, :], Vb, 80, 0, 64)]))
                    if s < 2:
                        for kt in range(15, -1, -1):
                            steps.append(dict(kap=kc.ap[0:128, s, kt * 128:(kt + 1) * 128], nk=128, ktl=kc, qc0=2048 + a_, nq=n_, lo=0, ntri=NTRI,
                                              none=NONE_, mask=None, vbias=zero_b.ap[:, 0:1], wtile=w_seg[s][kt % 3], wlo=a_, csb_lo=0, clear=None,
                                              pvs=[(0, 128, 0, 80, vc.ap[:, s, kt, hh, :], vc, 80, 0, 64)]))
                run_steps(steps, ob, [(80, 0, 64, 16)])

    def mixer_out2(self, w_out_dram, gain_idx):
        cx = self.cx
        cx.barrier()
        cx.arena_off = self.cd_mark2
        cx.h_off = cx.h_base
        for t in range(NTT):
            r = tt_rows(t)
            cx.dma(cx.sp, self.h[t].ap[0:r, :], self.hsave[t * 128:t * 128 + r, :], writes=[self.h[t]])
        wout = cx.tile([128, 8, D], BF16, "wout")
        cx.dma(cx.pool, wout.ap, w_out_dram.rearrange("(k p) n -> p k n", p=128), writes=[wout])
        self.load_gain(self.gpost, gain_idx)
        self.junk = cx.tile([128, D], F32, "junk")
        self.tmpn = cx.tile([128, D], F32, "tmpn")
        mT = [cx.tile([128, 8, 128], BF16, f"mT{i}") for i in range(2)]
        ot = [Tl(cx.sb([128, D], F32, f"ot{i}"), f"ot{i}") for i in range(2)]
        for t in range(NTT):
            r = tt_rows(t)
            bank = cx.banks[t % 2]
            pt = bank.ap.bitcast(BF16).rearrange("p (k c) -> p k c", k=8)
            for k in range(8):
                src = self.mixC if k < 4 else self.mixF
                cx.op(cx.pe, lambda e: e.transpose(out=pt[:, k, 0:r], in_=src.ap[0:r, t, (k % 4) * 128:(k % 4 + 1) * 128],
                                                   identity=self.identb.ap[0:r, 0:r]), [src, self.identb], [bank])
            m = mT[t % 2]
            cx.op(cx.act, lambda e: e.copy(out=m.ap[:, :, 0:r], in_=pt[:, :, 0:r]), [bank], [m])
            o = ot[t % 2]
            for hf in range(2):
                ob = cx.banks[2 + (2 * t + hf) % 4]
                for k in range(8):
                    cx.op(cx.pe, lambda e: e.matmul(ob.ap[0:r, :], lhsT=m.ap[:, k, 0:r], rhs=wout.ap[:, k, hf * 512:(hf + 1) * 512],
                                                    start=(k == 0), stop=(k == 7)), [m, wout], [ob])
                cx.op(cx.act, lambda e: e.copy(out=o.ap[0:r, hf * 512:(hf + 1) * 512], in_=ob.ap[0:r, :]), [ob], [o])
            self.postnorm_add([t], [o], self.gpost, 1.0)
        cx.barrier()

    def write_y(self):
        cx = self.cx
        for t in range(NTT):
            r = tt_rows(t)
            cx.dma(cx.sp, self.y[t * 128:t * 128 + r, :], self.h[t].ap[0:r, :], reads=[self.h[t]], is_output=True)


def make_consts():
    c = np.zeros((128, 1408), np.float32)
    c[:, 0:128] = np.eye(128, dtype=np.float32)
    tri = np.triu(np.ones((128, 128), np.float32))
    c[:, 128:256] = tri
    c[:, 256:384] = 1.0
    bt = np.zeros((128, 128), np.float32)
    for (a, b) in ((0, 32), (32, 64), (64, 80)):
        bt[a:b, a:b] = tri[a:b, a:b]
    c[:, 384:512] = bt
    c[64:80, 512:640] = 1.0
    c[0:32, 640:768] = 1.0
    c[32:64, 768:896] = 1.0
    c[127, 896:1024] = 1.0
    c[31, 1024:1152] = 1.0
    c[63, 1152:1280] = 1.0
    c[79, 1280:1408] = 1.0
    return c


def state_layout(a):
    return np.ascontiguousarray(a.reshape(16, 2, 64).transpose(1, 2, 0).reshape(128, 16))


def state_unlayout(a):
    return np.ascontiguousarray(a.reshape(2, 64, 16).transpose(2, 0, 1).reshape(32, 64))


def make_consts2():
    c = np.zeros((128, 896), np.float32)
    tri = np.triu(np.ones((128, 128), np.float32))
    sl = np.tril(np.ones((128, 128), np.float32), -1)
    su = np.triu(np.ones((128, 128), np.float32), 1)
    blocks = ((0, 32), (32, 64), (64, 80))
    c[:, 0:128] = sl
    for (a, b) in blocks:
        c[a:b, 128 + a:128 + b] = sl[a:b, a:b]
        c[a:b, 256 + a:256 + b] = 1.0
        c[a:b, 512 + a:512 + b] = su[a:b, a:b]
    c[:, 384:512] = su
    c[:, 640:768] = tri.T
    return c


def cd_inputs(inputs, c):
    m = {}
    m["cd_w_in"] = inputs["cd_w_in"][0]
    m["cst2"] = make_consts2()
    cw = inputs["ssd_conv_w"][0]
    cb = inputs["ssd_conv_b"][0]
    cc = np.concatenate([cw, cb[None]], 0).reshape(5, 8, 128).transpose(2, 1, 0)
    m["convw"] = cc.reshape(128, 40)
    m["ssdp"] = np.concatenate([inputs["ssd_dt_bias"][0], inputs["ssd_a_log"][0], inputs["ssd_d"][0]])[None]
    m["ssd_norm"] = inputs["ssd_norm"]
    st = inputs["state_ssd"][0, 2 * c:2 * c + 2]
    m["ssdinit"] = st.transpose(0, 3, 1, 2).reshape(2, 128, 512)
    cv = inputs["state_conv"][0, 2 * c:2 * c + 2]
    m["convinit"] = cv.reshape(2, 3, 8, 128).transpose(3, 0, 2, 1).reshape(128, 48)
    m["cd_w_out"] = inputs["cd_w_out"][0]
    m["c_sbk"] = inputs["cache_sb_k"][0, 2 * c:2 * c + 2].reshape(2, 2048, 512)
    m["c_sbv"] = inputs["cache_sb_v"][0, 2 * c:2 * c + 2].reshape(2, 2048, 512)
    return m


def ab_inputs(inputs, c):
    q = c % 4
    m = {}
    m["ab_w_in"] = inputs["ab_w_in"][0]
    m["fox_b_f"] = inputs["fox_b_f"].reshape(1, 8)
    a_re, a_im = inputs["s5_a_re"][0], inputs["s5_a_im"][0]
    ldt = np.repeat(inputs["s5_log_dt"][0][:, None], 64, axis=1)
    m["s5p"] = np.concatenate([state_layout(a_re), state_layout(a_im), state_layout(ldt)], axis=1)
    b = np.stack([inputs["s5_b_re"][0], inputs["s5_b_im"][0]], 0)
    cc = np.stack([inputs["s5_c_re"][0], inputs["s5_c_im"][0]], 0)
    bl = np.zeros((128, 16, 2, 128), np.float32)
    cl = np.zeros((128, 16, 2, 128), np.float32)
    for st in range(16):
        for gl in range(2):
            g = 2 * st + gl
            g8 = g % 8
            for ri in range(2):
                bl[g8 * 16:(g8 + 1) * 16, st, ri, gl * 64:(gl + 1) * 64] = b[ri, g].T
                cl[gl * 64:(gl + 1) * 64, st, ri, g8 * 16:(g8 + 1) * 16] = cc[ri, g].T
    m["s5b"] = bl.reshape(128, -1)
    m["s5c"] = cl.reshape(128, -1)
    dd = inputs["s5_d"][0].reshape(4, 128)
    dl = np.zeros((128, 4, 128), np.float32)
    for k in range(4):
        dl[np.arange(128), k, np.arange(128)] = dd[k]
    m["s5d"] = dl.reshape(128, -1)
    m["s5init"] = np.concatenate([state_layout(inputs["state_s5_re"][0, 2 * c]), state_layout(inputs["state_s5_im"][0, 2 * c]),
                                  state_layout(inputs["state_s5_re"][0, 2 * c + 1]), state_layout(inputs["state_s5_im"][0, 2 * c + 1])], axis=1)
    m["s5_w_glu"] = inputs["s5_w_glu"][0]
    m["s5_b_glu"] = np.ascontiguousarray(inputs["s5_b_glu"][0].reshape(4, 128).T)
    m["ab_w_out"] = inputs["ab_w_out"][0]
    sel = np.zeros((128, 32), np.float32)
    sel[:, q] = 1.0
    for j in range(3):
        sel[:, 4 + j] = 0.0 if j < q else -30000.0
        sel[:, 24 + j] = 1.0 if j < q else 0.0
        sel[:, 27 + j] = 1.0 if j == q - 1 else 0.0
        for jp in range(4):
            sel[:, 8 + j * 4 + jp] = 1.0 if (j <= jp < q) else 0.0
    for jp in range(4):
        sel[:, 20 + jp] = 1.0 if jp < q else 0.0
    m["sel"] = sel
    m["c_foxk"] = inputs["cache_fox_k"][0, 2 * c:2 * c + 2].reshape(2, 2048, 512)
    m["c_foxv"] = inputs["cache_fox_v"][0, 2 * c:2 * c + 2].reshape(2, 2048, 512)
    m["c_foxf"] = inputs["cache_fox_logf"][0, 2 * c:2 * c + 2]
    return m


def core_inputs(inputs, c):
    seq, q = c // 4, c % 4
    xin = np.concatenate([inputs["x_prompt"][seq, q * TP:(q + 1) * TP], inputs["x_sample"][2 * c],
                          inputs["x_sample"][2 * c + 1], inputs["meta_tokens"]], axis=0)
    m = {"xin": np.ascontiguousarray(xin, np.float32), "cst": make_consts()}
    m["norms"] = np.ascontiguousarray(np.concatenate(
        [inputs[k] for k in ("norm_ffn1_pre", "norm_ffn1_post", "norm_mix_pre", "norm_mix_post",
                             "norm_ffn2_pre", "norm_ffn2_post")], axis=0), np.float32)
    for f, nm in enumerate(("ffn1", "ffn2")):
        for l in range(2):
            m[f"w_gate{f}_{l}"] = inputs[f"{nm}_w_gate"][l]
            m[f"w_up{f}_{l}"] = inputs[f"{nm}_w_up"][l]
            m[f"w_down{f}_{l}"] = inputs[f"{nm}_w_down"][l]
    m.update(ab_inputs(inputs, c))
    m.update(cd_inputs(inputs, c))
    return m


def run(inputs, stop=None):
    inputs = {k: np.asarray(v) for k, v in inputs.items()}
    b = Builder(stop)
    nc = b.build()
    in_maps = []
    for c in range(NCORES):
        m = core_inputs(inputs, c)
        in_maps.append({k: np.ascontiguousarray(m[k], np.float32) for k in b.dram_in})
    res = run_bass_kernel_spmd(nc, in_maps, core_ids=list(range(NCORES)))
    return res.results


def kernel(**inputs):
    res = run(inputs)
    f32 = np.float32
    y_p = np.zeros((2, 8192, D), f32)
    y_s = np.zeros((16, 32, D), f32)
    L0 = 8208
    s5_re_p = np.zeros((1, 2, 32, 64), f32); s5_im_p = np.zeros((1, 2, 32, 64), f32)
    fox_k_p = np.zeros((1, 2, L0, 8, 64), f32); fox_v_p = np.zeros((1, 2, L0, 8, 64), f32)
    fox_f_p = np.zeros((1, 2, L0, 8), f32)
    ssd_p = np.zeros((1, 2, 8, 64, 128), f32); conv_p = np.zeros((1, 2, 3, 1024), f32)
    sb_k_p = np.zeros((1, 2, L0, 8, 64), f32); sb_v_p = np.zeros((1, 2, L0, 8, 64), f32)
    s5_re_s = np.zeros((1, 16, 32, 64), f32); s5_im_s = np.zeros((1, 16, 32, 64), f32)
    fox_k_s = np.zeros((1, 16, 32, 8, 64), f32); fox_v_s = np.zeros((1, 16, 32, 8, 64), f32)
    fox_f_s = np.zeros((1, 16, 32, 8), f32)
    ssd_s = np.zeros((1, 16, 8, 64, 128), f32); conv_s = np.zeros((1, 16, 3, 1024), f32)
    sb_k_s = np.zeros((1, 16, 32, 8, 64), f32); sb_v_s = np.zeros((1, 16, 32, 8, 64), f32)

    def hst(a):
        return a.reshape(128, 8, 64).transpose(1, 2, 0)

    for c in range(NCORES):
        r = res[c]
        seq, q = c // 4, c % 4
        y = r["y"]
        y_p[seq, q * TP:(q + 1) * TP] = y[0:TP]
        lo, hi = 16 + q * TP, 16 + (q + 1) * TP
        for dstp, dsts, key in ((fox_k_p, fox_k_s, "o_foxk"), (fox_v_p, fox_v_s, "o_foxv"), (sb_k_p, sb_k_s, "o_sbk"), (sb_v_p, sb_v_s, "o_sbv")):
            a = r[key]
            dstp[0, seq, lo:hi] = a[0:TP].reshape(TP, 8, 64)
            if q == 0:
                dstp[0, seq, 0:16] = a[2112:2128].reshape(16, 8, 64)
            dsts[0, 2 * c] = a[2048:2080].reshape(32, 8, 64)
            dsts[0, 2 * c + 1] = a[2080:2112].reshape(32, 8, 64)
        a = r["o_foxf"]
        fox_f_p[0, seq, lo:hi] = a[0:TP]
        if q == 0:
            fox_f_p[0, seq, 0:16] = a[2112:2128]
        fox_f_s[0, 2 * c] = a[2048:2080]
        fox_f_s[0, 2 * c + 1] = a[2080:2112]
        st = r["o_s5st"]
        for i in range(2):
            y_s[2 * c + i] = y[2048 + 32 * i:2080 + 32 * i]
            s5_re_s[0, 2 * c + i] = state_unlayout(st[:, 32 + 32 * i:48 + 32 * i])
            s5_im_s[0, 2 * c + i] = state_unlayout(st[:, 48 + 32 * i:64 + 32 * i])
            ssd_s[0, 2 * c + i] = hst(r["o_ssd"][1 + i])
            conv_s[0, 2 * c + i] = r["o_conv"][1 + i]
        if q == 3:
            s5_re_p[0, seq] = state_unlayout(st[:, 0:16])
            s5_im_p[0, seq] = state_unlayout(st[:, 16:32])
            ssd_p[0, seq] = hst(r["o_ssd"][0])
            conv_p[0, seq] = r["o_conv"][0]
    return (y_p, y_s, s5_re_p, s5_im_p, fox_k_p, fox_v_p, fox_f_p, ssd_p, conv_p, sb_k_p, sb_v_p,
            s5_re_s, s5_im_s, fox_k_s, fox_v_s, fox_f_s, ssd_s, conv_s, sb_k_s, sb_v_s)
```

```python
import numpy as np
import concourse.bass as bass
import concourse.mybir as mybir
from concourse.bass_utils import run_bass_kernel_spmd

F32 = mybir.dt.float32
BF16 = mybir.dt.bfloat16
AF = mybir.ActivationFunctionType
ALU = mybir.AluOpType

NCORES = 8
D = 1024
DFF = 4096
TP = 2048
NT = 2128
NTT = 17
EPS = 1e-6
COLT = [(0, 512), (512, 512), (1024, 512), (1536, 512), (2048, 80)]


def tt_rows(t):
    return 128 if t < 16 else 80


class Tl:
    __slots__ = ("ap", "w", "r", "name", "excl")

    def __init__(self, ap, name="", excl=False):
        self.ap = ap
        self.w = None
        self.r = {}
        self.name = name
        self.excl = excl


class Eng:
    def __init__(self, nc, h, name, compute=True):
        self.h = h
        self.name = name
        self.sem = nc.alloc_semaphore("prog_" + name) if compute else None
        self.cnt = 0
        self.known = {}


class Ctx:
    def __init__(self, nc):
        self.nc = nc
        self.pe = Eng(nc, nc.tensor, "pe")
        self.act = Eng(nc, nc.scalar, "act")
        self.dve = Eng(nc, nc.vector, "dve")
        self.pool = Eng(nc, nc.gpsimd, "pool")
        self.sp = Eng(nc, nc.sync, "sp", compute=False)
        self.engs = [self.pe, self.act, self.dve, self.pool, self.sp]
        self.dsems = [[nc.alloc_semaphore(f"dma{i}"), 0] for i in range(56)]
        self.dnext = 0
        self.ccsem = nc.alloc_semaphore("ccsem")
        self.cccnt = 0
        self.uid = 0
        self.out_events = []

    def init_arena(self, n32):
        self.arena = self.nc.alloc_sbuf_tensor("arena", [128, n32], F32).ap()
        self.arena_n = n32
        self.arena_off = 0
        self.banks = [Tl(self.nc.alloc_psum_tensor(f"bank{i}", [128, 512], F32).ap(), f"bank{i}", excl=True) for i in range(8)]

    def sb(self, shape, dt=F32, name=None, hreg=False):
        shape = list(shape)
        esz = 4 if dt == F32 else 2
        n = 1
        for d in shape[1:]:
            n *= d
        n32 = (n * esz + 31) // 32 * 8
        if hreg:
            off = self.h_off
            assert off + n32 <= self.h_end, (name, off, n32, self.h_end)
            self.h_off = off + n32
        else:
            off = self.arena_off
            assert off + n32 <= self.arena_n, (name, off, n32, self.arena_n)
            self.arena_off = off + n32
        ap = self.arena[0:shape[0], off:off + (n * esz) // 4]
        if dt != F32:
            ap = ap.bitcast(dt)
        if len(shape) == 3:
            ap = ap.rearrange("p (a b) -> p a b", a=shape[1])
        elif len(shape) == 4:
            ap = ap.rearrange("p (a b c) -> p a b c", a=shape[1], b=shape[2])
        elif len(shape) == 5:
            ap = ap.rearrange("p (a b c d) -> p a b c d", a=shape[1], b=shape[2], c=shape[3])
        return ap

    def ps(self, shape, dt=F32, name=None):
        self.uid += 1
        return self.nc.alloc_psum_tensor(f"{name or 'ps'}_{self.uid}", list(shape), dt).ap()

    def tile(self, shape, dt=F32, name=None, hreg=False):
        return Tl(self.sb(shape, dt, name, hreg), name or "")

    def ptile(self, shape, dt=F32, name=None):
        return Tl(self.ps(shape, dt, name), name or "")

    def _need(self, E, waits, ev, same_ok):
        if ev is None:
            return
        sem, val, src = ev
        if same_ok and src is E:
            return
        k = sem.num
        if E.known.get(k, 0) >= val:
            return
        if waits.get(k, (None, 0))[1] < val:
            waits[k] = (sem, val)

    def _deps(self, E, reads, writes):
        waits = {}
        for t in reads:
            self._need(E, waits, t.w, False)
            if t.excl:
                for ev in t.r.values():
                    self._need(E, waits, ev, True)
        for t in writes:
            self._need(E, waits, t.w, True)
            for ev in t.r.values():
                self._need(E, waits, ev, True)
        for k, (sem, val) in waits.items():
            E.h.wait_ge(sem, val)
            E.known[k] = val

    def _commit(self, ev, reads, writes):
        k = ev[0].num
        for t in reads:
            t.r[k] = ev
        for t in writes:
            t.w = ev
            t.r = {}

    def op(self, E, fn, reads=(), writes=()):
        self._deps(E, reads, writes)
        ins = fn(E.h)
        E.cnt += 1
        ins.then_inc(E.sem, 1)
        self._commit((E.sem, E.cnt, E), reads, writes)

    def dma(self, Q, out_ap, in_ap, reads=(), writes=(), is_output=False):
        slot = self.dsems[self.dnext]
        self.dnext = (self.dnext + 1) % len(self.dsems)
        sem, tot = slot
        if tot > 0 and Q.known.get(sem.num, 0) < tot:
            Q.h.wait_ge(sem, tot)
            Q.known[sem.num] = tot
        self._deps(Q, reads, writes)
        Q.h.dma_start(out=out_ap, in_=in_ap).then_inc(sem, 16)
        slot[1] = tot + 16
        ev = (sem, tot + 16, None)
        self._commit(ev, reads, writes)
        if is_output:
            self.out_events.append(ev)
        return ev

    def allgather(self, in_t, out_t, groups):
        import os
        if os.environ.get("NOCC") == "1":
            return
        Q = self.pool
        self._deps(Q, [in_t], [out_t])
        Q.h.collective_compute("AllGather", ALU.bypass, replica_groups=groups,
                               ins=[in_t.ap.opt()], outs=[out_t.ap.opt()]).then_inc(self.ccsem)
        self.cccnt += 1
        self._commit((self.ccsem, self.cccnt, None), [in_t], [out_t])

    def barrier(self):
        for E in self.engs:
            for O in self.engs:
                if O is E or O.sem is None or O.cnt == 0:
                    continue
                if E.known.get(O.sem.num, 0) < O.cnt:
                    E.h.wait_ge(O.sem, O.cnt)
                    E.known[O.sem.num] = O.cnt
            for sem, tot in self.dsems:
                if tot > 0 and E.known.get(sem.num, 0) < tot:
                    E.h.wait_ge(sem, tot)
                    E.known[sem.num] = tot
            if self.cccnt and E.known.get(self.ccsem.num, 0) < self.cccnt:
                E.h.wait_ge(self.ccsem, self.cccnt)
                E.known[self.ccsem.num] = self.cccnt

    def finish(self):
        Q = self.sp
        for sem, val, _ in self.out_events:
            if Q.known.get(sem.num, 0) < val:
                Q.h.wait_ge(sem, val)
                Q.known[sem.num] = val


GROUPS = [(list(range(0, 8)), [(0, 512), (512, 512)], 0, 1024),
          (list(range(8, 17)), [(1024, 512), (1536, 512), (2048, 80)], 1024, 1104)]


class Builder:
    def __init__(self, stop=None):
        self.stop = stop
        self.nc = bass.Bass("TRN2", target_bir_lowering=False)
        self.cx = Ctx(self.nc)
        self.dram_in = {}
        self.dram_out = {}

    def din(self, name, shape, dt=F32):
        ap = self.nc.dram_tensor(name, list(shape), dt, kind="ExternalInput").ap()
        self.dram_in[name] = (tuple(shape), dt)
        return ap

    def dout(self, name, shape, dt=F32):
        ap = self.nc.dram_tensor(name, list(shape), dt, kind="ExternalOutput").ap()
        self.dram_out[name] = tuple(shape)
        return ap

    def build(self):
        cx = self.cx
        nc = self.nc
        self.xin = self.din("xin", [NT, D])
        self.cst = self.din("cst", [128, 1408])
        self.norms = self.din("norms", [12, D])
        self.wg = [[self.din(f"w_gate{f}_{l}", [D, DFF]) for l in range(2)] for f in range(2)]
        self.wu = [[self.din(f"w_up{f}_{l}", [D, DFF]) for l in range(2)] for f in range(2)]
        self.wd = [[self.din(f"w_down{f}_{l}", [DFF, D]) for l in range(2)] for f in range(2)]
        self.y = self.dout("y", [NT, D])
        self.c_foxk = self.din("c_foxk", [2, 2048, 512])
        self.c_foxv = self.din("c_foxv", [2, 2048, 512])
        self.c_foxf = self.din("c_foxf", [2, 2048, 8])
        self.declare_ab()
        self.declare_cd()

        cx.init_arena(53184)
        cx.h_base = cx.arena_off
        hbig = cx.sb([128, NTT, D], F32, "h")
        cx.h_end = cx.arena_off
        cx.h_off = cx.h_base
        self.h = [Tl(hbig[:, t, :], f"h{t}") for t in range(NTT)]
        self.cstt = cx.tile([128, 1408], F32, "cst")
        self.identb = cx.tile([128, 128], BF16, "identb")
        self.gpre = cx.tile([128, D], F32, "gpre")
        self.gpost = cx.tile([128, D], F32, "gpost")
        self.neghalf = cx.tile([128, 1], F32, "neghalf")
        self.nrm_ss = [cx.tile([128, 1], F32, "ss") for _ in range(2)]
        self.nrm_rstd = [cx.tile([128, 1], F32, "rstd") for _ in range(2)]
        self.mark = cx.arena_off
        cx.dma(cx.sp, self.cstt.ap, self.cst, writes=[self.cstt])
        self.ident = Tl(self.cstt.ap[:, 0:128], "ident")
        self.ident.w = self.cstt.w
        cx.op(cx.dve, lambda e: e.tensor_copy(out=self.identb.ap, in_=self.cstt.ap[:, 0:128]), [self.cstt], [self.identb])
        cx.op(cx.dve, lambda e: e.memset(self.neghalf.ap, -0.5), [], [self.neghalf])
        for t in range(NTT):
            r = tt_rows(t)
            cx.dma(cx.sp, self.h[t].ap[0:r, :], self.xin[t * 128:t * 128 + r, :], writes=[self.h[t]])

        self.alloc_ffn()
        self.steps = []
        n_ffn = {"ffn1_l0": 1, "ab0": 1, "ab1a": 1, "ab1b": 1, "ab1c": 1, "ab1": 1, "s5": 1, "fox": 1, "mix_l0": 1, "ffn2_l0": 2, "ffn1_l1": 3, "mix_l1": 3, None: 4}[self.stop]
        for fi in range(n_ffn):
            for g in range(2):
                for fb in range(8):
                    self.steps.append((fi % 2, fi // 2, g, fb))
        self.step_i = 0
        self.load_gu(0)
        self.load_d(0)
        si = 0
        while si < len(self.steps):
            which, layer, g, fb = self.steps[si]
            if which == 1:
                cx.barrier()
                self.alloc_ffn()
                self.load_gu(si)
                self.load_d(si)
            self.ffn(which, layer)
            si += 16
            if which == 0 and layer == 1:
                if self.stop == "ffn1_l1":
                    break
                self.mixer_cd()
                if self.stop == "mix_l1":
                    break
            if which == 0 and layer == 0:
                if self.stop == "ffn1_l0":
                    break
                self.mixer_ab()
                if self.stop in ("mix_l0", "ab0", "ab1a", "ab1b", "ab1c", "ab1", "s5", "fox"):
                    break
        self.write_y()
        cx.finish()
        return nc

    def load_gain(self, tl, idx):
        self.cx.dma(self.cx.sp, tl.ap, self.norms[idx:idx + 1, :].partition_broadcast(128), writes=[tl])

    def rms_stats(self, src, r, par):
        cx = self.cx
        ss, rstd = self.nrm_ss[par], self.nrm_rstd[par]
        cx.op(cx.dve, lambda e: e.scalar_tensor_tensor(out=self.junk.ap[0:r, :], in0=src.ap[0:r, :], scalar=1.0,
                                                       in1=src.ap[0:r, :], op0=ALU.mult, op1=ALU.mult,
                                                       accum_out=ss.ap[0:r, :]), [src], [self.junk, ss])
        cx.op(cx.dve, lambda e: e.tensor_scalar(out=ss.ap[0:r, :], in0=ss.ap[0:r, :], scalar1=1.0 / D,
                                                scalar2=EPS, op0=ALU.mult, op1=ALU.add), [ss], [ss])
        cx.op(cx.pool, lambda e: e.tensor_tensor(out=rstd.ap[0:r, :], in0=ss.ap[0:r, :],
                                                 in1=self.neghalf.ap[0:r, :], op=ALU.pow),
              [ss, self.neghalf], [rstd])
        return rstd

    def prenorm_T(self, tiles, gain, xnT, col_base):
        cx = self.cx
        for t in tiles:
            r = tt_rows(t)
            ht = self.h[t]
            xb = self.nrm_xb[t % 2]
            rstd = self.rms_stats(ht, r, t % 2)
            cx.op(cx.dve, lambda e: e.scalar_tensor_tensor(out=xb.ap[0:r, :], in0=ht.ap[0:r, :],
                                                           scalar=rstd.ap[0:r, :], in1=gain.ap[0:r, :],
                                                           op0=ALU.mult, op1=ALU.mult),
                  [ht, rstd, gain], [xb])
            bank = cx.banks[t % 2]
            pt = bank.ap.bitcast(BF16).rearrange("p (k c) -> p k c", k=8)
            for k in range(8):
                cx.op(cx.pe, lambda e: e.transpose(out=pt[:, k, 0:r], in_=xb.ap[0:r, k * 128:(k + 1) * 128],
                                                   identity=self.identb.ap[0:r, 0:r]),
                      [xb, self.identb], [bank])
            c0 = t * 128 - col_base
            cx.op(cx.act, lambda e: e.copy(out=xnT.ap[:, :, c0:c0 + r], in_=pt[:, :, 0:r]), [bank], [xnT])

    def alloc_ffn(self):
        cx = self.cx
        cx.arena_off = self.mark
        self.junk = cx.tile([128, D], F32, "junk")
        self.tmpn = cx.tile([128, D], F32, "tmpn")
        self.nrm_xb = [cx.tile([128, D], BF16, "xb") for _ in range(2)]
        self.wgb = [cx.tile([128, 8, 512], BF16, "wgb") for _ in range(2)]
        self.wub = [cx.tile([128, 8, 512], BF16, "wub") for _ in range(2)]
        self.wdb = [cx.tile([128, 4, D], BF16, "wdb") for _ in range(1)]
        self.xnT = cx.tile([128, 8, 1104], BF16, "xnT")
        self.hid = cx.tile([128, 4, 1104], BF16, "hid")
        accbig = cx.sb([128, 9, D], F32, "acc")
        self.acc = [Tl(accbig[:, t, :], f"acc{t}") for t in range(9)]
        self.sil = [cx.tile([128, 512], F32, "sil") for _ in range(2)]

    def load_gu(self, si):
        if si >= len(self.steps):
            return
        cx = self.cx
        which, layer, g, fb = self.steps[si]
        b = si % 2
        wgv = self.wg[which][layer].rearrange("(k p) f -> p k f", p=128)
        wuv = self.wu[which][layer].rearrange("(k p) f -> p k f", p=128)
        cx.dma(cx.pool, self.wgb[b].ap, wgv[:, :, fb * 512:(fb + 1) * 512], writes=[self.wgb[b]])
        cx.dma(cx.pool, self.wub[b].ap, wuv[:, :, fb * 512:(fb + 1) * 512], writes=[self.wub[b]])

    def load_d(self, si):
        if si >= len(self.steps):
            return
        cx = self.cx
        which, layer, g, fb = self.steps[si]
        wdv = self.wd[which][layer].rearrange("(c p) d -> p c d", p=128)
        cx.dma(cx.pool, self.wdb[0].ap, wdv[:, fb * 4:(fb + 1) * 4, :], writes=[self.wdb[0]])

    def ffn(self, which, layer):
        cx = self.cx
        g_pre = (0 if which == 0 else 4) * 2 + layer
        g_post = (1 if which == 0 else 5) * 2 + layer
        if not getattr(self, "pre_done", False):
            self.load_gain(self.gpre, g_pre)
        self.load_gain(self.gpost, g_post)
        cnt = 0
        for g, (tiles, colts, cbase, ncols) in enumerate(GROUPS):
            if g == 0 and not getattr(self, "pre_done", False):
                self.prenorm_T(tiles, self.gpre, self.xnT, cbase)
            self.pre_done = False
            for fb in range(8):
                si = self.step_i
                b = si % 2
                nxt_ok = si + 1 < len(self.steps) and not (self.steps[si][0] == 0 and self.steps[si + 1][0] == 1)
                if nxt_ok:
                    self.load_gu(si + 1)
                wgb, wub, wdb, hid = self.wgb[b], self.wub[b], self.wdb[0], self.hid
                for (c0g, cw) in colts:
                    c0 = c0g - cbase
                    for c in range(4):
                        gp, up, sl = cx.banks[2 + cnt % 2], cx.banks[4 + cnt % 2], self.sil[cnt % 2]
                        cnt += 1
                        for k in range(8):
                            cx.op(cx.pe, lambda e: e.matmul(gp.ap[:, 0:cw], lhsT=wgb.ap[:, k, c * 128:(c + 1) * 128],
                                                            rhs=self.xnT.ap[:, k, c0:c0 + cw], start=(k == 0), stop=(k == 7)),
                                  [wgb, self.xnT], [gp])
                        for k in range(8):
                            cx.op(cx.pe, lambda e: e.matmul(up.ap[:, 0:cw], lhsT=wub.ap[:, k, c * 128:(c + 1) * 128],
                                                            rhs=self.xnT.ap[:, k, c0:c0 + cw], start=(k == 0), stop=(k == 7)),
                                  [wub, self.xnT], [up])
                        cx.op(cx.act, lambda e: e.activation(out=sl.ap[:, 0:cw], in_=gp.ap[:, 0:cw], func=AF.Silu), [gp], [sl])
                        cx.op(cx.dve, lambda e: e.tensor_tensor(out=hid.ap[:, c, c0:c0 + cw], in0=sl.ap[:, 0:cw],
                                                                in1=up.ap[:, 0:cw], op=ALU.mult), [sl, up], [hid])
                for ti, t in enumerate(tiles):
                    r = tt_rows(t)
                    a = self.acc[ti]
                    for hf in range(2):
                        op_ = cx.banks[6 + (ti * 2 + hf) % 2]
                        for c in range(4):
                            cx.op(cx.pe, lambda e: e.matmul(op_.ap[0:r, :], lhsT=hid.ap[:, c, ti * 128:ti * 128 + r],
                                                            rhs=wdb.ap[:, c, hf * 512:(hf + 1) * 512],
                                                            start=(c == 0), stop=(c == 3)),
                                  [hid, wdb], [op_])
                        if fb == 0:
                            cx.op(cx.dve, lambda e: e.tensor_copy(out=a.ap[0:r, hf * 512:(hf + 1) * 512], in_=op_.ap[0:r, :]),
                                  [op_], [a])
                        else:
                            cx.op(cx.dve, lambda e: e.tensor_tensor(out=a.ap[0:r, hf * 512:(hf + 1) * 512],
                                                                    in0=a.ap[0:r, hf * 512:(hf + 1) * 512],
                                                                    in1=op_.ap[0:r, :], op=ALU.add), [op_, a], [a])
                if nxt_ok:
                    self.load_d(si + 1)
                self.step_i += 1
            if g == 0:
                t2, _, cb2, _ = GROUPS[1]
                self.prenorm_T(t2, self.gpre, self.xnT, cb2)
            elif which == 1 and self.step_i < len(self.steps):
                nw, nl = self.steps[self.step_i][0], self.steps[self.step_i][1]
                self.load_gain(self.gpre, (0 if nw == 0 else 4) * 2 + nl)
                t2, _, cb2, _ = GROUPS[0]
                self.prenorm_T(t2, self.gpre, self.xnT, cb2)
                self.pre_done = True
            self.postnorm_add(tiles, self.acc, self.gpost, 0.5)

    def postnorm_add(self, tiles, src, gain, coef):
        cx = self.cx
        for ti, t in enumerate(tiles):
            r = tt_rows(t)
            a = src[ti]
            ht = self.h[t]
            rstd = self.rms_stats(a, r, t % 2)
            cx.op(cx.dve, lambda e: e.scalar_tensor_tensor(out=self.tmpn.ap[0:r, :], in0=a.ap[0:r, :],
                                                           scalar=rstd.ap[0:r, :], in1=gain.ap[0:r, :],
                                                           op0=ALU.mult, op1=ALU.mult),
                  [a, rstd, gain], [self.tmpn])
            cx.op(cx.dve, lambda e: e.scalar_tensor_tensor(out=ht.ap[0:r, :], in0=self.tmpn.ap[0:r, :], scalar=coef,
                                                           in1=ht.ap[0:r, :], op0=ALU.mult, op1=ALU.add),
                  [self.tmpn, ht], [ht])

    def declare_ab(self):
        self.w_ab_in = self.din("ab_w_in", [D, 2056])
        self.fox_bf = self.din("fox_b_f", [1, 8])
        self.s5p = self.din("s5p", [128, 48])
        self.s5b = self.din("s5b", [128, 16 * 2 * 128])
        self.s5c = self.din("s5c", [128, 16 * 2 * 128])
        self.s5d = self.din("s5d", [128, 4 * 128])
        self.s5init = self.din("s5init", [128, 64])
        self.w_glu = self.din("s5_w_glu", [512, 512])
        self.b_glu = self.din("s5_b_glu", [128, 4])
        self.w_ab_out = self.din("ab_w_out", [D, D])
        self.sel = self.din("sel", [128, 32])
        self.o_s5st = self.dout("o_s5st", [128, 96])
        self.o_foxk = self.dout("o_foxk", [NT, 512])
        self.o_foxv = self.dout("o_foxv", [NT, 512])
        self.o_foxf = self.dout("o_foxf", [NT, 8])
        self.hsave = self.nc.dram_tensor("hsave", [NT, D], F32).ap()
        self.s5F_send = Tl(self.nc.dram_tensor("s5F_send", [128, 32], F32).ap(), "s5F_send")
        self.s5F_g = Tl(self.nc.dram_tensor("s5F_g", [512, 32], F32).ap(), "s5F_g")
        self.kT_send = [Tl(self.nc.dram_tensor(f"kT_send{i}", [256, 2048], BF16).ap(), "kT_send") for i in range(2)]
        self.kT_g = [Tl(self.nc.dram_tensor(f"kT_g{i}", [1024, 2048], BF16).ap(), "kT_g") for i in range(2)]
        self.V_send = [Tl(self.nc.dram_tensor(f"V_send{i}", [512, 528], BF16).ap(), "V_send") for i in range(4)]
        self.V_g = [Tl(self.nc.dram_tensor(f"V_g{i}", [2048, 528], BF16).ap(), "V_g") for i in range(4)]
        self.fk_send = Tl(self.nc.dram_tensor("fk_send", [2048, 8], F32).ap(), "fk_send")
        self.fk_g = Tl(self.nc.dram_tensor("fk_g", [8192, 8], F32).ap(), "fk_g")
        self.tot_send = Tl(self.nc.dram_tensor("tot_send", [128, 8], F32).ap(), "tot_send")
        self.tot_g = Tl(self.nc.dram_tensor("tot_g", [512, 8], F32).ap(), "tot_g")

    def cconst(self, i):
        return self.cstt.ap[:, i * 128:(i + 1) * 128]

    def spill_h(self):
        cx = self.cx
        for t in range(NTT):
            r = tt_rows(t)
            cx.dma(cx.sp, self.hsave[t * 128:t * 128 + r, :], self.h[t].ap[0:r, :], reads=[self.h[t]])

    def mixer_ab(self):
        cx = self.cx
        GR = [[0, 1, 2, 3], [4, 5, 6, 7]]
        cx.barrier()
        cx.arena_off = self.mark
        cx.h_off = cx.h_base
        self.mixS = cx.tile([128, 4, NT], BF16, "mixS")
        self.mixF = cx.tile([128, NTT, 512], BF16, "mixF")
        self.Vb = cx.tile([128, NTT, 8, 66], BF16, "Vb")
        self.logf = cx.tile([128, NTT, 8], F32, "logf")
        self.cum = cx.tile([128, NTT, 8], F32, "cum")
        self.totP = cx.tile([128, 8], F32, "totP")
        self.selt = cx.tile([128, 32], F32, "selt")
        cx.dma(cx.sp, self.selt.ap, self.sel, writes=[self.selt])
        ab_mark = cx.arena_off
        self.ab_mark = ab_mark
        self.load_gain(self.gpre, 2 * 2 + 0)
        xnT = cx.tile([128, 8, NT], BF16, "xnTf")
        mark2 = cx.arena_off
        self.junk = cx.tile([128, D], F32, "junk")
        self.nrm_xb = [cx.tile([128, D], BF16, "xb") for _ in range(2)]
        self.prenorm_T(list(range(NTT)), self.gpre, xnT, 0)
        self.spill_h()
        cx.barrier()
        if self.stop == "ab0":
            return
        cx.arena_off = mark2
        win = cx.tile([128, 8, 2056], BF16, "win")
        cx.dma(cx.pool, win.ap, self.w_ab_in.rearrange("(k p) f -> p k f", p=128), writes=[win])
        self.uT = cx.tile([128, 4, NT], BF16, "uT", hreg=True)
        self.qT = cx.tile([128, 4, NT], BF16, "qT", hreg=True)
        self.kT = cx.tile([128, 4, NT], BF16, "kT", hreg=True)
        stg = [cx.tile([128, 512], F32, "stg", hreg=True) for _ in range(2)]
        bfb = cx.tile([128, 8], F32, "bfb", hreg=True)
        cx.dma(cx.sp, bfb.ap, self.fox_bf.partition_broadcast(128), writes=[bfb])
        dests = [self.uT] * 4 + [self.qT] * 4 + [self.kT] * 4
        cnt = 0
        for j in range(12):
            dst = dests[j]
            for (c0, cw) in COLT:
                bank = cx.banks[2 + cnt % 2]
                cnt += 1
                for k in range(8):
                    cx.op(cx.pe, lambda e: e.matmul(bank.ap[:, 0:cw], lhsT=win.ap[:, k, j * 128:(j + 1) * 128],
                                                    rhs=xnT.ap[:, k, c0:c0 + cw], start=(k == 0), stop=(k == 7)),
                          [win, xnT], [bank])
                sc = 0.125 if 4 <= j < 8 else 1.0
                cx.op(cx.act, lambda e: e.activation(out=dst.ap[:, j % 4, c0:c0 + cw], in_=bank.ap[:, 0:cw],
                                                     func=AF.Copy, scale=sc), [bank], [dst])
        if self.stop == "ab1a":
            return
        cx.op(cx.dve, lambda e: e.memset(self.Vb.ap[:, :, :, 64:66], 1.0), [], [self.Vb])
        for t in range(NTT):
            r = tt_rows(t)
            c0 = t * 128
            import os
            dbg = os.environ.get("DBG", "")
            for which, (w0, oap) in enumerate(((1024, self.o_foxk), (1536, self.o_foxv))):
                if dbg and str(which) not in dbg:
                    continue
                bank = cx.banks[4 + which]
                st = stg[which]
                for k in range(8):
                    cx.op(cx.pe, lambda e: e.matmul(bank.ap[0:r, :], lhsT=xnT.ap[:, k, c0:c0 + r],
                                                    rhs=win.ap[:, k, w0:w0 + 512], start=(k == 0), stop=(k == 7)),
                          [win, xnT], [bank])
                cx.op(cx.act, lambda e: e.copy(out=st.ap[0:r, :], in_=bank.ap[0:r, :]), [bank], [st])
                cx.dma(cx.sp, oap[c0:c0 + r, :], st.ap[0:r, :], reads=[st], is_output=True)
                if which == 1:
                    cx.op(cx.dve, lambda e: e.tensor_copy(out=self.Vb.ap[0:r, t, :, 0:64],
                                                          in_=bank.ap[0:r, :].rearrange("p (h d) -> p h d", h=8)),
                          [bank], [self.Vb])
            if dbg and "2" not in dbg:
                continue
            bank = cx.banks[6]
            for k in range(8):
                cx.op(cx.pe, lambda e: e.matmul(bank.ap[0:r, 0:8], lhsT=xnT.ap[:, k, c0:c0 + r],
                                                rhs=win.ap[:, k, 2048:2056], start=(k == 0), stop=(k == 7)),
                      [win, xnT], [bank])
            lf = self.logf
            cx.op(cx.dve, lambda e: e.tensor_tensor(out=lf.ap[0:r, t, :], in0=bank.ap[0:r, 0:8], in1=bfb.ap[0:r, :],
                                                    op=ALU.add), [bank, bfb], [lf])
        if self.stop == "ab1b":
            return
        lf = self.logf
        cx.op(cx.act, lambda e: e.activation(out=lf.ap, in_=lf.ap, func=AF.Exp, scale=-1.0), [lf], [lf])
        cx.op(cx.act, lambda e: e.activation(out=lf.ap, in_=lf.ap, func=AF.Ln, bias=1.0, scale=1.0), [lf], [lf])
        cx.op(cx.dve, lambda e: e.tensor_scalar(out=lf.ap, in0=lf.ap, scalar1=-1.0, scalar2=None, op0=ALU.mult), [lf], [lf])
        cx.dma(cx.sp, self.o_foxf[0:2048, :].rearrange("(t p) h -> p t h", p=128), lf.ap[:, 0:16, :], reads=[lf], is_output=True)
        cx.dma(cx.sp, self.o_foxf[2048:2128, :], lf.ap[0:80, 16, :], reads=[lf], is_output=True)
        cumt = self.cum
        for t in range(16):
            bank = cx.banks[6 + t % 2]
            for tp in range(t):
                cx.op(cx.pe, lambda e: e.matmul(bank.ap[:, 0:8], lhsT=self.cconst(2), rhs=lf.ap[:, tp, :],
                                                start=(tp == 0), stop=False), [self.cstt, lf], [bank])
            cx.op(cx.pe, lambda e: e.matmul(bank.ap[:, 0:8], lhsT=self.cconst(1), rhs=lf.ap[:, t, :],
                                            start=(t == 0), stop=True), [self.cstt, lf], [bank])
            cx.op(cx.dve, lambda e: e.tensor_copy(out=cumt.ap[:, t, :], in_=bank.ap[:, 0:8]), [bank], [cumt])
        bank = cx.banks[6]
        cx.op(cx.pe, lambda e: e.matmul(bank.ap[0:80, 0:8], lhsT=self.cstt.ap[0:80, 384:464], rhs=lf.ap[0:80, 16, :],
                                        start=True, stop=True), [self.cstt, lf], [bank])
        cx.op(cx.dve, lambda e: e.tensor_copy(out=cumt.ap[0:80, 16, :], in_=bank.ap[0:80, 0:8]), [bank], [cumt])
        bank = cx.banks[7]
        for t in range(16):
            cx.op(cx.pe, lambda e: e.matmul(bank.ap[:, 0:8], lhsT=self.cconst(2), rhs=lf.ap[:, t, :],
                                            start=(t == 0), stop=(t == 15)), [self.cstt, lf], [bank])
        cx.op(cx.dve, lambda e: e.tensor_copy(out=self.totP.ap, in_=bank.ap[:, 0:8]), [bank], [self.totP])
        if self.stop == "ab1c":
            return
        for c in range(4):
            cx.dma(cx.sp, self.kT_send[c // 2].ap[(c % 2) * 128:(c % 2 + 1) * 128, :], self.kT.ap[:, c, 0:2048], reads=[self.kT],
                   writes=[self.kT_send[c // 2]])
        for i in range(4):
            cx.dma(cx.sp, self.V_send[i].ap.rearrange("(t p) c -> p t c", p=128),
                   self.Vb.ap[:, 4 * i:4 * i + 4, :, :].rearrange("p t h d -> p t (h d)"), reads=[self.Vb], writes=[self.V_send[i]])
        cx.dma(cx.sp, self.fk_send.ap.rearrange("(t p) h -> p t h", p=128), cumt.ap[:, 0:16, :], reads=[cumt],
               writes=[self.fk_send])
        cx.dma(cx.sp, self.tot_send.ap, self.totP.ap, reads=[self.totP], writes=[self.tot_send])
        for i in range(2):
            cx.allgather(self.kT_send[i], self.kT_g[i], GR)
        for i in range(4):
            cx.allgather(self.V_send[i], self.V_g[i], GR)
        cx.allgather(self.fk_send, self.fk_g, GR)
        cx.allgather(self.tot_send, self.tot_g, GR)
        cx.barrier()
        if self.stop == "ab1":
            return
        cx.arena_off = ab_mark
        self.h_mark_ab = cx.h_off = self.h_mark_after_qk()
        self.s5_phase()
        cx.barrier()
        if self.stop == "s5":
            return
        cx.arena_off = ab_mark
        cx.h_off = self.h_mark_ab
        self.fox_phase()
        if self.stop == "fox":
            return
        self.mixer_out(self.w_ab_out, 3 * 2 + 0)

    def h_mark_after_qk(self):
        return self.cx.h_base + 3 * ((4 * NT * 2 + 31) // 32 * 8)
    def s5_phase(self):
        cx = self.cx
        V, P = cx.dve, cx.pool
        TC = 128

        def t16(name):
            return cx.tile([128, 16], F32, name)

        prm = cx.tile([128, 48], F32, "s5prm")
        cx.dma(cx.sp, prm.ap, self.s5p, writes=[prm])
        a_re, a_im, ldt = prm.ap[:, 0:16], prm.ap[:, 16:32], prm.ap[:, 32:48]
        dt, mag, th, c_, s_, t1, t2 = (t16(n) for n in ("dt", "mag", "th", "c_", "s_", "t1", "t2"))
        halfpi = cx.tile([128, 1], F32, "halfpi")
        cx.op(V, lambda e: e.memset(halfpi.ap, float(np.pi / 2)), [], [halfpi])
        cx.op(cx.act, lambda e: e.activation(out=dt.ap, in_=ldt, func=AF.Exp), [prm], [dt])
        cx.op(V, lambda e: e.tensor_tensor(out=t1.ap, in0=dt.ap, in1=a_re, op=ALU.mult), [dt, prm], [t1])
        cx.op(cx.act, lambda e: e.activation(out=mag.ap, in_=t1.ap, func=AF.Exp), [t1], [mag])
        cx.op(V, lambda e: e.tensor_tensor(out=th.ap, in0=dt.ap, in1=a_im, op=ALU.mult), [dt, prm], [th])
        cx.op(cx.act, lambda e: e.activation(out=s_.ap, in_=th.ap, func=AF.Sin, scale=1.0 / 16), [th], [s_])
        cx.op(cx.act, lambda e: e.activation(out=c_.ap, in_=th.ap, func=AF.Sin, scale=1.0 / 16, bias=halfpi.ap),
              [th, halfpi], [c_])

        def csquare(cr, ci, outr, outi):
            cx.op(V, lambda e: e.tensor_tensor(out=t1.ap, in0=cr.ap, in1=cr.ap, op=ALU.mult), [cr], [t1])
            cx.op(V, lambda e: e.tensor_tensor(out=t2.ap, in0=ci.ap, in1=ci.ap, op=ALU.mult), [ci], [t2])
            cx.op(V, lambda e: e.tensor_tensor(out=outi.ap, in0=cr.ap, in1=ci.ap, op=ALU.mult), [cr, ci], [outi])
            cx.op(V, lambda e: e.tensor_scalar(out=outi.ap, in0=outi.ap, scalar1=2.0, scalar2=None, op0=ALU.mult), [outi], [outi])
            cx.op(V, lambda e: e.tensor_tensor(out=outr.ap, in0=t1.ap, in1=t2.ap, op=ALU.subtract), [t1, t2], [outr])

        for _ in range(4):
            csquare(c_, s_, c_, s_)
        Ur = [t16(f"Ur{k}") for k in range(7)]
        Ui = [t16(f"Ui{k}") for k in range(7)]
        cx.op(V, lambda e: e.tensor_copy(out=Ur[0].ap, in_=c_.ap), [c_], [Ur[0]])
        cx.op(V, lambda e: e.tensor_copy(out=Ui[0].ap, in_=s_.ap), [s_], [Ui[0]])
        for k in range(1, 7):
            csquare(Ur[k - 1], Ui[k - 1], Ur[k], Ui[k])
        ab_re, ab_im, cf_re, cf_im, den, nr = (t16(n) for n in ("ab_re", "ab_im", "cf_re", "cf_im", "den", "nr"))
        cx.op(V, lambda e: e.tensor_tensor(out=ab_re.ap, in0=mag.ap, in1=c_.ap, op=ALU.mult), [mag, c_], [ab_re])
        cx.op(V, lambda e: e.tensor_tensor(out=ab_im.ap, in0=mag.ap, in1=s_.ap, op=ALU.mult), [mag, s_], [ab_im])
        cx.op(V, lambda e: e.tensor_tensor(out=t1.ap, in0=a_re, in1=a_re, op=ALU.mult), [prm], [t1])
        cx.op(V, lambda e: e.tensor_tensor(out=t2.ap, in0=a_im, in1=a_im, op=ALU.mult), [prm], [t2])
        cx.op(V, lambda e: e.tensor_tensor(out=den.ap, in0=t1.ap, in1=t2.ap, op=ALU.add), [t1, t2], [den])
        cx.op(V, lambda e: e.reciprocal(out=den.ap, in_=den.ap), [den], [den])
        cx.op(V, lambda e: e.tensor_scalar(out=nr.ap, in0=ab_re.ap, scalar1=-1.0, scalar2=None, op0=ALU.add), [ab_re], [nr])
        cx.op(V, lambda e: e.tensor_tensor(out=t1.ap, in0=nr.ap, in1=a_re, op=ALU.mult), [nr, prm], [t1])
        cx.op(V, lambda e: e.tensor_tensor(out=t2.ap, in0=ab_im.ap, in1=a_im, op=ALU.mult), [ab_im, prm], [t2])
        cx.op(V, lambda e: e.tensor_tensor(out=cf_re.ap, in0=t1.ap, in1=t2.ap, op=ALU.add), [t1, t2], [cf_re])
        cx.op(V, lambda e: e.tensor_tensor(out=cf_re.ap, in0=cf_re.ap, in1=den.ap, op=ALU.mult), [cf_re, den], [cf_re])
        cx.op(V, lambda e: e.tensor_tensor(out=t1.ap, in0=ab_im.ap, in1=a_re, op=ALU.mult), [ab_im, prm], [t1])
        cx.op(V, lambda e: e.tensor_tensor(out=t2.ap, in0=nr.ap, in1=a_im, op=ALU.mult), [nr, prm], [t2])
        cx.op(V, lambda e: e.tensor_tensor(out=cf_im.ap, in0=t1.ap, in1=t2.ap, op=ALU.subtract), [t1, t2], [cf_im])
        cx.op(V, lambda e: e.tensor_tensor(out=cf_im.ap, in0=cf_im.ap, in1=den.ap, op=ALU.mult), [cf_im, den], [cf_im])
        p_re, p_im = t16("p_re"), t16("p_im")
        csquare(ab_re, ab_im, p_re, p_im)
        for _ in range(10):
            csquare(p_re, p_im, p_re, p_im)
        E_re = cx.tile([128, 16, TC], F32, "E_re")
        E_im = cx.tile([128, 16, TC], F32, "E_im")
        R_re = cx.tile([128, 16, TC], F32, "R_re")
        R_im = cx.tile([128, 16, TC], F32, "R_im")
        rtab = cx.tile([128, 16, TC], F32, "rtab")
        tmpa = cx.tile([128, TC], F32, "tmpa")
        tmpb = cx.tile([128, TC], F32, "tmpb")
        for st in range(16):
            cx.op(V, lambda e: e.tensor_copy(out=E_re.ap[:, st, 0:1], in_=Ur[0].ap[:, st:st + 1]), [Ur[0]], [E_re])
            cx.op(V, lambda e: e.tensor_copy(out=E_im.ap[:, st, 0:1], in_=Ui[0].ap[:, st:st + 1]), [Ui[0]], [E_im])
            for k in range(7):
                ln = 1 << k
                ur, ui = Ur[k].ap[:, st:st + 1], Ui[k].ap[:, st:st + 1]
                cx.op(V, lambda e: e.tensor_scalar(out=tmpa.ap[:, 0:ln], in0=E_im.ap[:, st, 0:ln], scalar1=ui, scalar2=None,
                                                   op0=ALU.mult), [E_im, Ui[k]], [tmpa])
                cx.op(V, lambda e: e.tensor_scalar(out=tmpb.ap[:, 0:ln], in0=E_im.ap[:, st, 0:ln], scalar1=ur, scalar2=None,
                                                   op0=ALU.mult), [E_im, Ur[k]], [tmpb])
                cx.op(V, lambda e: e.scalar_tensor_tensor(out=E_im.ap[:, st, ln:2 * ln], in0=E_re.ap[:, st, 0:ln], scalar=ui,
                                                          in1=tmpb.ap[:, 0:ln], op0=ALU.mult, op1=ALU.add),
                      [E_re, Ui[k], tmpb], [E_im])
                cx.op(V, lambda e: e.scalar_tensor_tensor(out=E_re.ap[:, st, ln:2 * ln], in0=E_re.ap[:, st, 0:ln], scalar=ur,
                                                          in1=tmpa.ap[:, 0:ln], op0=ALU.mult, op1=ALU.subtract),
                      [E_re, Ur[k], tmpa], [E_re])
            cr, ci = cf_re.ap[:, st:st + 1], cf_im.ap[:, st:st + 1]
            cx.op(V, lambda e: e.tensor_scalar(out=tmpa.ap, in0=E_im.ap[:, st, :], scalar1=ci, scalar2=None, op0=ALU.mult),
                  [E_im, cf_im], [tmpa])
            cx.op(V, lambda e: e.scalar_tensor_tensor(out=R_re.ap[:, st, :], in0=E_re.ap[:, st, :], scalar=cr, in1=tmpa.ap,
                                                      op0=ALU.mult, op1=ALU.add), [E_re, cf_re, tmpa], [R_re])
            cx.op(V, lambda e: e.tensor_scalar(out=tmpb.ap, in0=E_im.ap[:, st, :], scalar1=cr, scalar2=None, op0=ALU.mult),
                  [E_im, cf_re], [tmpb])
            cx.op(V, lambda e: e.scalar_tensor_tensor(out=R_im.ap[:, st, :], in0=E_re.ap[:, st, :], scalar=ci, in1=tmpb.ap,
                                                      op0=ALU.mult, op1=ALU.subtract), [E_re, cf_im, tmpb], [R_im])
            cx.op(V, lambda e: e.tensor_scalar(out=rtab.ap[:, st, :], in0=self.cconst(2), scalar1=mag.ap[:, st:st + 1],
                                               scalar2=None, op0=ALU.mult), [self.cstt, mag], [rtab])
        bl = cx.tile([128, 16, 2, 128], BF16, "bl")
        cl = cx.tile([128, 16, 2, 128], BF16, "cl")
        dl = cx.tile([128, 4, 128], BF16, "dl")
        wgl = cx.tile([128, 4, 512], BF16, "wgl")
        bgl = cx.tile([128, 4], F32, "bgl")
        cx.dma(cx.pool, bl.ap, self.s5b.rearrange("p (s r c) -> p s r c", s=16, r=2), writes=[bl])
        cx.dma(cx.pool, cl.ap, self.s5c.rearrange("p (s r c) -> p s r c", s=16, r=2), writes=[cl])
        cx.dma(cx.pool, dl.ap, self.s5d.rearrange("p (s c) -> p s c", s=4), writes=[dl])
        cx.dma(cx.pool, wgl.ap, self.w_glu.rearrange("(k p) f -> p k f", p=128), writes=[wgl])
        cx.dma(cx.sp, bgl.ap, self.b_glu, writes=[bgl])
        cx.op(V, lambda e: e.tensor_scalar(out=cl.ap[:, :, 1, :], in0=cl.ap[:, :, 1, :], scalar1=-1.0, scalar2=None,
                                           op0=ALU.mult), [cl], [cl])
        wks = [[cx.tile([128, TC], F32, f"wk{i}", hreg=True) for i in range(8)] for _ in range(3)]
        hre = [cx.tile([128, TC], BF16, f"hre{i}") for i in range(4)]
        him = [cx.tile([128, TC], BF16, f"him{i}") for i in range(4)]
        gf = [cx.tile([128, TC], F32, f"gf{i}") for i in range(4)]
        gb = [cx.tile([128, TC], BF16, f"gb{i}") for i in range(4)]
        sg = cx.tile([128, TC], F32, "sg")
        car_re, car_im = t16("car_re"), t16("car_im")
        c1 = [cx.tile([128, 1], F32, f"c1_{i}") for i in range(2)]
        uT, mixS = self.uT, self.mixS
        cnt = [0]

        def run(c0, n, want_y):
            units = []
            pos = 0
            while pos < n:
                ln = min(TC, n - pos)
                for cc in range(4):
                    for sl in range(4):
                        units.append((c0 + pos, ln, cc, sl))
                pos += ln

            def S1(i):
                cc0, ln, cc, sl = units[i]
                st = cc * 4 + sl
                pr, pi = cx.banks[(i % 2) * 2], cx.banks[(i % 2) * 2 + 1]
                wk = wks[i % 3]
                cx.op(cx.pe, lambda e: e.matmul(pr.ap[:, 0:ln], lhsT=bl.ap[:, st, 0, :], rhs=uT.ap[:, cc, cc0:cc0 + ln],
                                                start=True, stop=True), [bl, uT], [pr])
                cx.op(cx.pe, lambda e: e.matmul(pi.ap[:, 0:ln], lhsT=bl.ap[:, st, 1, :], rhs=uT.ap[:, cc, cc0:cc0 + ln],
                                                start=True, stop=True), [bl, uT], [pi])
                Rr, Ri = R_re.ap[:, st, 0:ln], R_im.ap[:, st, 0:ln]
                w = [x.ap[:, 0:ln] for x in wk]
                cx.op(V, lambda e: e.tensor_tensor(out=w[0], in0=pr.ap[:, 0:ln], in1=Rr, op=ALU.mult), [pr, R_re], [wk[0]])
                cx.op(V, lambda e: e.tensor_tensor(out=w[1], in0=pi.ap[:, 0:ln], in1=Ri, op=ALU.mult), [pi, R_im], [wk[1]])
                cx.op(V, lambda e: e.tensor_tensor(out=w[2], in0=pi.ap[:, 0:ln], in1=Rr, op=ALU.mult), [pi, R_re], [wk[2]])
                cx.op(V, lambda e: e.tensor_tensor(out=w[3], in0=pr.ap[:, 0:ln], in1=Ri, op=ALU.mult), [pr, R_im], [wk[3]])
                cx.op(P, lambda e: e.tensor_tensor(out=w[0], in0=w[0], in1=w[1], op=ALU.subtract), [wk[0], wk[1]], [wk[0]])
                cx.op(P, lambda e: e.tensor_tensor(out=w[2], in0=w[2], in1=w[3], op=ALU.add), [wk[2], wk[3]], [wk[2]])

            def S2(i):
                cc0, ln, cc, sl = units[i]
                st = cc * 4 + sl
                wk = wks[i % 3]
                w = [x.ap[:, 0:ln] for x in wk]
                Er, Ei = E_re.ap[:, st, 0:ln], E_im.ap[:, st, 0:ln]
                cx.op(V, lambda e: e.tensor_tensor_scan(out=w[4], data0=rtab.ap[:, st, 0:ln], data1=w[0],
                                                        initial=car_re.ap[:, st:st + 1], op0=ALU.mult, op1=ALU.add),
                      [rtab, wk[0], car_re], [wk[4]])
                cx.op(V, lambda e: e.tensor_tensor_scan(out=w[5], data0=rtab.ap[:, st, 0:ln], data1=w[2],
                                                        initial=car_im.ap[:, st:st + 1], op0=ALU.mult, op1=ALU.add),
                      [rtab, wk[2], car_im], [wk[5]])
                L = ln - 1
                gr, gi = wk[4].ap[:, L:L + 1], wk[5].ap[:, L:L + 1]
                er, ei = E_re.ap[:, st, L:L + 1], E_im.ap[:, st, L:L + 1]
                cx.op(V, lambda e: e.tensor_tensor(out=c1[0].ap, in0=gi, in1=ei, op=ALU.mult), [wk[5], E_im], [c1[0]])
                cx.op(V, lambda e: e.tensor_tensor(out=c1[1].ap, in0=gi, in1=er, op=ALU.mult), [wk[5], E_re], [c1[1]])
                cx.op(V, lambda e: e.scalar_tensor_tensor(out=car_re.ap[:, st:st + 1], in0=gr, scalar=er, in1=c1[0].ap,
                                                          op0=ALU.mult, op1=ALU.subtract), [wk[4], E_re, c1[0]], [car_re])
                cx.op(V, lambda e: e.scalar_tensor_tensor(out=car_im.ap[:, st:st + 1], in0=gr, scalar=ei, in1=c1[1].ap,
                                                          op0=ALU.mult, op1=ALU.add), [wk[4], E_im, c1[1]], [car_im])
                if want_y:
                    cx.op(P, lambda e: e.tensor_tensor(out=w[6], in0=w[5], in1=Ei, op=ALU.mult), [wk[5], E_im], [wk[6]])
                    cx.op(P, lambda e: e.tensor_tensor(out=w[7], in0=w[5], in1=Er, op=ALU.mult), [wk[5], E_re], [wk[7]])
                    cx.op(V, lambda e: e.tensor_tensor(out=w[0], in0=w[4], in1=Er, op=ALU.mult), [wk[4], E_re], [wk[0]])
                    cx.op(V, lambda e: e.tensor_tensor(out=w[1], in0=w[4], in1=Ei, op=ALU.mult), [wk[4], E_im], [wk[1]])
                    cx.op(P, lambda e: e.tensor_tensor(out=hre[sl].ap[:, 0:ln], in0=w[0], in1=w[6], op=ALU.subtract),
                          [wk[0], wk[6]], [hre[sl]])
                    cx.op(P, lambda e: e.tensor_tensor(out=him[sl].ap[:, 0:ln], in0=w[1], in1=w[7], op=ALU.add),
                          [wk[1], wk[7]], [him[sl]])
                    if sl == 3:
                        yb = cx.banks[4 + cc % 2]
                        for s2 in range(4):
                            st2 = cc * 4 + s2
                            cx.op(cx.pe, lambda e: e.matmul(yb.ap[:, 0:ln], lhsT=cl.ap[:, st2, 0, :], rhs=hre[s2].ap[:, 0:ln],
                                                            start=(s2 == 0), stop=False), [cl, hre[s2]], [yb])
                            cx.op(cx.pe, lambda e: e.matmul(yb.ap[:, 0:ln], lhsT=cl.ap[:, st2, 1, :], rhs=him[s2].ap[:, 0:ln],
                                                            start=False, stop=False), [cl, him[s2]], [yb])
                        cx.op(cx.pe, lambda e: e.matmul(yb.ap[:, 0:ln], lhsT=dl.ap[:, cc, :], rhs=uT.ap[:, cc, cc0:cc0 + ln],
                                                        start=False, stop=True), [dl, uT], [yb])
                        cx.op(cx.act, lambda e: e.activation(out=gf[cc].ap[:, 0:ln], in_=yb.ap[:, 0:ln], func=AF.Gelu_apprx_tanh),
                              [yb], [gf[cc]])
                        cx.op(cx.act, lambda e: e.copy(out=gb[cc].ap[:, 0:ln], in_=gf[cc].ap[:, 0:ln]), [gf[cc]], [gb[cc]])
                        if cc == 3:
                            for co in range(4):
                                zb = cx.banks[6 + co % 2]
                                for ci in range(4):
                                    cx.op(cx.pe, lambda e: e.matmul(zb.ap[:, 0:ln], lhsT=wgl.ap[:, ci, co * 128:(co + 1) * 128],
                                                                    rhs=gb[ci].ap[:, 0:ln], start=(ci == 0), stop=(ci == 3)),
                                          [wgl, gb[ci]], [zb])
                                cx.op(cx.act, lambda e: e.activation(out=sg.ap[:, 0:ln], in_=zb.ap[:, 0:ln], func=AF.Sigmoid,
                                                                     bias=bgl.ap[:, co:co + 1], scale=1.0), [zb, bgl], [sg])
                                cx.op(V, lambda e: e.tensor_tensor(out=mixS.ap[:, co, cc0:cc0 + ln], in0=gf[co].ap[:, 0:ln],
                                                                   in1=sg.ap[:, 0:ln], op=ALU.mult), [gf[co], sg], [mixS])

            NU = len(units)
            for i in range(NU + 1):
                if i < NU:
                    S1(i)
                if i >= 1:
                    S2(i - 1)

        def set_carry_zero():
            cx.op(V, lambda e: e.memset(car_re.ap, 0.0), [], [car_re])
            cx.op(V, lambda e: e.memset(car_im.ap, 0.0), [], [car_im])

        st_out = cx.tile([128, 96], F32, "s5st_out")
        ini = cx.tile([128, 64], F32, "s5ini")
        cx.dma(cx.sp, ini.ap, self.s5init, writes=[ini])
        for si in range(2):
            cx.op(V, lambda e: e.tensor_copy(out=car_re.ap, in_=ini.ap[:, si * 32:si * 32 + 16]), [ini], [car_re])
            cx.op(V, lambda e: e.tensor_copy(out=car_im.ap, in_=ini.ap[:, si * 32 + 16:si * 32 + 32]), [ini], [car_im])
            run(2048 + 32 * si, 32, True)
            cx.op(V, lambda e: e.tensor_copy(out=st_out.ap[:, 32 + si * 32:48 + si * 32], in_=car_re.ap), [car_re], [st_out])
            cx.op(V, lambda e: e.tensor_copy(out=st_out.ap[:, 48 + si * 32:64 + si * 32], in_=car_im.ap), [car_im], [st_out])
        set_carry_zero()
        run(2112, 16, True)
        m_re, m_im = t16("m_re"), t16("m_im")
        cx.op(V, lambda e: e.tensor_copy(out=m_re.ap, in_=car_re.ap), [car_re], [m_re])
        cx.op(V, lambda e: e.tensor_copy(out=m_im.ap, in_=car_im.ap), [car_im], [m_im])
        set_carry_zero()
        run(0, 2048, False)
        fsend = cx.tile([128, 32], F32, "fsend")
        cx.op(V, lambda e: e.tensor_copy(out=fsend.ap[:, 0:16], in_=car_re.ap), [car_re], [fsend])
        cx.op(V, lambda e: e.tensor_copy(out=fsend.ap[:, 16:32], in_=car_im.ap), [car_im], [fsend])
        cx.dma(cx.sp, self.s5F_send.ap, fsend.ap, reads=[fsend], writes=[self.s5F_send])
        cx.allgather(self.s5F_send, self.s5F_g, [[0, 1, 2, 3], [4, 5, 6, 7]])
        fg = cx.tile([128, 4, 32], F32, "fg")
        cx.dma(cx.sp, fg.ap, self.s5F_g.ap.rearrange("(j p) c -> p j c", p=128), reads=[self.s5F_g], writes=[fg])
        s_re, s_im, n_re, n_im = t16("s_re"), t16("s_im"), t16("n_re"), t16("n_im")
        cx.op(V, lambda e: e.tensor_copy(out=s_re.ap, in_=m_re.ap), [m_re], [s_re])
        cx.op(V, lambda e: e.tensor_copy(out=s_im.ap, in_=m_im.ap), [m_im], [s_im])
        selt = self.selt
        cx.op(V, lambda e: e.tensor_scalar(out=car_re.ap, in0=s_re.ap, scalar1=selt.ap[:, 0:1], scalar2=None, op0=ALU.mult),
              [s_re, selt], [car_re])
        cx.op(V, lambda e: e.tensor_scalar(out=car_im.ap, in0=s_im.ap, scalar1=selt.ap[:, 0:1], scalar2=None, op0=ALU.mult),
              [s_im, selt], [car_im])
        for j in range(3):
            cx.op(V, lambda e: e.tensor_tensor(out=t1.ap, in0=p_re.ap, in1=s_re.ap, op=ALU.mult), [p_re, s_re], [t1])
            cx.op(V, lambda e: e.tensor_tensor(out=t2.ap, in0=p_im.ap, in1=s_im.ap, op=ALU.mult), [p_im, s_im], [t2])
            cx.op(V, lambda e: e.tensor_tensor(out=n_re.ap, in0=t1.ap, in1=t2.ap, op=ALU.subtract), [t1, t2], [n_re])
            cx.op(V, lambda e: e.tensor_tensor(out=t1.ap, in0=p_re.ap, in1=s_im.ap, op=ALU.mult), [p_re, s_im], [t1])
            cx.op(V, lambda e: e.tensor_tensor(out=t2.ap, in0=p_im.ap, in1=s_re.ap, op=ALU.mult), [p_im, s_re], [t2])
            cx.op(V, lambda e: e.tensor_tensor(out=n_im.ap, in0=t1.ap, in1=t2.ap, op=ALU.add), [t1, t2], [n_im])
            cx.op(V, lambda e: e.tensor_tensor(out=s_re.ap, in0=n_re.ap, in1=fg.ap[:, j, 0:16], op=ALU.add), [n_re, fg], [s_re])
            cx.op(V, lambda e: e.tensor_tensor(out=s_im.ap, in0=n_im.ap, in1=fg.ap[:, j, 16:32], op=ALU.add), [n_im, fg], [s_im])
            cx.op(V, lambda e: e.scalar_tensor_tensor(out=car_re.ap, in0=s_re.ap, scalar=selt.ap[:, j + 1:j + 2], in1=car_re.ap,
                                                      op0=ALU.mult, op1=ALU.add), [s_re, selt, car_re], [car_re])
            cx.op(V, lambda e: e.scalar_tensor_tensor(out=car_im.ap, in0=s_im.ap, scalar=selt.ap[:, j + 1:j + 2], in1=car_im.ap,
                                                      op0=ALU.mult, op1=ALU.add), [s_im, selt, car_im], [car_im])
        run(0, 2048, True)
        cx.op(V, lambda e: e.tensor_copy(out=st_out.ap[:, 0:16], in_=car_re.ap), [car_re], [st_out])
        cx.op(V, lambda e: e.tensor_copy(out=st_out.ap[:, 16:32], in_=car_im.ap), [car_im], [st_out])
        cx.dma(cx.sp, self.o_s5st, st_out.ap, reads=[st_out], is_output=True)
    def cumsum_tiles(self, lf2, ntile, within_out, tot_out):
        cx = self.cx
        V = cx.dve
        n = ntile * 8
        ba, bb = cx.banks[6], cx.banks[7]
        cx.op(cx.pe, lambda e: e.matmul(ba.ap[:, 0:n], lhsT=self.cconst(1), rhs=lf2, start=True, stop=True), [self.cstt, self.lf_src], [ba])
        cx.op(cx.pe, lambda e: e.matmul(bb.ap[:, 0:n], lhsT=self.cconst(2), rhs=lf2, start=True, stop=True), [self.cstt, self.lf_src], [bb])
        tots = self.cs_tots
        incl = self.cs_incl
        cx.op(V, lambda e: e.tensor_copy(out=tots.ap[:, 0:n], in_=bb.ap[:, 0:n]), [bb], [tots])
        t3 = tots.ap[:, 0:n].rearrange("p (t h) -> p t h", h=8)
        i3 = incl.ap[:, 0:n].rearrange("p (t h) -> p t h", h=8)
        for h in range(8):
            cx.op(V, lambda e: e.tensor_tensor_scan(out=i3[:, :, h], data0=self.cconst(2)[:, 0:ntile], data1=t3[:, :, h],
                                                    initial=0.0, op0=ALU.mult, op1=ALU.add), [tots, self.cstt], [incl])
        cx.op(V, lambda e: e.tensor_copy(out=tot_out.ap, in_=i3[:, ntile - 1, :]), [incl], [tot_out])
        cx.op(V, lambda e: e.tensor_tensor(out=incl.ap[:, 0:n], in0=incl.ap[:, 0:n], in1=tots.ap[:, 0:n], op=ALU.subtract),
              [incl, tots], [incl])
        cx.op(V, lambda e: e.tensor_tensor(out=within_out, in0=incl.ap[:, 0:n], in1=ba.ap[:, 0:n], op=ALU.add),
              [incl, ba], [self.cs_dst])

    def fox_phase(self):
        cx = self.cx
        V, P, A, PE = cx.dve, cx.pool, cx.act, cx.pe
        selt, cum, lf = self.selt, self.cum, self.logf
        totg = cx.tile([128, 4, 8], F32, "totg")
        cx.dma(cx.sp, totg.ap, self.tot_g.ap.rearrange("(j p) h -> p j h", p=128), reads=[self.tot_g], writes=[totg])
        totmeta = cx.tile([128, 8], F32, "totmeta")
        bk = cx.banks[7]
        cx.op(PE, lambda e: e.matmul(bk.ap[:, 0:8], lhsT=self.cstt.ap[0:80, 512:640], rhs=lf.ap[0:80, 16, :], start=True, stop=True),
              [self.cstt, lf], [bk])
        cx.op(V, lambda e: e.tensor_copy(out=totmeta.ap, in_=bk.ap[:, 0:8]), [bk], [totmeta])
        dl = cx.tile([128, 4, 8], F32, "delta")
        for j in range(3):
            cx.op(V, lambda e: e.tensor_scalar(out=dl.ap[:, j, :], in0=totg.ap[:, 0, :], scalar1=selt.ap[:, 8 + j * 4:9 + j * 4],
                                               scalar2=selt.ap[:, 4 + j:5 + j], op0=ALU.mult, op1=ALU.add), [totg, selt], [dl])
            for jp in range(1, 4):
                cx.op(V, lambda e: e.scalar_tensor_tensor(out=dl.ap[:, j, :], in0=totg.ap[:, jp, :],
                                                          scalar=selt.ap[:, 8 + j * 4 + jp:9 + j * 4 + jp], in1=dl.ap[:, j, :],
                                                          op0=ALU.mult, op1=ALU.add), [totg, selt, dl], [dl])
        cx.op(V, lambda e: e.tensor_copy(out=dl.ap[:, 3, :], in_=totmeta.ap), [totmeta], [dl])
        for jp in range(4):
            cx.op(V, lambda e: e.scalar_tensor_tensor(out=dl.ap[:, 3, :], in0=totg.ap[:, jp, :], scalar=selt.ap[:, 20 + jp:21 + jp],
                                                      in1=dl.ap[:, 3, :], op0=ALU.mult, op1=ALU.add), [totg, selt, dl], [dl])
        cref = cx.tile([128, 16, 8], F32, "cref")
        cx.op(PE, lambda e: e.matmul(bk.ap[:, 0:128], lhsT=self.cstt.ap[:, 896:1024], rhs=cum.ap[:, 0:16, :].rearrange("p t h -> p (t h)"),
                                     start=True, stop=True), [self.cstt, cum], [bk])
        cx.op(V, lambda e: e.tensor_copy(out=cref.ap.rearrange("p t h -> p (t h)"), in_=bk.ap[:, 0:128]), [bk], [cref])
        cref16 = cx.tile([128, 3, 8], F32, "cref16")
        for si in range(3):
            cx.op(PE, lambda e: e.matmul(bk.ap[:, 0:8], lhsT=self.cstt.ap[0:80, 1024 + si * 128:1152 + si * 128], rhs=cum.ap[0:80, 16, :],
                                         start=True, stop=True), [self.cstt, cum], [bk])
            cx.op(V, lambda e: e.tensor_copy(out=cref16.ap[:, si, :], in_=bk.ap[:, 0:8]), [bk], [cref16])
        fkg = cx.tile([128, 3, 16, 8], F32, "fkg")
        cx.dma(cx.sp, fkg.ap, self.fk_g.ap[0:6144, :].rearrange("(j t p) h -> p j t h", p=128, t=16), reads=[self.fk_g], writes=[fkg])
        for j in range(3):
            for t in range(16):
                cx.op(V, lambda e: e.tensor_tensor(out=fkg.ap[:, j, t, :], in0=dl.ap[:, j, :], in1=fkg.ap[:, j, t, :], op=ALU.subtract),
                      [dl, fkg], [fkg])
        lfc = cx.tile([128, 2, 16, 8], F32, "lfc")
        cx.dma(cx.sp, lfc.ap, self.c_foxf.rearrange("s (t p) h -> p s t h", p=128), writes=[lfc])
        cumc = cx.tile([128, 2, 16, 8], F32, "cumc")
        totc = cx.tile([128, 2, 8], F32, "totc")
        self.cs_tots = cx.tile([128, 128], F32, "cs_tots", hreg=True)
        self.cs_incl = cx.tile([128, 128], F32, "cs_incl", hreg=True)
        for s in range(2):
            self.lf_src = lfc
            self.cs_dst = cumc
            tt_ = Tl(totc.ap[:, s, :], "totc_s")
            self.cumsum_tiles(lfc.ap[:, s, :, :].rearrange("p t h -> p (t h)"), 16,
                              cumc.ap[:, s, :, :].rearrange("p t h -> p (t h)"), tt_)
            totc.w = tt_.w
            for t in range(16):
                cx.op(V, lambda e: e.tensor_tensor(out=cumc.ap[:, s, t, :], in0=totc.ap[:, s, :], in1=cumc.ap[:, s, t, :], op=ALU.subtract),
                      [totc, cumc], [cumc])
        kg1 = cx.tile([128, 3, 2048], BF16, "kg")
        kg = [kg1, kg1]
        dfull = cx.tile([128, 512], F32, "dfull")
        dr1 = cx.tile([128, 512], F32, "dr1")
        dhb = [cx.tile([128, 512], BF16, f"dhb{i}") for i in range(3)]
        dq4 = cx.tile([128, 2048], BF16, "dq4")
        cx.op(V, lambda e: e.memset(dq4.ap, 0.0), [], [dq4])
        sel3 = cx.tile([128, 128], BF16, "sel3")
        cx.op(V, lambda e: e.memset(sel3.ap, 0.0), [], [sel3])
        for r_ in (0, 32, 64):
            cx.op(V, lambda e: e.memset(sel3.ap[r_:r_ + 1, :], 1.0), [], [sel3])
        vg = [cx.tile([128, 3, 16, 132], BF16, f"vg{i}", hreg=(i == 1)) for i in range(2)]
        kc = cx.tile([128, 2, 2048], BF16, "kc")
        vc = cx.tile([128, 2, 16, 2, 66], BF16, "vc")
        kstg = cx.tile([128, 4, 128], F32, "kstg")
        pt_seg = [[cx.tile([128, 80], BF16, f"ptseg{s_}{i}") for i in range(3)] for s_ in range(2)]
        for s_ in range(2):
            for i in range(3):
                cx.op(V, lambda e: e.memset(pt_seg[s_][i].ap, 0.0), [], [pt_seg[s_][i]])
        bias_o = cx.tile([128, 16, 16], F32, "bias_o")
        bias_g = cx.tile([128, 3, 16, 16], F32, "bias_g")
        bias_m = cx.tile([128, 16], F32, "bias_m")
        bias_s = cx.tile([128, 2, 17], F32, "bias_s")
        bias_mm = cx.tile([128, 1], F32, "bias_mm")
        pts = [cx.tile([128, 512], BF16, f"pt{i}", hreg=True) for i in range(3)]
        rec = cx.tile([128, 1], F32, "rec")
        qms = [cx.tile([128, NT], BF16, f"qm{i}") for i in range(2)]
        maskb = cx.tile([128, 128], BF16, "maskb")
        cx.op(V, lambda e: e.tensor_copy(out=maskb.ap, in_=self.cconst(1)), [self.cstt], [maskb])
        mask80 = cx.tile([128, 128], BF16, "mask80")
        cx.op(V, lambda e: e.tensor_copy(out=mask80.ap, in_=self.cconst(3)), [self.cstt], [mask80])
        cx.op(P, lambda e: e.memset(vc.ap[:, :, :, :, 64:66], 1.0), [], [vc])
        qT, kT, Vb, mixF = self.qT, self.kT, self.Vb, self.mixF
        sb_i = [0]
        ob_i = [0]
        pt_i = [0]

        def load_k(c):
            b = c % 2
            cx.dma(cx.sp, kg[b].ap, self.kT_g[c // 2].ap[0:768, :].rearrange("(j r) k -> r j k", r=256)[(c % 2) * 128:(c % 2 + 1) * 128, :, :],
                   reads=[self.kT_g[c // 2]], writes=[kg[b]])

        def load_v(c):
            b = c % 2
            for i in range(4):
                for j in range(3):
                    cx.dma(cx.sp, vg[b].ap[:, j, 4 * i:4 * i + 4, :],
                           self.V_g[i].ap[j * 512:(j + 1) * 512, :].rearrange("(t p) x -> p t x", p=128)[:, :, c * 132:(c + 1) * 132],
                           reads=[self.V_g[i]], writes=[vg[b]])

        load_v(0)
        for c in range(4):
            load_k(c)
            if c + 1 < 4:
                load_v(c + 1)
            b = c % 2
            for s in range(2):
                for t in range(16):
                    if t % 4 == 0:
                        cx.dma(cx.sp, kstg.ap, self.c_foxk[s].rearrange("(t p) x -> p t x", p=128)[:, t:t + 4, c * 128:(c + 1) * 128],
                               writes=[kstg])
                    tb = cx.banks[5 + t % 2]
                    cx.op(PE, lambda e: e.transpose(out=tb.ap[:, 0:128], in_=kstg.ap[:, t % 4, :], identity=self.cconst(0)),
                          [kstg, self.cstt], [tb])
                    cx.op(A, lambda e: e.copy(out=kc.ap[:, s, t * 128:(t + 1) * 128], in_=tb.ap[:, 0:128]), [tb], [kc])
                for hh in range(2):
                    h = 2 * c + hh
                    cx.dma(cx.pool, vc.ap[:, s, :, hh, 0:64],
                           self.c_foxv[s].rearrange("(t p) x -> p t x", p=128)[:, :, h * 64:(h + 1) * 64], writes=[vc])
            for hh in range(2):
                h = 2 * c + hh
                pb = hh * 64
                qm = qms[hh]
                cx.op(P, lambda e: e.tensor_copy(out=qm.ap, in_=qT.ap[:, c, :]), [qT], [qm])
                cx.op(P, lambda e: e.memset(qm.ap[(1 - hh) * 64:(1 - hh) * 64 + 64, :], 0.0), [], [qm])
                for kt in range(16):
                    cx.op(V, lambda e: e.tensor_scalar(out=bias_o.ap[:, kt, :], in0=cref.ap[:, :, h], scalar1=cum.ap[:, kt, h:h + 1],
                                                       scalar2=None, op0=ALU.subtract), [cref, cum], [bias_o])
                    for j in range(3):
                        cx.op(V, lambda e: e.tensor_scalar(out=bias_g.ap[:, j, kt, :], in0=cref.ap[:, :, h],
                                                           scalar1=fkg.ap[:, j, kt, h:h + 1], scalar2=None, op0=ALU.add),
                              [cref, fkg], [bias_g])
                cx.op(V, lambda e: e.tensor_tensor(out=bias_mm.ap, in0=dl.ap[:, 3, h:h + 1], in1=cum.ap[:, 16, h:h + 1], op=ALU.subtract),
                      [dl, cum], [bias_mm])
                cx.op(V, lambda e: e.tensor_scalar(out=bias_m.ap, in0=cref.ap[:, :, h], scalar1=bias_mm.ap[:, 0:1], scalar2=None,
                                                   op0=ALU.add), [cref, bias_mm], [bias_m])
                for s in range(2):
                    cx.op(V, lambda e: e.tensor_scalar(out=bias_s.ap[:, s, 0:16], in0=cumc.ap[:, s, :, h], scalar1=cref16.ap[:, s, h:h + 1],
                                                       scalar2=None, op0=ALU.add), [cumc, cref16], [bias_s])
                    cx.op(V, lambda e: e.tensor_tensor(out=bias_s.ap[:, s, 16:17], in0=cref16.ap[:, s, h:h + 1], in1=cum.ap[:, 16, h:h + 1],
                                                       op=ALU.subtract), [cref16, cum], [bias_s])
                cx.op(V, lambda e: e.tensor_tensor(out=bias_mm.ap, in0=cref16.ap[:, 2, h:h + 1], in1=cum.ap[:, 16, h:h + 1],
                                                   op=ALU.subtract), [cref16, cum], [bias_mm])

                for B_ in range(4):
                    for sub in range(4):
                        qs_ = 4 * B_ + sub
                        cx.op(V, lambda e: e.tensor_scalar(out=dfull.ap[0:65, sub * 128:(sub + 1) * 128], in0=self.cconst(2)[0:65, :],
                                                           scalar1=cref.ap[0:65, qs_, h:h + 1], scalar2=cref.ap[0:65, 4 * B_ + 3, h:h + 1],
                                                           op0=ALU.mult, op1=ALU.subtract), [cref, self.cstt], [dfull])
                    cx.op(V, lambda e: e.tensor_copy(out=dhb[0].ap[0:65, :], in_=dfull.ap[0:65, :]), [dfull], [dhb[0]])
                    cx.op(V, lambda e: e.tensor_tensor(out=dr1.ap[0:65, :], in0=dfull.ap[0:65, :], in1=dhb[0].ap[0:65, :], op=ALU.subtract),
                          [dfull, dhb[0]], [dr1])
                    cx.op(V, lambda e: e.tensor_copy(out=dhb[1].ap[0:65, :], in_=dr1.ap[0:65, :]), [dr1], [dhb[1]])
                    cx.op(V, lambda e: e.tensor_tensor(out=dfull.ap[0:65, :], in0=dr1.ap[0:65, :], in1=dhb[1].ap[0:65, :], op=ALU.subtract),
                          [dr1, dhb[1]], [dfull])
                    cx.op(V, lambda e: e.tensor_copy(out=dhb[2].ap[0:65, :], in_=dfull.ap[0:65, :]), [dfull], [dhb[2]])
                    for i_, r_ in enumerate((0, 32, 64)):
                        cx.op(V, lambda e: e.tensor_copy(out=dq4.ap[r_:r_ + 1, B_ * 512:(B_ + 1) * 512], in_=dhb[i_].ap[r_:r_ + 1, :]),
                              [dhb[i_]], [dq4])

                def block(qc0, nq, sources, outs):
                    ob = cx.banks[3 + ob_i[0] % 2]
                    ob_i[0] += 1
                    first = [True]
                    npv = sum(len(sr[6]) for sr in sources)
                    ipv = [0]
                    pts_used = {}

                    def A_(n):
                        (kap, nk, vap, r0, r1, pieces, pvs, ptt) = sources[n]
                        sbk = cx.banks[sb_i[0] % 3]
                        sb_i[0] += 1
                        if ptt is None:
                            pt = pts[pt_i[0] % 3]
                            pt_i[0] += 1
                        else:
                            pt = ptt
                        pts_used[n] = pt
                        use_d = self.cur_dq
                        cx.op(PE, lambda e: e.matmul(sbk.ap[0:nk, 0:nq], lhsT=kap, rhs=qm.ap[:, qc0:qc0 + nq],
                                                     start=True, stop=(not use_d)), [*self.cur_k, qm], [sbk])
                        if use_d:
                            cx.op(PE, lambda e: e.matmul(sbk.ap[0:nk, 0:nq], lhsT=sel3.ap[:, 0:nk], rhs=dq4.ap[:, qc0:qc0 + nq],
                                                         start=False, stop=True), [sel3, dq4], [sbk])
                        for (c0, w, bap, m) in pieces:
                            cx.op(A, lambda e: e.activation(out=pt.ap[0:nk, c0:c0 + w], in_=sbk.ap[0:nk, c0:c0 + w], func=AF.Exp,
                                                            bias=bap, scale=1.0), [sbk, *self.cur_bias], [pt])
                            if m is not None:
                                mw = m.shape[1]
                                cx.op(P, lambda e: e.tensor_tensor(out=pt.ap[0:nk, c0:c0 + mw], in0=pt.ap[0:nk, c0:c0 + mw], in1=m,
                                                                   op=ALU.mult), [pt, maskb, mask80], [pt])

                    def C_(n):
                        (kap, nk, vap, r0, r1, pieces, pvs, ptt) = sources[n]
                        pt = pts_used[n]
                        for (c0, w, oc) in pvs:
                            ipv[0] += 1
                            cx.op(PE, lambda e: e.matmul(ob.ap[0:w, oc:oc + 65], lhsT=pt.ap[r0:r1, c0:c0 + w], rhs=vap,
                                                         start=first[0], stop=(ipv[0] == npv), skip_group_check=True),
                                  [pt, *self.cur_v], [ob])
                            first[0] = False

                    NS = len(sources)
                    for n in range(NS + 2):
                        if n < NS:
                            A_(n)
                        if n >= 2:
                            C_(n - 2)
                    for (oc, w, ot) in outs:
                        cx.op(V, lambda e: e.reciprocal(out=rec.ap[0:w, 0:1], in_=ob.ap[0:w, oc + 64:oc + 65]), [ob], [rec])
                        cx.op(V, lambda e: e.tensor_scalar(out=mixF.ap[0:w, ot, h * 64:(h + 1) * 64], in0=ob.ap[0:w, oc:oc + 64],
                                                           scalar1=rec.ap[0:w, 0:1], scalar2=None, op0=ALU.mult), [ob, rec], [mixF])

                for B in range(4):
                    srcs = []
                    pcs = [(0, 512, bias_m.ap[0:80, 4 * B + 3:4 * B + 4], None)]
                    pvs = [(sub * 128, 128, sub * 65) for sub in range(4)]
                    srcs.append((kT.ap[0:128, c, 2048:2128], 80, Vb.ap[64:80, 16, h, 0:65], 64, 80, pcs, pvs, None))
                    for j in range(3):
                        for kt in range(16):
                            pcs = [(0, 512, bias_g.ap[:, j, kt, 4 * B + 3:4 * B + 4], None)]
                            srcs.append((kg[b].ap[0:128, j, kt * 128:(kt + 1) * 128], 128, vg[b].ap[:, j, kt, hh * 66:hh * 66 + 65],
                                         0, 128, pcs, pvs, None))
                    for kt in range(4 * B + 4):
                        subs = [sub for sub in range(4) if 4 * B + sub >= kt]
                        c_lo = subs[0] * 128
                        pcs = [(c_lo, 512 - c_lo, bias_o.ap[:, kt, 4 * B + 3:4 * B + 4], (maskb.ap if kt >= 4 * B else None))]
                        srcs.append((kT.ap[0:128, c, kt * 128:(kt + 1) * 128], 128, Vb.ap[:, kt, h, 0:65], 0, 128, pcs,
                                     [(sub * 128, 128, sub * 65) for sub in subs], None))
                    self.cur_dq = True
                    self.fox_block_srcs(block, B * 512, 512, srcs, [(sub * 65, 128, 4 * B + sub) for sub in range(4)],
                                        [kT, kg[b]], [Vb, vg[b]], [bias_m, bias_g, bias_o])
                    self.cur_dq = False
                srcs = []
                for s in range(2):
                    for kt in range(16):
                        srcs.append((kc.ap[0:128, s, kt * 128:(kt + 1) * 128], 128, vc.ap[:, s, kt, hh, 0:65], 0, 128,
                                     [(32 * s, 32, bias_s.ap[:, s, kt:kt + 1], None)], [(0, 80, 0)], pt_seg[s][kt % 3]))
                pcs = [(0, 32, bias_s.ap[0:80, 0, 16:17], mask80.ap[0:80, 0:32]), (32, 32, bias_s.ap[0:80, 1, 16:17], mask80.ap[0:80, 32:64]),
                       (64, 16, bias_mm.ap[0:80, 0:1], mask80.ap[0:80, 64:80])]
                srcs.append((kT.ap[0:128, c, 2048:2128], 80, Vb.ap[0:80, 16, h, 0:65], 0, 80, pcs, [(0, 80, 0)], None))
                self.cur_dq = False
                self.fox_block_srcs(block, 2048, 80, srcs, [(0, 80, 16)], [kT, kc], [Vb, vc], [bias_s, bias_mm])

    def fox_block_srcs(self, block, qc0, nq, srcs, out_tiles, ktls, vtls, btls):
        self.cur_k = list(ktls)
        self.cur_v = list(vtls)
        self.cur_bias = list(btls)
        block(qc0, nq, srcs, out_tiles)
    def mixer_out(self, w_out_dram, gain_idx):
        cx = self.cx
        cx.barrier()
        cx.arena_off = self.ab_mark
        cx.h_off = cx.h_base
        for t in range(NTT):
            r = tt_rows(t)
            cx.dma(cx.sp, self.h[t].ap[0:r, :], self.hsave[t * 128:t * 128 + r, :], writes=[self.h[t]])
        wout = cx.tile([128, 8, D], BF16, "wout")
        cx.dma(cx.pool, wout.ap, w_out_dram.rearrange("(k p) n -> p k n", p=128), writes=[wout])
        self.load_gain(self.gpost, gain_idx)
        self.junk = cx.tile([128, D], F32, "junk")
        self.tmpn = cx.tile([128, D], F32, "tmpn")
        mT = [cx.tile([128, 4, 128], BF16, f"mT{i}") for i in range(2)]
        ot = [Tl(cx.sb([128, D], F32, f"ot{i}"), f"ot{i}") for i in range(2)]
        mixS, mixF = self.mixS, self.mixF
        for t in range(NTT):
            r = tt_rows(t)
            bank = cx.banks[t % 2]
            pt = bank.ap.bitcast(BF16).rearrange("p (k c) -> p k c", k=8)
            for k in range(4):
                cx.op(cx.pe, lambda e: e.transpose(out=pt[:, k, 0:r], in_=mixF.ap[0:r, t, k * 128:(k + 1) * 128],
                                                   identity=self.identb.ap[0:r, 0:r]), [mixF, self.identb], [bank])
            m = mT[t % 2]
            cx.op(cx.act, lambda e: e.copy(out=m.ap[:, :, 0:r], in_=pt[:, 0:4, 0:r]), [bank], [m])
            o = ot[t % 2]
            for hf in range(2):
                ob = cx.banks[2 + (2 * t + hf) % 4]
                for k in range(4):
                    cx.op(cx.pe, lambda e: e.matmul(ob.ap[0:r, :], lhsT=mixS.ap[:, k, t * 128:t * 128 + r],
                                                    rhs=wout.ap[:, k, hf * 512:(hf + 1) * 512], start=(k == 0), stop=False),
                          [mixS, wout], [ob])
                for k in range(4):
                    cx.op(cx.pe, lambda e: e.matmul(ob.ap[0:r, :], lhsT=m.ap[:, k, 0:r],
                                                    rhs=wout.ap[:, 4 + k, hf * 512:(hf + 1) * 512], start=False, stop=(k == 3)),
                          [m, wout], [ob])
                cx.op(cx.act, lambda e: e.copy(out=o.ap[0:r, hf * 512:(hf + 1) * 512], in_=ob.ap[0:r, :]), [ob], [o])
            self.postnorm_add([t], [o], self.gpost, 1.0)
        cx.barrier()

    def declare_cd(self):
        self.w_cd_in = self.din("cd_w_in", [D, 3080])
        self.cst2 = self.din("cst2", [128, 896])
        self.convw = self.din("convw", [128, 8 * 5])
        self.ssdp = self.din("ssdp", [1, 24])
        self.ssd_norm = self.din("ssd_norm", [1, 512])
        self.ssdinit = self.din("ssdinit", [2, 128, 512])
        self.convinit = self.din("convinit", [128, 2 * 8 * 3])
        self.w_cd_out = self.din("cd_w_out", [D, D])
        self.c_sbk = self.din("c_sbk", [2, 2048, 512])
        self.c_sbv = self.din("c_sbv", [2, 2048, 512])
        self.o_ssd = self.dout("o_ssd", [3, 128, 512])
        self.o_conv = self.dout("o_conv", [3, 3, 1024])
        self.o_sbk = self.dout("o_sbk", [NT, 512])
        self.o_sbv = self.dout("o_sbv", [NT, 512])
        mk = lambda n, sh, dt=F32: Tl(self.nc.dram_tensor(n, sh, dt).ap(), n)
        self.tail_send, self.tail_g = mk("tail_send", [128, 24]), mk("tail_g", [512, 24])
        self.ssF_send, self.ssF_g = mk("ssF_send", [128, 520]), mk("ssF_g", [512, 520])
        self.k2_send = [mk(f"k2_send{i}", [256, 2048], BF16) for i in range(2)]
        self.k2_g = [mk(f"k2_g{i}", [1024, 2048], BF16) for i in range(2)]
        self.v2_send = [mk(f"v2_send{i}", [512, 512], BF16) for i in range(4)]
        self.v2_g = [mk(f"v2_g{i}", [2048, 512], BF16) for i in range(4)]

    def c2(self, i):
        return self.cst2t.ap[:, i * 128:(i + 1) * 128]

    def mixer_cd(self):
        cx = self.cx
        V, P, A, PE = cx.dve, cx.pool, cx.act, cx.pe
        GR = [[0, 1, 2, 3], [4, 5, 6, 7]]
        SEGS = [(0, 2048, 0), (2048, 32, 1), (2080, 32, 2), (2112, 16, 3)]
        cx.barrier()
        cx.arena_off = self.mark
        cx.h_off = cx.h_base
        self.mixC = cx.tile([128, NTT, 512], BF16, "mixC")
        self.dtt = cx.tile([128, NTT, 8], F32, "dtt")
        self.selt = cx.tile([128, 32], F32, "selt")
        cx.dma(cx.sp, self.selt.ap, self.sel, writes=[self.selt])
        self.cst2t = cx.tile([128, 896], F32, "cst2t")
        cx.dma(cx.sp, self.cst2t.ap, self.cst2, writes=[self.cst2t])
        sp24 = cx.tile([128, 24], F32, "sp24")
        cx.dma(cx.sp, sp24.ap, self.ssdp.partition_broadcast(128), writes=[sp24])
        cwt = cx.tile([128, 8, 5], F32, "cwt")
        cx.dma(cx.sp, cwt.ap, self.convw.rearrange("p (c w) -> p c w", w=5), writes=[cwt])
        self.sb_mark = cx.arena_off
        self.xdt = cx.tile([128, NTT, 512], BF16, "xdt")
        self.zs = cx.tile([128, NTT, 512], BF16, "zs")
        self.cd_mark = cx.arena_off
        self.load_gain(self.gpre, 2 * 2 + 1)
        xnT = cx.tile([128, 8, NT], BF16, "xnTf")
        mark2 = cx.arena_off
        self.junk = cx.tile([128, D], F32, "junk")
        self.nrm_xb = [cx.tile([128, D], BF16, "xb") for _ in range(2)]
        self.prenorm_T(list(range(NTT)), self.gpre, xnT, 0)
        self.spill_h()
        cx.barrier()
        cx.arena_off = mark2
        self.qT = cx.tile([128, 4, NT], BF16, "qT", hreg=True)
        self.kT = cx.tile([128, 4, NT], BF16, "kT", hreg=True)
        self.Vb = cx.tile([128, NTT, 8, 64], BF16, "Vb2", hreg=True)
        self.h_mark_cd = cx.h_off
        self.BCT = cx.tile([128, 4, NT], BF16, "BCT", hreg=True)
        wsl = [cx.tile([128, 8, 512], BF16, f"wsl{i}") for i in range(2)]
        wv = self.w_cd_in.rearrange("(k p) f -> p k f", p=128)
        wi = [0]

        def load_w(c0, n):
            t_ = wsl[wi[0] % 2]
            wi[0] += 1
            cx.dma(cx.pool, t_.ap[:, :, 0:n], wv[:, :, c0:c0 + n], writes=[t_])
            return t_

        stg = [cx.tile([128, 512], F32, f"stg{i}") for i in range(2)]
        a_b = cx.tile([128, 8], F32, "a_b")
        cx.op(A, lambda e: e.activation(out=a_b.ap, in_=sp24.ap[:, 8:16], func=AF.Exp), [sp24], [a_b])
        cx.op(V, lambda e: e.tensor_scalar(out=a_b.ap, in0=a_b.ap, scalar1=-1.0, scalar2=None, op0=ALU.mult), [a_b], [a_b])
        self.a_b = a_b
        self.sp24 = sp24

        wk_ = None
        for part, (c0w, n) in enumerate(((0, 512), (2056, 512), (2568, 512), (1536, 8))):
            w_ = load_w(c0w, n)
            for t in range(NTT):
                r = tt_rows(t)
                c0 = t * 128
                bank = cx.banks[4 + t % 2]
                for k in range(8):
                    cx.op(PE, lambda e: e.matmul(bank.ap[0:r, 0:n], lhsT=xnT.ap[:, k, c0:c0 + r], rhs=w_.ap[:, k, 0:n],
                                                 start=(k == 0), stop=(k == 7)), [w_, xnT], [bank])
                if part == 0:
                    cx.op(A, lambda e: e.activation(out=self.zs.ap[0:r, t, :], in_=bank.ap[0:r, :], func=AF.Silu), [bank], [self.zs])
                elif part == 3:
                    cx.op(V, lambda e: e.tensor_tensor(out=self.dtt.ap[0:r, t, :], in0=bank.ap[0:r, 0:8], in1=sp24.ap[0:r, 0:8],
                                                       op=ALU.add), [bank, sp24], [self.dtt])
                else:
                    st = stg[t % 2]
                    cx.op(A, lambda e: e.copy(out=st.ap[0:r, :], in_=bank.ap[0:r, :]), [bank], [st])
                    oap = self.o_sbk if part == 1 else self.o_sbv
                    cx.dma(cx.sp, oap[c0:c0 + r, :], st.ap[0:r, :], reads=[st], is_output=True)
                    if part == 2:
                        cx.op(V, lambda e: e.tensor_copy(out=self.Vb.ap[0:r, t, :, :],
                                                         in_=st.ap[0:r, :].rearrange("p (h d) -> p h d", h=8)), [st], [self.Vb])
        dtt = self.dtt
        cx.op(A, lambda e: e.activation(out=dtt.ap, in_=dtt.ap, func=AF.Exp), [dtt], [dtt])
        cx.op(A, lambda e: e.activation(out=dtt.ap, in_=dtt.ap, func=AF.Ln, bias=1.0, scale=1.0), [dtt], [dtt])
        cnt = 0
        for j in range(8):
            dst = self.qT if j < 4 else self.kT
            c0w = (1544 if j < 4 else 2056) + (j % 4) * 128
            if j % 4 == 0:
                w_ = load_w(c0w, 512)
            for (c0, cw) in COLT:
                bank = cx.banks[2 + cnt % 2]
                cnt += 1
                for k in range(8):
                    cx.op(PE, lambda e: e.matmul(bank.ap[:, 0:cw], lhsT=w_.ap[:, k, (j % 4) * 128:(j % 4 + 1) * 128],
                                                 rhs=xnT.ap[:, k, c0:c0 + cw], start=(k == 0), stop=(k == 7)), [w_, xnT], [bank])
                sc = 0.125 if j < 4 else 1.0
                cx.op(A, lambda e: e.activation(out=dst.ap[:, j % 4, c0:c0 + cw], in_=bank.ap[:, 0:cw], func=AF.Copy, scale=sc),
                      [bank], [dst])
        for c in range(4):
            cx.dma(cx.sp, self.k2_send[c // 2].ap[(c % 2) * 128:(c % 2 + 1) * 128, :], self.kT.ap[:, c, 0:2048], reads=[self.kT],
                   writes=[self.k2_send[c // 2]])
        for i in range(4):
            cx.dma(cx.sp, self.v2_send[i].ap.rearrange("(t p) c -> p t c", p=128),
                   self.Vb.ap[:, 4 * i:4 * i + 4, :, :].rearrange("p t h d -> p t (h d)"), reads=[self.Vb], writes=[self.v2_send[i]])
        for i in range(2):
            cx.allgather(self.k2_send[i], self.k2_g[i], GR)
        for i in range(4):
            cx.allgather(self.v2_send[i], self.v2_g[i], GR)
        XW = [cx.tile([128, 515], F32, f"XW{i}") for i in range(2)]
        cv = [cx.tile([128, 512], F32, f"cv{i}") for i in range(2)]
        hist = cx.tile([128, 8, 3], F32, "hist")
        tails = cx.tile([128, 4, 8, 3], F32, "tails")
        tg = cx.tile([128, 4, 24], F32, "tg")
        ci_t = cx.tile([128, 2, 8, 3], F32, "ci_t")
        cx.dma(cx.sp, ci_t.ap, self.convinit.rearrange("p (s c w) -> p s c w", s=2, c=8), writes=[ci_t])
        xst = [cx.tile([128, 512], F32, f"xst{i}") for i in range(2)]
        x16 = cx.tile([128, 80], F32, "x16")
        wx = [load_w(512, 512), load_w(1024, 512)]
        xi = [0]

        def xbc_tile(ch, c0, cw, dst_ap):
            bank = cx.banks[2 + xi[0] % 2]
            xi[0] += 1
            w_ = wx[ch // 4]
            for k in range(8):
                cx.op(PE, lambda e: e.matmul(bank.ap[:, 0:cw], lhsT=w_.ap[:, k, (ch % 4) * 128:(ch % 4 + 1) * 128],
                                             rhs=xnT.ap[:, k, c0:c0 + cw], start=(k == 0), stop=(k == 7)), [w_, xnT], [bank])
            return bank

        for ch in range(8):
            bank = xbc_tile(ch, 2045, 83 - 0, None)
            cx.op(V, lambda e: e.tensor_copy(out=tails.ap[:, 0, ch, :], in_=bank.ap[:, 0:3]), [bank], [tails])
            cx.op(V, lambda e: e.tensor_copy(out=tails.ap[:, 1, ch, :], in_=bank.ap[:, 32:35]), [bank], [tails])
            cx.op(V, lambda e: e.tensor_copy(out=tails.ap[:, 2, ch, :], in_=bank.ap[:, 64:67]), [bank], [tails])
            cx.op(V, lambda e: e.tensor_copy(out=tails.ap[:, 3, ch, :], in_=bank.ap[:, 80:83]), [bank], [tails])
        cx.dma(cx.sp, self.tail_send.ap, tails.ap[:, 0, :, :].rearrange("p c w -> p (c w)"), reads=[tails], writes=[self.tail_send])
        cx.allgather(self.tail_send, self.tail_g, GR)
        cx.dma(cx.sp, tg.ap, self.tail_g.ap.rearrange("(j p) x -> p j x", p=128), reads=[self.tail_g], writes=[tg])
        selt = self.selt
        h2 = hist.ap.rearrange("p c w -> p (c w)")
        cx.op(V, lambda e: e.tensor_scalar(out=h2, in0=tails.ap[:, 3, :, :].rearrange("p c w -> p (c w)"), scalar1=selt.ap[:, 0:1],
                                           scalar2=None, op0=ALU.mult), [tails, selt], [hist])
        for j in range(3):
            cx.op(V, lambda e: e.scalar_tensor_tensor(out=h2, in0=tg.ap[:, j, :], scalar=selt.ap[:, 27 + j:28 + j], in1=h2,
                                                      op0=ALU.mult, op1=ALU.add), [tg, selt, hist], [hist])
        cst_o = [cx.tile([128, 128], F32, f"cst_o{i}") for i in range(2)]
        for sid, oi in ((0, 0), (1, 1), (2, 2)):
            for ch in range(8):
                bank = cx.banks[6 + ch % 2]
                co = cst_o[ch % 2]
                cx.op(PE, lambda e: e.transpose(out=bank.ap[0:3, 0:128], in_=tails.ap[:, sid, ch, :], identity=self.cconst(0)),
                      [tails, self.cstt], [bank])
                cx.op(V, lambda e: e.tensor_copy(out=co.ap[0:3, :], in_=bank.ap[0:3, 0:128]), [bank], [co])
                cx.dma(cx.sp, self.o_conv[oi, :, ch * 128:(ch + 1) * 128], co.ap[0:3, :], reads=[co], is_output=True)
        pieces = [(0, 512, 0, None), (512, 512, 0, None), (1024, 512, 0, None), (1536, 512, 0, None),
                  (2048, 32, 1, 0), (2080, 32, 2, 1), (2112, 16, 3, None)]
        xw_i = 0
        for ch in range(8):
            for (c0, cw, sid, si) in pieces:
                xw = XW[xw_i % 2]
                cvt = cv[xw_i % 2]
                xw_i += 1
                bank = xbc_tile(ch, c0, cw, None)
                if c0 == 0:
                    cx.op(V, lambda e: e.tensor_copy(out=xw.ap[:, 0:3], in_=hist.ap[:, ch, :]), [hist], [xw])
                elif sid == 0:
                    prev = XW[(xw_i - 2) % 2]
                    cx.op(V, lambda e: e.tensor_copy(out=xw.ap[:, 0:3], in_=prev.ap[:, 512:515]), [prev], [xw])
                elif si is not None:
                    cx.op(V, lambda e: e.tensor_copy(out=xw.ap[:, 0:3], in_=ci_t.ap[:, si, ch, :]), [ci_t], [xw])
                else:
                    cx.op(V, lambda e: e.memset(xw.ap[:, 0:3], 0.0), [], [xw])
                cx.op(A, lambda e: e.copy(out=xw.ap[:, 3:3 + cw], in_=bank.ap[:, 0:cw]), [bank], [xw])
                cx.op(V, lambda e: e.tensor_scalar(out=cvt.ap[:, 0:cw], in0=xw.ap[:, 3:3 + cw], scalar1=cwt.ap[:, ch, 3:4],
                                                   scalar2=cwt.ap[:, ch, 4:5], op0=ALU.mult, op1=ALU.add), [xw, cwt], [cvt])
                for w in range(3):
                    cx.op(V, lambda e: e.scalar_tensor_tensor(out=cvt.ap[:, 0:cw], in0=xw.ap[:, w:w + cw], scalar=cwt.ap[:, ch, w:w + 1],
                                                              in1=cvt.ap[:, 0:cw], op0=ALU.mult, op1=ALU.add), [xw, cwt, cvt], [cvt])
                if ch >= 4:
                    cx.op(A, lambda e: e.activation(out=self.BCT.ap[:, ch - 4, c0:c0 + cw], in_=cvt.ap[:, 0:cw], func=AF.Silu),
                          [cvt], [self.BCT])
                else:
                    if sid == 0:
                        xs_ = xst[xw_i % 2]
                        off = 0
                    else:
                        xs_ = x16
                        off = c0 - 2048
                    cx.op(A, lambda e: e.activation(out=xs_.ap[:, off:off + cw], in_=cvt.ap[:, 0:cw], func=AF.Silu), [cvt], [xs_])
                    if sid in (1, 2):
                        continue
                    tot_w = cw if sid == 0 else 80
                    base = c0 if sid == 0 else 2048
                    nsub = (tot_w + 127) // 128
                    for sb_ in range(nsub):
                        w_ = min(128, tot_w - sb_ * 128)
                        t = (base + sb_ * 128) // 128
                        tb = cx.banks[6 + sb_ % 2]
                        cx.op(PE, lambda e: e.transpose(out=tb.ap[0:w_, 0:128], in_=xs_.ap[:, sb_ * 128:sb_ * 128 + w_],
                                                        identity=self.cconst(0)), [xs_, self.cstt], [tb])
                        for hh in range(2):
                            h = 2 * ch + hh
                            cx.op(A, lambda e: e.activation(out=self.xdt.ap[0:w_, t, h * 64:(h + 1) * 64],
                                                            in_=tb.ap[0:w_, hh * 64:(hh + 1) * 64], func=AF.Copy,
                                                            scale=self.dtt.ap[0:w_, t, h:h + 1]), [tb, self.dtt], [self.xdt])
        cx.barrier()
        cx.arena_off = self.cd_mark
        self.ssd_phase()
        cx.barrier()
        cx.arena_off = self.sb_mark
        cx.h_off = self.h_mark_cd
        self.sb_phase()
        self.mixer_out2(self.w_cd_out, 3 * 2 + 1)

    def ssd_phase(self):
        cx = self.cx
        V, P, A, PE = cx.dve, cx.pool, cx.act, cx.pe
        xdt, zs, BCT, dtt, a_b, selt = self.xdt, self.zs, self.BCT, self.dtt, self.a_b, self.selt
        HT = cx.tile([128, 512], F32, "HT")
        HTb = cx.tile([128, 512], BF16, "HTb")
        adt = cx.tile([128, 8], F32, "adt")
        acs = cx.tile([128, 8], F32, "acs")
        tot = cx.tile([128, 8], F32, "tot")
        eacs = cx.tile([128, 8], F32, "eacs")
        dec = cx.tile([128, 8], F32, "dec")
        etot = cx.tile([128, 8], F32, "etot")
        ddt = cx.tile([128, 8], F32, "ddt")
        logD = cx.tile([128, 8], F32, "logD")
        Am = [cx.tile([128, 128], F32, f"Am{i}") for i in range(2)]
        Ex = [cx.tile([128, 128], F32, f"Ex{i}") for i in range(8)]
        CBm = [cx.tile([128, 128], F32, f"CBm{i}") for i in range(2)]
        Wt = [cx.tile([128, 128], BF16, f"Wt{i}") for i in range(8)]
        Btok = [cx.tile([128, 128], BF16, f"Btok{i}") for i in range(2)]
        xdd = cx.tile([128, 512], BF16, "xdd")
        ysb = cx.tile([128, 512], F32, "ysb")
        yg = cx.tile([128, 512], F32, "yg")
        ss2 = cx.tile([128, 2], F32, "ss2")
        rs2 = cx.tile([128, 2], F32, "rs2")
        db = cx.tile([128, 8], F32, "db")
        cx.op(V, lambda e: e.tensor_copy(out=db.ap, in_=self.sp24.ap[:, 16:24]), [self.sp24], [db])
        ng = cx.tile([128, 512], F32, "ng")
        cx.dma(cx.sp, ng.ap, self.ssd_norm.partition_broadcast(128), writes=[ng])
        nh2 = cx.tile([128, 2], F32, "nh2")
        cx.op(V, lambda e: e.memset(nh2.ap, -0.5), [], [nh2])
        identf = self.cconst(0)
        TRI, ONES = self.cconst(1), self.cconst(2)
        SL, BSL80, BT80, BO80 = self.c2(0), self.c2(1), self.cconst(3), self.c2(2)
        ci = [0]

        def chunk(t, rows, want_y, seg16):
            r = rows
            tri = BT80[0:r, 0:r] if seg16 else TRI[0:r, 0:r]
            ones = BO80[0:r, 0:r] if seg16 else ONES[0:r, 0:r]
            sl = BSL80[0:r, 0:r] if seg16 else SL[0:r, 0:r]
            cstl = [self.cstt, self.cst2t]
            cx.op(V, lambda e: e.tensor_tensor(out=adt.ap[0:r, :], in0=dtt.ap[0:r, t, :], in1=a_b.ap[0:r, :], op=ALU.mult),
                  [dtt, a_b], [adt])
            b0, b1 = cx.banks[0], cx.banks[1]
            cx.op(PE, lambda e: e.matmul(b0.ap[0:r, 0:8], lhsT=tri, rhs=adt.ap[0:r, :], start=True, stop=True), [adt, *cstl], [b0])
            cx.op(PE, lambda e: e.matmul(b1.ap[0:r, 0:8], lhsT=ones, rhs=adt.ap[0:r, :], start=True, stop=True), [adt, *cstl], [b1])
            cx.op(V, lambda e: e.tensor_copy(out=acs.ap[0:r, :], in_=b0.ap[0:r, 0:8]), [b0], [acs])
            cx.op(V, lambda e: e.tensor_copy(out=tot.ap[0:r, :], in_=b1.ap[0:r, 0:8]), [b1], [tot])
            cx.op(A, lambda e: e.activation(out=eacs.ap[0:r, :], in_=acs.ap[0:r, :], func=AF.Exp), [acs], [eacs])
            cx.op(A, lambda e: e.activation(out=etot.ap[0:r, :], in_=tot.ap[0:r, :], func=AF.Exp), [tot], [etot])
            cx.op(V, lambda e: e.tensor_tensor(out=dec.ap[0:r, :], in0=tot.ap[0:r, :], in1=acs.ap[0:r, :], op=ALU.subtract), [tot, acs], [dec])
            cx.op(A, lambda e: e.activation(out=dec.ap[0:r, :], in_=dec.ap[0:r, :], func=AF.Exp), [dec], [dec])
            c0 = t * 128
            if want_y:
                cx.op(V, lambda e: e.reciprocal(out=ddt.ap[0:r, :], in_=dtt.ap[0:r, t, :]), [dtt], [ddt])
                cx.op(V, lambda e: e.tensor_tensor(out=ddt.ap[0:r, :], in0=ddt.ap[0:r, :], in1=db.ap[0:r, :], op=ALU.mult), [ddt, db], [ddt])
                for g in range(2):
                    cb = cx.banks[2 + g]
                    cx.op(PE, lambda e: e.matmul(cb.ap[0:r, 0:r], lhsT=BCT.ap[:, g, c0:c0 + r], rhs=BCT.ap[:, 2 + g, c0:c0 + r],
                                                 start=True, stop=True), [BCT], [cb])
                    cx.op(V, lambda e: e.tensor_tensor(out=CBm[g].ap[0:r, 0:r], in0=cb.ap[0:r, 0:r], in1=tri, op=ALU.mult),
                          [cb, *cstl], [CBm[g]])
                yb = cx.banks[6]
                for h in range(8):
                    am, ex = Am[h % 2], Ex[h]
                    sb_ = cx.banks[4 + h % 2]
                    cx.op(P, lambda e: e.tensor_scalar(out=am.ap[0:r, 0:r], in0=sl, scalar1=adt.ap[0:r, h:h + 1], scalar2=None,
                                                       op0=ALU.mult), [adt, *cstl], [am])
                    cx.op(PE, lambda e: e.matmul(sb_.ap[0:r, 0:r], lhsT=am.ap[0:r, 0:r], rhs=tri, start=True, stop=True),
                          [am, *cstl], [sb_])
                    cx.op(A, lambda e: e.activation(out=ex.ap[0:r, 0:r], in_=sb_.ap[0:r, 0:r], func=AF.Exp), [sb_], [ex])
                for h in range(8):
                    ex = Ex[h]
                    cx.op(V, lambda e: e.tensor_tensor(out=ex.ap[0:r, 0:r], in0=ex.ap[0:r, 0:r], in1=CBm[h // 4].ap[0:r, 0:r], op=ALU.mult),
                          [ex, CBm[h // 4]], [ex])
                    cx.op(V, lambda e: e.scalar_tensor_tensor(out=Wt[h].ap[0:r, 0:r], in0=identf[0:r, 0:r], scalar=ddt.ap[0:r, h:h + 1],
                                                              in1=ex.ap[0:r, 0:r], op0=ALU.mult, op1=ALU.add), [ddt, ex, self.cstt], [Wt[h]])
                    cx.op(PE, lambda e: e.matmul(yb.ap[0:r, h * 64:(h + 1) * 64], lhsT=Wt[h].ap[0:r, 0:r], rhs=xdt.ap[0:r, t, h * 64:(h + 1) * 64],
                                                 start=(h == 0), stop=True, skip_group_check=True), [Wt[h], xdt], [yb])
                ob = cx.banks[7]
                if seg16:
                    CT16_, Hsb_ = self._yoff16
                    for g in range(2):
                        for si_ in range(2):
                            cx.op(PE, lambda e: e.matmul(ob.ap[0:r, g * 256:(g + 1) * 256], lhsT=CT16_.ap[:, si_, g, :],
                                                         rhs=Hsb_[si_].ap[:, g * 256:(g + 1) * 256], start=(g == 0 and si_ == 0), stop=(si_ == 1),
                                                         skip_group_check=True), [CT16_, Hsb_[si_]], [ob])
                else:
                    for g in range(2):
                        cx.op(PE, lambda e: e.matmul(ob.ap[0:r, g * 256:(g + 1) * 256], lhsT=BCT.ap[:, 2 + g, c0:c0 + r],
                                                     rhs=HTb.ap[:, g * 256:(g + 1) * 256], start=(g == 0), stop=True, skip_group_check=True),
                              [BCT, HTb], [ob])
                cx.op(A, lambda e: e.copy(out=ysb.ap[0:r, :], in_=yb.ap[0:r, :]), [yb], [ysb])
                for h in range(8):
                    cx.op(V, lambda e: e.scalar_tensor_tensor(out=ysb.ap[0:r, h * 64:(h + 1) * 64], in0=ob.ap[0:r, h * 64:(h + 1) * 64],
                                                              scalar=eacs.ap[0:r, h:h + 1], in1=ysb.ap[0:r, h * 64:(h + 1) * 64],
                                                              op0=ALU.mult, op1=ALU.add), [ob, eacs, ysb], [ysb])
                cx.op(V, lambda e: e.tensor_tensor(out=yg.ap[0:r, :], in0=ysb.ap[0:r, :], in1=zs.ap[0:r, t, :], op=ALU.mult), [ysb, zs], [yg])
                for g in range(2):
                    cx.op(V, lambda e: e.scalar_tensor_tensor(out=ysb.ap[0:r, g * 256:(g + 1) * 256], in0=yg.ap[0:r, g * 256:(g + 1) * 256],
                                                              scalar=1.0, in1=yg.ap[0:r, g * 256:(g + 1) * 256], op0=ALU.mult, op1=ALU.mult,
                                                              accum_out=ss2.ap[0:r, g:g + 1]), [yg], [ysb, ss2])
                cx.op(V, lambda e: e.tensor_scalar(out=ss2.ap[0:r, :], in0=ss2.ap[0:r, :], scalar1=1.0 / 256, scalar2=EPS,
                                                   op0=ALU.mult, op1=ALU.add), [ss2], [ss2])
                cx.op(P, lambda e: e.tensor_tensor(out=rs2.ap[0:r, :], in0=ss2.ap[0:r, :], in1=nh2.ap[0:r, :], op=ALU.pow), [ss2, nh2], [rs2])
                for g in range(2):
                    cx.op(V, lambda e: e.scalar_tensor_tensor(out=self.mixC.ap[0:r, t, g * 256:(g + 1) * 256],
                                                              in0=yg.ap[0:r, g * 256:(g + 1) * 256], scalar=rs2.ap[0:r, g:g + 1],
                                                              in1=ng.ap[0:r, g * 256:(g + 1) * 256], op0=ALU.mult, op1=ALU.mult),
                          [yg, rs2, ng], [self.mixC])

        def prep_B(t, r):
            c0 = t * 128
            for g in range(2):
                tb = cx.banks[0 + g]
                tbv = tb.ap.bitcast(BF16)
                cx.op(PE, lambda e: e.transpose(out=tbv[0:r, 0:128], in_=BCT.ap[:, g, c0:c0 + r], identity=self.identb.ap),
                      [BCT, self.identb], [tb])
                cx.op(V, lambda e: e.tensor_copy(out=Btok[g].ap[0:r, :], in_=tbv[0:r, 0:128]), [tb], [Btok[g]])

        def upd(t, r0, r1, Ht, Htb, et):
            for h in range(8):
                cx.op(A, lambda e: e.activation(out=xdd.ap[r0:r1, h * 64:(h + 1) * 64], in_=xdt.ap[r0:r1, t, h * 64:(h + 1) * 64],
                                                func=AF.Copy, scale=dec.ap[r0:r1, h:h + 1]), [xdt, dec], [xdd])
            hb = cx.banks[3]
            for g in range(2):
                cx.op(PE, lambda e: e.matmul(hb.ap[:, g * 256:(g + 1) * 256], lhsT=Btok[g].ap[r0:r1, :], rhs=xdd.ap[r0:r1, g * 256:(g + 1) * 256],
                                             start=True, stop=True, skip_group_check=True), [Btok[g], xdd], [hb])
            for h in range(8):
                cx.op(V, lambda e: e.scalar_tensor_tensor(out=Ht.ap[:, h * 64:(h + 1) * 64], in0=Ht.ap[:, h * 64:(h + 1) * 64],
                                                          scalar=et.ap[:, h:h + 1], in1=hb.ap[:, h * 64:(h + 1) * 64],
                                                          op0=ALU.mult, op1=ALU.add), [Ht, et, hb], [Ht])
            cx.op(P, lambda e: e.tensor_copy(out=Htb.ap, in_=Ht.ap), [Ht], [Htb])

        o_st = self.o_ssd
        Hs = [cx.tile([128, 512], F32, f"Hs{i}") for i in range(3)]
        Hsb = [cx.tile([128, 512], BF16, f"Hsb{i}") for i in range(3)]
        for s_ in range(2):
            cx.dma(cx.sp, Hs[s_].ap, self.ssdinit[s_], writes=[Hs[s_]])
            cx.op(P, lambda e: e.tensor_copy(out=Hsb[s_].ap, in_=Hs[s_].ap), [Hs[s_]], [Hsb[s_]])
        cx.op(V, lambda e: e.memset(Hs[2].ap, 0.0), [], [Hs[2]])
        cx.op(V, lambda e: e.memset(Hsb[2].ap, 0.0), [], [Hsb[2]])
        CT16 = cx.tile([128, 3, 2, 80], BF16, "CT16")
        cx.op(V, lambda e: e.memset(CT16.ap, 0.0), [], [CT16])
        for si_, (a_, b_) in enumerate(((0, 32), (32, 64), (64, 80))):
            for g in range(2):
                cx.op(V, lambda e: e.tensor_copy(out=CT16.ap[:, si_, g, a_:b_], in_=BCT.ap[:, 2 + g, 2048 + a_:2048 + b_]), [BCT], [CT16])
        self._yoff16 = (CT16, Hsb)
        chunk(16, 80, True, True)
        prep_B(16, 80)
        et16 = cx.tile([128, 8], F32, "et16")
        for si_, (a_, b_, selc) in enumerate(((0, 32, self.cstt.ap[0:80, 640:768]), (32, 64, self.cstt.ap[0:80, 768:896]),
                                               (64, 80, self.cstt.ap[0:80, 512:640]))):
            bk = cx.banks[2]
            cx.op(PE, lambda e: e.matmul(bk.ap[:, 0:8], lhsT=selc, rhs=adt.ap[0:80, :], start=True, stop=True), [self.cstt, adt], [bk])
            cx.op(A, lambda e: e.activation(out=et16.ap, in_=bk.ap[:, 0:8], func=AF.Exp), [bk], [et16])
            upd(16, a_, b_, Hs[si_], Hsb[si_], et16)
            if si_ < 2:
                cx.dma(cx.sp, o_st[1 + si_], Hs[si_].ap, reads=[Hs[si_]], is_output=True)
        self._yoff16 = None
        cx.op(V, lambda e: e.memset(HT.ap, 0.0), [], [HT])
        cx.op(V, lambda e: e.memset(logD.ap, 0.0), [], [logD])
        for t in range(16):
            chunk(t, 128, False, False)
            prep_B(t, 128)
            upd(t, 0, 128, HT, HTb, etot)
            cx.op(V, lambda e: e.tensor_tensor(out=logD.ap, in0=logD.ap, in1=tot.ap, op=ALU.add), [logD, tot], [logD])
        fs = cx.tile([128, 520], F32, "fs")
        cx.op(V, lambda e: e.tensor_copy(out=fs.ap[:, 0:512], in_=HT.ap), [HT], [fs])
        cx.op(V, lambda e: e.tensor_copy(out=fs.ap[:, 512:520], in_=logD.ap), [logD], [fs])
        cx.dma(cx.sp, self.ssF_send.ap, fs.ap, reads=[fs], writes=[self.ssF_send])
        cx.allgather(self.ssF_send, self.ssF_g, [[0, 1, 2, 3], [4, 5, 6, 7]])
        fg = cx.tile([128, 3, 520], F32, "fg")
        cx.dma(cx.sp, fg.ap, self.ssF_g.ap[0:384, :].rearrange("(j p) c -> p j c", p=128), reads=[self.ssF_g], writes=[fg])
        Sc = Hs[2]
        ed = cx.tile([128, 8], F32, "ed")
        cx.op(V, lambda e: e.tensor_scalar(out=HT.ap, in0=Sc.ap, scalar1=selt.ap[:, 0:1], scalar2=None, op0=ALU.mult), [Sc, selt], [HT])
        for j in range(3):
            cx.op(A, lambda e: e.activation(out=ed.ap, in_=fg.ap[:, j, 512:520], func=AF.Exp), [fg], [ed])
            for h in range(8):
                cx.op(V, lambda e: e.scalar_tensor_tensor(out=Sc.ap[:, h * 64:(h + 1) * 64], in0=Sc.ap[:, h * 64:(h + 1) * 64],
                                                          scalar=ed.ap[:, h:h + 1], in1=fg.ap[:, j, h * 64:(h + 1) * 64],
                                                          op0=ALU.mult, op1=ALU.add), [Sc, ed, fg], [Sc])
            cx.op(V, lambda e: e.scalar_tensor_tensor(out=HT.ap, in0=Sc.ap, scalar=selt.ap[:, j + 1:j + 2], in1=HT.ap,
                                                      op0=ALU.mult, op1=ALU.add), [Sc, selt, HT], [HT])
        cx.op(P, lambda e: e.tensor_copy(out=HTb.ap, in_=HT.ap), [HT], [HTb])
        self._yoffP = HTb
        for t in range(16):
            chunk(t, 128, True, False)
            prep_B(t, 128)
            upd(t, 0, 128, HT, HTb, etot)
        cx.dma(cx.sp, o_st[0], HT.ap, reads=[HT], is_output=True)
    def sb_phase(self):
        cx = self.cx
        V, P, A, PE = cx.dve, cx.pool, cx.act, cx.pe
        selt = self.selt
        self.mixF = cx.tile([128, NTT, 512], BF16, "mixD")
        self.cd_mark2 = cx.arena_off
        qT, kT, Vb, mixF = self.qT, self.kT, self.Vb, self.mixF
        NTRI = cx.tile([128, 128], BF16, "ntri")
        NONE_ = cx.tile([128, 128], BF16, "none")
        cx.op(V, lambda e: e.tensor_scalar(out=NONE_.ap, in0=self.cconst(2), scalar1=-1.0, scalar2=None, op0=ALU.mult), [self.cstt], [NONE_])
        cx.op(V, lambda e: e.tensor_scalar(out=NTRI.ap, in0=self.c2(5), scalar1=-1.0, scalar2=None, op0=ALU.mult), [self.cst2t], [NTRI])
        ntri_j = [cx.tile([128, 128], BF16, f"ntri{j}") for j in range(3)]
        none_j = [cx.tile([128, 128], BF16, f"none{j}") for j in range(3)]
        for j in range(3):
            cx.op(V, lambda e: e.tensor_scalar(out=ntri_j[j].ap, in0=self.c2(5), scalar1=selt.ap[:, 24 + j:25 + j], scalar2=-1.0,
                                               op0=ALU.mult, op1=ALU.mult), [self.cst2t, selt], [ntri_j[j]])
            cx.op(V, lambda e: e.tensor_scalar(out=none_j[j].ap, in0=self.cconst(2), scalar1=selt.ap[:, 24 + j:25 + j], scalar2=-1.0,
                                               op0=ALU.mult, op1=ALU.mult), [self.cstt, selt], [none_j[j]])
        maskS = cx.tile([128, 128], BF16, "maskS")
        cx.op(V, lambda e: e.tensor_copy(out=maskS.ap, in_=self.c2(3)), [self.cst2t], [maskS])
        bsu80 = cx.tile([128, 128], BF16, "bsu80")
        cx.op(V, lambda e: e.tensor_copy(out=bsu80.ap, in_=self.c2(4)), [self.cst2t], [bsu80])
        zero_b = cx.tile([128, 1], F32, "zero_b")
        cx.op(V, lambda e: e.memset(zero_b.ap, 0.0), [], [zero_b])
        kg = [cx.tile([128, 3, 2048], BF16, f"kg{i}") for i in range(2)]
        vg = [cx.tile([128, 3, 16, 128], BF16, f"vg{i}", hreg=(i == 1)) for i in range(2)]
        kc = cx.tile([128, 2, 2048], BF16, "kc")
        vc = cx.tile([128, 2, 16, 2, 64], BF16, "vc")
        kstg = cx.tile([128, 4, 128], F32, "kstg")
        ef = [cx.tile([128, 512], F32, f"ef{i}") for i in range(5)]
        spb = [cx.tile([128, 512], BF16, f"spb{i}") for i in range(3)]
        uf = [cx.tile([128, 512], F32, f"uf{i}", hreg=(i == 2)) for i in range(3)]
        wb = [cx.tile([128, 512], BF16, f"wb{i}", hreg=True) for i in range(3)]
        Csb = cx.tile([128, 512], F32, "Csb")
        qms = [cx.tile([128, NT], BF16, f"qm{i}") for i in range(2)]
        w_seg = [[cx.tile([128, 80], BF16, f"wseg{s_}{i}") for i in range(3)] for s_ in range(3)]
        for s_ in range(3):
            for i in range(3):
                cx.op(V, lambda e: e.memset(w_seg[s_][i].ap, 0.0), [], [w_seg[s_][i]])
        w_own = [cx.tile([128, 80], BF16, f"wown{s_}") for s_ in range(3)]
        for s_ in range(3):
            cx.op(V, lambda e: e.memset(w_own[s_].ap, 0.0), [], [w_own[s_]])
        it = [0]
        ob_i = [0]

        def load_pair(c):
            b = c % 2
            cx.dma(cx.sp, kg[b].ap, self.k2_g[c // 2].ap[0:768, :].rearrange("(j r) k -> r j k", r=256)[(c % 2) * 128:(c % 2 + 1) * 128, :, :],
                   reads=[self.k2_g[c // 2]], writes=[kg[b]])
            for i in range(4):
                for j in range(3):
                    cx.dma(cx.sp, vg[b].ap[:, j, 4 * i:4 * i + 4, :],
                           self.v2_g[i].ap[j * 512:(j + 1) * 512, :].rearrange("(t p) x -> p t x", p=128)[:, :, c * 128:(c + 1) * 128],
                           reads=[self.v2_g[i]], writes=[vg[b]])

        load_pair(0)
        for c in range(4):
            if c + 1 < 4:
                load_pair(c + 1)
            b = c % 2
            for s in range(2):
                for t in range(16):
                    if t % 4 == 0:
                        cx.dma(cx.sp, kstg.ap, self.c_sbk[s].rearrange("(t p) x -> p t x", p=128)[:, t:t + 4, c * 128:(c + 1) * 128], writes=[kstg])
                    tb = cx.banks[6 + t % 2]
                    cx.op(PE, lambda e: e.transpose(out=tb.ap[:, 0:128], in_=kstg.ap[:, t % 4, :], identity=self.cconst(0)), [kstg, self.cstt], [tb])
                    cx.op(A, lambda e: e.copy(out=kc.ap[:, s, t * 128:(t + 1) * 128], in_=tb.ap[:, 0:128]), [tb], [kc])
                for hh in range(2):
                    h = 2 * c + hh
                    cx.dma(cx.pool, vc.ap[:, s, :, hh, :], self.c_sbv[s].rearrange("(t p) x -> p t x", p=128)[:, :, h * 64:(h + 1) * 64], writes=[vc])
            for hh in range(2):
                h = 2 * c + hh
                pb = hh * 64
                qm = qms[hh]
                cx.op(P, lambda e: e.tensor_copy(out=qm.ap, in_=qT.ap[:, c, :]), [qT], [qm])
                cx.op(P, lambda e: e.memset(qm.ap[(1 - hh) * 64:(1 - hh) * 64 + 64, :], 0.0), [], [qm])

                def run_steps(steps, ob, outs):
                    N = len(steps)
                    first_pv = [True]
                    npv = sum(len(st_["pvs"]) for st_ in steps)
                    ipv = [0]

                    def A_(n):
                        st_ = steps[n]
                        i = n % 2
                        nk, n_ = st_["nk"], st_["nq"] - st_["lo"]
                        zb, e_, sp_ = cx.banks[0 + i], ef[n % 5], spb[n % 3]
                        cx.op(PE, lambda e: e.matmul(zb.ap[0:nk, 0:n_], lhsT=st_["kap"], rhs=qm.ap[:, st_["qc0"] + st_["lo"]:st_["qc0"] + st_["nq"]],
                                                     start=True, stop=True), [st_["ktl"], qm], [zb])
                        cx.op(A, lambda e: e.activation(out=e_.ap[0:nk, 0:n_], in_=zb.ap[0:nk, 0:n_], func=AF.Exp), [zb], [e_])
                        cx.op(A, lambda e: e.activation(out=sp_.ap[0:nk, 0:n_], in_=e_.ap[0:nk, 0:n_], func=AF.Ln, bias=1.0, scale=1.0), [e_], [sp_])
                        m = st_["mask"]
                        if m is not None:
                            mw = m.shape[1]
                            cx.op(P, lambda e: e.tensor_tensor(out=sp_.ap[0:nk, 0:mw], in0=sp_.ap[0:nk, 0:mw], in1=m, op=ALU.mult),
                                  [sp_, maskS, bsu80], [sp_])

                    def B_(n):
                        st_ = steps[n]
                        i = n % 2
                        nk, n_ = st_["nk"], st_["nq"] - st_["lo"]
                        cl = st_["csb_lo"]
                        lb, cb, e_, sp_, u_ = cx.banks[2 + i], cx.banks[4 + i], ef[n % 5], spb[n % 3], uf[n % 3]
                        w_ = wb[n % 3] if st_["wtile"] is None else st_["wtile"]
                        wlo = st_["wlo"]
                        if st_["clear"] is not None:
                            cx.op(V, lambda e: e.memset(Csb.ap[:, 0:st_["clear"]], 0.0), [], [Csb])
                        cx.op(PE, lambda e: e.matmul(lb.ap[0:nk, 0:n_], lhsT=st_["ntri"].ap[0:nk, 0:nk], rhs=sp_.ap[0:nk, 0:n_], start=True, stop=True),
                              [st_["ntri"], sp_], [lb])
                        cx.op(PE, lambda e: e.matmul(cb.ap[:, 0:n_], lhsT=st_["none"].ap[0:nk, :], rhs=sp_.ap[0:nk, 0:n_], start=True, stop=True),
                              [st_["none"], sp_], [cb])
                        cx.op(V, lambda e: e.tensor_tensor(out=u_.ap[0:nk, 0:n_], in0=lb.ap[0:nk, 0:n_], in1=Csb.ap[0:nk, cl:cl + n_], op=ALU.add),
                              [lb, Csb], [u_])
                        cx.op(V, lambda e: e.tensor_tensor(out=Csb.ap[:, cl:cl + n_], in0=Csb.ap[:, cl:cl + n_], in1=cb.ap[:, 0:n_], op=ALU.add),
                              [cb, Csb], [Csb])
                        cx.op(A, lambda e: e.activation(out=u_.ap[0:nk, 0:n_], in_=u_.ap[0:nk, 0:n_], func=AF.Exp, bias=st_["vbias"], scale=1.0),
                              [u_, selt, zero_b], [u_])
                        hv = (n_ // 2) if n_ >= 256 else 0
                        if hv:
                            cx.op(V, lambda e: e.tensor_tensor(out=w_.ap[0:nk, wlo:wlo + hv], in0=e_.ap[0:nk, 0:hv], in1=u_.ap[0:nk, 0:hv], op=ALU.mult),
                                  [e_, u_], [w_])
                        cx.op(P, lambda e: e.tensor_tensor(out=w_.ap[0:nk, wlo + hv:wlo + n_], in0=e_.ap[0:nk, hv:n_], in1=u_.ap[0:nk, hv:n_], op=ALU.mult),
                              [e_, u_], [w_])
                        m = st_["mask"]
                        if m is not None:
                            mw = m.shape[1]
                            cx.op(P, lambda e: e.tensor_tensor(out=w_.ap[0:nk, wlo:wlo + mw], in0=w_.ap[0:nk, wlo:wlo + mw], in1=m, op=ALU.mult),
                                  [w_, maskS, bsu80], [w_])
                        st_["w"] = w_

                    def C_(n):
                        st_ = steps[n]
                        w_ = st_["w"]
                        for (r0, r1, c0_, c1_, vap, vtl, orows, oc0, oc1) in st_["pvs"]:
                            ipv[0] += 1
                            cx.op(PE, lambda e: e.matmul(ob.ap[0:orows, oc0:oc1], lhsT=w_.ap[r0:r1, c0_:c1_], rhs=vap, start=first_pv[0],
                                                         stop=(ipv[0] == npv), skip_group_check=True), [w_, vtl], [ob])
                            first_pv[0] = False

                    for n in range(N + 4):
                        if n < N:
                            A_(n)
                        if 2 <= n <= N + 1:
                            B_(n - 2)
                        if 4 <= n:
                            C_(n - 4)
                    for (orows, oc0, oc1, ot_) in outs:
                        cx.op(A, lambda e: e.copy(out=mixF.ap[0:orows, ot_, h * 64:(h + 1) * 64], in_=ob.ap[0:orows, oc0:oc1]), [ob], [mixF])

                for B in range(4):
                    ob = cx.banks[6 + ob_i[0] % 2]
                    ob_i[0] += 1
                    steps = []

                    def mk(kap, nk, ktl, lo, ntri_t, none_t, mask, vbias, vap, vtl, r0, r1):
                        pvs = [(r0, r1, sub * 128 - lo, (sub + 1) * 128 - lo, vap, vtl, 128, sub * 64, (sub + 1) * 64) for sub in range(lo // 128, 4)]
                        return dict(kap=kap, nk=nk, ktl=ktl, qc0=B * 512, nq=512, lo=lo, ntri=ntri_t, none=none_t, mask=mask, vbias=vbias,
                                    wtile=None, wlo=0, csb_lo=lo, clear=None, pvs=pvs)

                    for kt in range(4 * B + 3, -1, -1):
                        lo = max(0, kt - 4 * B) * 128
                        diag = kt >= 4 * B
                        steps.append(mk(kT.ap[0:128, c, kt * 128:(kt + 1) * 128], 128, kT, lo, NTRI, NONE_, (maskS.ap if diag else None),
                                        zero_b.ap[:, 0:1], Vb.ap[:, kt, h, :], Vb, 0, 128))
                    for j in range(2, -1, -1):
                        for kt in range(15, -1, -1):
                            steps.append(mk(kg[b].ap[0:128, j, kt * 128:(kt + 1) * 128], 128, kg[b], 0, ntri_j[j], none_j[j], None,
                                            selt.ap[:, 4 + j:5 + j], vg[b].ap[:, j, kt, hh * 64:(hh + 1) * 64], vg[b], 0, 128))
                    steps.append(mk(kT.ap[0:128, c, 2048:2128], 80, kT, 0, NTRI, NONE_, None, zero_b.ap[0:80, 0:1],
                                    Vb.ap[64:80, 16, h, :], Vb, 64, 80))
                    steps[0]["clear"] = 512
                    run_steps(steps, ob, [(128, sub * 64, (sub + 1) * 64, 4 * B + sub) for sub in range(4)])
                ob = cx.banks[6 + ob_i[0] % 2]
                ob_i[0] += 1
                steps = []
                for s in range(3):
                    a_, n_ = (0, 32) if s == 0 else ((32, 32) if s == 1 else (64, 16))
                    steps.append(dict(kap=kT.ap[0:128, c, 2048:2128], nk=80, ktl=kT, qc0=2048 + a_, nq=n_, lo=0, ntri=NTRI, none=NONE_,
                                      mask=bsu80.ap[0:80, a_:a_ + n_], vbias=zero_b.ap[0:80, 0:1], wtile=w_own[s], wlo=a_, csb_lo=0, clear=n_,
                                      pvs=[(0, 80, 0, 80, Vb.ap[0:80, 16, h, :], Vb, 80, 0, 64)]))
                    if s < 2:
                        for kt in range(15, -1, -1):
                            steps.append(dict(kap=kc.ap[0:128, s, kt * 128:(kt + 1) * 128], nk=128, ktl=kc, qc0=2048 + a_, nq=n_, lo=0, ntri=NTRI,
                                              none=NONE_, mask=None, vbias=zero_b.ap[:, 0:1], wtile=w_seg[s][kt % 3], wlo=a_, csb_lo=0, clear=None,
                                              pvs=[(0, 128, 0, 80, vc.ap[:, s, kt, hh, :], vc, 80, 0, 64)]))
                run_steps(steps, ob, [(80, 0, 64, 16)])

    def mixer_out2(self, w_out_dram, gain_idx):
        cx = self.cx
        cx.barrier()
        cx.arena_off = self.cd_mark2
        cx.h_off = cx.h_base
        for t in range(NTT):
            r = tt_rows(t)
            cx.dma(cx.sp, self.h[t].ap[0:r, :], self.hsave[t * 128:t * 128 + r, :], writes=[self.h[t]])
        wout = cx.tile([128, 8, D], BF16, "wout")
        cx.dma(cx.pool, wout.ap, w_out_dram.rearrange("(k p) n -> p k n", p=128), writes=[wout])
        self.load_gain(self.gpost, gain_idx)
        self.junk = cx.tile([128, D], F32, "junk")
        self.tmpn = cx.tile([128, D], F32, "tmpn")
        mT = [cx.tile([128, 8, 128], BF16, f"mT{i}") for i in range(2)]
        ot = [Tl(cx.sb([128, D], F32, f"ot{i}"), f"ot{i}") for i in range(2)]
        for t in range(NTT):
            r = tt_rows(t)
            bank = cx.banks[t % 2]
            pt = bank.ap.bitcast(BF16).rearrange("p (k c) -> p k c", k=8)
            for k in range(8):
                src = self.mixC if k < 4 else self.mixF
                cx.op(cx.pe, lambda e: e.transpose(out=pt[:, k, 0:r], in_=src.ap[0:r, t, (k % 4) * 128:(k % 4 + 1) * 128],
                                                   identity=self.identb.ap[0:r, 0:r]), [src, self.identb], [bank])
            m = mT[t % 2]
            cx.op(cx.act, lambda e: e.copy(out=m.ap[:, :, 0:r], in_=pt[:, :, 0:r]), [bank], [m])
            o = ot[t % 2]
            for hf in range(2):
                ob = cx.banks[2 + (2 * t + hf) % 4]
                for k in range(8):
                    cx.op(cx.pe, lambda e: e.matmul(ob.ap[0:r, :], lhsT=m.ap[:, k, 0:r], rhs=wout.ap[:, k, hf * 512:(hf + 1) * 512],
                                                    start=(k == 0), stop=(k == 7)), [m, wout], [ob])
                cx.op(cx.act, lambda e: e.copy(out=o.ap[0:r, hf * 512:(hf + 1) * 512], in_=ob.ap[0:r, :]), [ob], [o])
            self.postnorm_add([t], [o], self.gpost, 1.0)
        cx.barrier()

    def write_y(self):
        cx = self.cx
        for t in range(NTT):
            r = tt_rows(t)
            cx.dma(cx.sp, self.y[t * 128:t * 128 + r, :], self.h[t].ap[0:r, :], reads=[self.h[t]], is_output=True)


def make_consts():
    c = np.zeros((128, 1408), np.float32)
    c[:, 0:128] = np.eye(128, dtype=np.float32)
    tri = np.triu(np.ones((128, 128), np.float32))
    c[:, 128:256] = tri
    c[:, 256:384] = 1.0
    bt = np.zeros((128, 128), np.float32)
    for (a, b) in ((0, 32), (32, 64), (64, 80)):
        bt[a:b, a:b] = tri[a:b, a:b]
    c[:, 384:512] = bt
    c[64:80, 512:640] = 1.0
    c[0:32, 640:768] = 1.0
    c[32:64, 768:896] = 1.0
    c[127, 896:1024] = 1.0
    c[31, 1024:1152] = 1.0
    c[63, 1152:1280] = 1.0
    c[79, 1280:1408] = 1.0
    return c


def state_layout(a):
    return np.ascontiguousarray(a.reshape(16, 2, 64).transpose(1, 2, 0).reshape(128, 16))


def state_unlayout(a):
    return np.ascontiguousarray(a.reshape(2, 64, 16).transpose(2, 0, 1).reshape(32, 64))


def make_consts2():
    c = np.zeros((128, 896), np.float32)
    tri = np.triu(np.ones((128, 128), np.float32))
    sl = np.tril(np.ones((128, 128), np.float32), -1)
    su = np.triu(np.ones((128, 128), np.float32), 1)
    blocks = ((0, 32), (32, 64), (64, 80))
    c[:, 0:128] = sl
    for (a, b) in blocks:
        c[a:b, 128 + a:128 + b] = sl[a:b, a:b]
        c[a:b, 256 + a:256 + b] = 1.0
        c[a:b, 512 + a:512 + b] = su[a:b, a:b]
    c[:, 384:512] = su
    c[:, 640:768] = tri.T
    return c


def cd_inputs(inputs, c):
    m = {}
    m["cd_w_in"] = inputs["cd_w_in"][0]
    m["cst2"] = make_consts2()
    cw = inputs["ssd_conv_w"][0]
    cb = inputs["ssd_conv_b"][0]
    cc = np.concatenate([cw, cb[None]], 0).reshape(5, 8, 128).transpose(2, 1, 0)
    m["convw"] = cc.reshape(128, 40)
    m["ssdp"] = np.concatenate([inputs["ssd_dt_bias"][0], inputs["ssd_a_log"][0], inputs["ssd_d"][0]])[None]
    m["ssd_norm"] = inputs["ssd_norm"]
    st = inputs["state_ssd"][0, 2 * c:2 * c + 2]
    m["ssdinit"] = st.transpose(0, 3, 1, 2).reshape(2, 128, 512)
    cv = inputs["state_conv"][0, 2 * c:2 * c + 2]
    m["convinit"] = cv.reshape(2, 3, 8, 128).transpose(3, 0, 2, 1).reshape(128, 48)
    m["cd_w_out"] = inputs["cd_w_out"][0]
    m["c_sbk"] = inputs["cache_sb_k"][0, 2 * c:2 * c + 2].reshape(2, 2048, 512)
    m["c_sbv"] = inputs["cache_sb_v"][0, 2 * c:2 * c + 2].reshape(2, 2048, 512)
    return m


def ab_inputs(inputs, c):
    q = c % 4
    m = {}
    m["ab_w_in"] = inputs["ab_w_in"][0]
    m["fox_b_f"] = inputs["fox_b_f"].reshape(1, 8)
    a_re, a_im = inputs["s5_a_re"][0], inputs["s5_a_im"][0]
    ldt = np.repeat(inputs["s5_log_dt"][0][:, None], 64, axis=1)
    m["s5p"] = np.concatenate([state_layout(a_re), state_layout(a_im), state_layout(ldt)], axis=1)
    b = np.stack([inputs["s5_b_re"][0], inputs["s5_b_im"][0]], 0)
    cc = np.stack([inputs["s5_c_re"][0], inputs["s5_c_im"][0]], 0)
    bl = np.zeros((128, 16, 2, 128), np.float32)
    cl = np.zeros((128, 16, 2, 128), np.float32)
    for st in range(16):
        for gl in range(2):
            g = 2 * st + gl
            g8 = g % 8
            for ri in range(2):
                bl[g8 * 16:(g8 + 1) * 16, st, ri, gl * 64:(gl + 1) * 64] = b[ri, g].T
                cl[gl * 64:(gl + 1) * 64, st, ri, g8 * 16:(g8 + 1) * 16] = cc[ri, g].T
    m["s5b"] = bl.reshape(128, -1)
    m["s5c"] = cl.reshape(128, -1)
    dd = inputs["s5_d"][0].reshape(4, 128)
    dl = np.zeros((128, 4, 128), np.float32)
    for k in range(4):
        dl[np.arange(128), k, np.arange(128)] = dd[k]
    m["s5d"] = dl.reshape(128, -1)
    m["s5init"] = np.concatenate([state_layout(inputs["state_s5_re"][0, 2 * c]), state_layout(inputs["state_s5_im"][0, 2 * c]),
                                  state_layout(inputs["state_s5_re"][0, 2 * c + 1]), state_layout(inputs["state_s5_im"][0, 2 * c + 1])], axis=1)
    m["s5_w_glu"] = inputs["s5_w_glu"][0]
    m["s5_b_glu"] = np.ascontiguousarray(inputs["s5_b_glu"][0].reshape(4, 128).T)
    m["ab_w_out"] = inputs["ab_w_out"][0]
    sel = np.zeros((128, 32), np.float32)
    sel[:, q] = 1.0
    for j in range(3):
        sel[:, 4 + j] = 0.0 if j < q else -30000.0
        sel[:, 24 + j] = 1.0 if j < q else 0.0
        sel[:, 27 + j] = 1.0 if j == q - 1 else 0.0
        for jp in range(4):
            sel[:, 8 + j * 4 + jp] = 1.0 if (j <= jp < q) else 0.0
    for jp in range(4):
        sel[:, 20 + jp] = 1.0 if jp < q else 0.0
    m["sel"] = sel
    m["c_foxk"] = inputs["cache_fox_k"][0, 2 * c:2 * c + 2].reshape(2, 2048, 512)
    m["c_foxv"] = inputs["cache_fox_v"][0, 2 * c:2 * c + 2].reshape(2, 2048, 512)
    m["c_foxf"] = inputs["cache_fox_logf"][0, 2 * c:2 * c + 2]
    return m


def core_inputs(inputs, c):
    seq, q = c // 4, c % 4
    xin = np.concatenate([inputs["x_prompt"][seq, q * TP:(q + 1) * TP], inputs["x_sample"][2 * c],
                          inputs["x_sample"][2 * c + 1], inputs["meta_tokens"]], axis=0)
    m = {"xin": np.ascontiguousarray(xin, np.float32), "cst": make_consts()}
    m["norms"] = np.ascontiguousarray(np.concatenate(
        [inputs[k] for k in ("norm_ffn1_pre", "norm_ffn1_post", "norm_mix_pre", "norm_mix_post",
                             "norm_ffn2_pre", "norm_ffn2_post")], axis=0), np.float32)
    for f, nm in enumerate(("ffn1", "ffn2")):
        for l in range(2):
            m[f"w_gate{f}_{l}"] = inputs[f"{nm}_w_gate"][l]
            m[f"w_up{f}_{l}"] = inputs[f"{nm}_w_up"][l]
            m[f"w_down{f}_{l}"] = inputs[f"{nm}_w_down"][l]
    m.update(ab_inputs(inputs, c))
    m.update(cd_inputs(inputs, c))
    return m


def run(inputs, stop=None):
    inputs = {k: np.asarray(v) for k, v in inputs.items()}
    b = Builder(stop)
    nc = b.build()
    in_maps = []
    for c in range(NCORES):
        m = core_inputs(inputs, c)
        in_maps.append({k: np.ascontiguousarray(m[k], np.float32) for k in b.dram_in})
    res = run_bass_kernel_spmd(nc, in_maps, core_ids=list(range(NCORES)))
    return res.results


def kernel(**inputs):
    res = run(inputs)
    f32 = np.float32
    y_p = np.zeros((2, 8192, D), f32)
    y_s = np.zeros((16, 32, D), f32)
    L0 = 8208
    s5_re_p = np.zeros((1, 2, 32, 64), f32); s5_im_p = np.zeros((1, 2, 32, 64), f32)
    fox_k_p = np.zeros((1, 2, L0, 8, 64), f32); fox_v_p = np.zeros((1, 2, L0, 8, 64), f32)
    fox_f_p = np.zeros((1, 2, L0, 8), f32)
    ssd_p = np.zeros((1, 2, 8, 64, 128), f32); conv_p = np.zeros((1, 2, 3, 1024), f32)
    sb_k_p = np.zeros((1, 2, L0, 8, 64), f32); sb_v_p = np.zeros((1, 2, L0, 8, 64), f32)
    s5_re_s = np.zeros((1, 16, 32, 64), f32); s5_im_s = np.zeros((1, 16, 32, 64), f32)
    fox_k_s = np.zeros((1, 16, 32, 8, 64), f32); fox_v_s = np.zeros((1, 16, 32, 8, 64), f32)
    fox_f_s = np.zeros((1, 16, 32, 8), f32)
    ssd_s = np.zeros((1, 16, 8, 64, 128), f32); conv_s = np.zeros((1, 16, 3, 1024), f32)
    sb_k_s = np.zeros((1, 16, 32, 8, 64), f32); sb_v_s = np.zeros((1, 16, 32, 8, 64), f32)

    def hst(a):
        return a.reshape(128, 8, 64).transpose(1, 2, 0)

    for c in range(NCORES):
        r = res[c]
        seq, q = c // 4, c % 4
        y = r["y"]
        y_p[seq, q * TP:(q + 1) * TP] = y[0:TP]
        lo, hi = 16 + q * TP, 16 + (q + 1) * TP
        for dstp, dsts, key in ((fox_k_p, fox_k_s, "o_foxk"), (fox_v_p, fox_v_s, "o_foxv"), (sb_k_p, sb_k_s, "o_sbk"), (sb_v_p, sb_v_s, "o_sbv")):
            a = r[key]
            dstp[0, seq, lo:hi] = a[0:TP].reshape(TP, 8, 64)
            if q == 0:
                dstp[0, seq, 0:16] = a[2112:2128].reshape(16, 8, 64)
            dsts[0, 2 * c] = a[2048:2080].reshape(32, 8, 64)
            dsts[0, 2 * c + 1] = a[2080:2112].reshape(32, 8, 64)
        a = r["o_foxf"]
        fox_f_p[0, seq, lo:hi] = a[0:TP]
        if q == 0:
            fox_f_p[0, seq, 0:16] = a[2112:2128]
        fox_f_s[0, 2 * c] = a[2048:2080]
        fox_f_s[0, 2 * c + 1] = a[2080:2112]
        st = r["o_s5st"]
        for i in range(2):
            y_s[2 * c + i] = y[2048 + 32 * i:2080 + 32 * i]
            s5_re_s[0, 2 * c + i] = state_unlayout(st[:, 32 + 32 * i:48 + 32 * i])
            s5_im_s[0, 2 * c + i] = state_unlayout(st[:, 48 + 32 * i:64 + 32 * i])
            ssd_s[0, 2 * c + i] = hst(r["o_ssd"][1 + i])
            conv_s[0, 2 * c + i] = r["o_conv"][1 + i]
        if q == 3:
            s5_re_p[0, seq] = state_unlayout(st[:, 0:16])
            s5_im_p[0, seq] = state_unlayout(st[:, 16:32])
            ssd_p[0, seq] = hst(r["o_ssd"][0])
            conv_p[0, seq] = r["o_conv"][0]
    return (y_p, y_s, s5_re_p, s5_im_p, fox_k_p, fox_v_p, fox_f_p, ssd_p, conv_p, sb_k_p, sb_v_p,
            s5_re_s, s5_im_s, fox_k_s, fox_v_s, fox_f_s, ssd_s, conv_s, sb_k_s, sb_v_s)
```

```python
import numpy as np
import concourse.bass as bass
import concourse.mybir as mybir
from concourse.bass_utils import run_bass_kernel_spmd

F32 = mybir.dt.float32
BF16 = mybir.dt.bfloat16
AF = mybir.ActivationFunctionType
ALU = mybir.AluOpType

NCORES = 8
D = 1024
DFF = 4096
TP = 2048
NT = 2128
NTT = 17
EPS = 1e-6
COLT = [(0, 512), (512, 512), (1024, 512), (1536, 512), (2048, 80)]


def tt_rows(t):
    return 128 if t < 16 else 80


class Tl:
    __slots__ = ("ap", "w", "r", "name", "excl")

    def __init__(self, ap, name="", excl=False):
        self.ap = ap
        self.w = None
        self.r = {}
        self.name = name
        self.excl = excl


class Eng:
    def __init__(self, nc, h, name, compute=True):
        self.h = h
        self.name = name
        self.sem = nc.alloc_semaphore("prog_" + name) if compute else None
        self.cnt = 0
        self.known = {}


class Ctx:
    def __init__(self, nc):
        self.nc = nc
        self.pe = Eng(nc, nc.tensor, "pe")
        self.act = Eng(nc, nc.scalar, "act")
        self.dve = Eng(nc, nc.vector, "dve")
        self.pool = Eng(nc, nc.gpsimd, "pool")
        self.sp = Eng(nc, nc.sync, "sp", compute=False)
        self.engs = [self.pe, self.act, self.dve, self.pool, self.sp]
        self.dsems = [[nc.alloc_semaphore(f"dma{i}"), 0] for i in range(56)]
        self.dnext = 0
        self.dnext_sw = 0
        self.ccsem = nc.alloc_semaphore("ccsem")
        self.cccnt = 0
        self.uid = 0
        self.out_events = []

    def init_arena(self, n32):
        self.arena = self.nc.alloc_sbuf_tensor("arena", [128, n32], F32).ap()
        self.arena_n = n32
        self.arena_off = 0
        self.banks = [Tl(self.nc.alloc_psum_tensor(f"bank{i}", [128, 512], F32).ap(), f"bank{i}", excl=True) for i in range(8)]

    def sb(self, shape, dt=F32, name=None, hreg=False):
        shape = list(shape)
        esz = 4 if dt == F32 else 2
        n = 1
        for d in shape[1:]:
            n *= d
        n32 = (n * esz + 31) // 32 * 8
        if hreg:
            off = self.h_off
            assert off + n32 <= self.h_end, (name, off, n32, self.h_end)
            self.h_off = off + n32
        else:
            off = self.arena_off
            assert off + n32 <= self.arena_n, (name, off, n32, self.arena_n)
            self.arena_off = off + n32
        ap = self.arena[0:shape[0], off:off + (n * esz) // 4]
        if dt != F32:
            ap = ap.bitcast(dt)
        if len(shape) == 3:
            ap = ap.rearrange("p (a b) -> p a b", a=shape[1])
        elif len(shape) == 4:
            ap = ap.rearrange("p (a b c) -> p a b c", a=shape[1], b=shape[2])
        elif len(shape) == 5:
            ap = ap.rearrange("p (a b c d) -> p a b c d", a=shape[1], b=shape[2], c=shape[3])
        return ap

    def ps(self, shape, dt=F32, name=None):
        self.uid += 1
        return self.nc.alloc_psum_tensor(f"{name or 'ps'}_{self.uid}", list(shape), dt).ap()

    def tile(self, shape, dt=F32, name=None, hreg=False):
        return Tl(self.sb(shape, dt, name, hreg), name or "")

    def ptile(self, shape, dt=F32, name=None):
        return Tl(self.ps(shape, dt, name), name or "")

    def _need(self, E, waits, ev, same_ok):
        if ev is None:
            return
        sem, val, src = ev
        if same_ok and src is E:
            return
        k = sem.num
        if E.known.get(k, 0) >= val:
            return
        if waits.get(k, (None, 0))[1] < val:
            waits[k] = (sem, val)

    def _deps(self, E, reads, writes):
        waits = {}
        for t in reads:
            self._need(E, waits, t.w, False)
            if t.excl:
                for ev in t.r.values():
                    self._need(E, waits, ev, True)
        for t in writes:
            self._need(E, waits, t.w, True)
            for ev in t.r.values():
                self._need(E, waits, ev, True)
        for k, (sem, val) in waits.items():
            E.h.wait_ge(sem, val)
            E.known[k] = val

    def _commit(self, ev, reads, writes):
        k = ev[0].num
        for t in reads:
            t.r[k] = ev
        for t in writes:
            t.w = ev
            t.r = {}

    def op(self, E, fn, reads=(), writes=()):
        self._deps(E, reads, writes)
        ins = fn(E.h)
        E.cnt += 1
        ins.then_inc(E.sem, 1)
        self._commit((E.sem, E.cnt, E), reads, writes)

    def dma(self, Q, out_ap, in_ap, reads=(), writes=(), is_output=False):
        if Q is self.pool:
            slot = self.dsems[40 + self.dnext_sw]
            self.dnext_sw = (self.dnext_sw + 1) % 16
        else:
            slot = self.dsems[self.dnext]
            self.dnext = (self.dnext + 1) % 40
        sem, tot = slot
        if tot > 0 and Q.known.get(sem.num, 0) < tot:
            Q.h.wait_ge(sem, tot)
            Q.known[sem.num] = tot
        self._deps(Q, reads, writes)
        Q.h.dma_start(out=out_ap, in_=in_ap).then_inc(sem, 16)
        slot[1] = tot + 16
        ev = (sem, tot + 16, None)
        self._commit(ev, reads, writes)
        if is_output:
            self.out_events.append(ev)
        return ev

    def allgather(self, in_t, out_t, groups):
        import os
        if os.environ.get("NOCC") == "1":
            return
        Q = self.pool
        self._deps(Q, [in_t], [out_t])
        Q.h.collective_compute("AllGather", ALU.bypass, replica_groups=groups,
                               ins=[in_t.ap.opt()], outs=[out_t.ap.opt()]).then_inc(self.ccsem)
        self.cccnt += 1
        self._commit((self.ccsem, self.cccnt, None), [in_t], [out_t])

    def barrier(self):
        for E in self.engs:
            for O in self.engs:
                if O is E or O.sem is None or O.cnt == 0:
                    continue
                if E.known.get(O.sem.num, 0) < O.cnt:
                    E.h.wait_ge(O.sem, O.cnt)
                    E.known[O.sem.num] = O.cnt
            for sem, tot in self.dsems:
                if tot > 0 and E.known.get(sem.num, 0) < tot:
                    E.h.wait_ge(sem, tot)
                    E.known[sem.num] = tot
            if self.cccnt and E.known.get(self.ccsem.num, 0) < self.cccnt:
                E.h.wait_ge(self.ccsem, self.cccnt)
                E.known[self.ccsem.num] = self.cccnt

    def finish(self):
        Q = self.sp
        for sem, val, _ in self.out_events:
            if Q.known.get(sem.num, 0) < val:
                Q.h.wait_ge(sem, val)
                Q.known[sem.num] = val


GROUPS = [(list(range(0, 8)), [(0, 512), (512, 512)], 0, 1024),
          (list(range(8, 17)), [(1024, 512), (1536, 512), (2048, 80)], 1024, 1104)]


class Builder:
    def __init__(self, stop=None):
        self.stop = stop
        self.nc = bass.Bass("TRN2", target_bir_lowering=False)
        self.cx = Ctx(self.nc)
        self.dram_in = {}
        self.dram_out = {}

    def din(self, name, shape, dt=F32):
        ap = self.nc.dram_tensor(name, list(shape), dt, kind="ExternalInput").ap()
        self.dram_in[name] = (tuple(shape), dt)
        return ap

    def dout(self, name, shape, dt=F32):
        ap = self.nc.dram_tensor(name, list(shape), dt, kind="ExternalOutput").ap()
        self.dram_out[name] = tuple(shape)
        return ap

    def build(self):
        cx = self.cx
        nc = self.nc
        self.xin = self.din("xin", [NT, D])
        self.cst = self.din("cst", [128, 1408])
        self.norms = self.din("norms", [12, D])
        self.wg = [[self.din(f"w_gate{f}_{l}", [D, DFF]) for l in range(2)] for f in range(2)]
        self.wu = [[self.din(f"w_up{f}_{l}", [D, DFF]) for l in range(2)] for f in range(2)]
        self.wd = [[self.din(f"w_down{f}_{l}", [DFF, D]) for l in range(2)] for f in range(2)]
        self.y = self.dout("y", [NT, D])
        self.c_foxk = self.din("c_foxk", [2, 2048, 512])
        self.c_foxv = self.din("c_foxv", [2, 2048, 512])
        self.c_foxf = self.din("c_foxf", [2, 2048, 8])
        self.declare_ab()
        self.declare_cd()

        cx.init_arena(53184)
        cx.h_base = cx.arena_off
        hbig = cx.sb([128, NTT, D], F32, "h")
        cx.h_end = cx.arena_off
        cx.h_off = cx.h_base
        self.h = [Tl(hbig[:, t, :], f"h{t}") for t in range(NTT)]
        self.cstt = cx.tile([128, 1408], F32, "cst")
        self.identb = cx.tile([128, 128], BF16, "identb")
        self.gpre = cx.tile([128, D], F32, "gpre")
        self.gpost = cx.tile([128, D], F32, "gpost")
        self.neghalf = cx.tile([128, 1], F32, "neghalf")
        self.nrm_ss = [cx.tile([128, 1], F32, "ss") for _ in range(2)]
        self.nrm_rstd = [cx.tile([128, 1], F32, "rstd") for _ in range(2)]
        self.mark = cx.arena_off
        cx.dma(cx.sp, self.cstt.ap, self.cst, writes=[self.cstt])
        self.ident = Tl(self.cstt.ap[:, 0:128], "ident")
        self.ident.w = self.cstt.w
        cx.op(cx.dve, lambda e: e.tensor_copy(out=self.identb.ap, in_=self.cstt.ap[:, 0:128]), [self.cstt], [self.identb])
        cx.op(cx.dve, lambda e: e.memset(self.neghalf.ap, -0.5), [], [self.neghalf])
        for t in range(NTT):
            r = tt_rows(t)
            cx.dma(cx.sp, self.h[t].ap[0:r, :], self.xin[t * 128:t * 128 + r, :], writes=[self.h[t]])

        self.alloc_ffn()
        self.steps = []
        n_ffn = {"ffn1_l0": 1, "ab0": 1, "ab1a": 1, "ab1b": 1, "ab1c": 1, "ab1": 1, "s5": 1, "fox": 1, "mix_l0": 1, "ffn2_l0": 2, "ffn1_l1": 3, "mix_l1": 3, None: 4}[self.stop]
        for fi in range(n_ffn):
            for g in range(2):
                for fb in range(8):
                    self.steps.append((fi % 2, fi // 2, g, fb))
        self.step_i = 0
        self.load_gu(0)
        self.load_d(0)
        si = 0
        while si < len(self.steps):
            which, layer, g, fb = self.steps[si]
            if which == 1:
                cx.barrier()
                self.alloc_ffn()
                self.load_gu(si)
                self.load_d(si)
            self.ffn(which, layer)
            si += 16
            if which == 0 and layer == 1:
                if self.stop == "ffn1_l1":
                    break
                self.mixer_cd()
                if self.stop == "mix_l1":
                    break
            if which == 0 and layer == 0:
                if self.stop == "ffn1_l0":
                    break
                self.mixer_ab()
                if self.stop in ("mix_l0", "ab0", "ab1a", "ab1b", "ab1c", "ab1", "s5", "fox"):
                    break
        self.write_y()
        cx.finish()
        return nc

    def load_gain(self, tl, idx):
        self.cx.dma(self.cx.sp, tl.ap, self.norms[idx:idx + 1, :].partition_broadcast(128), writes=[tl])

    def rms_stats(self, src, r, par):
        cx = self.cx
        ss, rstd = self.nrm_ss[par], self.nrm_rstd[par]
        cx.op(cx.dve, lambda e: e.scalar_tensor_tensor(out=self.junk.ap[0:r, :], in0=src.ap[0:r, :], scalar=1.0,
                                                       in1=src.ap[0:r, :], op0=ALU.mult, op1=ALU.mult,
                                                       accum_out=ss.ap[0:r, :]), [src], [self.junk, ss])
        cx.op(cx.dve, lambda e: e.tensor_scalar(out=ss.ap[0:r, :], in0=ss.ap[0:r, :], scalar1=1.0 / D,
                                                scalar2=EPS, op0=ALU.mult, op1=ALU.add), [ss], [ss])
        cx.op(cx.pool, lambda e: e.tensor_tensor(out=rstd.ap[0:r, :], in0=ss.ap[0:r, :],
                                                 in1=self.neghalf.ap[0:r, :], op=ALU.pow),
              [ss, self.neghalf], [rstd])
        return rstd

    def prenorm_T(self, tiles, gain, xnT, col_base):
        cx = self.cx
        for t in tiles:
            r = tt_rows(t)
            ht = self.h[t]
            xb = self.nrm_xb[t % 2]
            rstd = self.rms_stats(ht, r, t % 2)
            cx.op(cx.dve, lambda e: e.scalar_tensor_tensor(out=xb.ap[0:r, :], in0=ht.ap[0:r, :],
                                                           scalar=rstd.ap[0:r, :], in1=gain.ap[0:r, :],
                                                           op0=ALU.mult, op1=ALU.mult),
                  [ht, rstd, gain], [xb])
            bank = cx.banks[t % 2]
            pt = bank.ap.bitcast(BF16).rearrange("p (k c) -> p k c", k=8)
            for k in range(8):
                cx.op(cx.pe, lambda e: e.transpose(out=pt[:, k, 0:r], in_=xb.ap[0:r, k * 128:(k + 1) * 128],
                                                   identity=self.identb.ap[0:r, 0:r]),
                      [xb, self.identb], [bank])
            c0 = t * 128 - col_base
            cx.op(cx.act, lambda e: e.copy(out=xnT.ap[:, :, c0:c0 + r], in_=pt[:, :, 0:r]), [bank], [xnT])

    def alloc_ffn(self):
        cx = self.cx
        cx.arena_off = self.mark
        self.junk = cx.tile([128, D], F32, "junk")
        self.tmpn = cx.tile([128, D], F32, "tmpn")
        self.nrm_xb = [cx.tile([128, D], BF16, "xb") for _ in range(2)]
        self.wgb = [cx.tile([128, 8, 512], BF16, "wgb") for _ in range(2)]
        self.wub = [cx.tile([128, 8, 512], BF16, "wub") for _ in range(2)]
        self.wdb = [cx.tile([128, 4, D], BF16, "wdb") for _ in range(1)]
        self.xnT = cx.tile([128, 8, 1104], BF16, "xnT")
        self.hid = cx.tile([128, 4, 1104], BF16, "hid")
        accbig = cx.sb([128, 9, D], F32, "acc")
        self.acc = [Tl(accbig[:, t, :], f"acc{t}") for t in range(9)]
        self.sil = [cx.tile([128, 512], F32, "sil") for _ in range(2)]

    def load_gu(self, si):
        if si >= len(self.steps):
            return
        cx = self.cx
        which, layer, g, fb = self.steps[si]
        b = si % 2
        wgv = self.wg[which][layer].rearrange("(k p) f -> p k f", p=128)
        wuv = self.wu[which][layer].rearrange("(k p) f -> p k f", p=128)
        cx.dma(cx.pool, self.wgb[b].ap, wgv[:, :, fb * 512:(fb + 1) * 512], writes=[self.wgb[b]])
        cx.dma(cx.pool, self.wub[b].ap, wuv[:, :, fb * 512:(fb + 1) * 512], writes=[self.wub[b]])

    def load_d(self, si):
        if si >= len(self.steps):
            return
        cx = self.cx
        which, layer, g, fb = self.steps[si]
        wdv = self.wd[which][layer].rearrange("(c p) d -> p c d", p=128)
        cx.dma(cx.pool, self.wdb[0].ap, wdv[:, fb * 4:(fb + 1) * 4, :], writes=[self.wdb[0]])

    def ffn(self, which, layer):
        cx = self.cx
        g_pre = (0 if which == 0 else 4) * 2 + layer
        g_post = (1 if which == 0 else 5) * 2 + layer
        if not getattr(self, "pre_done", False):
            self.load_gain(self.gpre, g_pre)
        self.load_gain(self.gpost, g_post)
        cnt = 0
        for g, (tiles, colts, cbase, ncols) in enumerate(GROUPS):
            if g == 0 and not getattr(self, "pre_done", False):
                self.prenorm_T(tiles, self.gpre, self.xnT, cbase)
            self.pre_done = False
            for fb in range(8):
                si = self.step_i
                b = si % 2
                nxt_ok = si + 1 < len(self.steps) and not (self.steps[si][0] == 0 and self.steps[si + 1][0] == 1)
                if nxt_ok:
                    self.load_gu(si + 1)
                wgb, wub, wdb, hid = self.wgb[b], self.wub[b], self.wdb[0], self.hid
                for (c0g, cw) in colts:
                    c0 = c0g - cbase
                    for c in range(4):
                        gp, up, sl = cx.banks[2 + cnt % 2], cx.banks[4 + cnt % 2], self.sil[cnt % 2]
                        cnt += 1
                        for k in range(8):
                            cx.op(cx.pe, lambda e: e.matmul(gp.ap[:, 0:cw], lhsT=wgb.ap[:, k, c * 128:(c + 1) * 128],
                                                            rhs=self.xnT.ap[:, k, c0:c0 + cw], start=(k == 0), stop=(k == 7)),
                                  [wgb, self.xnT], [gp])
                        for k in range(8):
                            cx.op(cx.pe, lambda e: e.matmul(up.ap[:, 0:cw], lhsT=wub.ap[:, k, c * 128:(c + 1) * 128],
                                                            rhs=self.xnT.ap[:, k, c0:c0 + cw], start=(k == 0), stop=(k == 7)),
                                  [wub, self.xnT], [up])
                        cx.op(cx.act, lambda e: e.activation(out=sl.ap[:, 0:cw], in_=gp.ap[:, 0:cw], func=AF.Silu), [gp], [sl])
                        cx.op(cx.dve, lambda e: e.tensor_tensor(out=hid.ap[:, c, c0:c0 + cw], in0=sl.ap[:, 0:cw],
                                                                in1=up.ap[:, 0:cw], op=ALU.mult), [sl, up], [hid])
                for ti, t in enumerate(tiles):
                    r = tt_rows(t)
                    a = self.acc[ti]
                    for hf in range(2):
                        op_ = cx.banks[6 + (ti * 2 + hf) % 2]
                        for c in range(4):
                            cx.op(cx.pe, lambda e: e.matmul(op_.ap[0:r, :], lhsT=hid.ap[:, c, ti * 128:ti * 128 + r],
                                                            rhs=wdb.ap[:, c, hf * 512:(hf + 1) * 512],
                                                            start=(c == 0), stop=(c == 3)),
                                  [hid, wdb], [op_])
                        if fb == 0:
                            cx.op(cx.dve, lambda e: e.tensor_copy(out=a.ap[0:r, hf * 512:(hf + 1) * 512], in_=op_.ap[0:r, :]),
                                  [op_], [a])
                        else:
                            cx.op(cx.dve, lambda e: e.tensor_tensor(out=a.ap[0:r, hf * 512:(hf + 1) * 512],
                                                                    in0=a.ap[0:r, hf * 512:(hf + 1) * 512],
                                                                    in1=op_.ap[0:r, :], op=ALU.add), [op_, a], [a])
                if nxt_ok:
                    self.load_d(si + 1)
                self.step_i += 1
            if g == 0:
                t2, _, cb2, _ = GROUPS[1]
                self.prenorm_T(t2, self.gpre, self.xnT, cb2)
            elif which == 1 and self.step_i < len(self.steps):
                nw, nl = self.steps[self.step_i][0], self.steps[self.step_i][1]
                self.load_gain(self.gpre, (0 if nw == 0 else 4) * 2 + nl)
                t2, _, cb2, _ = GROUPS[0]
                self.prenorm_T(t2, self.gpre, self.xnT, cb2)
                self.pre_done = True
            self.postnorm_add(tiles, self.acc, self.gpost, 0.5)

    def postnorm_add(self, tiles, src, gain, coef):
        cx = self.cx
        for ti, t in enumerate(tiles):
            r = tt_rows(t)
            a = src[ti]
            ht = self.h[t]
            rstd = self.rms_stats(a, r, t % 2)
            cx.op(cx.dve, lambda e: e.scalar_tensor_tensor(out=self.tmpn.ap[0:r, :], in0=a.ap[0:r, :],
                                                           scalar=rstd.ap[0:r, :], in1=gain.ap[0:r, :],
                                                           op0=ALU.mult, op1=ALU.mult),
                  [a, rstd, gain], [self.tmpn])
            cx.op(cx.dve, lambda e: e.scalar_tensor_tensor(out=ht.ap[0:r, :], in0=self.tmpn.ap[0:r, :], scalar=coef,
                                                           in1=ht.ap[0:r, :], op0=ALU.mult, op1=ALU.add),
                  [self.tmpn, ht], [ht])

    def declare_ab(self):
        self.w_ab_in = self.din("ab_w_in", [D, 2056])
        self.fox_bf = self.din("fox_b_f", [1, 8])
        self.s5p = self.din("s5p", [128, 48])
        self.s5b = self.din("s5b", [128, 16 * 2 * 128])
        self.s5c = self.din("s5c", [128, 16 * 2 * 128])
        self.s5d = self.din("s5d", [128, 4 * 128])
        self.s5init = self.din("s5init", [128, 64])
        self.w_glu = self.din("s5_w_glu", [512, 512])
        self.b_glu = self.din("s5_b_glu", [128, 4])
        self.w_ab_out = self.din("ab_w_out", [D, D])
        self.sel = self.din("sel", [128, 32])
        self.o_s5st = self.dout("o_s5st", [128, 96])
        self.o_foxk = self.dout("o_foxk", [NT, 512])
        self.o_foxv = self.dout("o_foxv", [NT, 512])
        self.o_foxf = self.dout("o_foxf", [NT, 8])
        self.hsave = self.nc.dram_tensor("hsave", [NT, D], F32).ap()
        self.s5F_send = Tl(self.nc.dram_tensor("s5F_send", [128, 32], F32).ap(), "s5F_send")
        self.s5F_g = Tl(self.nc.dram_tensor("s5F_g", [512, 32], F32).ap(), "s5F_g")
        self.kT_send = [Tl(self.nc.dram_tensor(f"kT_send{i}", [256, 2048], BF16).ap(), "kT_send") for i in range(2)]
        self.kT_g = [Tl(self.nc.dram_tensor(f"kT_g{i}", [1024, 2048], BF16).ap(), "kT_g") for i in range(2)]
        self.V_send = [Tl(self.nc.dram_tensor(f"V_send{i}", [512, 528], BF16).ap(), "V_send") for i in range(4)]
        self.V_g = [Tl(self.nc.dram_tensor(f"V_g{i}", [2048, 528], BF16).ap(), "V_g") for i in range(4)]
        self.fk_send = Tl(self.nc.dram_tensor("fk_send", [2048, 8], F32).ap(), "fk_send")
        self.fk_g = Tl(self.nc.dram_tensor("fk_g", [8192, 8], F32).ap(), "fk_g")
        self.tot_send = Tl(self.nc.dram_tensor("tot_send", [128, 8], F32).ap(), "tot_send")
        self.tot_g = Tl(self.nc.dram_tensor("tot_g", [512, 8], F32).ap(), "tot_g")

    def cconst(self, i):
        return self.cstt.ap[:, i * 128:(i + 1) * 128]

    def spill_h(self):
        cx = self.cx
        for t in range(NTT):
            r = tt_rows(t)
            cx.dma(cx.sp, self.hsave[t * 128:t * 128 + r, :], self.h[t].ap[0:r, :], reads=[self.h[t]])

    def mixer_ab(self):
        cx = self.cx
        GR = [[0, 1, 2, 3], [4, 5, 6, 7]]
        cx.barrier()
        cx.arena_off = self.mark
        cx.h_off = cx.h_base
        self.mixS = cx.tile([128, 4, NT], BF16, "mixS")
        self.mixF = cx.tile([128, NTT, 512], BF16, "mixF")
        self.Vb = cx.tile([128, NTT, 8, 66], BF16, "Vb")
        self.logf = cx.tile([128, NTT, 8], F32, "logf")
        self.cum = cx.tile([128, NTT, 8], F32, "cum")
        self.totP = cx.tile([128, 8], F32, "totP")
        self.selt = cx.tile([128, 32], F32, "selt")
        cx.dma(cx.sp, self.selt.ap, self.sel, writes=[self.selt])
        ab_mark = cx.arena_off
        self.ab_mark = ab_mark
        self.load_gain(self.gpre, 2 * 2 + 0)
        xnT = cx.tile([128, 8, NT], BF16, "xnTf")
        mark2 = cx.arena_off
        self.junk = cx.tile([128, D], F32, "junk")
        self.nrm_xb = [cx.tile([128, D], BF16, "xb") for _ in range(2)]
        self.prenorm_T(list(range(NTT)), self.gpre, xnT, 0)
        self.spill_h()
        cx.barrier()
        if self.stop == "ab0":
            return
        cx.arena_off = mark2
        win = cx.tile([128, 8, 2056], BF16, "win")
        cx.dma(cx.pool, win.ap, self.w_ab_in.rearrange("(k p) f -> p k f", p=128), writes=[win])
        self.uT = cx.tile([128, 4, NT], BF16, "uT", hreg=True)
        self.qT = cx.tile([128, 4, NT], BF16, "qT", hreg=True)
        self.kT = cx.tile([128, 4, NT], BF16, "kT", hreg=True)
        stg = [cx.tile([128, 512], F32, "stg", hreg=True) for _ in range(2)]
        bfb = cx.tile([128, 8], F32, "bfb", hreg=True)
        cx.dma(cx.sp, bfb.ap, self.fox_bf.partition_broadcast(128), writes=[bfb])
        dests = [self.uT] * 4 + [self.qT] * 4 + [self.kT] * 4
        cnt = 0
        for j in range(12):
            dst = dests[j]
            for (c0, cw) in COLT:
                bank = cx.banks[2 + cnt % 2]
                cnt += 1
                for k in range(8):
                    cx.op(cx.pe, lambda e: e.matmul(bank.ap[:, 0:cw], lhsT=win.ap[:, k, j * 128:(j + 1) * 128],
                                                    rhs=xnT.ap[:, k, c0:c0 + cw], start=(k == 0), stop=(k == 7)),
                          [win, xnT], [bank])
                sc = 0.125 if 4 <= j < 8 else 1.0
                cx.op(cx.act, lambda e: e.activation(out=dst.ap[:, j % 4, c0:c0 + cw], in_=bank.ap[:, 0:cw],
                                                     func=AF.Copy, scale=sc), [bank], [dst])
        if self.stop == "ab1a":
            return
        cx.op(cx.dve, lambda e: e.memset(self.Vb.ap[:, :, :, 64:66], 1.0), [], [self.Vb])
        for t in range(NTT):
            r = tt_rows(t)
            c0 = t * 128
            import os
            dbg = os.environ.get("DBG", "")
            for which, (w0, oap) in enumerate(((1024, self.o_foxk), (1536, self.o_foxv))):
                if dbg and str(which) not in dbg:
                    continue
                bank = cx.banks[4 + which]
                st = stg[which]
                for k in range(8):
                    cx.op(cx.pe, lambda e: e.matmul(bank.ap[0:r, :], lhsT=xnT.ap[:, k, c0:c0 + r],
                                                    rhs=win.ap[:, k, w0:w0 + 512], start=(k == 0), stop=(k == 7)),
                          [win, xnT], [bank])
                cx.op(cx.act, lambda e: e.copy(out=st.ap[0:r, :], in_=bank.ap[0:r, :]), [bank], [st])
                cx.dma(cx.sp, oap[c0:c0 + r, :], st.ap[0:r, :], reads=[st], is_output=True)
                if which == 1:
                    cx.op(cx.dve, lambda e: e.tensor_copy(out=self.Vb.ap[0:r, t, :, 0:64],
                                                          in_=bank.ap[0:r, :].rearrange("p (h d) -> p h d", h=8)),
                          [bank], [self.Vb])
            if dbg and "2" not in dbg:
                continue
            bank = cx.banks[6]
            for k in range(8):
                cx.op(cx.pe, lambda e: e.matmul(bank.ap[0:r, 0:8], lhsT=xnT.ap[:, k, c0:c0 + r],
                                                rhs=win.ap[:, k, 2048:2056], start=(k == 0), stop=(k == 7)),
                      [win, xnT], [bank])
            lf = self.logf
            cx.op(cx.dve, lambda e: e.tensor_tensor(out=lf.ap[0:r, t, :], in0=bank.ap[0:r, 0:8], in1=bfb.ap[0:r, :],
                                                    op=ALU.add), [bank, bfb], [lf])
        if self.stop == "ab1b":
            return
        lf = self.logf
        cx.op(cx.act, lambda e: e.activation(out=lf.ap, in_=lf.ap, func=AF.Exp, scale=-1.0), [lf], [lf])
        cx.op(cx.act, lambda e: e.activation(out=lf.ap, in_=lf.ap, func=AF.Ln, bias=1.0, scale=1.0), [lf], [lf])
        cx.op(cx.dve, lambda e: e.tensor_scalar(out=lf.ap, in0=lf.ap, scalar1=-1.0, scalar2=None, op0=ALU.mult), [lf], [lf])
        cx.dma(cx.sp, self.o_foxf[0:2048, :].rearrange("(t p) h -> p t h", p=128), lf.ap[:, 0:16, :], reads=[lf], is_output=True)
        cx.dma(cx.sp, self.o_foxf[2048:2128, :], lf.ap[0:80, 16, :], reads=[lf], is_output=True)
        cumt = self.cum
        for t in range(16):
            bank = cx.banks[6 + t % 2]
            for tp in range(t):
                cx.op(cx.pe, lambda e: e.matmul(bank.ap[:, 0:8], lhsT=self.cconst(2), rhs=lf.ap[:, tp, :],
                                                start=(tp == 0), stop=False), [self.cstt, lf], [bank])
            cx.op(cx.pe, lambda e: e.matmul(bank.ap[:, 0:8], lhsT=self.cconst(1), rhs=lf.ap[:, t, :],
                                            start=(t == 0), stop=True), [self.cstt, lf], [bank])
            cx.op(cx.dve, lambda e: e.tensor_copy(out=cumt.ap[:, t, :], in_=bank.ap[:, 0:8]), [bank], [cumt])
        bank = cx.banks[6]
        cx.op(cx.pe, lambda e: e.matmul(bank.ap[0:80, 0:8], lhsT=self.cstt.ap[0:80, 384:464], rhs=lf.ap[0:80, 16, :],
                                        start=True, stop=True), [self.cstt, lf], [bank])
        cx.op(cx.dve, lambda e: e.tensor_copy(out=cumt.ap[0:80, 16, :], in_=bank.ap[0:80, 0:8]), [bank], [cumt])
        bank = cx.banks[7]
        for t in range(16):
            cx.op(cx.pe, lambda e: e.matmul(bank.ap[:, 0:8], lhsT=self.cconst(2), rhs=lf.ap[:, t, :],
                                            start=(t == 0), stop=(t == 15)), [self.cstt, lf], [bank])
        cx.op(cx.dve, lambda e: e.tensor_copy(out=self.totP.ap, in_=bank.ap[:, 0:8]), [bank], [self.totP])
        if self.stop == "ab1c":
            return
        for c in range(4):
            cx.dma(cx.sp, self.kT_send[c // 2].ap[(c % 2) * 128:(c % 2 + 1) * 128, :], self.kT.ap[:, c, 0:2048], reads=[self.kT],
                   writes=[self.kT_send[c // 2]])
        for i in range(4):
            cx.dma(cx.sp, self.V_send[i].ap.rearrange("(t p) c -> p t c", p=128),
                   self.Vb.ap[:, 4 * i:4 * i + 4, :, :].rearrange("p t h d -> p t (h d)"), reads=[self.Vb], writes=[self.V_send[i]])
        cx.dma(cx.sp, self.fk_send.ap.rearrange("(t p) h -> p t h", p=128), cumt.ap[:, 0:16, :], reads=[cumt],
               writes=[self.fk_send])
        cx.dma(cx.sp, self.tot_send.ap, self.totP.ap, reads=[self.totP], writes=[self.tot_send])
        for i in range(2):
            cx.allgather(self.kT_send[i], self.kT_g[i], GR)
        for i in range(4):
            cx.allgather(self.V_send[i], self.V_g[i], GR)
        cx.allgather(self.fk_send, self.fk_g, GR)
        cx.allgather(self.tot_send, self.tot_g, GR)
        cx.barrier()
        if self.stop == "ab1":
            return
        cx.arena_off = ab_mark
        self.h_mark_ab = cx.h_off = self.h_mark_after_qk()
        self.s5_phase()
        cx.barrier()
        if self.stop == "s5":
            return
        cx.arena_off = ab_mark
        cx.h_off = self.h_mark_ab
        self.fox_phase()
        if self.stop == "fox":
            return
        self.mixer_out(self.w_ab_out, 3 * 2 + 0)

    def h_mark_after_qk(self):
        return self.cx.h_base + 3 * ((4 * NT * 2 + 31) // 32 * 8)
    def s5_phase(self):
        cx = self.cx
        V, P = cx.dve, cx.pool
        TC = 128

        def t16(name):
            return cx.tile([128, 16], F32, name)

        prm = cx.tile([128, 48], F32, "s5prm")
        cx.dma(cx.sp, prm.ap, self.s5p, writes=[prm])
        a_re, a_im, ldt = prm.ap[:, 0:16], prm.ap[:, 16:32], prm.ap[:, 32:48]
        dt, mag, th, c_, s_, t1, t2 = (t16(n) for n in ("dt", "mag", "th", "c_", "s_", "t1", "t2"))
        halfpi = cx.tile([128, 1], F32, "halfpi")
        cx.op(V, lambda e: e.memset(halfpi.ap, float(np.pi / 2)), [], [halfpi])
        cx.op(cx.act, lambda e: e.activation(out=dt.ap, in_=ldt, func=AF.Exp), [prm], [dt])
        cx.op(V, lambda e: e.tensor_tensor(out=t1.ap, in0=dt.ap, in1=a_re, op=ALU.mult), [dt, prm], [t1])
        cx.op(cx.act, lambda e: e.activation(out=mag.ap, in_=t1.ap, func=AF.Exp), [t1], [mag])
        cx.op(V, lambda e: e.tensor_tensor(out=th.ap, in0=dt.ap, in1=a_im, op=ALU.mult), [dt, prm], [th])
        cx.op(cx.act, lambda e: e.activation(out=s_.ap, in_=th.ap, func=AF.Sin, scale=1.0 / 16), [th], [s_])
        cx.op(cx.act, lambda e: e.activation(out=c_.ap, in_=th.ap, func=AF.Sin, scale=1.0 / 16, bias=halfpi.ap),
              [th, halfpi], [c_])

        def csquare(cr, ci, outr, outi):
            cx.op(V, lambda e: e.tensor_tensor(out=t1.ap, in0=cr.ap, in1=cr.ap, op=ALU.mult), [cr], [t1])
            cx.op(V, lambda e: e.tensor_tensor(out=t2.ap, in0=ci.ap, in1=ci.ap, op=ALU.mult), [ci], [t2])
            cx.op(V, lambda e: e.tensor_tensor(out=outi.ap, in0=cr.ap, in1=ci.ap, op=ALU.mult), [cr, ci], [outi])
            cx.op(V, lambda e: e.tensor_scalar(out=outi.ap, in0=outi.ap, scalar1=2.0, scalar2=None, op0=ALU.mult), [outi], [outi])
            cx.op(V, lambda e: e.tensor_tensor(out=outr.ap, in0=t1.ap, in1=t2.ap, op=ALU.subtract), [t1, t2], [outr])

        for _ in range(4):
            csquare(c_, s_, c_, s_)
        Ur = [t16(f"Ur{k}") for k in range(7)]
        Ui = [t16(f"Ui{k}") for k in range(7)]
        cx.op(V, lambda e: e.tensor_copy(out=Ur[0].ap, in_=c_.ap), [c_], [Ur[0]])
        cx.op(V, lambda e: e.tensor_copy(out=Ui[0].ap, in_=s_.ap), [s_], [Ui[0]])
        for k in range(1, 7):
            csquare(Ur[k - 1], Ui[k - 1], Ur[k], Ui[k])
        ab_re, ab_im, cf_re, cf_im, den, nr = (t16(n) for n in ("ab_re", "ab_im", "cf_re", "cf_im", "den", "nr"))
        cx.op(V, lambda e: e.tensor_tensor(out=ab_re.ap, in0=mag.ap, in1=c_.ap, op=ALU.mult), [mag, c_], [ab_re])
        cx.op(V, lambda e: e.tensor_tensor(out=ab_im.ap, in0=mag.ap, in1=s_.ap, op=ALU.mult), [mag, s_], [ab_im])
        cx.op(V, lambda e: e.tensor_tensor(out=t1.ap, in0=a_re, in1=a_re, op=ALU.mult), [prm], [t1])
        cx.op(V, lambda e: e.tensor_tensor(out=t2.ap, in0=a_im, in1=a_im, op=ALU.mult), [prm], [t2])
        cx.op(V, lambda e: e.tensor_tensor(out=den.ap, in0=t1.ap, in1=t2.ap, op=ALU.add), [t1, t2], [den])
        cx.op(V, lambda e: e.reciprocal(out=den.ap, in_=den.ap), [den], [den])
        cx.op(V, lambda e: e.tensor_scalar(out=nr.ap, in0=ab_re.ap, scalar1=-1.0, scalar2=None, op0=ALU.add), [ab_re], [nr])
        cx.op(V, lambda e: e.tensor_tensor(out=t1.ap, in0=nr.ap, in1=a_re, op=ALU.mult), [nr, prm], [t1])
        cx.op(V, lambda e: e.tensor_tensor(out=t2.ap, in0=ab_im.ap, in1=a_im, op=ALU.mult), [ab_im, prm], [t2])
        cx.op(V, lambda e: e.tensor_tensor(out=cf_re.ap, in0=t1.ap, in1=t2.ap, op=ALU.add), [t1, t2], [cf_re])
        cx.op(V, lambda e: e.tensor_tensor(out=cf_re.ap, in0=cf_re.ap, in1=den.ap, op=ALU.mult), [cf_re, den], [cf_re])
        cx.op(V, lambda e: e.tensor_tensor(out=t1.ap, in0=ab_im.ap, in1=a_re, op=ALU.mult), [ab_im, prm], [t1])
        cx.op(V, lambda e: e.tensor_tensor(out=t2.ap, in0=nr.ap, in1=a_im, op=ALU.mult), [nr, prm], [t2])
        cx.op(V, lambda e: e.tensor_tensor(out=cf_im.ap, in0=t1.ap, in1=t2.ap, op=ALU.subtract), [t1, t2], [cf_im])
        cx.op(V, lambda e: e.tensor_tensor(out=cf_im.ap, in0=cf_im.ap, in1=den.ap, op=ALU.mult), [cf_im, den], [cf_im])
        p_re, p_im = t16("p_re"), t16("p_im")
        csquare(ab_re, ab_im, p_re, p_im)
        for _ in range(10):
            csquare(p_re, p_im, p_re, p_im)
        E_re = cx.tile([128, 16, TC], F32, "E_re")
        E_im = cx.tile([128, 16, TC], F32, "E_im")
        R_re = cx.tile([128, 16, TC], F32, "R_re")
        R_im = cx.tile([128, 16, TC], F32, "R_im")
        rtab = cx.tile([128, 16, TC], F32, "rtab")
        tmpa = cx.tile([128, TC], F32, "tmpa")
        tmpb = cx.tile([128, TC], F32, "tmpb")
        for st in range(16):
            cx.op(V, lambda e: e.tensor_copy(out=E_re.ap[:, st, 0:1], in_=Ur[0].ap[:, st:st + 1]), [Ur[0]], [E_re])
            cx.op(V, lambda e: e.tensor_copy(out=E_im.ap[:, st, 0:1], in_=Ui[0].ap[:, st:st + 1]), [Ui[0]], [E_im])
            for k in range(7):
                ln = 1 << k
                ur, ui = Ur[k].ap[:, st:st + 1], Ui[k].ap[:, st:st + 1]
                cx.op(V, lambda e: e.tensor_scalar(out=tmpa.ap[:, 0:ln], in0=E_im.ap[:, st, 0:ln], scalar1=ui, scalar2=None,
                                                   op0=ALU.mult), [E_im, Ui[k]], [tmpa])
                cx.op(V, lambda e: e.tensor_scalar(out=tmpb.ap[:, 0:ln], in0=E_im.ap[:, st, 0:ln], scalar1=ur, scalar2=None,
                                                   op0=ALU.mult), [E_im, Ur[k]], [tmpb])
                cx.op(V, lambda e: e.scalar_tensor_tensor(out=E_im.ap[:, st, ln:2 * ln], in0=E_re.ap[:, st, 0:ln], scalar=ui,
                                                          in1=tmpb.ap[:, 0:ln], op0=ALU.mult, op1=ALU.add),
                      [E_re, Ui[k], tmpb], [E_im])
                cx.op(V, lambda e: e.scalar_tensor_tensor(out=E_re.ap[:, st, ln:2 * ln], in0=E_re.ap[:, st, 0:ln], scalar=ur,
                                                          in1=tmpa.ap[:, 0:ln], op0=ALU.mult, op1=ALU.subtract),
                      [E_re, Ur[k], tmpa], [E_re])
            cr, ci = cf_re.ap[:, st:st + 1], cf_im.ap[:, st:st + 1]
            cx.op(V, lambda e: e.tensor_scalar(out=tmpa.ap, in0=E_im.ap[:, st, :], scalar1=ci, scalar2=None, op0=ALU.mult),
                  [E_im, cf_im], [tmpa])
            cx.op(V, lambda e: e.scalar_tensor_tensor(out=R_re.ap[:, st, :], in0=E_re.ap[:, st, :], scalar=cr, in1=tmpa.ap,
                                                      op0=ALU.mult, op1=ALU.add), [E_re, cf_re, tmpa], [R_re])
            cx.op(V, lambda e: e.tensor_scalar(out=tmpb.ap, in0=E_im.ap[:, st, :], scalar1=cr, scalar2=None, op0=ALU.mult),
                  [E_im, cf_re], [tmpb])
            cx.op(V, lambda e: e.scalar_tensor_tensor(out=R_im.ap[:, st, :], in0=E_re.ap[:, st, :], scalar=ci, in1=tmpb.ap,
                                                      op0=ALU.mult, op1=ALU.subtract), [E_re, cf_im, tmpb], [R_im])
            cx.op(V, lambda e: e.tensor_scalar(out=rtab.ap[:, st, :], in0=self.cconst(2), scalar1=mag.ap[:, st:st + 1],
                                               scalar2=None, op0=ALU.mult), [self.cstt, mag], [rtab])
        bl = cx.tile([128, 16, 2, 128], BF16, "bl")
        cl = cx.tile([128, 16, 2, 128], BF16, "cl")
        dl = cx.tile([128, 4, 128], BF16, "dl")
        wgl = cx.tile([128, 4, 512], BF16, "wgl")
        bgl = cx.tile([128, 4], F32, "bgl")
        cx.dma(cx.pool, bl.ap, self.s5b.rearrange("p (s r c) -> p s r c", s=16, r=2), writes=[bl])
        cx.dma(cx.pool, cl.ap, self.s5c.rearrange("p (s r c) -> p s r c", s=16, r=2), writes=[cl])
        cx.dma(cx.pool, dl.ap, self.s5d.rearrange("p (s c) -> p s c", s=4), writes=[dl])
        cx.dma(cx.pool, wgl.ap, self.w_glu.rearrange("(k p) f -> p k f", p=128), writes=[wgl])
        cx.dma(cx.sp, bgl.ap, self.b_glu, writes=[bgl])
        cx.op(V, lambda e: e.tensor_scalar(out=cl.ap[:, :, 1, :], in0=cl.ap[:, :, 1, :], scalar1=-1.0, scalar2=None,
                                           op0=ALU.mult), [cl], [cl])
        wks = [[cx.tile([128, TC], F32, f"wk{i}", hreg=True) for i in range(8)] for _ in range(3)]
        hre = [cx.tile([128, TC], BF16, f"hre{i}") for i in range(4)]
        him = [cx.tile([128, TC], BF16, f"him{i}") for i in range(4)]
        gf = [cx.tile([128, TC], F32, f"gf{i}") for i in range(4)]
        gb = [cx.tile([128, TC], BF16, f"gb{i}") for i in range(4)]
        sg = cx.tile([128, TC], F32, "sg")
        car_re, car_im = t16("car_re"), t16("car_im")
        c1 = [cx.tile([128, 1], F32, f"c1_{i}") for i in range(2)]
        uT, mixS = self.uT, self.mixS
        cnt = [0]

        def run(c0, n, want_y):
            units = []
            pos = 0
            while pos < n:
                ln = min(TC, n - pos)
                for cc in range(4):
                    for sl in range(4):
                        units.append((c0 + pos, ln, cc, sl))
                pos += ln

            def S1(i):
                cc0, ln, cc, sl = units[i]
                st = cc * 4 + sl
                pr, pi = cx.banks[(i % 2) * 2], cx.banks[(i % 2) * 2 + 1]
                wk = wks[i % 3]
                cx.op(cx.pe, lambda e: e.matmul(pr.ap[:, 0:ln], lhsT=bl.ap[:, st, 0, :], rhs=uT.ap[:, cc, cc0:cc0 + ln],
                                                start=True, stop=True), [bl, uT], [pr])
                cx.op(cx.pe, lambda e: e.matmul(pi.ap[:, 0:ln], lhsT=bl.ap[:, st, 1, :], rhs=uT.ap[:, cc, cc0:cc0 + ln],
                                                start=True, stop=True), [bl, uT], [pi])
                Rr, Ri = R_re.ap[:, st, 0:ln], R_im.ap[:, st, 0:ln]
                w = [x.ap[:, 0:ln] for x in wk]
                cx.op(V, lambda e: e.tensor_tensor(out=w[0], in0=pr.ap[:, 0:ln], in1=Rr, op=ALU.mult), [pr, R_re], [wk[0]])
                cx.op(V, lambda e: e.tensor_tensor(out=w[1], in0=pi.ap[:, 0:ln], in1=Ri, op=ALU.mult), [pi, R_im], [wk[1]])
                cx.op(V, lambda e: e.tensor_tensor(out=w[2], in0=pi.ap[:, 0:ln], in1=Rr, op=ALU.mult), [pi, R_re], [wk[2]])
                cx.op(V, lambda e: e.tensor_tensor(out=w[3], in0=pr.ap[:, 0:ln], in1=Ri, op=ALU.mult), [pr, R_im], [wk[3]])
                cx.op(P, lambda e: e.tensor_tensor(out=w[0], in0=w[0], in1=w[1], op=ALU.subtract), [wk[0], wk[1]], [wk[0]])
                cx.op(P, lambda e: e.tensor_tensor(out=w[2], in0=w[2], in1=w[3], op=ALU.add), [wk[2], wk[3]], [wk[2]])

            def S2(i):
                cc0, ln, cc, sl = units[i]
                st = cc * 4 + sl
                wk = wks[i % 3]
                w = [x.ap[:, 0:ln] for x in wk]
                Er, Ei = E_re.ap[:, st, 0:ln], E_im.ap[:, st, 0:ln]
                cx.op(V, lambda e: e.tensor_tensor_scan(out=w[4], data0=rtab.ap[:, st, 0:ln], data1=w[0],
                                                        initial=car_re.ap[:, st:st + 1], op0=ALU.mult, op1=ALU.add),
                      [rtab, wk[0], car_re], [wk[4]])
                cx.op(V, lambda e: e.tensor_tensor_scan(out=w[5], data0=rtab.ap[:, st, 0:ln], data1=w[2],
                                                        initial=car_im.ap[:, st:st + 1], op0=ALU.mult, op1=ALU.add),
                      [rtab, wk[2], car_im], [wk[5]])
                L = ln - 1
                gr, gi = wk[4].ap[:, L:L + 1], wk[5].ap[:, L:L + 1]
                er, ei = E_re.ap[:, st, L:L + 1], E_im.ap[:, st, L:L + 1]
                cx.op(V, lambda e: e.tensor_tensor(out=c1[0].ap, in0=gi, in1=ei, op=ALU.mult), [wk[5], E_im], [c1[0]])
                cx.op(V, lambda e: e.tensor_tensor(out=c1[1].ap, in0=gi, in1=er, op=ALU.mult), [wk[5], E_re], [c1[1]])
                cx.op(V, lambda e: e.scalar_tensor_tensor(out=car_re.ap[:, st:st + 1], in0=gr, scalar=er, in1=c1[0].ap,
                                                          op0=ALU.mult, op1=ALU.subtract), [wk[4], E_re, c1[0]], [car_re])
                cx.op(V, lambda e: e.scalar_tensor_tensor(out=car_im.ap[:, st:st + 1], in0=gr, scalar=ei, in1=c1[1].ap,
                                                          op0=ALU.mult, op1=ALU.add), [wk[4], E_im, c1[1]], [car_im])
                if want_y:
                    cx.op(P, lambda e: e.tensor_tensor(out=w[6], in0=w[5], in1=Ei, op=ALU.mult), [wk[5], E_im], [wk[6]])
                    cx.op(P, lambda e: e.tensor_tensor(out=w[7], in0=w[5], in1=Er, op=ALU.mult), [wk[5], E_re], [wk[7]])
                    cx.op(V, lambda e: e.tensor_tensor(out=w[0], in0=w[4], in1=Er, op=ALU.mult), [wk[4], E_re], [wk[0]])
                    cx.op(V, lambda e: e.tensor_tensor(out=w[1], in0=w[4], in1=Ei, op=ALU.mult), [wk[4], E_im], [wk[1]])
                    cx.op(P, lambda e: e.tensor_tensor(out=hre[sl].ap[:, 0:ln], in0=w[0], in1=w[6], op=ALU.subtract),
                          [wk[0], wk[6]], [hre[sl]])
                    cx.op(P, lambda e: e.tensor_tensor(out=him[sl].ap[:, 0:ln], in0=w[1], in1=w[7], op=ALU.add),
                          [wk[1], wk[7]], [him[sl]])
                    if sl == 3:
                        yb = cx.banks[4 + cc % 2]
                        for s2 in range(4):
                            st2 = cc * 4 + s2
                            cx.op(cx.pe, lambda e: e.matmul(yb.ap[:, 0:ln], lhsT=cl.ap[:, st2, 0, :], rhs=hre[s2].ap[:, 0:ln],
                                                            start=(s2 == 0), stop=False), [cl, hre[s2]], [yb])
                            cx.op(cx.pe, lambda e: e.matmul(yb.ap[:, 0:ln], lhsT=cl.ap[:, st2, 1, :], rhs=him[s2].ap[:, 0:ln],
                                                            start=False, stop=False), [cl, him[s2]], [yb])
                        cx.op(cx.pe, lambda e: e.matmul(yb.ap[:, 0:ln], lhsT=dl.ap[:, cc, :], rhs=uT.ap[:, cc, cc0:cc0 + ln],
                                                        start=False, stop=True), [dl, uT], [yb])
                        cx.op(cx.act, lambda e: e.activation(out=gf[cc].ap[:, 0:ln], in_=yb.ap[:, 0:ln], func=AF.Gelu_apprx_tanh),
                              [yb], [gf[cc]])
                        cx.op(cx.act, lambda e: e.copy(out=gb[cc].ap[:, 0:ln], in_=gf[cc].ap[:, 0:ln]), [gf[cc]], [gb[cc]])
                        if cc == 3:
                            for co in range(4):
                                zb = cx.banks[6 + co % 2]
                                for ci in range(4):
                                    cx.op(cx.pe, lambda e: e.matmul(zb.ap[:, 0:ln], lhsT=wgl.ap[:, ci, co * 128:(co + 1) * 128],
                                                                    rhs=gb[ci].ap[:, 0:ln], start=(ci == 0), stop=(ci == 3)),
                                          [wgl, gb[ci]], [zb])
                                cx.op(cx.act, lambda e: e.activation(out=sg.ap[:, 0:ln], in_=zb.ap[:, 0:ln], func=AF.Sigmoid,
                                                                     bias=bgl.ap[:, co:co + 1], scale=1.0), [zb, bgl], [sg])
                                cx.op(V, lambda e: e.tensor_tensor(out=mixS.ap[:, co, cc0:cc0 + ln], in0=gf[co].ap[:, 0:ln],
                                                                   in1=sg.ap[:, 0:ln], op=ALU.mult), [gf[co], sg], [mixS])

            NU = len(units)
            for i in range(NU + 1):
                if i < NU:
                    S1(i)
                if i >= 1:
                    S2(i - 1)

        def set_carry_zero():
            cx.op(V, lambda e: e.memset(car_re.ap, 0.0), [], [car_re])
            cx.op(V, lambda e: e.memset(car_im.ap, 0.0), [], [car_im])

        st_out = cx.tile([128, 96], F32, "s5st_out")
        ini = cx.tile([128, 64], F32, "s5ini")
        cx.dma(cx.sp, ini.ap, self.s5init, writes=[ini])
        for si in range(2):
            cx.op(V, lambda e: e.tensor_copy(out=car_re.ap, in_=ini.ap[:, si * 32:si * 32 + 16]), [ini], [car_re])
            cx.op(V, lambda e: e.tensor_copy(out=car_im.ap, in_=ini.ap[:, si * 32 + 16:si * 32 + 32]), [ini], [car_im])
            run(2048 + 32 * si, 32, True)
            cx.op(V, lambda e: e.tensor_copy(out=st_out.ap[:, 32 + si * 32:48 + si * 32], in_=car_re.ap), [car_re], [st_out])
            cx.op(V, lambda e: e.tensor_copy(out=st_out.ap[:, 48 + si * 32:64 + si * 32], in_=car_im.ap), [car_im], [st_out])
        set_carry_zero()
        run(2112, 16, True)
        m_re, m_im = t16("m_re"), t16("m_im")
        cx.op(V, lambda e: e.tensor_copy(out=m_re.ap, in_=car_re.ap), [car_re], [m_re])
        cx.op(V, lambda e: e.tensor_copy(out=m_im.ap, in_=car_im.ap), [car_im], [m_im])
        set_carry_zero()
        run(0, 2048, False)
        fsend = cx.tile([128, 32], F32, "fsend")
        cx.op(V, lambda e: e.tensor_copy(out=fsend.ap[:, 0:16], in_=car_re.ap), [car_re], [fsend])
        cx.op(V, lambda e: e.tensor_copy(out=fsend.ap[:, 16:32], in_=car_im.ap), [car_im], [fsend])
        cx.dma(cx.sp, self.s5F_send.ap, fsend.ap, reads=[fsend], writes=[self.s5F_send])
        cx.allgather(self.s5F_send, self.s5F_g, [[0, 1, 2, 3], [4, 5, 6, 7]])
        fg = cx.tile([128, 4, 32], F32, "fg")
        cx.dma(cx.sp, fg.ap, self.s5F_g.ap.rearrange("(j p) c -> p j c", p=128), reads=[self.s5F_g], writes=[fg])
        s_re, s_im, n_re, n_im = t16("s_re"), t16("s_im"), t16("n_re"), t16("n_im")
        cx.op(V, lambda e: e.tensor_copy(out=s_re.ap, in_=m_re.ap), [m_re], [s_re])
        cx.op(V, lambda e: e.tensor_copy(out=s_im.ap, in_=m_im.ap), [m_im], [s_im])
        selt = self.selt
        cx.op(V, lambda e: e.tensor_scalar(out=car_re.ap, in0=s_re.ap, scalar1=selt.ap[:, 0:1], scalar2=None, op0=ALU.mult),
              [s_re, selt], [car_re])
        cx.op(V, lambda e: e.tensor_scalar(out=car_im.ap, in0=s_im.ap, scalar1=selt.ap[:, 0:1], scalar2=None, op0=ALU.mult),
              [s_im, selt], [car_im])
        for j in range(3):
            cx.op(V, lambda e: e.tensor_tensor(out=t1.ap, in0=p_re.ap, in1=s_re.ap, op=ALU.mult), [p_re, s_re], [t1])
            cx.op(V, lambda e: e.tensor_tensor(out=t2.ap, in0=p_im.ap, in1=s_im.ap, op=ALU.mult), [p_im, s_im], [t2])
            cx.op(V, lambda e: e.tensor_tensor(out=n_re.ap, in0=t1.ap, in1=t2.ap, op=ALU.subtract), [t1, t2], [n_re])
            cx.op(V, lambda e: e.tensor_tensor(out=t1.ap, in0=p_re.ap, in1=s_im.ap, op=ALU.mult), [p_re, s_im], [t1])
            cx.op(V, lambda e: e.tensor_tensor(out=t2.ap, in0=p_im.ap, in1=s_re.ap, op=ALU.mult), [p_im, s_re], [t2])
            cx.op(V, lambda e: e.tensor_tensor(out=n_im.ap, in0=t1.ap, in1=t2.ap, op=ALU.add), [t1, t2], [n_im])
            cx.op(V, lambda e: e.tensor_tensor(out=s_re.ap, in0=n_re.ap, in1=fg.ap[:, j, 0:16], op=ALU.add), [n_re, fg], [s_re])
            cx.op(V, lambda e: e.tensor_tensor(out=s_im.ap, in0=n_im.ap, in1=fg.ap[:, j, 16:32], op=ALU.add), [n_im, fg], [s_im])
            cx.op(V, lambda e: e.scalar_tensor_tensor(out=car_re.ap, in0=s_re.ap, scalar=selt.ap[:, j + 1:j + 2], in1=car_re.ap,
                                                      op0=ALU.mult, op1=ALU.add), [s_re, selt, car_re], [car_re])
            cx.op(V, lambda e: e.scalar_tensor_tensor(out=car_im.ap, in0=s_im.ap, scalar=selt.ap[:, j + 1:j + 2], in1=car_im.ap,
                                                      op0=ALU.mult, op1=ALU.add), [s_im, selt, car_im], [car_im])
        run(0, 2048, True)
        cx.op(V, lambda e: e.tensor_copy(out=st_out.ap[:, 0:16], in_=car_re.ap), [car_re], [st_out])
        cx.op(V, lambda e: e.tensor_copy(out=st_out.ap[:, 16:32], in_=car_im.ap), [car_im], [st_out])
        cx.dma(cx.sp, self.o_s5st, st_out.ap, reads=[st_out], is_output=True)
    def cumsum_tiles(self, lf2, ntile, within_out, tot_out):
        cx = self.cx
        V = cx.dve
        n = ntile * 8
        ba, bb = cx.banks[6], cx.banks[7]
        cx.op(cx.pe, lambda e: e.matmul(ba.ap[:, 0:n], lhsT=self.cconst(1), rhs=lf2, start=True, stop=True), [self.cstt, self.lf_src], [ba])
        cx.op(cx.pe, lambda e: e.matmul(bb.ap[:, 0:n], lhsT=self.cconst(2), rhs=lf2, start=True, stop=True), [self.cstt, self.lf_src], [bb])
        tots = self.cs_tots
        incl = self.cs_incl
        cx.op(V, lambda e: e.tensor_copy(out=tots.ap[:, 0:n], in_=bb.ap[:, 0:n]), [bb], [tots])
        t3 = tots.ap[:, 0:n].rearrange("p (t h) -> p t h", h=8)
        i3 = incl.ap[:, 0:n].rearrange("p (t h) -> p t h", h=8)
        for h in range(8):
            cx.op(V, lambda e: e.tensor_tensor_scan(out=i3[:, :, h], data0=self.cconst(2)[:, 0:ntile], data1=t3[:, :, h],
                                                    initial=0.0, op0=ALU.mult, op1=ALU.add), [tots, self.cstt], [incl])
        cx.op(V, lambda e: e.tensor_copy(out=tot_out.ap, in_=i3[:, ntile - 1, :]), [incl], [tot_out])
        cx.op(V, lambda e: e.tensor_tensor(out=incl.ap[:, 0:n], in0=incl.ap[:, 0:n], in1=tots.ap[:, 0:n], op=ALU.subtract),
              [incl, tots], [incl])
        cx.op(V, lambda e: e.tensor_tensor(out=within_out, in0=incl.ap[:, 0:n], in1=ba.ap[:, 0:n], op=ALU.add),
              [incl, ba], [self.cs_dst])

    def fox_phase(self):
        cx = self.cx
        V, P, A, PE = cx.dve, cx.pool, cx.act, cx.pe
        selt, cum, lf = self.selt, self.cum, self.logf
        totg = cx.tile([128, 4, 8], F32, "totg")
        cx.dma(cx.sp, totg.ap, self.tot_g.ap.rearrange("(j p) h -> p j h", p=128), reads=[self.tot_g], writes=[totg])
        totmeta = cx.tile([128, 8], F32, "totmeta")
        bk = cx.banks[7]
        cx.op(PE, lambda e: e.matmul(bk.ap[:, 0:8], lhsT=self.cstt.ap[0:80, 512:640], rhs=lf.ap[0:80, 16, :], start=True, stop=True),
              [self.cstt, lf], [bk])
        cx.op(V, lambda e: e.tensor_copy(out=totmeta.ap, in_=bk.ap[:, 0:8]), [bk], [totmeta])
        dl = cx.tile([128, 4, 8], F32, "delta")
        for j in range(3):
            cx.op(V, lambda e: e.tensor_scalar(out=dl.ap[:, j, :], in0=totg.ap[:, 0, :], scalar1=selt.ap[:, 8 + j * 4:9 + j * 4],
                                               scalar2=selt.ap[:, 4 + j:5 + j], op0=ALU.mult, op1=ALU.add), [totg, selt], [dl])
            for jp in range(1, 4):
                cx.op(V, lambda e: e.scalar_tensor_tensor(out=dl.ap[:, j, :], in0=totg.ap[:, jp, :],
                                                          scalar=selt.ap[:, 8 + j * 4 + jp:9 + j * 4 + jp], in1=dl.ap[:, j, :],
                                                          op0=ALU.mult, op1=ALU.add), [totg, selt, dl], [dl])
        cx.op(V, lambda e: e.tensor_copy(out=dl.ap[:, 3, :], in_=totmeta.ap), [totmeta], [dl])
        for jp in range(4):
            cx.op(V, lambda e: e.scalar_tensor_tensor(out=dl.ap[:, 3, :], in0=totg.ap[:, jp, :], scalar=selt.ap[:, 20 + jp:21 + jp],
                                                      in1=dl.ap[:, 3, :], op0=ALU.mult, op1=ALU.add), [totg, selt, dl], [dl])
        cref = cx.tile([128, 16, 8], F32, "cref")
        cx.op(PE, lambda e: e.matmul(bk.ap[:, 0:128], lhsT=self.cstt.ap[:, 896:1024], rhs=cum.ap[:, 0:16, :].rearrange("p t h -> p (t h)"),
                                     start=True, stop=True), [self.cstt, cum], [bk])
        cx.op(V, lambda e: e.tensor_copy(out=cref.ap.rearrange("p t h -> p (t h)"), in_=bk.ap[:, 0:128]), [bk], [cref])
        cref16 = cx.tile([128, 3, 8], F32, "cref16")
        for si in range(3):
            cx.op(PE, lambda e: e.matmul(bk.ap[:, 0:8], lhsT=self.cstt.ap[0:80, 1024 + si * 128:1152 + si * 128], rhs=cum.ap[0:80, 16, :],
                                         start=True, stop=True), [self.cstt, cum], [bk])
            cx.op(V, lambda e: e.tensor_copy(out=cref16.ap[:, si, :], in_=bk.ap[:, 0:8]), [bk], [cref16])
        fkg = cx.tile([128, 3, 16, 8], F32, "fkg")
        cx.dma(cx.sp, fkg.ap, self.fk_g.ap[0:6144, :].rearrange("(j t p) h -> p j t h", p=128, t=16), reads=[self.fk_g], writes=[fkg])
        for j in range(3):
            for t in range(16):
                cx.op(V, lambda e: e.tensor_tensor(out=fkg.ap[:, j, t, :], in0=dl.ap[:, j, :], in1=fkg.ap[:, j, t, :], op=ALU.subtract),
                      [dl, fkg], [fkg])
        lfc = cx.tile([128, 2, 16, 8], F32, "lfc")
        cx.dma(cx.sp, lfc.ap, self.c_foxf.rearrange("s (t p) h -> p s t h", p=128), writes=[lfc])
        cumc = cx.tile([128, 2, 16, 8], F32, "cumc")
        totc = cx.tile([128, 2, 8], F32, "totc")
        self.cs_tots = cx.tile([128, 128], F32, "cs_tots", hreg=True)
        self.cs_incl = cx.tile([128, 128], F32, "cs_incl", hreg=True)
        for s in range(2):
            self.lf_src = lfc
            self.cs_dst = cumc
            tt_ = Tl(totc.ap[:, s, :], "totc_s")
            self.cumsum_tiles(lfc.ap[:, s, :, :].rearrange("p t h -> p (t h)"), 16,
                              cumc.ap[:, s, :, :].rearrange("p t h -> p (t h)"), tt_)
            totc.w = tt_.w
            for t in range(16):
                cx.op(V, lambda e: e.tensor_tensor(out=cumc.ap[:, s, t, :], in0=totc.ap[:, s, :], in1=cumc.ap[:, s, t, :], op=ALU.subtract),
                      [totc, cumc], [cumc])
        kg1 = cx.tile([128, 3, 2048], BF16, "kg")
        kg = [kg1, kg1]
        dfull = cx.tile([128, 512], F32, "dfull")
        dr1 = cx.tile([128, 512], F32, "dr1")
        dhb = [cx.tile([128, 512], BF16, f"dhb{i}") for i in range(3)]
        dq4 = cx.tile([128, 2048], BF16, "dq4")
        cx.op(V, lambda e: e.memset(dq4.ap, 0.0), [], [dq4])
        sel3 = cx.tile([128, 128], BF16, "sel3")
        cx.op(V, lambda e: e.memset(sel3.ap, 0.0), [], [sel3])
        for r_ in (0, 32, 64):
            cx.op(V, lambda e: e.memset(sel3.ap[r_:r_ + 1, :], 1.0), [], [sel3])
        vg = [cx.tile([128, 3, 16, 132], BF16, f"vg{i}", hreg=(i == 1)) for i in range(2)]
        kc = cx.tile([128, 2, 2048], BF16, "kc")
        vc = cx.tile([128, 2, 16, 2, 66], BF16, "vc")
        kstg = cx.tile([128, 4, 128], F32, "kstg")
        pt_seg = [[cx.tile([128, 80], BF16, f"ptseg{s_}{i}") for i in range(3)] for s_ in range(2)]
        for s_ in range(2):
            for i in range(3):
                cx.op(V, lambda e: e.memset(pt_seg[s_][i].ap, 0.0), [], [pt_seg[s_][i]])
        bias_o = cx.tile([128, 16, 16], F32, "bias_o")
        bias_g = cx.tile([128, 3, 16, 16], F32, "bias_g")
        bias_m = cx.tile([128, 16], F32, "bias_m")
        bias_s = cx.tile([128, 2, 17], F32, "bias_s")
        bias_mm = cx.tile([128, 1], F32, "bias_mm")
        pts = [cx.tile([128, 512], BF16, f"pt{i}", hreg=True) for i in range(3)]
        rec = cx.tile([128, 1], F32, "rec")
        qms = [cx.tile([128, NT], BF16, f"qm{i}") for i in range(2)]
        maskb = cx.tile([128, 128], BF16, "maskb")
        cx.op(V, lambda e: e.tensor_copy(out=maskb.ap, in_=self.cconst(1)), [self.cstt], [maskb])
        mask80 = cx.tile([128, 128], BF16, "mask80")
        cx.op(V, lambda e: e.tensor_copy(out=mask80.ap, in_=self.cconst(3)), [self.cstt], [mask80])
        cx.op(P, lambda e: e.memset(vc.ap[:, :, :, :, 64:66], 1.0), [], [vc])
        qT, kT, Vb, mixF = self.qT, self.kT, self.Vb, self.mixF
        sb_i = [0]
        ob_i = [0]
        pt_i = [0]

        def load_k(c):
            b = c % 2
            cx.dma(cx.sp, kg[b].ap, self.kT_g[c // 2].ap[0:768, :].rearrange("(j r) k -> r j k", r=256)[(c % 2) * 128:(c % 2 + 1) * 128, :, :],
                   reads=[self.kT_g[c // 2]], writes=[kg[b]])

        def load_v(c):
            b = c % 2
            for i in range(4):
                for j in range(3):
                    cx.dma(cx.sp, vg[b].ap[:, j, 4 * i:4 * i + 4, :],
                           self.V_g[i].ap[j * 512:(j + 1) * 512, :].rearrange("(t p) x -> p t x", p=128)[:, :, c * 132:(c + 1) * 132],
                           reads=[self.V_g[i]], writes=[vg[b]])

        load_v(0)
        for c in range(4):
            load_k(c)
            if c + 1 < 4:
                load_v(c + 1)
            b = c % 2
            for s in range(2):
                for t in range(16):
                    if t % 4 == 0:
                        cx.dma(cx.sp, kstg.ap, self.c_foxk[s].rearrange("(t p) x -> p t x", p=128)[:, t:t + 4, c * 128:(c + 1) * 128],
                               writes=[kstg])
                    tb = cx.banks[5 + t % 2]
                    cx.op(PE, lambda e: e.transpose(out=tb.ap[:, 0:128], in_=kstg.ap[:, t % 4, :], identity=self.cconst(0)),
                          [kstg, self.cstt], [tb])
                    cx.op(A, lambda e: e.copy(out=kc.ap[:, s, t * 128:(t + 1) * 128], in_=tb.ap[:, 0:128]), [tb], [kc])
                for hh in range(2):
                    h = 2 * c + hh
                    cx.dma(cx.pool, vc.ap[:, s, :, hh, 0:64],
                           self.c_foxv[s].rearrange("(t p) x -> p t x", p=128)[:, :, h * 64:(h + 1) * 64], writes=[vc])
            for hh in range(2):
                h = 2 * c + hh
                pb = hh * 64
                qm = qms[hh]
                cx.op(P, lambda e: e.tensor_copy(out=qm.ap, in_=qT.ap[:, c, :]), [qT], [qm])
                cx.op(P, lambda e: e.memset(qm.ap[(1 - hh) * 64:(1 - hh) * 64 + 64, :], 0.0), [], [qm])
                for kt in range(16):
                    cx.op(V, lambda e: e.tensor_scalar(out=bias_o.ap[:, kt, :], in0=cref.ap[:, :, h], scalar1=cum.ap[:, kt, h:h + 1],
                                                       scalar2=None, op0=ALU.subtract), [cref, cum], [bias_o])
                    for j in range(3):
                        cx.op(V, lambda e: e.tensor_scalar(out=bias_g.ap[:, j, kt, :], in0=cref.ap[:, :, h],
                                                           scalar1=fkg.ap[:, j, kt, h:h + 1], scalar2=None, op0=ALU.add),
                              [cref, fkg], [bias_g])
                cx.op(V, lambda e: e.tensor_tensor(out=bias_mm.ap, in0=dl.ap[:, 3, h:h + 1], in1=cum.ap[:, 16, h:h + 1], op=ALU.subtract),
                      [dl, cum], [bias_mm])
                cx.op(V, lambda e: e.tensor_scalar(out=bias_m.ap, in0=cref.ap[:, :, h], scalar1=bias_mm.ap[:, 0:1], scalar2=None,
                                                   op0=ALU.add), [cref, bias_mm], [bias_m])
                for s in range(2):
                    cx.op(V, lambda e: e.tensor_scalar(out=bias_s.ap[:, s, 0:16], in0=cumc.ap[:, s, :, h], scalar1=cref16.ap[:, s, h:h + 1],
                                                       scalar2=None, op0=ALU.add), [cumc, cref16], [bias_s])
                    cx.op(V, lambda e: e.tensor_tensor(out=bias_s.ap[:, s, 16:17], in0=cref16.ap[:, s, h:h + 1], in1=cum.ap[:, 16, h:h + 1],
                                                       op=ALU.subtract), [cref16, cum], [bias_s])
                cx.op(V, lambda e: e.tensor_tensor(out=bias_mm.ap, in0=cref16.ap[:, 2, h:h + 1], in1=cum.ap[:, 16, h:h + 1],
                                                   op=ALU.subtract), [cref16, cum], [bias_mm])

                for B_ in range(4):
                    for sub in range(4):
                        qs_ = 4 * B_ + sub
                        cx.op(V, lambda e: e.tensor_scalar(out=dfull.ap[0:65, sub * 128:(sub + 1) * 128], in0=self.cconst(2)[0:65, :],
                                                           scalar1=cref.ap[0:65, qs_, h:h + 1], scalar2=cref.ap[0:65, 4 * B_ + 3, h:h + 1],
                                                           op0=ALU.mult, op1=ALU.subtract), [cref, self.cstt], [dfull])
                    cx.op(V, lambda e: e.tensor_copy(out=dhb[0].ap[0:65, :], in_=dfull.ap[0:65, :]), [dfull], [dhb[0]])
                    cx.op(V, lambda e: e.tensor_tensor(out=dr1.ap[0:65, :], in0=dfull.ap[0:65, :], in1=dhb[0].ap[0:65, :], op=ALU.subtract),
                          [dfull, dhb[0]], [dr1])
                    cx.op(V, lambda e: e.tensor_copy(out=dhb[1].ap[0:65, :], in_=dr1.ap[0:65, :]), [dr1], [dhb[1]])
                    cx.op(V, lambda e: e.tensor_tensor(out=dfull.ap[0:65, :], in0=dr1.ap[0:65, :], in1=dhb[1].ap[0:65, :], op=ALU.subtract),
                          [dr1, dhb[1]], [dfull])
                    cx.op(V, lambda e: e.tensor_copy(out=dhb[2].ap[0:65, :], in_=dfull.ap[0:65, :]), [dfull], [dhb[2]])
                    for i_, r_ in enumerate((0, 32, 64)):
                        cx.op(V, lambda e: e.tensor_copy(out=dq4.ap[r_:r_ + 1, B_ * 512:(B_ + 1) * 512], in_=dhb[i_].ap[r_:r_ + 1, :]),
                              [dhb[i_]], [dq4])

                def block(qc0, nq, sources, outs):
                    ob = cx.banks[3 + ob_i[0] % 2]
                    ob_i[0] += 1
                    first = [True]
                    npv = sum(len(sr[6]) for sr in sources)
                    ipv = [0]
                    pts_used = {}

                    def A_(n):
                        (kap, nk, vap, r0, r1, pieces, pvs, ptt) = sources[n]
                        sbk = cx.banks[sb_i[0] % 3]
                        sb_i[0] += 1
                        if ptt is None:
                            pt = pts[pt_i[0] % 3]
                            pt_i[0] += 1
                        else:
                            pt = ptt
                        pts_used[n] = pt
                        use_d = self.cur_dq
                        cx.op(PE, lambda e: e.matmul(sbk.ap[0:nk, 0:nq], lhsT=kap, rhs=qm.ap[:, qc0:qc0 + nq],
                                                     start=True, stop=(not use_d)), [*self.cur_k, qm], [sbk])
                        if use_d:
                            cx.op(PE, lambda e: e.matmul(sbk.ap[0:nk, 0:nq], lhsT=sel3.ap[:, 0:nk], rhs=dq4.ap[:, qc0:qc0 + nq],
                                                         start=False, stop=True), [sel3, dq4], [sbk])
                        for (c0, w, bap, m) in pieces:
                            cx.op(A, lambda e: e.activation(out=pt.ap[0:nk, c0:c0 + w], in_=sbk.ap[0:nk, c0:c0 + w], func=AF.Exp,
                                                            bias=bap, scale=1.0), [sbk, *self.cur_bias], [pt])
                            if m is not None:
                                mw = m.shape[1]
                                cx.op(P, lambda e: e.tensor_tensor(out=pt.ap[0:nk, c0:c0 + mw], in0=pt.ap[0:nk, c0:c0 + mw], in1=m,
                                                                   op=ALU.mult), [pt, maskb, mask80], [pt])

                    def C_(n):
                        (kap, nk, vap, r0, r1, pieces, pvs, ptt) = sources[n]
                        pt = pts_used[n]
                        for (c0, w, oc) in pvs:
                            ipv[0] += 1
                            cx.op(PE, lambda e: e.matmul(ob.ap[0:w, oc:oc + 65], lhsT=pt.ap[r0:r1, c0:c0 + w], rhs=vap,
                                                         start=first[0], stop=(ipv[0] == npv), skip_group_check=True),
                                  [pt, *self.cur_v], [ob])
                            first[0] = False

                    NS = len(sources)
                    for n in range(NS + 2):
                        if n < NS:
                            A_(n)
                        if n >= 2:
                            C_(n - 2)
                    for (oc, w, ot) in outs:
                        cx.op(V, lambda e: e.reciprocal(out=rec.ap[0:w, 0:1], in_=ob.ap[0:w, oc + 64:oc + 65]), [ob], [rec])
                        cx.op(V, lambda e: e.tensor_scalar(out=mixF.ap[0:w, ot, h * 64:(h + 1) * 64], in0=ob.ap[0:w, oc:oc + 64],
                                                           scalar1=rec.ap[0:w, 0:1], scalar2=None, op0=ALU.mult), [ob, rec], [mixF])

                for B in range(4):
                    srcs = []
                    pcs = [(0, 512, bias_m.ap[0:80, 4 * B + 3:4 * B + 4], None)]
                    pvs = [(sub * 128, 128, sub * 65) for sub in range(4)]
                    srcs.append((kT.ap[0:128, c, 2048:2128], 80, Vb.ap[64:80, 16, h, 0:65], 64, 80, pcs, pvs, None))
                    for j in range(3):
                        for kt in range(16):
                            pcs = [(0, 512, bias_g.ap[:, j, kt, 4 * B + 3:4 * B + 4], None)]
                            srcs.append((kg[b].ap[0:128, j, kt * 128:(kt + 1) * 128], 128, vg[b].ap[:, j, kt, hh * 66:hh * 66 + 65],
                                         0, 128, pcs, pvs, None))
                    for kt in range(4 * B + 4):
                        subs = [sub for sub in range(4) if 4 * B + sub >= kt]
                        c_lo = subs[0] * 128
                        pcs = [(c_lo, 512 - c_lo, bias_o.ap[:, kt, 4 * B + 3:4 * B + 4], (maskb.ap if kt >= 4 * B else None))]
                        srcs.append((kT.ap[0:128, c, kt * 128:(kt + 1) * 128], 128, Vb.ap[:, kt, h, 0:65], 0, 128, pcs,
                                     [(sub * 128, 128, sub * 65) for sub in subs], None))
                    self.cur_dq = True
                    self.fox_block_srcs(block, B * 512, 512, srcs, [(sub * 65, 128, 4 * B + sub) for sub in range(4)],
                                        [kT, kg[b]], [Vb, vg[b]], [bias_m, bias_g, bias_o])
                    self.cur_dq = False
                srcs = []
                for s in range(2):
                    for kt in range(16):
                        srcs.append((kc.ap[0:128, s, kt * 128:(kt + 1) * 128], 128, vc.ap[:, s, kt, hh, 0:65], 0, 128,
                                     [(32 * s, 32, bias_s.ap[:, s, kt:kt + 1], None)], [(0, 80, 0)], pt_seg[s][kt % 3]))
                pcs = [(0, 32, bias_s.ap[0:80, 0, 16:17], mask80.ap[0:80, 0:32]), (32, 32, bias_s.ap[0:80, 1, 16:17], mask80.ap[0:80, 32:64]),
                       (64, 16, bias_mm.ap[0:80, 0:1], mask80.ap[0:80, 64:80])]
                srcs.append((kT.ap[0:128, c, 2048:2128], 80, Vb.ap[0:80, 16, h, 0:65], 0, 80, pcs, [(0, 80, 0)], None))
                self.cur_dq = False
                self.fox_block_srcs(block, 2048, 80, srcs, [(0, 80, 16)], [kT, kc], [Vb, vc], [bias_s, bias_mm])

    def fox_block_srcs(self, block, qc0, nq, srcs, out_tiles, ktls, vtls, btls):
        self.cur_k = list(ktls)
        self.cur_v = list(vtls)
        self.cur_bias = list(btls)
        block(qc0, nq, srcs, out_tiles)
    def mixer_out(self, w_out_dram, gain_idx):
        cx = self.cx
        cx.barrier()
        cx.arena_off = self.ab_mark
        cx.h_off = cx.h_base
        for t in range(NTT):
            r = tt_rows(t)
            cx.dma(cx.sp, self.h[t].ap[0:r, :], self.hsave[t * 128:t * 128 + r, :], writes=[self.h[t]])
        wout = cx.tile([128, 8, D], BF16, "wout")
        cx.dma(cx.pool, wout.ap, w_out_dram.rearrange("(k p) n -> p k n", p=128), writes=[wout])
        self.load_gain(self.gpost, gain_idx)
        self.junk = cx.tile([128, D], F32, "junk")
        self.tmpn = cx.tile([128, D], F32, "tmpn")
        mT = [cx.tile([128, 4, 128], BF16, f"mT{i}") for i in range(2)]
        ot = [Tl(cx.sb([128, D], F32, f"ot{i}"), f"ot{i}") for i in range(2)]
        mixS, mixF = self.mixS, self.mixF
        for t in range(NTT):
            r = tt_rows(t)
            bank = cx.banks[t % 2]
            pt = bank.ap.bitcast(BF16).rearrange("p (k c) -> p k c", k=8)
            for k in range(4):
                cx.op(cx.pe, lambda e: e.transpose(out=pt[:, k, 0:r], in_=mixF.ap[0:r, t, k * 128:(k + 1) * 128],
                                                   identity=self.identb.ap[0:r, 0:r]), [mixF, self.identb], [bank])
            m = mT[t % 2]
            cx.op(cx.act, lambda e: e.copy(out=m.ap[:, :, 0:r], in_=pt[:, 0:4, 0:r]), [bank], [m])
            o = ot[t % 2]
            for hf in range(2):
                ob = cx.banks[2 + (2 * t + hf) % 4]
                for k in range(4):
                    cx.op(cx.pe, lambda e: e.matmul(ob.ap[0:r, :], lhsT=mixS.ap[:, k, t * 128:t * 128 + r],
                                                    rhs=wout.ap[:, k, hf * 512:(hf + 1) * 512], start=(k == 0), stop=False),
                          [mixS, wout], [ob])
                for k in range(4):
                    cx.op(cx.pe, lambda e: e.matmul(ob.ap[0:r, :], lhsT=m.ap[:, k, 0:r],
                                                    rhs=wout.ap[:, 4 + k, hf * 512:(hf + 1) * 512], start=False, stop=(k == 3)),
                          [m, wout], [ob])
                cx.op(cx.act, lambda e: e.copy(out=o.ap[0:r, hf * 512:(hf + 1) * 512], in_=ob.ap[0:r, :]), [ob], [o])
            self.postnorm_add([t], [o], self.gpost, 1.0)
        cx.barrier()

    def declare_cd(self):
        self.w_cd_in = self.din("cd_w_in", [D, 3080])
        self.cst2 = self.din("cst2", [128, 896])
        self.convw = self.din("convw", [128, 8 * 5])
        self.ssdp = self.din("ssdp", [1, 24])
        self.ssd_norm = self.din("ssd_norm", [1, 512])
        self.ssdinit = self.din("ssdinit", [2, 128, 512])
        self.convinit = self.din("convinit", [128, 2 * 8 * 3])
        self.w_cd_out = self.din("cd_w_out", [D, D])
        self.c_sbk = self.din("c_sbk", [2, 2048, 512])
        self.c_sbv = self.din("c_sbv", [2, 2048, 512])
        self.o_ssd = self.dout("o_ssd", [3, 128, 512])
        self.o_conv = self.dout("o_conv", [3, 3, 1024])
        self.o_sbk = self.dout("o_sbk", [NT, 512])
        self.o_sbv = self.dout("o_sbv", [NT, 512])
        mk = lambda n, sh, dt=F32: Tl(self.nc.dram_tensor(n, sh, dt).ap(), n)
        self.tail_send, self.tail_g = mk("tail_send", [128, 24]), mk("tail_g", [512, 24])
        self.ssF_send, self.ssF_g = mk("ssF_send", [128, 520]), mk("ssF_g", [512, 520])
        self.k2_send = [mk(f"k2_send{i}", [256, 2048], BF16) for i in range(2)]
        self.k2_g = [mk(f"k2_g{i}", [1024, 2048], BF16) for i in range(2)]
        self.v2_send = [mk(f"v2_send{i}", [512, 512], BF16) for i in range(4)]
        self.v2_g = [mk(f"v2_g{i}", [2048, 512], BF16) for i in range(4)]

    def c2(self, i):
        return self.cst2t.ap[:, i * 128:(i + 1) * 128]

    def mixer_cd(self):
        cx = self.cx
        V, P, A, PE = cx.dve, cx.pool, cx.act, cx.pe
        GR = [[0, 1, 2, 3], [4, 5, 6, 7]]
        SEGS = [(0, 2048, 0), (2048, 32, 1), (2080, 32, 2), (2112, 16, 3)]
        cx.barrier()
        cx.arena_off = self.mark
        cx.h_off = cx.h_base
        self.mixC = cx.tile([128, NTT, 512], BF16, "mixC")
        self.dtt = cx.tile([128, NTT, 8], F32, "dtt")
        self.selt = cx.tile([128, 32], F32, "selt")
        cx.dma(cx.sp, self.selt.ap, self.sel, writes=[self.selt])
        self.cst2t = cx.tile([128, 896], F32, "cst2t")
        cx.dma(cx.sp, self.cst2t.ap, self.cst2, writes=[self.cst2t])
        sp24 = cx.tile([128, 24], F32, "sp24")
        cx.dma(cx.sp, sp24.ap, self.ssdp.partition_broadcast(128), writes=[sp24])
        cwt = cx.tile([128, 8, 5], F32, "cwt")
        cx.dma(cx.sp, cwt.ap, self.convw.rearrange("p (c w) -> p c w", w=5), writes=[cwt])
        self.sb_mark = cx.arena_off
        self.xdt = cx.tile([128, NTT, 512], BF16, "xdt")
        self.zs = cx.tile([128, NTT, 512], BF16, "zs")
        self.cd_mark = cx.arena_off
        self.load_gain(self.gpre, 2 * 2 + 1)
        xnT = cx.tile([128, 8, NT], BF16, "xnTf")
        mark2 = cx.arena_off
        self.junk = cx.tile([128, D], F32, "junk")
        self.nrm_xb = [cx.tile([128, D], BF16, "xb") for _ in range(2)]
        self.prenorm_T(list(range(NTT)), self.gpre, xnT, 0)
        self.spill_h()
        cx.barrier()
        cx.arena_off = mark2
        self.qT = cx.tile([128, 4, NT], BF16, "qT", hreg=True)
        self.kT = cx.tile([128, 4, NT], BF16, "kT", hreg=True)
        self.Vb = cx.tile([128, NTT, 8, 64], BF16, "Vb2", hreg=True)
        self.h_mark_cd = cx.h_off
        self.BCT = cx.tile([128, 4, NT], BF16, "BCT", hreg=True)
        wsl = [cx.tile([128, 8, 512], BF16, f"wsl{i}") for i in range(2)]
        wv = self.w_cd_in.rearrange("(k p) f -> p k f", p=128)
        wi = [0]

        def load_w(c0, n):
            t_ = wsl[wi[0] % 2]
            wi[0] += 1
            cx.dma(cx.pool, t_.ap[:, :, 0:n], wv[:, :, c0:c0 + n], writes=[t_])
            return t_

        stg = [cx.tile([128, 512], F32, f"stg{i}") for i in range(2)]
        a_b = cx.tile([128, 8], F32, "a_b")
        cx.op(A, lambda e: e.activation(out=a_b.ap, in_=sp24.ap[:, 8:16], func=AF.Exp), [sp24], [a_b])
        cx.op(V, lambda e: e.tensor_scalar(out=a_b.ap, in0=a_b.ap, scalar1=-1.0, scalar2=None, op0=ALU.mult), [a_b], [a_b])
        self.a_b = a_b
        self.sp24 = sp24

        wk_ = None
        for part, (c0w, n) in enumerate(((0, 512), (2056, 512), (2568, 512), (1536, 8))):
            w_ = load_w(c0w, n)
            for t in range(NTT):
                r = tt_rows(t)
                c0 = t * 128
                bank = cx.banks[4 + t % 2]
                for k in range(8):
                    cx.op(PE, lambda e: e.matmul(bank.ap[0:r, 0:n], lhsT=xnT.ap[:, k, c0:c0 + r], rhs=w_.ap[:, k, 0:n],
                                                 start=(k == 0), stop=(k == 7)), [w_, xnT], [bank])
                if part == 0:
                    cx.op(A, lambda e: e.activation(out=self.zs.ap[0:r, t, :], in_=bank.ap[0:r, :], func=AF.Silu), [bank], [self.zs])
                elif part == 3:
                    cx.op(V, lambda e: e.tensor_tensor(out=self.dtt.ap[0:r, t, :], in0=bank.ap[0:r, 0:8], in1=sp24.ap[0:r, 0:8],
                                                       op=ALU.add), [bank, sp24], [self.dtt])
                else:
                    st = stg[t % 2]
                    cx.op(A, lambda e: e.copy(out=st.ap[0:r, :], in_=bank.ap[0:r, :]), [bank], [st])
                    oap = self.o_sbk if part == 1 else self.o_sbv
                    cx.dma(cx.sp, oap[c0:c0 + r, :], st.ap[0:r, :], reads=[st], is_output=True)
                    if part == 2:
                        cx.op(V, lambda e: e.tensor_copy(out=self.Vb.ap[0:r, t, :, :],
                                                         in_=st.ap[0:r, :].rearrange("p (h d) -> p h d", h=8)), [st], [self.Vb])
        dtt = self.dtt
        cx.op(A, lambda e: e.activation(out=dtt.ap, in_=dtt.ap, func=AF.Exp), [dtt], [dtt])
        cx.op(A, lambda e: e.activation(out=dtt.ap, in_=dtt.ap, func=AF.Ln, bias=1.0, scale=1.0), [dtt], [dtt])
        cnt = 0
        for j in range(8):
            dst = self.qT if j < 4 else self.kT
            c0w = (1544 if j < 4 else 2056) + (j % 4) * 128
            if j % 4 == 0:
                w_ = load_w(c0w, 512)
            for (c0, cw) in COLT:
                bank = cx.banks[2 + cnt % 2]
                cnt += 1
                for k in range(8):
                    cx.op(PE, lambda e: e.matmul(bank.ap[:, 0:cw], lhsT=w_.ap[:, k, (j % 4) * 128:(j % 4 + 1) * 128],
                                                 rhs=xnT.ap[:, k, c0:c0 + cw], start=(k == 0), stop=(k == 7)), [w_, xnT], [bank])
                sc = 0.125 if j < 4 else 1.0
                cx.op(A, lambda e: e.activation(out=dst.ap[:, j % 4, c0:c0 + cw], in_=bank.ap[:, 0:cw], func=AF.Copy, scale=sc),
                      [bank], [dst])
        for c in range(4):
            cx.dma(cx.sp, self.k2_send[c // 2].ap[(c % 2) * 128:(c % 2 + 1) * 128, :], self.kT.ap[:, c, 0:2048], reads=[self.kT],
                   writes=[self.k2_send[c // 2]])
        for i in range(4):
            cx.dma(cx.sp, self.v2_send[i].ap.rearrange("(t p) c -> p t c", p=128),
                   self.Vb.ap[:, 4 * i:4 * i + 4, :, :].rearrange("p t h d -> p t (h d)"), reads=[self.Vb], writes=[self.v2_send[i]])
        for i in range(2):
            cx.allgather(self.k2_send[i], self.k2_g[i], GR)
        for i in range(4):
            cx.allgather(self.v2_send[i], self.v2_g[i], GR)
        XW = [cx.tile([128, 515], F32, f"XW{i}") for i in range(2)]
        cv = [cx.tile([128, 512], F32, f"cv{i}") for i in range(2)]
        hist = cx.tile([128, 8, 3], F32, "hist")
        tails = cx.tile([128, 4, 8, 3], F32, "tails")
        tg = cx.tile([128, 4, 24], F32, "tg")
        ci_t = cx.tile([128, 2, 8, 3], F32, "ci_t")
        cx.dma(cx.sp, ci_t.ap, self.convinit.rearrange("p (s c w) -> p s c w", s=2, c=8), writes=[ci_t])
        xst = [cx.tile([128, 512], F32, f"xst{i}") for i in range(2)]
        x16 = cx.tile([128, 80], F32, "x16")
        wx = [load_w(512, 512), load_w(1024, 512)]
        xi = [0]

        def xbc_tile(ch, c0, cw, dst_ap):
            bank = cx.banks[2 + xi[0] % 2]
            xi[0] += 1
            w_ = wx[ch // 4]
            for k in range(8):
                cx.op(PE, lambda e: e.matmul(bank.ap[:, 0:cw], lhsT=w_.ap[:, k, (ch % 4) * 128:(ch % 4 + 1) * 128],
                                             rhs=xnT.ap[:, k, c0:c0 + cw], start=(k == 0), stop=(k == 7)), [w_, xnT], [bank])
            return bank

        for ch in range(8):
            bank = xbc_tile(ch, 2045, 83 - 0, None)
            cx.op(V, lambda e: e.tensor_copy(out=tails.ap[:, 0, ch, :], in_=bank.ap[:, 0:3]), [bank], [tails])
            cx.op(V, lambda e: e.tensor_copy(out=tails.ap[:, 1, ch, :], in_=bank.ap[:, 32:35]), [bank], [tails])
            cx.op(V, lambda e: e.tensor_copy(out=tails.ap[:, 2, ch, :], in_=bank.ap[:, 64:67]), [bank], [tails])
            cx.op(V, lambda e: e.tensor_copy(out=tails.ap[:, 3, ch, :], in_=bank.ap[:, 80:83]), [bank], [tails])
        cx.dma(cx.sp, self.tail_send.ap, tails.ap[:, 0, :, :].rearrange("p c w -> p (c w)"), reads=[tails], writes=[self.tail_send])
        cx.allgather(self.tail_send, self.tail_g, GR)
        cx.dma(cx.sp, tg.ap, self.tail_g.ap.rearrange("(j p) x -> p j x", p=128), reads=[self.tail_g], writes=[tg])
        selt = self.selt
        h2 = hist.ap.rearrange("p c w -> p (c w)")
        cx.op(V, lambda e: e.tensor_scalar(out=h2, in0=tails.ap[:, 3, :, :].rearrange("p c w -> p (c w)"), scalar1=selt.ap[:, 0:1],
                                           scalar2=None, op0=ALU.mult), [tails, selt], [hist])
        for j in range(3):
            cx.op(V, lambda e: e.scalar_tensor_tensor(out=h2, in0=tg.ap[:, j, :], scalar=selt.ap[:, 27 + j:28 + j], in1=h2,
                                                      op0=ALU.mult, op1=ALU.add), [tg, selt, hist], [hist])
        cst_o = [cx.tile([128, 128], F32, f"cst_o{i}") for i in range(2)]
        for sid, oi in ((0, 0), (1, 1), (2, 2)):
            for ch in range(8):
                bank = cx.banks[6 + ch % 2]
                co = cst_o[ch % 2]
                cx.op(PE, lambda e: e.transpose(out=bank.ap[0:3, 0:128], in_=tails.ap[:, sid, ch, :], identity=self.cconst(0)),
                      [tails, self.cstt], [bank])
                cx.op(V, lambda e: e.tensor_copy(out=co.ap[0:3, :], in_=bank.ap[0:3, 0:128]), [bank], [co])
                cx.dma(cx.sp, self.o_conv[oi, :, ch * 128:(ch + 1) * 128], co.ap[0:3, :], reads=[co], is_output=True)
        pieces = [(0, 512, 0, None), (512, 512, 0, None), (1024, 512, 0, None), (1536, 512, 0, None),
                  (2048, 32, 1, 0), (2080, 32, 2, 1), (2112, 16, 3, None)]
        xw_i = 0
        for ch in range(8):
            for (c0, cw, sid, si) in pieces:
                xw = XW[xw_i % 2]
                cvt = cv[xw_i % 2]
                xw_i += 1
                bank = xbc_tile(ch, c0, cw, None)
                if c0 == 0:
                    cx.op(V, lambda e: e.tensor_copy(out=xw.ap[:, 0:3], in_=hist.ap[:, ch, :]), [hist], [xw])
                elif sid == 0:
                    prev = XW[(xw_i - 2) % 2]
                    cx.op(V, lambda e: e.tensor_copy(out=xw.ap[:, 0:3], in_=prev.ap[:, 512:515]), [prev], [xw])
                elif si is not None:
                    cx.op(V, lambda e: e.tensor_copy(out=xw.ap[:, 0:3], in_=ci_t.ap[:, si, ch, :]), [ci_t], [xw])
                else:
                    cx.op(V, lambda e: e.memset(xw.ap[:, 0:3], 0.0), [], [xw])
                cx.op(A, lambda e: e.copy(out=xw.ap[:, 3:3 + cw], in_=bank.ap[:, 0:cw]), [bank], [xw])
                cx.op(V, lambda e: e.tensor_scalar(out=cvt.ap[:, 0:cw], in0=xw.ap[:, 3:3 + cw], scalar1=cwt.ap[:, ch, 3:4],
                                                   scalar2=cwt.ap[:, ch, 4:5], op0=ALU.mult, op1=ALU.add), [xw, cwt], [cvt])
                for w in range(3):
                    cx.op(V, lambda e: e.scalar_tensor_tensor(out=cvt.ap[:, 0:cw], in0=xw.ap[:, w:w + cw], scalar=cwt.ap[:, ch, w:w + 1],
                                                              in1=cvt.ap[:, 0:cw], op0=ALU.mult, op1=ALU.add), [xw, cwt, cvt], [cvt])
                if ch >= 4:
                    cx.op(A, lambda e: e.activation(out=self.BCT.ap[:, ch - 4, c0:c0 + cw], in_=cvt.ap[:, 0:cw], func=AF.Silu),
                          [cvt], [self.BCT])
                else:
                    if sid == 0:
                        xs_ = xst[xw_i % 2]
                        off = 0
                    else:
                        xs_ = x16
                        off = c0 - 2048
                    cx.op(A, lambda e: e.activation(out=xs_.ap[:, off:off + cw], in_=cvt.ap[:, 0:cw], func=AF.Silu), [cvt], [xs_])
                    if sid in (1, 2):
                        continue
                    tot_w = cw if sid == 0 else 80
                    base = c0 if sid == 0 else 2048
                    nsub = (tot_w + 127) // 128
                    for sb_ in range(nsub):
                        w_ = min(128, tot_w - sb_ * 128)
                        t = (base + sb_ * 128) // 128
                        tb = cx.banks[6 + sb_ % 2]
                        cx.op(PE, lambda e: e.transpose(out=tb.ap[0:w_, 0:128], in_=xs_.ap[:, sb_ * 128:sb_ * 128 + w_],
                                                        identity=self.cconst(0)), [xs_, self.cstt], [tb])
                        for hh in range(2):
                            h = 2 * ch + hh
                            cx.op(A, lambda e: e.activation(out=self.xdt.ap[0:w_, t, h * 64:(h + 1) * 64],
                                                            in_=tb.ap[0:w_, hh * 64:(hh + 1) * 64], func=AF.Copy,
                                                            scale=self.dtt.ap[0:w_, t, h:h + 1]), [tb, self.dtt], [self.xdt])
        cx.barrier()
        cx.arena_off = self.cd_mark
        self.ssd_phase()
        cx.barrier()
        cx.arena_off = self.sb_mark
        cx.h_off = self.h_mark_cd
        self.sb_phase()
        self.mixer_out2(self.w_cd_out, 3 * 2 + 1)

    def ssd_phase(self):
        cx = self.cx
        V, P, A, PE = cx.dve, cx.pool, cx.act, cx.pe
        xdt, zs, BCT, dtt, a_b, selt = self.xdt, self.zs, self.BCT, self.dtt, self.a_b, self.selt
        HT = cx.tile([128, 512], F32, "HT")
        HTb = cx.tile([128, 512], BF16, "HTb")
        adt = cx.tile([128, 8], F32, "adt")
        acs = cx.tile([128, 8], F32, "acs")
        tot = cx.tile([128, 8], F32, "tot")
        eacs = cx.tile([128, 8], F32, "eacs")
        dec = cx.tile([128, 8], F32, "dec")
        etot = cx.tile([128, 8], F32, "etot")
        ddt = cx.tile([128, 8], F32, "ddt")
        logD = cx.tile([128, 8], F32, "logD")
        Am = [cx.tile([128, 128], F32, f"Am{i}") for i in range(2)]
        Ex = [cx.tile([128, 128], F32, f"Ex{i}") for i in range(8)]
        CBm = [cx.tile([128, 128], F32, f"CBm{i}") for i in range(2)]
        Wt = [cx.tile([128, 128], BF16, f"Wt{i}") for i in range(8)]
        Btok = [cx.tile([128, 128], BF16, f"Btok{i}") for i in range(2)]
        xdd = cx.tile([128, 512], BF16, "xdd")
        ysb = cx.tile([128, 512], F32, "ysb")
        yg = cx.tile([128, 512], F32, "yg")
        ss2 = cx.tile([128, 2], F32, "ss2")
        rs2 = cx.tile([128, 2], F32, "rs2")
        db = cx.tile([128, 8], F32, "db")
        cx.op(V, lambda e: e.tensor_copy(out=db.ap, in_=self.sp24.ap[:, 16:24]), [self.sp24], [db])
        ng = cx.tile([128, 512], F32, "ng")
        cx.dma(cx.sp, ng.ap, self.ssd_norm.partition_broadcast(128), writes=[ng])
        nh2 = cx.tile([128, 2], F32, "nh2")
        cx.op(V, lambda e: e.memset(nh2.ap, -0.5), [], [nh2])
        identf = self.cconst(0)
        TRI, ONES = self.cconst(1), self.cconst(2)
        SL, BSL80, BT80, BO80 = self.c2(0), self.c2(1), self.cconst(3), self.c2(2)
        ci = [0]

        def chunk(t, rows, want_y, seg16):
            r = rows
            tri = BT80[0:r, 0:r] if seg16 else TRI[0:r, 0:r]
            ones = BO80[0:r, 0:r] if seg16 else ONES[0:r, 0:r]
            sl = BSL80[0:r, 0:r] if seg16 else SL[0:r, 0:r]
            cstl = [self.cstt, self.cst2t]
            cx.op(V, lambda e: e.tensor_tensor(out=adt.ap[0:r, :], in0=dtt.ap[0:r, t, :], in1=a_b.ap[0:r, :], op=ALU.mult),
                  [dtt, a_b], [adt])
            b0, b1 = cx.banks[0], cx.banks[1]
            cx.op(PE, lambda e: e.matmul(b0.ap[0:r, 0:8], lhsT=tri, rhs=adt.ap[0:r, :], start=True, stop=True), [adt, *cstl], [b0])
            cx.op(PE, lambda e: e.matmul(b1.ap[0:r, 0:8], lhsT=ones, rhs=adt.ap[0:r, :], start=True, stop=True), [adt, *cstl], [b1])
            cx.op(V, lambda e: e.tensor_copy(out=acs.ap[0:r, :], in_=b0.ap[0:r, 0:8]), [b0], [acs])
            cx.op(V, lambda e: e.tensor_copy(out=tot.ap[0:r, :], in_=b1.ap[0:r, 0:8]), [b1], [tot])
            cx.op(A, lambda e: e.activation(out=eacs.ap[0:r, :], in_=acs.ap[0:r, :], func=AF.Exp), [acs], [eacs])
            cx.op(A, lambda e: e.activation(out=etot.ap[0:r, :], in_=tot.ap[0:r, :], func=AF.Exp), [tot], [etot])
            cx.op(V, lambda e: e.tensor_tensor(out=dec.ap[0:r, :], in0=tot.ap[0:r, :], in1=acs.ap[0:r, :], op=ALU.subtract), [tot, acs], [dec])
            cx.op(A, lambda e: e.activation(out=dec.ap[0:r, :], in_=dec.ap[0:r, :], func=AF.Exp), [dec], [dec])
            c0 = t * 128
            if want_y:
                cx.op(V, lambda e: e.reciprocal(out=ddt.ap[0:r, :], in_=dtt.ap[0:r, t, :]), [dtt], [ddt])
                cx.op(V, lambda e: e.tensor_tensor(out=ddt.ap[0:r, :], in0=ddt.ap[0:r, :], in1=db.ap[0:r, :], op=ALU.mult), [ddt, db], [ddt])
                for g in range(2):
                    cb = cx.banks[2 + g]
                    cx.op(PE, lambda e: e.matmul(cb.ap[0:r, 0:r], lhsT=BCT.ap[:, g, c0:c0 + r], rhs=BCT.ap[:, 2 + g, c0:c0 + r],
                                                 start=True, stop=True), [BCT], [cb])
                    cx.op(V, lambda e: e.tensor_tensor(out=CBm[g].ap[0:r, 0:r], in0=cb.ap[0:r, 0:r], in1=tri, op=ALU.mult),
                          [cb, *cstl], [CBm[g]])
                yb = cx.banks[6]
                for h in range(8):
                    am, ex = Am[h % 2], Ex[h]
                    sb_ = cx.banks[4 + h % 2]
                    cx.op(P, lambda e: e.tensor_scalar(out=am.ap[0:r, 0:r], in0=sl, scalar1=adt.ap[0:r, h:h + 1], scalar2=None,
                                                       op0=ALU.mult), [adt, *cstl], [am])
                    cx.op(PE, lambda e: e.matmul(sb_.ap[0:r, 0:r], lhsT=am.ap[0:r, 0:r], rhs=tri, start=True, stop=True),
                          [am, *cstl], [sb_])
                    cx.op(A, lambda e: e.activation(out=ex.ap[0:r, 0:r], in_=sb_.ap[0:r, 0:r], func=AF.Exp), [sb_], [ex])
                for h in range(8):
                    ex = Ex[h]
                    cx.op(V, lambda e: e.tensor_tensor(out=ex.ap[0:r, 0:r], in0=ex.ap[0:r, 0:r], in1=CBm[h // 4].ap[0:r, 0:r], op=ALU.mult),
                          [ex, CBm[h // 4]], [ex])
                    cx.op(V, lambda e: e.scalar_tensor_tensor(out=Wt[h].ap[0:r, 0:r], in0=identf[0:r, 0:r], scalar=ddt.ap[0:r, h:h + 1],
                                                              in1=ex.ap[0:r, 0:r], op0=ALU.mult, op1=ALU.add), [ddt, ex, self.cstt], [Wt[h]])
                    cx.op(PE, lambda e: e.matmul(yb.ap[0:r, h * 64:(h + 1) * 64], lhsT=Wt[h].ap[0:r, 0:r], rhs=xdt.ap[0:r, t, h * 64:(h + 1) * 64],
                                                 start=(h == 0), stop=True, skip_group_check=True), [Wt[h], xdt], [yb])
                ob = cx.banks[7]
                if seg16:
                    CT16_, Hsb_ = self._yoff16
                    for g in range(2):
                        for si_ in range(2):
                            cx.op(PE, lambda e: e.matmul(ob.ap[0:r, g * 256:(g + 1) * 256], lhsT=CT16_.ap[:, si_, g, :],
                                                         rhs=Hsb_[si_].ap[:, g * 256:(g + 1) * 256], start=(g == 0 and si_ == 0), stop=(si_ == 1),
                                                         skip_group_check=True), [CT16_, Hsb_[si_]], [ob])
                else:
                    for g in range(2):
                        cx.op(PE, lambda e: e.matmul(ob.ap[0:r, g * 256:(g + 1) * 256], lhsT=BCT.ap[:, 2 + g, c0:c0 + r],
                                                     rhs=HTb.ap[:, g * 256:(g + 1) * 256], start=(g == 0), stop=True, skip_group_check=True),
                              [BCT, HTb], [ob])
                cx.op(A, lambda e: e.copy(out=ysb.ap[0:r, :], in_=yb.ap[0:r, :]), [yb], [ysb])
                for h in range(8):
                    cx.op(V, lambda e: e.scalar_tensor_tensor(out=ysb.ap[0:r, h * 64:(h + 1) * 64], in0=ob.ap[0:r, h * 64:(h + 1) * 64],
                                                              scalar=eacs.ap[0:r, h:h + 1], in1=ysb.ap[0:r, h * 64:(h + 1) * 64],
                                                              op0=ALU.mult, op1=ALU.add), [ob, eacs, ysb], [ysb])
                cx.op(V, lambda e: e.tensor_tensor(out=yg.ap[0:r, :], in0=ysb.ap[0:r, :], in1=zs.ap[0:r, t, :], op=ALU.mult), [ysb, zs], [yg])
                for g in range(2):
                    cx.op(V, lambda e: e.scalar_tensor_tensor(out=ysb.ap[0:r, g * 256:(g + 1) * 256], in0=yg.ap[0:r, g * 256:(g + 1) * 256],
                                                              scalar=1.0, in1=yg.ap[0:r, g * 256:(g + 1) * 256], op0=ALU.mult, op1=ALU.mult,
                                                              accum_out=ss2.ap[0:r, g:g + 1]), [yg], [ysb, ss2])
                cx.op(V, lambda e: e.tensor_scalar(out=ss2.ap[0:r, :], in0=ss2.ap[0:r, :], scalar1=1.0 / 256, scalar2=EPS,
                                                   op0=ALU.mult, op1=ALU.add), [ss2], [ss2])
                cx.op(P, lambda e: e.tensor_tensor(out=rs2.ap[0:r, :], in0=ss2.ap[0:r, :], in1=nh2.ap[0:r, :], op=ALU.pow), [ss2, nh2], [rs2])
                for g in range(2):
                    cx.op(V, lambda e: e.scalar_tensor_tensor(out=self.mixC.ap[0:r, t, g * 256:(g + 1) * 256],
                                                              in0=yg.ap[0:r, g * 256:(g + 1) * 256], scalar=rs2.ap[0:r, g:g + 1],
                                                              in1=ng.ap[0:r, g * 256:(g + 1) * 256], op0=ALU.mult, op1=ALU.mult),
                          [yg, rs2, ng], [self.mixC])

        def prep_B(t, r):
            c0 = t * 128
            for g in range(2):
                tb = cx.banks[0 + g]
                tbv = tb.ap.bitcast(BF16)
                cx.op(PE, lambda e: e.transpose(out=tbv[0:r, 0:128], in_=BCT.ap[:, g, c0:c0 + r], identity=self.identb.ap),
                      [BCT, self.identb], [tb])
                cx.op(V, lambda e: e.tensor_copy(out=Btok[g].ap[0:r, :], in_=tbv[0:r, 0:128]), [tb], [Btok[g]])

        def upd(t, r0, r1, Ht, Htb, et):
            for h in range(8):
                cx.op(A, lambda e: e.activation(out=xdd.ap[r0:r1, h * 64:(h + 1) * 64], in_=xdt.ap[r0:r1, t, h * 64:(h + 1) * 64],
                                                func=AF.Copy, scale=dec.ap[r0:r1, h:h + 1]), [xdt, dec], [xdd])
            hb = cx.banks[3]
            for g in range(2):
                cx.op(PE, lambda e: e.matmul(hb.ap[:, g * 256:(g + 1) * 256], lhsT=Btok[g].ap[r0:r1, :], rhs=xdd.ap[r0:r1, g * 256:(g + 1) * 256],
                                             start=True, stop=True, skip_group_check=True), [Btok[g], xdd], [hb])
            for h in range(8):
                cx.op(V, lambda e: e.scalar_tensor_tensor(out=Ht.ap[:, h * 64:(h + 1) * 64], in0=Ht.ap[:, h * 64:(h + 1) * 64],
                                                          scalar=et.ap[:, h:h + 1], in1=hb.ap[:, h * 64:(h + 1) * 64],
                                                          op0=ALU.mult, op1=ALU.add), [Ht, et, hb], [Ht])
            cx.op(P, lambda e: e.tensor_copy(out=Htb.ap, in_=Ht.ap), [Ht], [Htb])

        o_st = self.o_ssd
        Hs = [cx.tile([128, 512], F32, f"Hs{i}") for i in range(3)]
        Hsb = [cx.tile([128, 512], BF16, f"Hsb{i}") for i in range(3)]
        for s_ in range(2):
            cx.dma(cx.sp, Hs[s_].ap, self.ssdinit[s_], writes=[Hs[s_]])
            cx.op(P, lambda e: e.tensor_copy(out=Hsb[s_].ap, in_=Hs[s_].ap), [Hs[s_]], [Hsb[s_]])
        cx.op(V, lambda e: e.memset(Hs[2].ap, 0.0), [], [Hs[2]])
        cx.op(V, lambda e: e.memset(Hsb[2].ap, 0.0), [], [Hsb[2]])
        CT16 = cx.tile([128, 3, 2, 80], BF16, "CT16")
        cx.op(V, lambda e: e.memset(CT16.ap, 0.0), [], [CT16])
        for si_, (a_, b_) in enumerate(((0, 32), (32, 64), (64, 80))):
            for g in range(2):
                cx.op(V, lambda e: e.tensor_copy(out=CT16.ap[:, si_, g, a_:b_], in_=BCT.ap[:, 2 + g, 2048 + a_:2048 + b_]), [BCT], [CT16])
        self._yoff16 = (CT16, Hsb)
        chunk(16, 80, True, True)
        prep_B(16, 80)
        et16 = cx.tile([128, 8], F32, "et16")
        for si_, (a_, b_, selc) in enumerate(((0, 32, self.cstt.ap[0:80, 640:768]), (32, 64, self.cstt.ap[0:80, 768:896]),
                                               (64, 80, self.cstt.ap[0:80, 512:640]))):
            bk = cx.banks[2]
            cx.op(PE, lambda e: e.matmul(bk.ap[:, 0:8], lhsT=selc, rhs=adt.ap[0:80, :], start=True, stop=True), [self.cstt, adt], [bk])
            cx.op(A, lambda e: e.activation(out=et16.ap, in_=bk.ap[:, 0:8], func=AF.Exp), [bk], [et16])
            upd(16, a_, b_, Hs[si_], Hsb[si_], et16)
            if si_ < 2:
                cx.dma(cx.sp, o_st[1 + si_], Hs[si_].ap, reads=[Hs[si_]], is_output=True)
        self._yoff16 = None
        cx.op(V, lambda e: e.memset(HT.ap, 0.0), [], [HT])
        cx.op(V, lambda e: e.memset(logD.ap, 0.0), [], [logD])
        for t in range(16):
            chunk(t, 128, False, False)
            prep_B(t, 128)
            upd(t, 0, 128, HT, HTb, etot)
            cx.op(V, lambda e: e.tensor_tensor(out=logD.ap, in0=logD.ap, in1=tot.ap, op=ALU.add), [logD, tot], [logD])
        fs = cx.tile([128, 520], F32, "fs")
        cx.op(V, lambda e: e.tensor_copy(out=fs.ap[:, 0:512], in_=HT.ap), [HT], [fs])
        cx.op(V, lambda e: e.tensor_copy(out=fs.ap[:, 512:520], in_=logD.ap), [logD], [fs])
        cx.dma(cx.sp, self.ssF_send.ap, fs.ap, reads=[fs], writes=[self.ssF_send])
        cx.allgather(self.ssF_send, self.ssF_g, [[0, 1, 2, 3], [4, 5, 6, 7]])
        fg = cx.tile([128, 3, 520], F32, "fg")
        cx.dma(cx.sp, fg.ap, self.ssF_g.ap[0:384, :].rearrange("(j p) c -> p j c", p=128), reads=[self.ssF_g], writes=[fg])
        Sc = Hs[2]
        ed = cx.tile([128, 8], F32, "ed")
        cx.op(V, lambda e: e.tensor_scalar(out=HT.ap, in0=Sc.ap, scalar1=selt.ap[:, 0:1], scalar2=None, op0=ALU.mult), [Sc, selt], [HT])
        for j in range(3):
            cx.op(A, lambda e: e.activation(out=ed.ap, in_=fg.ap[:, j, 512:520], func=AF.Exp), [fg], [ed])
            for h in range(8):
                cx.op(V, lambda e: e.scalar_tensor_tensor(out=Sc.ap[:, h * 64:(h + 1) * 64], in0=Sc.ap[:, h * 64:(h + 1) * 64],
                                                          scalar=ed.ap[:, h:h + 1], in1=fg.ap[:, j, h * 64:(h + 1) * 64],
                                                          op0=ALU.mult, op1=ALU.add), [Sc, ed, fg], [Sc])
            cx.op(V, lambda e: e.scalar_tensor_tensor(out=HT.ap, in0=Sc.ap, scalar=selt.ap[:, j + 1:j + 2], in1=HT.ap,
                                                      op0=ALU.mult, op1=ALU.add), [Sc, selt, HT], [HT])
        cx.op(P, lambda e: e.tensor_copy(out=HTb.ap, in_=HT.ap), [HT], [HTb])
        self._yoffP = HTb
        for t in range(16):
            chunk(t, 128, True, False)
            prep_B(t, 128)
            upd(t, 0, 128, HT, HTb, etot)
        cx.dma(cx.sp, o_st[0], HT.ap, reads=[HT], is_output=True)
    def sb_phase(self):
        cx = self.cx
        V, P, A, PE = cx.dve, cx.pool, cx.act, cx.pe
        selt = self.selt
        self.mixF = cx.tile([128, NTT, 512], BF16, "mixD")
        self.cd_mark2 = cx.arena_off
        qT, kT, Vb, mixF = self.qT, self.kT, self.Vb, self.mixF
        NTRI = cx.tile([128, 128], BF16, "ntri")
        NONE_ = cx.tile([128, 128], BF16, "none")
        cx.op(V, lambda e: e.tensor_scalar(out=NONE_.ap, in0=self.cconst(2), scalar1=-1.0, scalar2=None, op0=ALU.mult), [self.cstt], [NONE_])
        cx.op(V, lambda e: e.tensor_scalar(out=NTRI.ap, in0=self.c2(5), scalar1=-1.0, scalar2=None, op0=ALU.mult), [self.cst2t], [NTRI])
        ntri_j = [cx.tile([128, 128], BF16, f"ntri{j}") for j in range(3)]
        none_j = [cx.tile([128, 128], BF16, f"none{j}") for j in range(3)]
        for j in range(3):
            cx.op(V, lambda e: e.tensor_scalar(out=ntri_j[j].ap, in0=self.c2(5), scalar1=selt.ap[:, 24 + j:25 + j], scalar2=-1.0,
                                               op0=ALU.mult, op1=ALU.mult), [self.cst2t, selt], [ntri_j[j]])
            cx.op(V, lambda e: e.tensor_scalar(out=none_j[j].ap, in0=self.cconst(2), scalar1=selt.ap[:, 24 + j:25 + j], scalar2=-1.0,
                                               op0=ALU.mult, op1=ALU.mult), [self.cstt, selt], [none_j[j]])
        maskS = cx.tile([128, 128], BF16, "maskS")
        cx.op(V, lambda e: e.tensor_copy(out=maskS.ap, in_=self.c2(3)), [self.cst2t], [maskS])
        bsu80 = cx.tile([128, 128], BF16, "bsu80")
        cx.op(V, lambda e: e.tensor_copy(out=bsu80.ap, in_=self.c2(4)), [self.cst2t], [bsu80])
        zero_b = cx.tile([128, 1], F32, "zero_b")
        cx.op(V, lambda e: e.memset(zero_b.ap, 0.0), [], [zero_b])
        kg = [cx.tile([128, 3, 2048], BF16, f"kg{i}") for i in range(2)]
        vg = [cx.tile([128, 3, 16, 128], BF16, f"vg{i}", hreg=(i == 1)) for i in range(2)]
        kc = cx.tile([128, 2, 2048], BF16, "kc")
        vc = cx.tile([128, 2, 16, 2, 64], BF16, "vc")
        kstg = cx.tile([128, 4, 128], F32, "kstg")
        ef = [cx.tile([128, 512], F32, f"ef{i}") for i in range(5)]
        spb = [cx.tile([128, 512], BF16, f"spb{i}") for i in range(3)]
        uf = [cx.tile([128, 512], F32, f"uf{i}", hreg=(i == 2)) for i in range(3)]
        wb = [cx.tile([128, 512], BF16, f"wb{i}", hreg=True) for i in range(3)]
        Csb = cx.tile([128, 512], F32, "Csb")
        qms = [cx.tile([128, NT], BF16, f"qm{i}") for i in range(2)]
        w_seg = [[cx.tile([128, 80], BF16, f"wseg{s_}{i}") for i in range(3)] for s_ in range(3)]
        for s_ in range(3):
            for i in range(3):
                cx.op(V, lambda e: e.memset(w_seg[s_][i].ap, 0.0), [], [w_seg[s_][i]])
        w_own = [cx.tile([128, 80], BF16, f"wown{s_}") for s_ in range(3)]
        for s_ in range(3):
            cx.op(V, lambda e: e.memset(w_own[s_].ap, 0.0), [], [w_own[s_]])
        it = [0]
        ob_i = [0]

        def load_pair(c):
            b = c % 2
            cx.dma(cx.sp, kg[b].ap, self.k2_g[c // 2].ap[0:768, :].rearrange("(j r) k -> r j k", r=256)[(c % 2) * 128:(c % 2 + 1) * 128, :, :],
                   reads=[self.k2_g[c // 2]], writes=[kg[b]])
            for i in range(4):
                for j in range(3):
                    cx.dma(cx.sp, vg[b].ap[:, j, 4 * i:4 * i + 4, :],
                           self.v2_g[i].ap[j * 512:(j + 1) * 512, :].rearrange("(t p) x -> p t x", p=128)[:, :, c * 128:(c + 1) * 128],
                           reads=[self.v2_g[i]], writes=[vg[b]])

        load_pair(0)
        for c in range(4):
            if c + 1 < 4:
                load_pair(c + 1)
            b = c % 2
            for s in range(2):
                for t in range(16):
                    if t % 4 == 0:
                        cx.dma(cx.sp, kstg.ap, self.c_sbk[s].rearrange("(t p) x -> p t x", p=128)[:, t:t + 4, c * 128:(c + 1) * 128], writes=[kstg])
                    tb = cx.banks[6 + t % 2]
                    cx.op(PE, lambda e: e.transpose(out=tb.ap[:, 0:128], in_=kstg.ap[:, t % 4, :], identity=self.cconst(0)), [kstg, self.cstt], [tb])
                    cx.op(A, lambda e: e.copy(out=kc.ap[:, s, t * 128:(t + 1) * 128], in_=tb.ap[:, 0:128]), [tb], [kc])
                for hh in range(2):
                    h = 2 * c + hh
                    cx.dma(cx.pool, vc.ap[:, s, :, hh, :], self.c_sbv[s].rearrange("(t p) x -> p t x", p=128)[:, :, h * 64:(h + 1) * 64], writes=[vc])
            for hh in range(2):
                h = 2 * c + hh
                pb = hh * 64
                qm = qms[hh]
                cx.op(P, lambda e: e.tensor_copy(out=qm.ap, in_=qT.ap[:, c, :]), [qT], [qm])
                cx.op(P, lambda e: e.memset(qm.ap[(1 - hh) * 64:(1 - hh) * 64 + 64, :], 0.0), [], [qm])

                def run_steps(steps, ob, outs):
                    N = len(steps)
                    first_pv = [True]
                    npv = sum(len(st_["pvs"]) for st_ in steps)
                    ipv = [0]

                    def A_(n):
                        st_ = steps[n]
                        i = n % 2
                        nk, n_ = st_["nk"], st_["nq"] - st_["lo"]
                        zb, e_, sp_ = cx.banks[0 + i], ef[n % 5], spb[n % 3]
                        cx.op(PE, lambda e: e.matmul(zb.ap[0:nk, 0:n_], lhsT=st_["kap"], rhs=qm.ap[:, st_["qc0"] + st_["lo"]:st_["qc0"] + st_["nq"]],
                                                     start=True, stop=True), [st_["ktl"], qm], [zb])
                        cx.op(A, lambda e: e.activation(out=e_.ap[0:nk, 0:n_], in_=zb.ap[0:nk, 0:n_], func=AF.Exp), [zb], [e_])
                        cx.op(A, lambda e: e.activation(out=sp_.ap[0:nk, 0:n_], in_=e_.ap[0:nk, 0:n_], func=AF.Ln, bias=1.0, scale=1.0), [e_], [sp_])
                        m = st_["mask"]
                        if m is not None:
                            mw = m.shape[1]
                            cx.op(P, lambda e: e.tensor_tensor(out=sp_.ap[0:nk, 0:mw], in0=sp_.ap[0:nk, 0:mw], in1=m, op=ALU.mult),
                                  [sp_, maskS, bsu80], [sp_])

                    def B_(n):
                        st_ = steps[n]
                        i = n % 2
                        nk, n_ = st_["nk"], st_["nq"] - st_["lo"]
                        cl = st_["csb_lo"]
                        lb, cb, e_, sp_, u_ = cx.banks[2 + i], cx.banks[4 + i], ef[n % 5], spb[n % 3], uf[n % 3]
                        w_ = wb[n % 3] if st_["wtile"] is None else st_["wtile"]
                        wlo = st_["wlo"]
                        if st_["clear"] is not None:
                            cx.op(V, lambda e: e.memset(Csb.ap[:, 0:st_["clear"]], 0.0), [], [Csb])
                        cx.op(PE, lambda e: e.matmul(lb.ap[0:nk, 0:n_], lhsT=st_["ntri"].ap[0:nk, 0:nk], rhs=sp_.ap[0:nk, 0:n_], start=True, stop=True),
                              [st_["ntri"], sp_], [lb])
                        cx.op(PE, lambda e: e.matmul(cb.ap[:, 0:n_], lhsT=st_["none"].ap[0:nk, :], rhs=sp_.ap[0:nk, 0:n_], start=True, stop=True),
                              [st_["none"], sp_], [cb])
                        cx.op(V, lambda e: e.tensor_tensor(out=u_.ap[0:nk, 0:n_], in0=lb.ap[0:nk, 0:n_], in1=Csb.ap[0:nk, cl:cl + n_], op=ALU.add),
                              [lb, Csb], [u_])
                        cx.op(V, lambda e: e.tensor_tensor(out=Csb.ap[:, cl:cl + n_], in0=Csb.ap[:, cl:cl + n_], in1=cb.ap[:, 0:n_], op=ALU.add),
                              [cb, Csb], [Csb])
                        cx.op(A, lambda e: e.activation(out=u_.ap[0:nk, 0:n_], in_=u_.ap[0:nk, 0:n_], func=AF.Exp, bias=st_["vbias"], scale=1.0),
                              [u_, selt, zero_b], [u_])
                        hv = (n_ // 2) if n_ >= 256 else 0
                        if hv:
                            cx.op(V, lambda e: e.tensor_tensor(out=w_.ap[0:nk, wlo:wlo + hv], in0=e_.ap[0:nk, 0:hv], in1=u_.ap[0:nk, 0:hv], op=ALU.mult),
                                  [e_, u_], [w_])
                        cx.op(P, lambda e: e.tensor_tensor(out=w_.ap[0:nk, wlo + hv:wlo + n_], in0=e_.ap[0:nk, hv:n_], in1=u_.ap[0:nk, hv:n_], op=ALU.mult),
                              [e_, u_], [w_])
                        m = st_["mask"]
                        if m is not None:
                            mw = m.shape[1]
                            cx.op(P, lambda e: e.tensor_tensor(out=w_.ap[0:nk, wlo:wlo + mw], in0=w_.ap[0:nk, wlo:wlo + mw], in1=m, op=ALU.mult),
                                  [w_, maskS, bsu80], [w_])
                        st_["w"] = w_

                    def C_(n):
                        st_ = steps[n]
                        w_ = st_["w"]
                        for (r0, r1, c0_, c1_, vap, vtl, orows, oc0, oc1) in st_["pvs"]:
                            ipv[0] += 1
                            cx.op(PE, lambda e: e.matmul(ob.ap[0:orows, oc0:oc1], lhsT=w_.ap[r0:r1, c0_:c1_], rhs=vap, start=first_pv[0],
                                                         stop=(ipv[0] == npv), skip_group_check=True), [w_, vtl], [ob])
                            first_pv[0] = False

                    for n in range(N + 4):
                        if n < N:
                            A_(n)
                        if 2 <= n <= N + 1:
                            B_(n - 2)
                        if 4 <= n:
                            C_(n - 4)
                    for (orows, oc0, oc1, ot_) in outs:
                        cx.op(A, lambda e: e.copy(out=mixF.ap[0:orows, ot_, h * 64:(h + 1) * 64], in_=ob.ap[0:orows, oc0:oc1]), [ob], [mixF])

                for B in range(4):
                    ob = cx.banks[6 + ob_i[0] % 2]
                    ob_i[0] += 1
                    steps = []

                    def mk(kap, nk, ktl, lo, ntri_t, none_t, mask, vbias, vap, vtl, r0, r1):
                        pvs = [(r0, r1, sub * 128 - lo, (sub + 1) * 128 - lo, vap, vtl, 128, sub * 64, (sub + 1) * 64) for sub in range(lo // 128, 4)]
                        return dict(kap=kap, nk=nk, ktl=ktl, qc0=B * 512, nq=512, lo=lo, ntri=ntri_t, none=none_t, mask=mask, vbias=vbias,
                                    wtile=None, wlo=0, csb_lo=lo, clear=None, pvs=pvs)

                    for kt in range(4 * B + 3, -1, -1):
                        lo = max(0, kt - 4 * B) * 128
                        diag = kt >= 4 * B
                        steps.append(mk(kT.ap[0:128, c, kt * 128:(kt + 1) * 128], 128, kT, lo, NTRI, NONE_, (maskS.ap if diag else None),
                                        zero_b.ap[:, 0:1], Vb.ap[:, kt, h, :], Vb, 0, 128))
                    for j in range(2, -1, -1):
                        for kt in range(15, -1, -1):
                            steps.append(mk(kg[b].ap[0:128, j, kt * 128:(kt + 1) * 128], 128, kg[b], 0, ntri_j[j], none_j[j], None,
                                            selt.ap[:, 4 + j:5 + j], vg[b].ap[:, j, kt, hh * 64:(hh + 1) * 64], vg[b], 0, 128))
                    steps.append(mk(kT.ap[0:128, c, 2048:2128], 80, kT, 0, NTRI, NONE_, None, zero_b.ap[0:80, 0:1],
                                    Vb.ap[64:80, 16, h, :], Vb, 64, 80))
                    steps[0]["clear"] = 512
                    run_steps(steps, ob, [(128, sub * 64, (sub + 1) * 64, 4 * B + sub) for sub in range(4)])
                ob = cx.banks[6 + ob_i[0] % 2]
                ob_i[0] += 1
                steps = []
                for s in range(3):
                    a_, n_ = (0, 32) if s == 0 else ((32, 32) if s == 1 else (64, 16))
                    steps.append(dict(kap=kT.ap[0:128, c, 2048:2128], nk=80, ktl=kT, qc0=2048 + a_, nq=n_, lo=0, ntri=NTRI, none=NONE_,
                                      mask=bsu80.ap[0:80, a_:a_ + n_], vbias=zero_b.ap[0:80, 0:1], wtile=w_own[s], wlo=a_, csb_lo=0, clear=n_,
                                      pvs=[(0, 80, 0, 80, Vb.ap[0:80, 16, h, :], Vb, 80, 0, 64)]))
                    if s < 2:
                        for kt in range(15, -1, -1):
                            steps.append(dict(kap=kc.ap[0:128, s, kt * 128:(kt + 1) * 128], nk=128, ktl=kc, qc0=2048 + a_, nq=n_, lo=0, ntri=NTRI,
                                              none=NONE_, mask=None, vbias=zero_b.ap[:, 0:1], wtile=w_seg[s][kt % 3], wlo=a_, csb_lo=0, clear=None,
                                              pvs=[(0, 128, 0, 80, vc.ap[:, s, kt, hh, :], vc, 80, 0, 64)]))
                run_steps(steps, ob, [(80, 0, 64, 16)])

    def mixer_out2(self, w_out_dram, gain_idx):
        cx = self.cx
        cx.barrier()
        cx.arena_off = self.cd_mark2
        cx.h_off = cx.h_base
        for t in range(NTT):
            r = tt_rows(t)
            cx.dma(cx.sp, self.h[t].ap[0:r, :], self.hsave[t * 128:t * 128 + r, :], writes=[self.h[t]])
        wout = cx.tile([128, 8, D], BF16, "wout")
        cx.dma(cx.pool, wout.ap, w_out_dram.rearrange("(k p) n -> p k n", p=128), writes=[wout])
        self.load_gain(self.gpost, gain_idx)
        self.junk = cx.tile([128, D], F32, "junk")
        self.tmpn = cx.tile([128, D], F32, "tmpn")
        mT = [cx.tile([128, 8, 128], BF16, f"mT{i}") for i in range(2)]
        ot = [Tl(cx.sb([128, D], F32, f"ot{i}"), f"ot{i}") for i in range(2)]
        for t in range(NTT):
            r = tt_rows(t)
            bank = cx.banks[t % 2]
            pt = bank.ap.bitcast(BF16).rearrange("p (k c) -> p k c", k=8)
            for k in range(8):
                src = self.mixC if k < 4 else self.mixF
                cx.op(cx.pe, lambda e: e.transpose(out=pt[:, k, 0:r], in_=src.ap[0:r, t, (k % 4) * 128:(k % 4 + 1) * 128],
                                                   identity=self.identb.ap[0:r, 0:r]), [src, self.identb], [bank])
            m = mT[t % 2]
            cx.op(cx.act, lambda e: e.copy(out=m.ap[:, :, 0:r], in_=pt[:, :, 0:r]), [bank], [m])
            o = ot[t % 2]
            for hf in range(2):
                ob = cx.banks[2 + (2 * t + hf) % 4]
                for k in range(8):
                    cx.op(cx.pe, lambda e: e.matmul(ob.ap[0:r, :], lhsT=m.ap[:, k, 0:r], rhs=wout.ap[:, k, hf * 512:(hf + 1) * 512],
                                                    start=(k == 0), stop=(k == 7)), [m, wout], [ob])
                cx.op(cx.act, lambda e: e.copy(out=o.ap[0:r, hf * 512:(hf + 1) * 512], in_=ob.ap[0:r, :]), [ob], [o])
            self.postnorm_add([t], [o], self.gpost, 1.0)
        cx.barrier()

    def write_y(self):
        cx = self.cx
        for t in range(NTT):
            r = tt_rows(t)
            cx.dma(cx.sp, self.y[t * 128:t * 128 + r, :], self.h[t].ap[0:r, :], reads=[self.h[t]], is_output=True)


def make_consts():
    c = np.zeros((128, 1408), np.float32)
    c[:, 0:128] = np.eye(128, dtype=np.float32)
    tri = np.triu(np.ones((128, 128), np.float32))
    c[:, 128:256] = tri
    c[:, 256:384] = 1.0
    bt = np.zeros((128, 128), np.float32)
    for (a, b) in ((0, 32), (32, 64), (64, 80)):
        bt[a:b, a:b] = tri[a:b, a:b]
    c[:, 384:512] = bt
    c[64:80, 512:640] = 1.0
    c[0:32, 640:768] = 1.0
    c[32:64, 768:896] = 1.0
    c[127, 896:1024] = 1.0
    c[31, 1024:1152] = 1.0
    c[63, 1152:1280] = 1.0
    c[79, 1280:1408] = 1.0
    return c


def state_layout(a):
    return np.ascontiguousarray(a.reshape(16, 2, 64).transpose(1, 2, 0).reshape(128, 16))


def state_unlayout(a):
    return np.ascontiguousarray(a.reshape(2, 64, 16).transpose(2, 0, 1).reshape(32, 64))


def make_consts2():
    c = np.zeros((128, 896), np.float32)
    tri = np.triu(np.ones((128, 128), np.float32))
    sl = np.tril(np.ones((128, 128), np.float32), -1)
    su = np.triu(np.ones((128, 128), np.float32), 1)
    blocks = ((0, 32), (32, 64), (64, 80))
    c[:, 0:128] = sl
    for (a, b) in blocks:
        c[a:b, 128 + a:128 + b] = sl[a:b, a:b]
        c[a:b, 256 + a:256 + b] = 1.0
        c[a:b, 512 + a:512 + b] = su[a:b, a:b]
    c[:, 384:512] = su
    c[:, 640:768] = tri.T
    return c


def cd_inputs(inputs, c):
    m = {}
    m["cd_w_in"] = inputs["cd_w_in"][0]
    m["cst2"] = make_consts2()
    cw = inputs["ssd_conv_w"][0]
    cb = inputs["ssd_conv_b"][0]
    cc = np.concatenate([cw, cb[None]], 0).reshape(5, 8, 128).transpose(2, 1, 0)
    m["convw"] = cc.reshape(128, 40)
    m["ssdp"] = np.concatenate([inputs["ssd_dt_bias"][0], inputs["ssd_a_log"][0], inputs["ssd_d"][0]])[None]
    m["ssd_norm"] = inputs["ssd_norm"]
    st = inputs["state_ssd"][0, 2 * c:2 * c + 2]
    m["ssdinit"] = st.transpose(0, 3, 1, 2).reshape(2, 128, 512)
    cv = inputs["state_conv"][0, 2 * c:2 * c + 2]
    m["convinit"] = cv.reshape(2, 3, 8, 128).transpose(3, 0, 2, 1).reshape(128, 48)
    m["cd_w_out"] = inputs["cd_w_out"][0]
    m["c_sbk"] = inputs["cache_sb_k"][0, 2 * c:2 * c + 2].reshape(2, 2048, 512)
    m["c_sbv"] = inputs["cache_sb_v"][0, 2 * c:2 * c + 2].reshape(2, 2048, 512)
    return m


def ab_inputs(inputs, c):
    q = c % 4
    m = {}
    m["ab_w_in"] = inputs["ab_w_in"][0]
    m["fox_b_f"] = inputs["fox_b_f"].reshape(1, 8)
    a_re, a_im = inputs["s5_a_re"][0], inputs["s5_a_im"][0]
    ldt = np.repeat(inputs["s5_log_dt"][0][:, None], 64, axis=1)
    m["s5p"] = np.concatenate([state_layout(a_re), state_layout(a_im), state_layout(ldt)], axis=1)
    b = np.stack([inputs["s5_b_re"][0], inputs["s5_b_im"][0]], 0)
    cc = np.stack([inputs["s5_c_re"][0], inputs["s5_c_im"][0]], 0)
    bl = np.zeros((128, 16, 2, 128), np.float32)
    cl = np.zeros((128, 16, 2, 128), np.float32)
    for st in range(16):
        for gl in range(2):
            g = 2 * st + gl
            g8 = g % 8
            for ri in range(2):
                bl[g8 * 16:(g8 + 1) * 16, st, ri, gl * 64:(gl + 1) * 64] = b[ri, g].T
                cl[gl * 64:(gl + 1) * 64, st, ri, g8 * 16:(g8 + 1) * 16] = cc[ri, g].T
    m["s5b"] = bl.reshape(128, -1)
    m["s5c"] = cl.reshape(128, -1)
    dd = inputs["s5_d"][0].reshape(4, 128)
    dl = np.zeros((128, 4, 128), np.float32)
    for k in range(4):
        dl[np.arange(128), k, np.arange(128)] = dd[k]
    m["s5d"] = dl.reshape(128, -1)
    m["s5init"] = np.concatenate([state_layout(inputs["state_s5_re"][0, 2 * c]), state_layout(inputs["state_s5_im"][0, 2 * c]),
                                  state_layout(inputs["state_s5_re"][0, 2 * c + 1]), state_layout(inputs["state_s5_im"][0, 2 * c + 1])], axis=1)
    m["s5_w_glu"] = inputs["s5_w_glu"][0]
    m["s5_b_glu"] = np.ascontiguousarray(inputs["s5_b_glu"][0].reshape(4, 128).T)
    m["ab_w_out"] = inputs["ab_w_out"][0]
    sel = np.zeros((128, 32), np.float32)
    sel[:, q] = 1.0
    for j in range(3):
        sel[:, 4 + j] = 0.0 if j < q else -30000.0
        sel[:, 24 + j] = 1.0 if j < q else 0.0
        sel[:, 27 + j] = 1.0 if j == q - 1 else 0.0
        for jp in range(4):
            sel[:, 8 + j * 4 + jp] = 1.0 if (j <= jp < q) else 0.0
    for jp in range(4):
        sel[:, 20 + jp] = 1.0 if jp < q else 0.0
    m["sel"] = sel
    m["c_foxk"] = inputs["cache_fox_k"][0, 2 * c:2 * c + 2].reshape(2, 2048, 512)
    m["c_foxv"] = inputs["cache_fox_v"][0, 2 * c:2 * c + 2].reshape(2, 2048, 512)
    m["c_foxf"] = inputs["cache_fox_logf"][0, 2 * c:2 * c + 2]
    return m


def core_inputs(inputs, c):
    seq, q = c // 4, c % 4
    xin = np.concatenate([inputs["x_prompt"][seq, q * TP:(q + 1) * TP], inputs["x_sample"][2 * c],
                          inputs["x_sample"][2 * c + 1], inputs["meta_tokens"]], axis=0)
    m = {"xin": np.ascontiguousarray(xin, np.float32), "cst": make_consts()}
    m["norms"] = np.ascontiguousarray(np.concatenate(
        [inputs[k] for k in ("norm_ffn1_pre", "norm_ffn1_post", "norm_mix_pre", "norm_mix_post",
                             "norm_ffn2_pre", "norm_ffn2_post")], axis=0), np.float32)
    for f, nm in enumerate(("ffn1", "ffn2")):
        for l in range(2):
            m[f"w_gate{f}_{l}"] = inputs[f"{nm}_w_gate"][l]
            m[f"w_up{f}_{l}"] = inputs[f"{nm}_w_up"][l]
            m[f"w_down{f}_{l}"] = inputs[f"{nm}_w_down"][l]
    m.update(ab_inputs(inputs, c))
    m.update(cd_inputs(inputs, c))
    return m


def run(inputs, stop=None):
    inputs = {k: np.asarray(v) for k, v in inputs.items()}
    b = Builder(stop)
    nc = b.build()
    in_maps = []
    for c in range(NCORES):
        m = core_inputs(inputs, c)
        in_maps.append({k: np.ascontiguousarray(m[k], np.float32) for k in b.dram_in})
    res = run_bass_kernel_spmd(nc, in_maps, core_ids=list(range(NCORES)))
    return res.results


def kernel(**inputs):
    res = run(inputs)
    f32 = np.float32
    y_p = np.zeros((2, 8192, D), f32)
    y_s = np.zeros((16, 32, D), f32)
    L0 = 8208
    s5_re_p = np.zeros((1, 2, 32, 64), f32); s5_im_p = np.zeros((1, 2, 32, 64), f32)
    fox_k_p = np.zeros((1, 2, L0, 8, 64), f32); fox_v_p = np.zeros((1, 2, L0, 8, 64), f32)
    fox_f_p = np.zeros((1, 2, L0, 8), f32)
    ssd_p = np.zeros((1, 2, 8, 64, 128), f32); conv_p = np.zeros((1, 2, 3, 1024), f32)
    sb_k_p = np.zeros((1, 2, L0, 8, 64), f32); sb_v_p = np.zeros((1, 2, L0, 8, 64), f32)
    s5_re_s = np.zeros((1, 16, 32, 64), f32); s5_im_s = np.zeros((1, 16, 32, 64), f32)
    fox_k_s = np.zeros((1, 16, 32, 8, 64), f32); fox_v_s = np.zeros((1, 16, 32, 8, 64), f32)
    fox_f_s = np.zeros((1, 16, 32, 8), f32)
    ssd_s = np.zeros((1, 16, 8, 64, 128), f32); conv_s = np.zeros((1, 16, 3, 1024), f32)
    sb_k_s = np.zeros((1, 16, 32, 8, 64), f32); sb_v_s = np.zeros((1, 16, 32, 8, 64), f32)

    def hst(a):
        return a.reshape(128, 8, 64).transpose(1, 2, 0)

    for c in range(NCORES):
        r = res[c]
        seq, q = c // 4, c % 4
        y = r["y"]
        y_p[seq, q * TP:(q + 1) * TP] = y[0:TP]
        lo, hi = 16 + q * TP, 16 + (q + 1) * TP
        for dstp, dsts, key in ((fox_k_p, fox_k_s, "o_foxk"), (fox_v_p, fox_v_s, "o_foxv"), (sb_k_p, sb_k_s, "o_sbk"), (sb_v_p, sb_v_s, "o_sbv")):
            a = r[key]
            dstp[0, seq, lo:hi] = a[0:TP].reshape(TP, 8, 64)
            if q == 0:
                dstp[0, seq, 0:16] = a[2112:2128].reshape(16, 8, 64)
            dsts[0, 2 * c] = a[2048:2080].reshape(32, 8, 64)
            dsts[0, 2 * c + 1] = a[2080:2112].reshape(32, 8, 64)
        a = r["o_foxf"]
        fox_f_p[0, seq, lo:hi] = a[0:TP]
        if q == 0:
            fox_f_p[0, seq, 0:16] = a[2112:2128]
        fox_f_s[0, 2 * c] = a[2048:2080]
        fox_f_s[0, 2 * c + 1] = a[2080:2112]
        st = r["o_s5st"]
        for i in range(2):
            y_s[2 * c + i] = y[2048 + 32 * i:2080 + 32 * i]
            s5_re_s[0, 2 * c + i] = state_unlayout(st[:, 32 + 32 * i:48 + 32 * i])
            s5_im_s[0, 2 * c + i] = state_unlayout(st[:, 48 + 32 * i:64 + 32 * i])
            ssd_s[0, 2 * c + i] = hst(r["o_ssd"][1 + i])
            conv_s[0, 2 * c + i] = r["o_conv"][1 + i]
        if q == 3:
            s5_re_p[0, seq] = state_unlayout(st[:, 0:16])
            s5_im_p[0, seq] = state_unlayout(st[:, 16:32])
            ssd_p[0, seq] = hst(r["o_ssd"][0])
            conv_p[0, seq] = r["o_conv"][0]
    return (y_p, y_s, s5_re_p, s5_im_p, fox_k_p, fox_v_p, fox_f_p, ssd_p, conv_p, sb_k_p, sb_v_p,
            s5_re_s, s5_im_s, fox_k_s, fox_v_s, fox_f_s, ssd_s, conv_s, sb_k_s, sb_v_s)
```
